# Optimizing a Trainium2 kernel written in Bass

```python
import jax, jax.numpy as jnp
from jax import lax
import numpy as np

D_MODEL = 1024
BATCH = 4
SEQ = 8192
DEPTH = 4

N_EVEN = (DEPTH + 1) // 2
N_ODD = DEPTH // 2
D_CONV_BR = D_MODEL // 2
CONF_KERNEL = 31
D_HGRN = D_MODEL // 2
HGRN_HEADS = 4
HGRN_DK = D_HGRN // HGRN_HEADS
HGRN_DV = D_HGRN // HGRN_HEADS
HGRN_CHUNK = 64
D_IN = 2 * D_CONV_BR + 4 * D_HGRN
SPLITS = (D_CONV_BR, 2 * D_CONV_BR, 2 * D_CONV_BR + D_HGRN, 2 * D_CONV_BR + 2 * D_HGRN, 2 * D_CONV_BR + 3 * D_HGRN)
RWKV_HEAD = 64
RWKV_HEADS = D_MODEL // RWKV_HEAD
LORA_DECAY = 64
LORA_A = 64
LORA_GATE = 160
GN_EPS = 64e-5
D_FF = 2816
FFN_KERNEL = 3
ALPHA = (2 * DEPTH) ** 0.25
BETA = (8 * DEPTH) ** -0.25
LN_EPS = 1e-5
F32 = jnp.float32

kernel_name = 'hybrid_conv_hgrn2_rwkv7_deepnorm'


def layer_norm(x, g, b, eps=LN_EPS):
    xf = x.astype(F32)
    mu = jnp.mean(xf, -1, keepdims=True)
    var = jnp.mean(jnp.square(xf - mu), -1, keepdims=True)
    return ((xf - mu) * lax.rsqrt(var + eps)).astype(x.dtype) * g + b


def causal_dwconv(x, w, b):
    K, C = w.shape
    y = lax.conv_general_dilated(x, w[:, None, :].astype(x.dtype), window_strides=(1,),
                                 padding=((K - 1, 0),), dimension_numbers=('NWC', 'WIO', 'NWC'),
                                 feature_group_count=C)
    return y + b


def hgrn2_chunked(q, fz, v, lb):
    Bsz, T, _ = q.shape
    nc = T // HGRN_CHUNK
    lbf = lb.astype(F32)
    fzf = fz.astype(F32)
    qf = jax.nn.silu(q.astype(F32))
    log_f = jnp.logaddexp(jnp.log(lbf), jnp.log1p(-lbf) + jax.nn.log_sigmoid(fzf))
    kf = (1.0 - lbf) * jax.nn.sigmoid(-fzf)

    def to_chunks(t, d):
        return t.reshape(Bsz, nc, HGRN_CHUNK, HGRN_HEADS, d).transpose(1, 0, 3, 2, 4)

    qc, lfc, kc = to_chunks(qf, HGRN_DK), to_chunks(log_f, HGRN_DK), to_chunks(kf, HGRN_DK)
    vc = to_chunks(v.astype(F32), HGRN_DV)
    causal = jnp.tril(jnp.ones((HGRN_CHUNK, HGRN_CHUNK), bool))[:, :, None]

    def step(S, inp):
        qb, lf, kb, vb = inp
        cum = jnp.cumsum(lf, axis=2)
        rel = jnp.where(causal, cum[:, :, :, None, :] - cum[:, :, None, :, :], -jnp.inf)
        scores = jnp.einsum('bhtk,bhsk,bhtsk->bhts', qb, kb, jnp.exp(rel))
        o = (jnp.einsum('bhts,bhsv->bhtv', scores, vb)
             + jnp.einsum('bhtk,bhkv->bhtv', qb * jnp.exp(cum), S))
        last = cum[:, :, -1:, :]
        S = (jnp.exp(last[:, :, 0, :])[..., None] * S
             + jnp.einsum('bhsk,bhsv->bhkv', kb * jnp.exp(last - cum), vb))
        return S, o

    S0 = jnp.zeros((Bsz, HGRN_HEADS, HGRN_DK, HGRN_DV), F32)
    _, oc = lax.scan(step, S0, (qc, lfc, kc, vc))
    return oc.transpose(1, 0, 3, 2, 4).reshape(Bsz, T, HGRN_HEADS, HGRN_DV)


def conv_hgrn_mixer(x, w_in, b_in, conv_w, conv_b, cln_g, cln_b, lb, onorm_g, w_out):
    z = x @ w_in + b_in
    a_val, a_gate, q, fz, iv, gate = jnp.split(z, SPLITS, axis=-1)
    ua = causal_dwconv(a_val * jax.nn.sigmoid(a_gate), conv_w, conv_b)
    ua = jax.nn.silu(layer_norm(ua, cln_g, cln_b))
    ob = hgrn2_chunked(q, fz, iv, lb)
    ob = ob * lax.rsqrt(jnp.mean(jnp.square(ob), -1, keepdims=True) + 1e-6)
    ob = ob.reshape(x.shape[0], x.shape[1], D_HGRN).astype(x.dtype) * onorm_g * jax.nn.silu(gate)
    return jnp.concatenate([ua, ob], axis=-1) @ w_out


def rwkv7_mixer(x, mu, w_r, w_k, w_v, w_o, w0, w1, w2, a0, a1, a2, g1, g2, k_k, k_a, r_k, gn_g, gn_b):
    Bsz, T, D = x.shape
    xx = jnp.pad(x, ((0, 0), (1, 0), (0, 0)))[:, :-1] - x
    xr, xw, xk, xv, xa, xg = (x + xx * mu[i] for i in range(6))
    r = xr @ w_r
    k = xk @ w_k
    v = xv @ w_v
    w_log = -jax.nn.softplus(-(w0 + jnp.tanh(xw @ w1) @ w2)) - 0.5
    a = jax.nn.sigmoid(a0 + (xa @ a1) @ a2)
    g = jax.nn.sigmoid(xg @ g1) @ g2

    def heads(t):
        return t.reshape(Bsz, T, RWKV_HEADS, RWKV_HEAD).astype(F32)

    kk = heads(k * k_k)
    kk = kk / jnp.maximum(jnp.sqrt(jnp.sum(jnp.square(kk), -1, keepdims=True)), 1e-12)
    k = k * (1.0 + (a - 1.0) * k_a)
    rf, kf, vf, af = heads(r), heads(k), heads(v), heads(a)
    decay = jnp.exp(-jnp.exp(heads(w_log)))

    def step(S, inp):
        r_t, w_t, k_t, v_t, kk_t, a_t = inp
        sa = jnp.einsum('bhvk,bhk->bhv', S, -kk_t)
        S = (S * w_t[:, :, None, :] + sa[..., None] * (kk_t * a_t)[:, :, None, :]
             + v_t[..., None] * k_t[:, :, None, :])
        return S, jnp.einsum('bhvk,bhk->bhv', S, r_t)

    xs = tuple(t.transpose(1, 0, 2, 3) for t in (rf, decay, kf, vf, kk, af))
    S0 = jnp.zeros((Bsz, RWKV_HEADS, RWKV_HEAD, RWKV_HEAD), F32)
    _, y = lax.scan(step, S0, xs)
    y = y.transpose(1, 0, 2, 3)
    ym = jnp.mean(y, -1, keepdims=True)
    yv = jnp.mean(jnp.square(y - ym), -1, keepdims=True)
    yn = ((y - ym) * lax.rsqrt(yv + GN_EPS)).reshape(Bsz, T, D).astype(x.dtype) * gn_g + gn_b
    bonus = (jnp.sum(rf * kf * r_k, -1, keepdims=True) * vf).reshape(Bsz, T, D).astype(x.dtype)
    return ((yn + bonus) * g) @ w_o


def conv_ffn(x, w_up, w_gate, conv_w, conv_b, w_down):
    u = causal_dwconv(x @ w_up, conv_w, conv_b)
    return (jax.nn.gelu(u, approximate=False) * (x @ w_gate)) @ w_down


def setup_inputs(seed: int = 0) -> dict:
    key = jax.random.key(seed)
    ks = iter(jax.random.split(key, 48))

    def nrm(shape, scale):
        return jax.random.normal(next(ks), shape, F32) * scale

    def gain(shape):
        return 1.0 + nrm(shape, 0.02)

    D = D_MODEL
    return {
        'x': nrm((BATCH, SEQ, D), 1.0),
        'ln_mix_g': gain((DEPTH, D)),
        'ln_mix_b': nrm((DEPTH, D), 0.02),
        'ln_ffn_g': gain((DEPTH, D)),
        'ln_ffn_b': nrm((DEPTH, D), 0.02),
        'ev_w_in': nrm((N_EVEN, D, D_IN), D ** -0.5),
        'ev_b_in': nrm((N_EVEN, D_IN), 0.02),
        'ev_conv_w': nrm((N_EVEN, CONF_KERNEL, D_CONV_BR), CONF_KERNEL ** -0.5),
        'ev_conv_b': nrm((N_EVEN, D_CONV_BR), 0.02),
        'ev_cln_g': gain((N_EVEN, D_CONV_BR)),
        'ev_cln_b': nrm((N_EVEN, D_CONV_BR), 0.02),
        'ev_lb_logits': nrm((N_EVEN, D_HGRN), 0.5),
        'ev_onorm_g': gain((N_EVEN, D_HGRN)),
        'ev_w_out': nrm((N_EVEN, D, D), BETA * D ** -0.5),
        'rw_mu': jax.random.uniform(next(ks), (N_ODD, 6, D), F32),
        'rw_w_r': nrm((N_ODD, D, D), D ** -0.5),
        'rw_w_k': nrm((N_ODD, D, D), D ** -0.5),
        'rw_w_v': nrm((N_ODD, D, D), D ** -0.5),
        'rw_w_o': nrm((N_ODD, D, D), BETA * D ** -0.5),
        'rw_w0': jnp.linspace(-6.0, 1.0, D, dtype=F32)[None, :] + nrm((N_ODD, D), 0.1),
        'rw_w1': nrm((N_ODD, D, LORA_DECAY), D ** -0.5),
        'rw_w2': nrm((N_ODD, LORA_DECAY, D), LORA_DECAY ** -0.5),
        'rw_a0': nrm((N_ODD, D), 0.1),
        'rw_a1': nrm((N_ODD, D, LORA_A), D ** -0.5),
        'rw_a2': nrm((N_ODD, LORA_A, D), LORA_A ** -0.5),
        'rw_g1': nrm((N_ODD, D, LORA_GATE), D ** -0.5),
        'rw_g2': nrm((N_ODD, LORA_GATE, D), LORA_GATE ** -0.5),
        'rw_k_k': 0.85 + nrm((N_ODD, D), 0.02),
        'rw_k_a': gain((N_ODD, D)),
        'rw_r_k': nrm((N_ODD, RWKV_HEADS, RWKV_HEAD), 0.1),
        'rw_gn_g': gain((N_ODD, D)),
        'rw_gn_b': nrm((N_ODD, D), 0.02),
        'ff_w_up': nrm((DEPTH, D, D_FF), D ** -0.5),
        'ff_w_gate': nrm((DEPTH, D, D_FF), D ** -0.5),
        'ff_conv_w': nrm((DEPTH, FFN_KERNEL, D_FF), FFN_KERNEL ** -0.5),
        'ff_conv_b': nrm((DEPTH, D_FF), 0.02),
        'ff_w_down': nrm((DEPTH, D_FF, D), BETA * D_FF ** -0.5),
    }


def reference(x, ln_mix_g, ln_mix_b, ln_ffn_g, ln_ffn_b,
              ev_w_in, ev_b_in, ev_conv_w, ev_conv_b, ev_cln_g, ev_cln_b, ev_lb_logits, ev_onorm_g, ev_w_out,
              rw_mu, rw_w_r, rw_w_k, rw_w_v, rw_w_o, rw_w0, rw_w1, rw_w2, rw_a0, rw_a1, rw_a2,
              rw_g1, rw_g2, rw_k_k, rw_k_a, rw_r_k, rw_gn_g, rw_gn_b,
              ff_w_up, ff_w_gate, ff_conv_w, ff_conv_b, ff_w_down):
    lb_all = jnp.cumsum(jax.nn.softmax(ev_lb_logits.astype(F32), axis=0), axis=0)
    lb_all = lb_all - lb_all[0]
    for layer in range(DEPTH):
        j = layer // 2
        if layer % 2 == 0:
            y = conv_hgrn_mixer(x, ev_w_in[j], ev_b_in[j], ev_conv_w[j], ev_conv_b[j],
                                ev_cln_g[j], ev_cln_b[j], lb_all[j], ev_onorm_g[j], ev_w_out[j])
        else:
            y = rwkv7_mixer(x, rw_mu[j], rw_w_r[j], rw_w_k[j], rw_w_v[j], rw_w_o[j],
                            rw_w0[j], rw_w1[j], rw_w2[j], rw_a0[j], rw_a1[j], rw_a2[j],
                            rw_g1[j], rw_g2[j], rw_k_k[j], rw_k_a[j], rw_r_k[j], rw_gn_g[j], rw_gn_b[j])
        x = layer_norm(ALPHA * x + y, ln_mix_g[layer], ln_mix_b[layer])
        f = conv_ffn(x, ff_w_up[layer], ff_w_gate[layer], ff_conv_w[layer], ff_conv_b[layer], ff_w_down[layer])
        x = layer_norm(ALPHA * x + f, ln_ffn_g[layer], ln_ffn_b[layer])
    return x
```

```python
import contextlib
import os
import numpy as np
import concourse.bass as bass
import concourse.mybir as mybir
from concourse.bass_utils import run_bass_kernel_spmd

F32 = mybir.dt.float32
BF16 = mybir.dt.bfloat16
ALU = mybir.AluOpType
AF = mybir.ActivationFunctionType

D = 1024
KC = D // 128
DFF = 2816
FB = DFF // 128
DEPTH = 4
ALPHA = (2 * DEPTH) ** 0.25
LN_EPS = 1e-5


class _Rec:
    def __init__(self):
        self.call = None

    def __getattr__(self, name):
        def f(*a, **k):
            assert self.call is None, "op lambda must make exactly one engine call"
            self.call = (name, a, k)
            return self
        return f


class Prog:
    ENGS = ("pe", "act", "dve", "pool", "sp")
    EPOCH = 16000
    NEP = dict(pe=8, act=5, dve=6, pool=4, sp=1)
    NDMA = 24
    NCC = 16

    def __init__(self, nc):
        self.nc = nc
        self.ops = []
        self.last_w = {}
        self.readers = {}
        self.gstack = contextlib.ExitStack()
        self.stack = None
        self.ntile = 0
        self.known = set()
        self.children = {}
        self.bank_of = {}
        self.bank_last = {}
        self.prefix = ""
        self.phase_start = 0
        self.nphase = 0
        self.cnt = {e: 0 for e in self.ENGS}
        self.ndma = 0
        self.ncc = 0
        self.dma_prev = {}
        self.sems = {}
        g = self.gstack
        for e in self.ENGS:
            for ep in range(self.NEP[e]):
                self.sems[(e, ep)] = g.enter_context(nc.semaphore(f"s_{e}_{ep}"))
        for i in range(self.NDMA):
            self.sems[("dma", i)] = g.enter_context(nc.semaphore(f"s_dma_{i}"))
        for i in range(self.NCC):
            self.sems[("cc", i)] = g.enter_context(nc.semaphore(f"s_cc_{i}"))
        self.bar = g.enter_context(nc.semaphore("s_bar"))
        self.stats = dict(n_ops=0)

    def begin_phase(self, prefix=""):
        self.stack = contextlib.ExitStack()
        self.prefix = prefix
        self.phase_start = len(self.ops)
        self.bank_of = {}
        self.bank_last = {}

    def end_phase(self):
        self._emit_phase()
        self.stack.close()
        self.stack = None

    def finish(self):
        self.gstack.close()
        self.stats = dict(n_ops=len(self.ops), milestones=dict(self.cnt), ndma=self.ndma, ncc=self.ncc)

    def sb(self, shape, dt, name=None):
        self.ntile += 1
        name = "sb_" + self.prefix + (name or f"t{self.ntile}")
        return self.stack.enter_context(self.nc.sbuf_tensor(name, list(shape), dt))

    def ps(self, shape, dt=F32, name=None, keys=None):
        self.ntile += 1
        nm = name or f"p{self.ntile}"
        for k in (keys if keys is not None else [nm]):
            k = k if isinstance(k, tuple) else (k,)
            self.bank_of[k] = nm
        return self.stack.enter_context(self.nc.psum_tensor("ps_" + self.prefix + nm, list(shape), dt))

    def _banks(self, keys):
        out = set()
        for k in keys:
            for i in range(1, len(k) + 1):
                if k[:i] in self.bank_of:
                    out.add(self.bank_of[k[:i]])
        return out

    def _norm(self, k):
        k = k if isinstance(k, tuple) else (k,)
        if k not in self.known:
            self.known.add(k)
            for i in range(1, len(k)):
                self.children.setdefault(k[:i], set()).add(k)
        return k

    def _related(self, k):
        for i in range(1, len(k) + 1):
            yield k[:i]
        for ext in self.children.get(k, ()):
            yield ext

    def op(self, eng, fn, reads=(), writes=(), dma=False, cc=False):
        idx = len(self.ops)
        deps = set()
        reads = [self._norm(k) for k in reads]
        writes = [self._norm(k) for k in writes]
        for k in reads:
            for r in self._related(k):
                if r in self.last_w:
                    deps.add(self.last_w[r])
        for k in writes:
            for r in self._related(k):
                if r in self.last_w:
                    deps.add(self.last_w[r])
                for x in self.readers.get(r, ()):
                    deps.add(x)
        for k in writes:
            self.last_w[k] = idx
            self.readers[k] = []
            for ext in self.children.get(k, ()):
                self.last_w.pop(ext, None)
                self.readers[ext] = []
        for k in reads:
            self.readers.setdefault(k, []).append(idx)
        for bk in self._banks(reads + writes):
            bl = self.bank_last.setdefault(bk, {})
            for e2, i2 in bl.items():
                if e2 != eng:
                    deps.add(i2)
            bl[eng] = idx
        deps.discard(idx)
        deps = {d for d in deps if d >= self.phase_start}
        rec = _Rec()
        fn(rec)
        assert rec.call is not None
        self.ops.append(dict(eng=eng, call=rec.call, deps=deps, dma=dma or cc, cc=cc))
        return idx

    def pe(self, fn, r=(), w=()):
        return self.op("pe", fn, r, w)

    def act(self, fn, r=(), w=()):
        return self.op("act", fn, r, w)

    def dve(self, fn, r=(), w=()):
        return self.op("dve", fn, r, w)

    def pool(self, fn, r=(), w=()):
        return self.op("pool", fn, r, w)

    def dma(self, eng, fn, r=(), w=()):
        return self.op(eng, fn, r, w, dma=True)

    def cc(self, fn, r=(), w=()):
        return self.op("pool", fn, r, w, cc=True)

    def _emit_phase(self):
        nc = self.nc
        ops = self.ops
        lo = self.phase_start
        idxs = range(lo, len(ops))
        for i in idxs:
            o = ops[i]
            best = {}
            keep = set()
            for d in o["deps"]:
                p = ops[d]
                if p["dma"]:
                    keep.add(d)
                elif best.get(p["eng"], -1) < d:
                    best[p["eng"]] = d
            keep.update(best.values())
            o["deps"] = keep
        needed = set()
        for i in idxs:
            o = ops[i]
            for d in o["deps"]:
                p = ops[d]
                if p["dma"]:
                    needed.add(d)
                elif p["eng"] == o["eng"] and o["eng"] == "pe" and not o["dma"]:
                    continue
                else:
                    needed.add(d)
        per_eng = {e: [i for i in idxs if ops[i]["eng"] == e] for e in self.ENGS}
        last_compute = {}
        for e in self.ENGS:
            for i in reversed(per_eng[e]):
                if not ops[i]["dma"]:
                    last_compute[e] = i
                    needed.add(i)
                    break
        for i in idxs:
            o = ops[i]
            if o["cc"]:
                assert self.ncc < self.NCC
                o["sig"] = ("cc", self.ncc, 1)
                o["prev_same_slot"] = None
                self.ncc += 1
            elif o["dma"]:
                slot = self.ndma % self.NDMA
                o["sig"] = ("dma", slot, 16 * (self.ndma // self.NDMA + 1))
                o["prev_same_slot"] = self.dma_prev.get(slot)
                self.dma_prev[slot] = i
                self.ndma += 1
            elif i in needed:
                c = self.cnt[o["eng"]]
                assert c // self.EPOCH < self.NEP[o["eng"]], "out of semaphore epochs"
                o["sig"] = (o["eng"], c // self.EPOCH, c % self.EPOCH + 1)
                self.cnt[o["eng"]] = c + 1
            else:
                o["sig"] = None
        sems = self.sems
        self.nphase += 1
        nph = self.nphase
        bar = self.bar

        def run_engine(ename, eng):
            seen = {}

            def wait(sig):
                key = (sig[0], sig[1])
                if seen.get(key, 0) >= sig[2]:
                    return
                eng.wait_ge(sems[key], sig[2])
                seen[key] = sig[2]

            for i in per_eng[ename]:
                o = ops[i]
                best = {}
                for d in o["deps"]:
                    p = ops[d]
                    if (not p["dma"]) and p["eng"] == ename and ename == "pe" and not o["dma"]:
                        continue
                    sg = p["sig"]
                    kk = (sg[0], sg[1])
                    if best.get(kk, 0) < sg[2]:
                        best[kk] = sg[2]
                for kk in sorted(best):
                    wait((kk[0], kk[1], best[kk]))
                if o["dma"] and o["prev_same_slot"] is not None and o["prev_same_slot"] >= lo:
                    wait(ops[o["prev_same_slot"]]["sig"])
                call = o["call"]
                ins = getattr(eng, call[0])(*call[1], **call[2])
                sig = o["sig"]
                if sig is not None:
                    if sig[0] == "dma":
                        ins.then_inc(sems[("dma", sig[1])], 16)
                    elif sig[0] == "cc":
                        ins.then_inc(sems[("cc", sig[1])])
                    else:
                        ins.then_inc(sems[(sig[0], sig[1])], 1)
            for i in per_eng[ename]:
                if ops[i]["dma"]:
                    wait(ops[i]["sig"])
            if ename in last_compute:
                wait(ops[last_compute[ename]]["sig"])
            eng.sem_inc(bar, 1)
            eng.wait_ge(bar, len(self.ENGS) * nph)

        with nc.Block() as block:
            @block.tensor
            def _(e):
                run_engine("pe", e)

            @block.scalar
            def _(e):
                run_engine("act", e)

            @block.vector
            def _(e):
                run_engine("dve", e)

            @block.gpsimd
            def _(e):
                run_engine("pool", e)

            @block.sync
            def _(e):
                run_engine("sp", e)


class Stager:
    def __init__(self, P, width, nbuf=2):
        self.P = P
        self.bufs = [P.sb([128, width], F32, f"stage{i}") for i in range(nbuf)]
        self.n = 0

    def load(self, dst_ap, src_ap, n, dkey, eng=None, a=1):
        P = self.P
        i = self.n % len(self.bufs)
        eng = eng or ("pool", "act", "dve")[self.n % 3]
        self.n += 1
        st = self.bufs[i]
        sv = st[:, 0:n] if a == 1 else st[:, 0:n].rearrange("p (a d) -> p a d", a=a)
        P.dma("sp", lambda e: e.dma_start(out=sv, in_=src_ap), w=[("stage", i)])
        cast_op(P, eng, dst_ap, sv, [("stage", i)], [dkey])


def cast_op(P, eng, dst, src, r, w):
    if eng == "act":
        P.op("act", lambda e: e.activation(out=dst, in_=src, func=AF.Identity), r, w)
    else:
        P.op(eng, lambda e: e.tensor_copy(out=dst, in_=src), r, w)


def _mm(out, lhsT, rhs, start, stop):
    return lambda e: e.matmul(out, lhsT, rhs, start=start, stop=stop)


def build_ffn(nc, P, NT, TT, io):
    HALO = 2
    ntiles = NT // TT
    xk = io.get("xin_key", ("xin_ext",))
    ok = io.get("out_key", ("outdram",))
    xin = io["xin"]
    xin_v = xin.rearrange("(kc p) t -> p kc t", p=128)
    out_v = io["out"].rearrange("(kc p) t -> p kc t", p=128)

    wup = P.sb([128, KC, DFF], BF16, "wup")
    wgt = P.sb([128, KC, DFF], BF16, "wgt")
    wdn = P.sb([128, FB, D], BF16, "wdn")
    cw = P.sb([128, FB * 3], F32, "cw")
    cb = P.sb([128, FB], F32, "cb")
    lng = P.sb([128, KC], F32, "lng")
    lnb = P.sb([128, KC], F32, "lnb")
    ones = P.sb([128, 128], F32, "ones")
    carry = P.sb([128, FB, 2], F32, "carry")
    xbf = [P.sb([128, KC, TT], BF16, f"xbf{i}") for i in range(2)]
    xhalo = P.sb([128, KC, HALO], BF16, "xhalo")
    xres = [P.sb([128, KC, TT], F32, f"xres{i}") for i in range(2)]
    hT = P.sb([128, FB, TT], BF16, "hT")
    NQ = 3
    usb = [P.sb([128, TT + 2], F32, f"usb{i}") for i in range(NQ)]
    t1 = [P.sb([128, TT], F32, f"t1_{i}") for i in range(NQ)]
    gg = [P.sb([128, TT], F32, f"gg{i}") for i in range(NQ)]
    gsb = [P.sb([128, TT], BF16, f"gsb{i}") for i in range(NQ)]
    sq = [P.sb([128, TT], F32, f"sq{i}") for i in range(2)]
    mean_sb = P.sb([128, TT], F32, "mean_sb")
    var_sb = P.sb([128, TT], F32, "var_sb")
    rstd_sb = P.sb([128, TT], F32, "rstd_sb")
    yt = [P.sb([128, TT], F32, f"yt{i}") for i in range(2)]

    up_ps = [P.ps([128, TT], F32, f"up_ps{i}", keys=[("up_ps", i)]) for i in range(2)]
    gt_ps = [P.ps([128, TT], F32, f"gt_ps{i}", keys=[("gt_ps", i)]) for i in range(2)]
    o_ps = [P.ps([128, TT], F32, f"o_ps{i}", keys=[("o_ps", i)]) for i in range(2)]
    mean_ps = P.ps([128, TT], F32, "mean_ps")
    msq_ps = P.ps([128, TT], F32, "msq_ps")

    P.dma("sp", lambda e: e.dma_start(out=cw[:], in_=io["cw"]), w=["cw"])
    P.dma("sp", lambda e: e.dma_start(out=cb[:], in_=io["cb"]), w=["cb"])
    P.dma("sp", lambda e: e.dma_start(out=lng[:], in_=io["lng"]), w=["lng"])
    P.dma("sp", lambda e: e.dma_start(out=lnb[:], in_=io["lnb"]), w=["lnb"])
    P.pool(lambda e: e.memset(ones[:], 1.0 / D), w=["ones"])
    xh32 = P.sb([128, KC, HALO], F32, "xh32")
    P.dma("sp", lambda e: e.dma_start(out=xh32[:], in_=xin_v[:, :, 0:HALO]), r=[xk], w=["xh32"])
    P.pool(lambda e: e.tensor_copy(out=xhalo[:], in_=xh32[:]), r=["xh32"], w=["xhalo"])

    def load_xres(j):
        s = j % 2
        P.dma("sp", lambda e: e.dma_start(out=xres[s][:], in_=xin_v[:, :, HALO + j * TT:HALO + (j + 1) * TT]),
              r=[xk], w=[("xres", s)])
        P.pool(lambda e: e.tensor_copy(out=xbf[s][:], in_=xres[s][:]), r=[("xres", s)], w=[("xbf", s)])

    def load_x(j):
        pass

    load_xres(0)
    stg = Stager(P, 1024, nbuf=3)
    wupv = io["w_up"].rearrange("(kc p) f -> p kc f", p=128)
    wgtv = io["w_gate"].rearrange("(kc p) f -> p kc f", p=128)
    wdnv = io["w_down"].rearrange("(fb p) d -> p fb d", p=128)
    pieces = [(0, 1024), (1024, 1024), (2048, DFF - 2048)]
    for kc in range(KC):
        for (c0, cn) in pieces:
            stg.load(wup[:, kc, c0:c0 + cn], wupv[:, kc, c0:c0 + cn], cn, ("wup", kc, c0))
    for kc in range(KC):
        for (c0, cn) in pieces:
            stg.load(wgt[:, kc, c0:c0 + cn], wgtv[:, kc, c0:c0 + cn], cn, ("wgt", kc, c0))
    for fb in range(FB):
        stg.load(wdn[:, fb, :], wdnv[:, fb, :], D, ("wdn", fb // 2, fb % 2))

    hp = up_ps[1]
    for fb in range(FB):
        for kc in range(KC):
            P.pe(_mm(hp[:, fb * 2:fb * 2 + 2], wup[:, kc, fb * 128:(fb + 1) * 128], xhalo[:, kc, :],
                     kc == 0, kc == KC - 1),
                 r=[("wup", kc), "xhalo"], w=[("up_ps", 1)])
    P.act(lambda e: e.activation(out=carry[:].rearrange("p a b -> p (a b)"), in_=hp[:, 0:FB * 2], func=AF.Identity),
          r=[("up_ps", 1)], w=["carry"])

    nblk = 0
    for j in range(ntiles):
        s = j % 2
        if j + 1 < ntiles:
            load_x(j + 1)
        xb = xbf[s]
        for fb in range(FB):
            b = nblk % 2
            nblk += 1
            fsl = slice(fb * 128, (fb + 1) * 128)
            for kc in range(KC):
                P.pe(_mm(up_ps[b][:], wup[:, kc, fsl], xb[:, kc, :], kc == 0, kc == KC - 1),
                     r=[("wup", kc), ("xbf", s)], w=[("up_ps", b)])
            for kc in range(KC):
                P.pe(_mm(gt_ps[b][:], wgt[:, kc, fsl], xb[:, kc, :], kc == 0, kc == KC - 1),
                     r=[("wgt", kc), ("xbf", s)], w=[("gt_ps", b)])
            q = (nblk - 1) % NQ
            u = usb[q]
            P.pool(lambda e, u=u, fb=fb: e.tensor_copy(out=u[:, 0:2], in_=carry[:, fb, :]),
                   r=[("carry", fb)], w=[("usb", q)])
            P.act(lambda e, u=u, b=b: e.activation(out=u[:, 2:2 + TT], in_=up_ps[b][:], func=AF.Identity),
                  r=[("up_ps", b)], w=[("usb", q)])
            P.act(lambda e, q=q, b=b: e.activation(out=gsb[q][:], in_=gt_ps[b][:], func=AF.Identity),
                  r=[("gt_ps", b)], w=[("gsb", q)])
            P.pool(lambda e, u=u, fb=fb: e.tensor_copy(out=carry[:, fb, :], in_=u[:, TT:TT + 2]),
                   r=[("usb", q)], w=[("carry", fb)])
            tt = t1[q]
            P.dve(lambda e, u=u, tt=tt, fb=fb: e.tensor_scalar(
                out=tt[:], in0=u[:, 2:2 + TT], scalar1=cw[:, fb * 3 + 2:fb * 3 + 3], scalar2=cb[:, fb:fb + 1],
                op0=ALU.mult, op1=ALU.add), r=[("usb", q), "cw", "cb"], w=[("t1", q)])
            P.dve(lambda e, u=u, tt=tt, fb=fb: e.scalar_tensor_tensor(
                out=tt[:], in0=u[:, 1:1 + TT], scalar=cw[:, fb * 3 + 1:fb * 3 + 2], in1=tt[:],
                op0=ALU.mult, op1=ALU.add), r=[("usb", q), ("t1", q)], w=[("t1", q)])
            P.dve(lambda e, u=u, tt=tt, fb=fb: e.scalar_tensor_tensor(
                out=tt[:], in0=u[:, 0:TT], scalar=cw[:, fb * 3:fb * 3 + 1], in1=tt[:],
                op0=ALU.mult, op1=ALU.add), r=[("usb", q), ("t1", q)], w=[("t1", q)])
            g = gg[q]
            P.act(lambda e, g=g, tt=tt: e.activation(out=g[:], in_=tt[:], func=AF.Gelu),
                  r=[("t1", q)], w=[("gg", q)])
            P.dve(lambda e, g=g, q=q, fb=fb: e.tensor_tensor(out=hT[:, fb, :], in0=g[:], in1=gsb[q][:], op=ALU.mult),
                  r=[("gg", q), ("gsb", q)], w=[("hT", fb)])
        if j + 1 < ntiles:
            load_xres(j + 1)
        xr = xres[s]
        for db in range(KC):
            b = db % 2
            dsl = slice(db * 128, (db + 1) * 128)
            for fb in range(FB):
                P.pe(_mm(o_ps[b][:], wdn[:, fb, dsl], hT[:, fb, :], fb == 0, fb == FB - 1),
                     r=[("wdn", fb // 2), ("hT", fb)], w=[("o_ps", b)])
            P.dve(lambda e, xr=xr, db=db, b=b: e.scalar_tensor_tensor(
                out=xr[:, db, :], in0=xr[:, db, :], scalar=float(ALPHA), in1=o_ps[b][:],
                op0=ALU.mult, op1=ALU.add), r=[("xres", s, db), ("o_ps", b)], w=[("xres", s, db)])
        emit_ln(P, xr, ("xres", s), KC, TT, ones, sq, mean_ps, msq_ps, mean_sb, var_sb, rstd_sb, yt,
                lng, lnb, xr, ("xres", s), LN_EPS)
        P.dma("sp", lambda e, j=j, xr=xr: e.dma_start(out=out_v[:, :, j * TT:(j + 1) * TT], in_=xr[:]),
              r=[("xres", s)], w=[ok + (j,)])


def emit_ln(P, xr, xkey, nch, TT, ones, sq, mean_ps, msq_ps, mean_sb, var_sb, rstd_sb, yt, lng, lnb, osb, okey,
            eps, silu=False, lkeys=("lng", "lnb", "ones")):
    for c in range(nch):
        P.pe(_mm(mean_ps[:], ones[:], xr[:, c, :], c == 0, c == nch - 1),
             r=[lkeys[2], xkey + (c,)], w=["mean_ps"])
    for c in range(nch):
        b = c % 2
        P.act(lambda e, c=c, b=b: e.activation(out=sq[b][:], in_=xr[:, c, :], func=AF.Square),
              r=[xkey + (c,)], w=[("sq", b)])
        P.pe(_mm(msq_ps[:], ones[:], sq[b][:], c == 0, c == nch - 1),
             r=[lkeys[2], ("sq", b)], w=["msq_ps"])
    P.act(lambda e: e.activation(out=mean_sb[:], in_=mean_ps[:], func=AF.Identity), r=["mean_ps"], w=["mean_sb"])
    P.dve(lambda e: e.tensor_tensor(out=var_sb[:], in0=mean_sb[:], in1=mean_sb[:], op=ALU.mult),
          r=["mean_sb"], w=["var_sb"])
    P.dve(lambda e: e.tensor_tensor(out=var_sb[:], in0=msq_ps[:], in1=var_sb[:], op=ALU.subtract),
          r=["msq_ps", "var_sb"], w=["var_sb"])
    P.dve(lambda e: e.tensor_scalar(out=var_sb[:], in0=var_sb[:], scalar1=0.0, scalar2=float(eps),
                                    op0=ALU.max, op1=ALU.add), r=["var_sb"], w=["var_sb"])
    P.act(lambda e: e.activation(out=var_sb[:], in_=var_sb[:], func=AF.Sqrt), r=["var_sb"], w=["var_sb"])
    P.dve(lambda e: e.reciprocal(out=rstd_sb[:], in_=var_sb[:]), r=["var_sb"], w=["rstd_sb"])
    for c in range(nch):
        b = c % 2
        y = yt[b]
        P.dve(lambda e, y=y, c=c: e.tensor_tensor(out=y[:], in0=xr[:, c, :], in1=mean_sb[:], op=ALU.subtract),
              r=[xkey + (c,), "mean_sb"], w=[("yt", b)])
        P.dve(lambda e, y=y: e.tensor_tensor(out=y[:], in0=y[:], in1=rstd_sb[:], op=ALU.mult),
              r=[("yt", b), "rstd_sb"], w=[("yt", b)])
        P.act(lambda e, y=y, c=c: e.activation(out=osb[:, c, :], in_=y[:], func=(AF.Silu if silu else AF.Identity),
                                               scale=lng[:, c:c + 1], bias=lnb[:, c:c + 1]),
              r=[("yt", b), lkeys[0], lkeys[1]], w=[okey + (c,)])


def vec_layout(v, nb):
    return np.ascontiguousarray(np.asarray(v, np.float32).reshape(nb, 128).T)


def build_ffn_nc(NT, TT):
    nc = bass.Bass("TRN2", target_bir_lowering=False)
    io = {}
    io["xin"] = nc.dram_tensor("xin", [D, 2 + NT], F32, kind="ExternalInput").ap()
    io["w_up"] = nc.dram_tensor("w_up", [D, DFF], F32, kind="ExternalInput").ap()
    io["w_gate"] = nc.dram_tensor("w_gate", [D, DFF], F32, kind="ExternalInput").ap()
    io["w_down"] = nc.dram_tensor("w_down", [DFF, D], F32, kind="ExternalInput").ap()
    io["cw"] = nc.dram_tensor("cw", [128, FB * 3], F32, kind="ExternalInput").ap()
    io["cb"] = nc.dram_tensor("cb", [128, FB], F32, kind="ExternalInput").ap()
    io["lng"] = nc.dram_tensor("lng", [128, KC], F32, kind="ExternalInput").ap()
    io["lnb"] = nc.dram_tensor("lnb", [128, KC], F32, kind="ExternalInput").ap()
    io["out"] = nc.dram_tensor("out", [D, NT], F32, kind="ExternalOutput").ap()
    P = Prog(nc)
    P.begin_phase()
    build_ffn(nc, P, NT, TT, io)
    P.end_phase()
    P.finish()
    return nc, P


def ffn_inputs(xin_T, w_up, w_gate, conv_w, conv_b, w_down, g, b):
    cwl = np.stack([vec_layout(conv_w[t], FB) for t in range(3)], axis=-1).reshape(128, FB * 3)
    return dict(xin=np.ascontiguousarray(xin_T, dtype=np.float32),
                w_up=np.ascontiguousarray(w_up), w_gate=np.ascontiguousarray(w_gate),
                w_down=np.ascontiguousarray(w_down), cw=np.ascontiguousarray(cwl),
                cb=vec_layout(conv_b, FB), lng=vec_layout(g, KC), lnb=vec_layout(b, KC))


STAGE = 9
CH = 64
HAL_E = 30


def build_even(nc, P, NT, TT, io, state_only=False):
    HALO = HAL_E
    ntiles = NT // TT
    xk = io.get("xin_key", ("xin_ext",))
    ok = io.get("out_key", ("outdram",))
    full = not state_only
    nblk = TT // 128
    xin_v = io["xin"].rearrange("(kc p) t -> p kc t", p=128)
    out_v = io["out"].rearrange("(kc p) t -> p kc t", p=128)
    win = P.sb([128, KC, 3072], BF16, "win")
    wout = P.sb([128, KC, D], BF16, "wout")
    bin_ = P.sb([128, 24], F32, "bin")
    nbin = P.sb([128, 24], F32, "nbin")
    convw = P.sb([128, 4 * 31], F32, "convw")
    convb = P.sb([128, 4], F32, "convb")
    clng = P.sb([128, 4], F32, "clng")
    clnb = P.sb([128, 4], F32, "clnb")
    lb0 = P.sb([128, 4], F32, "lb0")
    lb1 = P.sb([128, 4], F32, "lb1")
    lbsel = P.sb([128, 1], F32, "lbsel")
    lb = P.sb([128, 4], F32, "lb")
    oml = P.sb([128, 4], F32, "oml")
    ong = P.sb([128, 4], F32, "ong")
    hsc = P.sb([128, 1], F32, "hsc")
    lng = P.sb([128, KC], F32, "lng")
    lnb = P.sb([128, KC], F32, "lnb")
    bivbc = P.sb([128, 512], F32, "bivbc")
    ones512 = P.sb([128, 128], F32, "ones512")
    ones128 = P.sb([128, 128], F32, "ones128")
    ones1024 = P.sb([128, 128], F32, "ones1024")
    ident = P.sb([128, 128], BF16, "ident")
    mask = P.sb([128, 128], F32, "mask")
    rmask = P.sb([128, TT], F32, "rmask")
    S = P.sb([128, 4, 128], F32, "S")
    Sbf = P.sb([128, 4, 128], BF16, "Sbf")
    xbf = [P.sb([128, KC, TT], BF16, f"xbf{i}") for i in range(2)]
    xhalo = P.sb([128, KC, HALO], BF16, "xhalo")
    xres = [P.sb([128, KC, TT], F32, f"xres{i}") for i in range(2)]
    glu = [P.sb([128, 4, HALO + TT], BF16, f"glu{i}") for i in range(2)]
    convd = P.sb([128, 4 * 31, 128], BF16, "convd") if not state_only else None
    sg = [P.sb([128, TT], F32, f"sg{i}") for i in range(2)]
    cacc = P.sb([128, 4, TT], F32, "cacc")
    cat = P.sb([128, KC, TT], BF16, "cat")
    qs = P.sb([128, TT], F32, "qs")
    sgp = P.sb([128, TT], F32, "sgp")
    sgn = P.sb([128, TT], F32, "sgn")
    logf = P.sb([128, TT], F32, "logf")
    cum = P.sb([128, TT], F32, "cum")
    eq = P.sb([128, TT], F32, "eq")
    en = P.sb([128, TT], F32, "en")
    gC = P.sb([128, 4, TT // CH], F32, "gC")
    qt = P.sb([128, 4, TT], BF16, "qt")
    kt = P.sb([128, 4, TT], F32, "kt")
    ktb = P.sb([128, 4, TT], BF16, "ktb")
    kh = P.sb([128, 4, TT], BF16, "kh")
    khT = P.sb([128, 4, 128], BF16, "khT")
    vtm = P.sb([128, 512], BF16, "vtm")
    pT = P.sb([128, 4, 128], BF16, "pT")
    osb = P.sb([128, 4, TT], F32, "osb")
    sqo = P.sb([128, TT], F32, "sqo")
    gsl = P.sb([128, TT], F32, "gsl")
    sq = [P.sb([128, TT], F32, f"sq{i}") for i in range(2)]
    mean_sb = P.sb([128, TT], F32, "mean_sb")
    var_sb = P.sb([128, TT], F32, "var_sb")
    rstd_sb = P.sb([128, TT], F32, "rstd_sb")
    yt = [P.sb([128, TT], F32, f"yt{i}") for i in range(2)]

    zps = [P.ps([128, 2, 256], F32, f"zps{i}", keys=[("zps", 2 * i), ("zps", 2 * i + 1)]) for i in range(2)]
    vtm_ps = P.ps([128, 512], F32, "vtm_ps")
    sc_ps = P.ps([128, 4, 128], F32, "sc_ps")
    o_ps = P.ps([128, 4, 128], F32, "o_ps")
    st_ps = P.ps([128, 4, 128], F32, "st_ps")
    tr_ps = P.ps([128, 4, 128], BF16, "tr_ps")
    stat_ps = P.ps([128, 2, 256], F32, "stat_ps", keys=["mean_ps", "msq_ps"])
    mean_ps = stat_ps[:, 0, 0:TT]
    msq_ps = stat_ps[:, 1, 0:TT]

    for nm, t in (("bin", bin_), ("convw", convw), ("convb", convb), ("clng", clng), ("clnb", clnb),
                  ("lb0", lb0), ("lb1", lb1), ("lbsel", lbsel), ("ong", ong), ("hsc", hsc), ("lng", lng),
                  ("lnb", lnb), ("bivbc", bivbc), ("mask", mask), ("rmask", rmask), ("S", S)):
        P.dma("sp", lambda e, t=t, nm=nm: e.dma_start(out=t[:], in_=io[nm]),
              r=([io.get("S_key", ("S_ext",))] if nm == "S" else []), w=[nm])
    id32 = P.sb([128, 128], F32, "id32")
    P.dma("sp", lambda e: e.dma_start(out=id32[:], in_=io["ident"]), w=["id32"])
    P.pool(lambda e: e.tensor_copy(out=ident[:], in_=id32[:]), r=["id32"], w=["ident"])
    if not state_only:
        for q_ in range(4 * 31):
            P.op(("pool", "dve")[q_ % 2], lambda e, q_=q_: e.tensor_scalar(
                out=convd[:, q_, :], in0=id32[:], scalar1=convw[:, q_:q_ + 1], scalar2=None, op0=ALU.mult),
                ["id32", "convw"], [("convd", q_)])
    P.pool(lambda e: e.memset(ones512[:], 1.0 / 512), w=["ones512"])
    P.pool(lambda e: e.memset(ones128[:], 1.0 / 128), w=["ones128"])
    P.pool(lambda e: e.memset(ones1024[:], 1.0 / D), w=["ones1024"])
    xh32 = P.sb([128, KC, HALO], F32, "xh32")
    P.dma("sp", lambda e: e.dma_start(out=xh32[:], in_=xin_v[:, :, 0:HALO]), r=[xk], w=["xh32"])
    P.pool(lambda e: e.tensor_copy(out=xhalo[:], in_=xh32[:]), r=["xh32"], w=["xhalo"])

    def load_xres(j):
        s = j % 2
        P.dma("sp", lambda e: e.dma_start(out=xres[s][:], in_=xin_v[:, :, HALO + j * TT:HALO + (j + 1) * TT]),
              r=[xk], w=[("xres", s)])
        P.pool(lambda e: e.tensor_copy(out=xbf[s][:], in_=xres[s][:]), r=[("xres", s)], w=[("xbf", s)])

    def load_x(j):
        pass

    load_xres(0)
    stg = Stager(P, 3072)
    winv = io["w_in"].rearrange("(kc p) f -> p kc f", p=128)
    woutv = io["w_out"].rearrange("(kc p) f -> p kc f", p=128)
    for kc in range(KC):
        stg.load(win[:, kc, :], winv[:, kc, :], 3072, ("win", kc))
    for kc in range(0, KC, 2):
        stg.load(wout[:, kc:kc + 2, :], woutv[:, kc:kc + 2, :], 2 * D, ("wout", kc // 2), a=2)
    P.dve(lambda e: e.tensor_scalar(out=nbin[:], in0=bin_[:], scalar1=-1.0, scalar2=None, op0=ALU.mult),
          r=["bin"], w=["nbin"])
    P.dve(lambda e: e.tensor_tensor(out=lb[:], in0=lb1[:], in1=lb0[:], op=ALU.subtract), r=["lb0", "lb1"], w=["lb"])
    P.act(lambda e: e.activation(out=lb[:], in_=lb[:], func=AF.Sigmoid), r=["lb"], w=["lb"])
    P.dve(lambda e: e.tensor_scalar(out=lb[:], in0=lb[:], scalar1=lbsel[:, 0:1], scalar2=None, op0=ALU.mult),
          r=["lb", "lbsel"], w=["lb"])
    P.dve(lambda e: e.tensor_scalar(out=oml[:], in0=lb[:], scalar1=-1.0, scalar2=1.0, op0=ALU.mult, op1=ALU.add),
          r=["lb"], w=["oml"])
    P.act(lambda e: e.activation(out=Sbf[:], in_=S[:], func=AF.Identity), r=["S"], w=["Sbf"])

    zcnt = [0]

    def proj(blk, rhs, n, rkeys):
        b = zcnt[0] % 4
        zcnt[0] += 1
        dst = zps[b // 2][:, b % 2, 0:n]
        for kc in range(KC):
            P.pe(_mm(dst, win[:, kc, blk * 128:(blk + 1) * 128], rhs[:, kc, :], kc == 0, kc == KC - 1),
                 r=[("win", kc)] + rkeys, w=[("zps", b)])
        return dst, ("zps", b)

    def glu_block(c, rhs, n, rkeys, dst_glu, dkey, col0, scale_ap=None):
        av, avk = proj(c, rhs, n, rkeys)
        ag, agk = proj(4 + c, rhs, n, rkeys)
        sgt = sg[c % 2]
        P.act(lambda e: e.activation(out=sgt[:, 0:n], in_=ag, func=AF.Sigmoid, bias=bin_[:, 4 + c:5 + c]),
              r=[agk, "bin"], w=[("sg", c % 2)])
        P.dve(lambda e: e.scalar_tensor_tensor(out=dst_glu[:, c, col0:col0 + n], in0=av, scalar=bin_[:, c:c + 1],
                                               in1=sgt[:, 0:n], op0=ALU.add, op1=ALU.mult),
              r=[avk, ("sg", c % 2), "bin"], w=[dkey + (c,)])
        if scale_ap is not None:
            P.dve(lambda e: e.tensor_scalar(out=dst_glu[:, c, col0:col0 + n], in0=dst_glu[:, c, col0:col0 + n],
                                            scalar1=scale_ap, scalar2=None, op0=ALU.mult),
                  r=[dkey + (c,), "hsc"], w=[dkey + (c,)])

    for c in (range(4) if full else ()):
        glu_block(c, xhalo, HALO, ["xhalo"], glu[1], ("glu", 1), TT, scale_ap=hsc[:, 0:1])

    for j in range(ntiles):
        s = j % 2
        if j + 1 < ntiles:
            load_x(j + 1)
        xb = xbf[s]
        G = glu[s]
        Gp = glu[1 - s]
        for c in (range(4) if full else ()):
            P.pool(lambda e, c=c, G=G, Gp=Gp: e.tensor_copy(out=G[:, c, 0:HALO], in_=Gp[:, c, TT:TT + HALO]),
                   r=[("glu", 1 - s, c)], w=[("glu", s, c)])
            glu_block(c, xb, TT, [("xbf", s)], G, ("glu", s), HALO)
            bq = zcnt[0] % 4
            zcnt[0] += 1
            cdst = zps[bq // 2][:, bq % 2, 0:TT]
            for tap in range(31):
                P.pe(_mm(cdst, convd[:, c * 31 + tap, :], G[:, c, tap:tap + TT], tap == 0, tap == 30),
                     r=[("convd", c * 31 + tap), ("glu", s, c)], w=[("zps", bq)])
            P.act(lambda e, c=c, cdst=cdst: e.activation(out=cacc[:, c, :], in_=cdst, func=AF.Identity,
                                                         bias=convb[:, c:c + 1]),
                  r=[("zps", bq), "convb"], w=[("cacc", c)])
        if full:
          emit_ln(P, cacc, ("cacc",), 4, TT, ones512, sq, mean_ps, msq_ps, mean_sb, var_sb, rstd_sb, yt,
                  clng, clnb, cat, ("cat",), LN_EPS, silu=True, lkeys=("clng", "clnb", "ones512"))
        for h in range(4):
            if full:
                qp, qk = proj(8 + h, xb, TT, [("xbf", s)])
                P.act(lambda e, qp=qp, h=h: e.activation(out=qs[:], in_=qp, func=AF.Silu, bias=bin_[:, 8 + h:9 + h]),
                      r=[qk, "bin"], w=["qs"])
            fp_, fk = proj(12 + h, xb, TT, [("xbf", s)])
            P.act(lambda e, fp_=fp_, h=h: e.activation(out=sgp[:], in_=fp_, func=AF.Sigmoid,
                                                       bias=bin_[:, 12 + h:13 + h]),
                  r=[fk, "bin"], w=["sgp"])
            P.act(lambda e, fp_=fp_, h=h: e.activation(out=sgn[:], in_=fp_, func=AF.Sigmoid, scale=-1.0,
                                                       bias=nbin[:, 12 + h:13 + h]),
                  r=[fk, "nbin"], w=["sgn"])
            P.dve(lambda e, h=h: e.tensor_scalar(out=logf[:], in0=sgp[:], scalar1=oml[:, h:h + 1],
                                                 scalar2=lb[:, h:h + 1], op0=ALU.mult, op1=ALU.add),
                  r=["sgp", "oml", "lb"], w=["logf"])
            P.act(lambda e: e.activation(out=logf[:], in_=logf[:], func=AF.Ln), r=["logf"], w=["logf"])
            P.dve(lambda e: e.tensor_tensor_scan(out=cum[:], data0=rmask[:], data1=logf[:], initial=0.0,
                                                 op0=ALU.mult, op1=ALU.add),
                  r=["rmask", "logf"], w=["cum"])
            if full:
                P.act(lambda e: e.activation(out=eq[:], in_=cum[:], func=AF.Exp), r=["cum"], w=["eq"])
            P.act(lambda e: e.activation(out=en[:], in_=cum[:], func=AF.Exp, scale=-1.0), r=["cum"], w=["en"])
            P.act(lambda e, h=h: e.activation(
                out=gC[:, h, :], in_=cum[:].rearrange("p (c t) -> p c t", t=CH)[:, :, CH - 1], func=AF.Exp),
                r=["cum"], w=[("gC", h)])
            if full:
                P.dve(lambda e, h=h: e.tensor_tensor(out=qt[:, h, :], in0=qs[:], in1=eq[:], op=ALU.mult),
                      r=["qs", "eq"], w=[("qt", h)])
            P.dve(lambda e, h=h: e.scalar_tensor_tensor(out=kt[:, h, :], in0=sgn[:], scalar=oml[:, h:h + 1],
                                                        in1=en[:], op0=ALU.mult, op1=ALU.mult),
                  r=["sgn", "en", "oml"], w=[("kt", h)])
            if full:
                P.act(lambda e, h=h: e.activation(out=ktb[:, h, :], in_=kt[:, h, :], func=AF.Identity),
                      r=[("kt", h)], w=[("ktb", h)])
            P.dve(lambda e, h=h: e.tensor_tensor(
                out=kh[:, h, :].rearrange("p (c t) -> p c t", t=CH),
                in0=kt[:, h, :].rearrange("p (c t) -> p c t", t=CH),
                in1=gC[:, h, :].unsqueeze(2).to_broadcast([128, TT // CH, CH]), op=ALU.mult),
                r=[("kt", h), ("gC", h)], w=[("kh", h)])
        for bi in range(nblk):
            tsl = slice(bi * 128, (bi + 1) * 128)
            for kc in range(KC):
                P.pe(_mm(vtm_ps[:], xb[:, kc, tsl], win[:, kc, 2048:2560], kc == 0, kc == KC - 1),
                     r=[("xbf", s), ("win", kc)], w=["vtm_ps"])
            P.dve(lambda e: e.tensor_tensor(out=vtm[:], in0=vtm_ps[:], in1=bivbc[:], op=ALU.add),
                  r=["vtm_ps", "bivbc"], w=["vtm"])
            for h in range(4):
                if full:
                    P.pe(_mm(sc_ps[:, h, :], ktb[:, h, tsl], qt[:, h, tsl], True, True),
                         r=[("ktb", h), ("qt", h)], w=[("sc_ps", h)])
                P.pe(lambda e, h=h, tsl=tsl: e.transpose(tr_ps[:, h, :], kh[:, h, tsl], ident[:]),
                     r=[("kh", h), "ident"], w=[("tr_ps", h)])
            if full:
                P.dve(lambda e: e.tensor_tensor(out=pT[:], in0=sc_ps[:],
                                                in1=mask[:].unsqueeze(1).to_broadcast([128, 4, 128]), op=ALU.mult),
                      r=["sc_ps", "mask"], w=["pT"])
            P.act(lambda e: e.activation(out=khT[:], in_=tr_ps[:], func=AF.Identity), r=["tr_ps"], w=["khT"])
            for h in (range(4) if full else ()):
                vh = vtm[:, h * 128:(h + 1) * 128]
                P.pe(_mm(o_ps[:, h, :], vh, pT[:, h, :], h == 0, False),
                     r=["vtm", "pT"], w=[("o_ps",)])
            for ci in range(2):
                csl = slice(ci * 64, (ci + 1) * 64)
                gcol = bi * 2 + ci
                for h in (range(4) if full else ()):
                    P.pe(_mm(o_ps[:, h, csl], Sbf[:, h, :], qt[:, h, bi * 128 + ci * 64:bi * 128 + (ci + 1) * 64],
                             False, ci == 1 and h == 3),
                         r=[("Sbf", h), ("qt", h)], w=[("o_ps",)])
                for h in range(4):
                    P.pe(_mm(st_ps[:, h, :], khT[csl, h, :], vtm[csl, h * 128:(h + 1) * 128], True, True),
                         r=["khT", "vtm"], w=[("st_ps", h)])
                for h in range(4):
                    P.dve(lambda e, h=h, gcol=gcol: e.scalar_tensor_tensor(
                        out=S[:, h, :], in0=S[:, h, :], scalar=gC[:, h, gcol:gcol + 1], in1=st_ps[:, h, :],
                        op0=ALU.mult, op1=ALU.add), r=[("S", h), ("gC", h), ("st_ps", h)], w=[("S", h)])
                    if full:
                        P.act(lambda e, h=h: e.activation(out=Sbf[:, h, :], in_=S[:, h, :], func=AF.Identity),
                              r=[("S", h)], w=[("Sbf", h)])
            if full:
                P.act(lambda e, tsl=tsl: e.activation(out=osb[:, :, tsl], in_=o_ps[:], func=AF.Identity),
                      r=["o_ps"], w=[("osb", bi)])
        for h in (range(4) if full else ()):
            if True:
                P.act(lambda e, h=h: e.activation(out=sqo[:], in_=osb[:, h, :], func=AF.Square), r=["osb"], w=["sqo"])
                P.pe(_mm(mean_ps, ones128[:], sqo[:], True, True), r=["ones128", "sqo"], w=["mean_ps"])
                P.dve(lambda e: e.tensor_scalar(out=var_sb[:], in0=mean_ps, scalar1=0.0, scalar2=1e-6,
                                                op0=ALU.max, op1=ALU.add), r=["mean_ps"], w=["var_sb"])
            if True:
                P.act(lambda e: e.activation(out=var_sb[:], in_=var_sb[:], func=AF.Sqrt), r=["var_sb"], w=["var_sb"])
                P.dve(lambda e: e.reciprocal(out=rstd_sb[:], in_=var_sb[:]), r=["var_sb"], w=["rstd_sb"])
            gp, gk = proj(20 + h, xb, TT, [("xbf", s)])
            P.act(lambda e, gp=gp, h=h: e.activation(out=gsl[:], in_=gp, func=AF.Silu, bias=bin_[:, 20 + h:21 + h]),
                  r=[gk, "bin"], w=["gsl"])
            if True:
                P.dve(lambda e, h=h: e.tensor_tensor(out=sqo[:], in0=osb[:, h, :], in1=rstd_sb[:], op=ALU.mult),
                      r=["osb", "rstd_sb"], w=["sqo"])
                P.dve(lambda e, h=h: e.scalar_tensor_tensor(out=cat[:, 4 + h, :], in0=sqo[:], scalar=ong[:, h:h + 1],
                                                            in1=gsl[:], op0=ALU.mult, op1=ALU.mult),
                      r=["sqo", "ong", "gsl"], w=[("cat", 4 + h)])
        if j + 1 < ntiles:
            load_xres(j + 1)
        xr = xres[s]
        if not full:
            continue
        for db in range(KC):
            b = zcnt[0] % 4
            zcnt[0] += 1
            dst = zps[b // 2][:, b % 2, 0:TT]
            for c in range(KC):
                P.pe(_mm(dst, wout[:, c, db * 128:(db + 1) * 128], cat[:, c, :], c == 0, c == KC - 1),
                     r=[("wout", c // 2), ("cat", c)], w=[("zps", b)])
            P.dve(lambda e, xr=xr, db=db, dst=dst: e.scalar_tensor_tensor(
                out=xr[:, db, :], in0=xr[:, db, :], scalar=float(ALPHA), in1=dst,
                op0=ALU.mult, op1=ALU.add), r=[("xres", s, db), ("zps", b)], w=[("xres", s, db)])
        emit_ln(P, xr, ("xres", s), KC, TT, ones1024, sq, mean_ps, msq_ps, mean_sb, var_sb, rstd_sb, yt,
                lng, lnb, xr, ("xres", s), LN_EPS, lkeys=("lng", "lnb", "ones1024"))
        P.dma("sp", lambda e, j=j, xr=xr: e.dma_start(out=out_v[:, :, j * TT:(j + 1) * TT], in_=xr[:]),
              r=[("xres", s)], w=[ok + (j,)])
    P.dma("sp", lambda e: e.dma_start(out=io["S_out"], in_=S[:]), r=["S"], w=[io.get("Sout_key", ("S_out",))])


def build_even_nc(NT, TT):
    nc = bass.Bass("TRN2", target_bir_lowering=False)
    io = {}

    def din(name, shape, dt=F32):
        io[name] = nc.dram_tensor(name, list(shape), dt, kind="ExternalInput").ap()

    din("xin", [D, HAL_E + NT])
    din("w_in", [D, 3072])
    din("w_out", [D, D])
    din("bin", [128, 24])
    din("convw", [128, 4 * 31])
    din("convb", [128, 4])
    din("clng", [128, 4])
    din("clnb", [128, 4])
    din("lb0", [128, 4])
    din("lb1", [128, 4])
    din("lbsel", [128, 1])
    din("ong", [128, 4])
    din("hsc", [128, 1])
    din("lng", [128, KC])
    din("lnb", [128, KC])
    din("bivbc", [128, 512])
    din("mask", [128, 128])
    din("rmask", [128, TT])
    din("ident", [128, 128])
    din("S", [128, 4, 128])
    io["out"] = nc.dram_tensor("out", [D, NT], F32, kind="ExternalOutput").ap()
    io["S_out"] = nc.dram_tensor("S_out", [128, 4, 128], F32, kind="ExternalOutput").ap()
    P = Prog(nc)
    P.begin_phase()
    build_even(nc, P, NT, TT, io)
    P.end_phase()
    P.finish()
    return nc, P


def even_consts(TT):
    i = np.arange(128)
    mask = ((i[:, None] // CH == i[None, :] // CH) & (i[:, None] <= i[None, :])).astype(np.float32)
    rmask = np.ones((128, TT), np.float32)
    rmask[:, ::CH] = 0.0
    return dict(mask=mask, rmask=rmask, ident=np.eye(128, dtype=np.float32))


def even_inputs(xin_T, w_in, b_in, conv_w, conv_b, cln_g, cln_b, lb_logits, j, onorm_g, w_out, g, b, S, first, TT):
    d = even_consts(TT)
    cwl = np.stack([vec_layout(conv_w[t], 4) for t in range(31)], axis=-1).reshape(128, 4 * 31)
    d.update(xin=np.ascontiguousarray(xin_T, dtype=np.float32), w_in=np.ascontiguousarray(w_in),
             w_out=np.ascontiguousarray(w_out), bin=vec_layout(b_in, 24), convw=np.ascontiguousarray(cwl),
             convb=vec_layout(conv_b, 4), clng=vec_layout(cln_g, 4), clnb=vec_layout(cln_b, 4),
             lb0=vec_layout(lb_logits[0], 4), lb1=vec_layout(lb_logits[1], 4),
             lbsel=np.full((128, 1), 1.0 if j == 1 else 0.0, np.float32),
             ong=vec_layout(onorm_g, 4), hsc=np.full((128, 1), 0.0 if first else 1.0, np.float32),
             lng=vec_layout(g, KC), lnb=vec_layout(b, KC),
             bivbc=np.ascontiguousarray(np.broadcast_to(np.asarray(b_in, np.float32)[2048:2560], (128, 512))),
             S=np.ascontiguousarray(S, dtype=np.float32))
    return d


C = 64
H = 16
C0 = float(np.exp(-0.5))


def build_odd(nc, P, NT, io, state_only=False):
    TT = C
    ntiles = NT // TT
    xk = io.get("xin_key", ("xin_ext",))
    ok = io.get("out_key", ("outdram",))
    full = not state_only
    xin_v = io["xin"].rearrange("(kc p) t -> p kc t", p=128)
    out_v = io["out"].rearrange("(kc p) t -> p kc t", p=128)
    sb, ps = P.sb, P.ps
    wr, wk, wv = (sb([128, KC, D], BF16, n) for n in ("wr", "wk", "wv"))
    wo = sb([64, H, D], BF16, "wo")
    w1 = sb([128, KC, 64], BF16, "w1"); a1 = sb([128, KC, 64], BF16, "a1"); g1 = sb([128, KC, 160], BF16, "g1")
    w2 = sb([64, D], BF16, "w2"); a2 = sb([64, D], BF16, "a2")
    g2a = sb([128, D], BF16, "g2a"); g2b = sb([32, D], BF16, "g2b")
    mu = sb([128, 6, KC], F32, "mu")
    hv = {n: sb([64, H], F32, n) for n in ("w0h", "a0h", "kkh", "kah", "rkh", "gng", "gnb")}
    omka = sb([64, H], F32, "omka")
    lng = sb([128, KC], F32, "lng"); lnb = sb([128, KC], F32, "lnb")
    mUs = sb([64, 64], F32, "mUs"); mUi = sb([64, 64], F32, "mUi"); mLs = sb([64, 64], F32, "mLs")
    eye = sb([64, 64], F32, "eye"); rmask = sb([64, H * C], F32, "rmask")
    id32 = sb([64, 64], F32, "id32"); ident = sb([64, 64], BF16, "ident")
    ones64 = sb([64, 64], F32, "ones64"); ones1 = sb([64, 64], F32, "ones1"); ones1024 = sb([128, 128], F32, "ones1024")
    M = sb([64, H, 64], F32, "M"); Mb = sb([64, H, 64], BF16, "Mb")
    XT = [sb([128, KC, TT + 1], F32, f"XT{i}") for i in range(2)]
    xx = sb([128, KC, TT], F32, "xx"); xt_ = sb([128, KC, TT], F32, "xt_")
    xm = [sb([128, KC, TT], BF16, f"xm{i}") for i in range(2)]
    F = {n: sb([64, H, C], F32, n) for n in ("rF", "kF", "vF", "aF", "gF", "sg", "cum", "kap", "t0", "t1")}
    F["sg"] = F["sg"]; F["cum"] = F["cum"]
    B = {n: sb([64, H, C], BF16, n) for n in ("KT", "BT", "KK", "RT", "BH", "KH", "Tt",
                                               "AkT", "PbT", "PkT", "zB", "BHT", "KHT")}
    for n in ("N0", "N1", "Nt0", "Nt1"):
        for g_ in range(2):
            B[f"{n}_{g_}"] = sb([64, 8, C], BF16, f"{n}_{g_}")
    B["nZ"] = B["BH"]; B["U"] = B["KH"]

    def hd(t, h):
        return t[:, h, :] if t.shape[1] == H else t[:, h % 8, :]

    def grp(t, g):
        return t[:, g * 8:(g + 1) * 8, :] if t.shape[1] == H else t[:]
    Vtm = sb([64, D], BF16, "Vtm")
    twB = sb([64, C], BF16, "twB"); taB = sb([64, C], BF16, "taB"); tgA = sb([128, C], BF16, "tgA"); tgB = sb([32, C], BF16, "tgB")
    gC = sb([64, H], F32, "gC")
    sq = [sb([128, TT], F32, f"sq{i}") for i in range(2)]
    mean_sb = sb([128, TT], F32, "mean_sb"); var_sb = sb([128, TT], F32, "var_sb"); rstd_sb = sb([128, TT], F32, "rstd_sb")
    yt = [sb([128, TT], F32, f"yt{i}") for i in range(2)]
    pb = [ps([128, 512], F32, f"pb{i}") for i in range(6)]
    trp = ps([64, 8, 64], BF16, "trp")
    stat = ps([128, 2, 256], F32, "stat", keys=["mean_ps", "msq_ps"])
    mean_ps = stat[:, 0, 0:TT]; msq_ps = stat[:, 1, 0:TT]

    names = {}
    for d_ in (F, B, hv):
        for n_, t_ in d_.items():
            names.setdefault(id(t_), n_)
    for n_, t_ in (("mUs", mUs), ("mUi", mUi), ("mLs", mLs), ("eye", eye), ("twB", twB), ("taB", taB), ("tgA", tgA), ("tgB", tgB)):
        names[id(t_)] = n_

    def kn(t):
        return names[id(t)]

    def bk(i):
        return pb[i][0:64, :].rearrange("p (h c) -> p h c", c=64)

    def ld(t, nm):
        P.dma("sp", lambda e: e.dma_start(out=t[:], in_=io[nm]), r=([io.get("M_key", ("M_ext",))] if nm == "M" else []), w=[nm])

    for nm, t in list(hv.items()) + [("mu", mu), ("lng", lng), ("lnb", lnb), ("mUs", mUs), ("mUi", mUi), ("mLs", mLs),
                                     ("eye", eye), ("rmask_o", rmask), ("id32_o", id32), ("M", M)]:
        ld(t, nm)
    P.pool(lambda e: e.tensor_copy(out=ident[:], in_=id32[:]), r=["id32_o"], w=["ident"])
    P.pool(lambda e: e.memset(ones64[:], 1.0 / 64), w=["ones64"])
    P.pool(lambda e: e.memset(ones1[:], 1.0), w=["ones1"])
    P.pool(lambda e: e.memset(ones1024[:], 1.0 / D), w=["ones1024"])
    P.dve(lambda e: e.tensor_scalar(out=omka[:], in0=hv["kah"][:], scalar1=-1.0, scalar2=1.0, op0=ALU.mult, op1=ALU.add),
          r=["kah"], w=["omka"])
    P.act(lambda e: e.activation(out=Mb[:], in_=M[:], func=AF.Identity), r=["M"], w=["Mb"])
    stg = Stager(P, 1024)
    for nm, t in (("w_r", wr), ("w_k", wk), ("w_v", wv)):
        v = io[nm].rearrange("(kc p) f -> p kc f", p=128)
        for kc in range(KC):
            stg.load(t[:, kc, :], v[:, kc, :], 1024, (nm, kc // 2))
    wov = io["w_o"].rearrange("(h p) d -> p h d", p=64)

    def load64(dst, src, n, key, rows=64):
        i = stg.n % 2
        stg.n += 1
        st = stg.bufs[i]
        P.dma("sp", lambda e: e.dma_start(out=st[0:rows, 0:n], in_=src), w=[("stage", i)])
        cast_op(P, ("pool", "act", "dve")[stg.n % 3], dst, st[0:rows, 0:n], [("stage", i)], [key])

    for h in range(H):
        load64(wo[:, h, :], wov[:, h, :], D, ("wo", h // 2))
    for nm, t, n in (("w1", w1, 64), ("a1", a1, 64)):
        v = io[nm].rearrange("(kc p) f -> p kc f", p=128)
        stg.load(t[:], v, KC * n, nm, a=KC)
    g1v = io["g1"].rearrange("(kc p) f -> p kc f", p=128)
    for q in range(2):
        stg.load(g1[:, q * 4:(q + 1) * 4, :], g1v[:, q * 4:(q + 1) * 4, :], 640, ("g1", q), a=4)
    load64(w2[:], io["w2"], D, "w2")
    load64(a2[:], io["a2"], D, "a2")
    stg.load(g2a[:], io["g2"][0:128, :], D, "g2a")
    load64(g2b[:], io["g2"][128:160, :], D, "g2b", rows=32)

    def bc(v):
        return v[:].unsqueeze(2).to_broadcast([64, H, C])

    def tt(eng, out, a, b, op, r, w):
        P.op(eng, lambda e: e.tensor_tensor(out=out, in0=a, in1=b, op=op), r, w)

    def headproj(w_t, wkey, xmi, dstF, post=None):
        for g in range(2):
            bank = bk(g)
            for hh in range(8):
                h = g * 8 + hh
                for kc in range(KC):
                    P.pe(_mm(bank[:, hh, :], w_t[:, kc, h * 64:(h + 1) * 64], xm[xmi][:, kc, :], kc == 0, kc == KC - 1),
                         r=[(wkey, kc // 2), ("xm", xmi)], w=[("pb", g)])
            P.act(lambda e, g=g, bank=bank: e.activation(out=dstF[:, g * 8:(g + 1) * 8, :], in_=bank, func=AF.Identity),
                  r=[("pb", g)], w=[(kn(dstF), g)])

    for i in range(6):
        P.bank_of[("pb", i)] = f"pb{i}"

    def load_x(j):
        s = j % 2
        P.dma("sp", lambda e: e.dma_start(out=XT[s][:], in_=xin_v[:, :, j * TT:j * TT + TT + 1]), r=[xk], w=[("XT", s)])

    xm4 = [xm[0], xm[1], P.sb([128, KC, TT], BF16, "xm2"), P.sb([128, KC, TT], BF16, "xm3")]
    xr_ = P.sb([128, KC, TT], F32, "xr_")

    def mixop(X, s, i, slot):
        P.dve(lambda e: e.tensor_tensor(out=xt_[:], in0=xx[:], in1=mu[:, i, :].unsqueeze(2).to_broadcast([128, KC, TT]),
                                        op=ALU.mult), r=["xx", "mu"], w=["xt_"])
        P.dve(lambda e: e.tensor_tensor(out=xm4[slot][:], in0=xt_[:], in1=X[:, :, 1:TT + 1], op=ALU.add),
              r=["xt_", ("XT", s)], w=[("xm", slot)])

    stg_tm = stg.bufs[1][0:64, :].bitcast(BF16)[:, 0:D]

    def hproj(w_t, wkey, slot, dstF, keep_tm=None):
        dst_tm = keep_tm if keep_tm is not None else stg_tm
        tmk = "Vtm" if keep_tm is not None else ("stage", 1)
        for half in range(2):
            for kc in range(KC):
                P.pe(_mm(pb[half][0:64, :], xm4[slot][:, kc, :], w_t[:, kc, half * 512:(half + 1) * 512], kc == 0, kc == KC - 1),
                     r=[("xm", slot), (wkey, kc // 2)], w=[("pb", half)])
            P.act(lambda e, half=half: e.activation(out=dst_tm[:, half * 512:(half + 1) * 512], in_=pb[half][0:64, :],
                                                    func=AF.Identity), r=[("pb", half)], w=[tmk if isinstance(tmk, tuple) else (tmk, half)])
        for g in range(2):
            for hh in range(8):
                h = g * 8 + hh
                P.pe(lambda e, hh=hh, h=h: e.transpose(trp[:, hh, :], dst_tm[:, h * 64:(h + 1) * 64], ident[:]),
                     r=[tmk if isinstance(tmk, tuple) else (tmk, g), "ident"], w=["trp"])
            P.act(lambda e, g=g: e.activation(out=dstF[:, g * 8:(g + 1) * 8, :], in_=trp[:], func=AF.Identity),
                  r=["trp"], w=[(kn(dstF), g)])

    def early_segments(j):
        s = j % 2
        X = XT[s]

        def e1():
            tt("dve", xx[:], X[:, :, 0:TT], X[:, :, 1:TT + 1], ALU.subtract, [("XT", s)], ["xx"])
            if full:
                mixop(X, s, 0, 0); hproj(wr, "w_r", 0, F["rF"])

        def e2():
            mixop(X, s, 2, 1); hproj(wk, "w_k", 1, F["kF"])

        def e3():
            mixop(X, s, 3, 2); hproj(wv, "w_v", 2, F["vF"], keep_tm=Vtm)

        def e4():
            for (i, l1, l1k, mid_, fn_) in ((1, w1, "w1", twB, AF.Tanh), (4, a1, "a1", taB, AF.Identity)):
                mixop(X, s, i, 3)
                for kc in range(KC):
                    P.pe(_mm(pb[3][0:64, 0:C], l1[:, kc, :], xm4[3][:, kc, :], kc == 0, kc == KC - 1),
                         r=[l1k, ("xm", 3)], w=[("pb", 3)])
                P.act(lambda e, mid_=mid_, fn_=fn_: e.activation(out=mid_[:], in_=pb[3][0:64, 0:C], func=fn_),
                      r=[("pb", 3)], w=[kn(mid_)])
            if full:
                mixop(X, s, 5, 3)
                for (dst, lo, n) in ((tgA, 0, 128), (tgB, 128, 32)):
                    for kc in range(KC):
                        P.pe(_mm(pb[3][0:n, 0:C], g1[:, kc, lo:lo + n], xm4[3][:, kc, :], kc == 0, kc == KC - 1),
                             r=["g1", ("xm", 3)], w=[("pb", 3)])
                    P.act(lambda e, dst=dst, n=n: e.activation(out=dst[:], in_=pb[3][0:n, 0:C], func=AF.Sigmoid),
                          r=[("pb", 3)], w=[kn(dst)])
        return [e1, e2, e3, e4]

    def lora2(l2, l2k, mid_, bias, outF, outfunc):
        for g in range(2):
            bank = bk(g)
            for hh in range(8):
                h = g * 8 + hh
                P.pe(_mm(bank[:, hh, :], l2[:, h * 64:(h + 1) * 64], mid_[:], True, True),
                     r=[l2k, kn(mid_)], w=[("pb", g)])
            tt("dve", outF[:, g * 8:(g + 1) * 8, :], bank, bias[:, g * 8:(g + 1) * 8].unsqueeze(2).to_broadcast([64, 8, C]),
               ALU.add, [("pb", g), kn(bias)], [(kn(outF), g)])
        P.act(lambda e: e.activation(out=outF[:], in_=outF[:], func=outfunc), r=[kn(outF)], w=[kn(outF)])

    def mid(j):
        lora2(w2, "w2", twB, hv["w0h"], F["sg"], AF.Sigmoid)
        lora2(a2, "a2", taB, hv["a0h"], F["aF"], AF.Sigmoid)
        for g in (range(2) if full else ()):
            bank = bk(g)
            for hh in range(8):
                h = g * 8 + hh
                P.pe(_mm(bank[:, hh, :], g2a[:, h * 64:(h + 1) * 64], tgA[:], True, False), r=["g2a", kn(tgA)], w=[("pb", g)])
                P.pe(_mm(bank[:, hh, :], g2b[:, h * 64:(h + 1) * 64], tgB[:], False, True), r=["g2b", kn(tgB)], w=[("pb", g)])
            P.act(lambda e, g=g, bank=bank: e.activation(out=F["gF"][:, g * 8:(g + 1) * 8, :], in_=bank, func=AF.Identity),
                  r=[("pb", g)], w=[("gF", g)])
        rF, kF, vF, aF, sg, cum, kap, t0, t1 = (F[n] for n in ("rF", "kF", "vF", "aF", "sg", "cum", "kap", "t0", "t1"))
        t2 = sg
        tt("dve", kap[:], kF[:], bc(hv["kkh"]), ALU.mult, ["kF", "kkh"], ["kap"])
        P.act(lambda e: e.activation(out=t0[:], in_=kap[:], func=AF.Square), r=["kap"], w=["t0"])
        t0f = t0[:].rearrange("p h c -> p (h c)")
        for half in range(2):
            P.pe(_mm(pb[2][0:64, :], ones1[:], t0f[:, half * 512:(half + 1) * 512], True, True), r=["ones1", "t0"], w=[("pb", 2)])
            P.dve(lambda e, half=half: e.tensor_scalar(out=t1[:].rearrange("p h c -> p (h c)")[:, half * 512:(half + 1) * 512],
                                                       in0=pb[2][0:64, :], scalar1=1e-24, scalar2=None, op0=ALU.max),
                  r=[("pb", 2)], w=[("t1", half)])
        P.act(lambda e: e.activation(out=t1[:], in_=t1[:], func=AF.Sqrt), r=["t1"], w=["t1"])
        P.dve(lambda e: e.reciprocal(out=t1[:], in_=t1[:]), r=["t1"], w=["t1"])
        tt("dve", kap[:], kap[:], t1[:], ALU.mult, ["kap", "t1"], ["kap"])
        tt("pool", t0[:], aF[:], bc(hv["kah"]), ALU.mult, ["aF", "kah"], ["t0"])
        tt("pool", t0[:], t0[:], bc(omka), ALU.add, ["t0", "omka"], ["t0"])
        tt("pool", kF[:], kF[:], t0[:], ALU.mult, ["kF", "t0"], ["kF"])
        P.dve(lambda e: e.tensor_tensor_scan(out=cum[:].rearrange("p h c -> p (h c)"), data0=rmask[:],
                                             data1=sg[:].rearrange("p h c -> p (h c)"), initial=0.0,
                                             op0=ALU.mult, op1=ALU.add), r=["rmask_o", "sg"], w=["cum"])
        tt("dve", t0[:], cum[:], sg[:], ALU.subtract, ["cum", "sg"], ["t0"])
        tt("dve", t2[:], kap[:], aF[:], ALU.mult, ["kap", "aF"], ["sg"])
        P.act(lambda e: e.activation(out=t0[:], in_=t0[:], func=AF.Exp, scale=-C0), r=["t0"], w=["t0"])
        tt("dve", B["KT"][:], kap[:], t0[:], ALU.mult, ["kap", "t0"], ["KT"])
        P.act(lambda e: e.activation(out=t0[:], in_=cum[:], func=AF.Exp, scale=-C0), r=["cum"], w=["t0"])
        if full:
            tt("dve", B["RT"][:], rF[:], t0[:], ALU.mult, ["rF", "t0"], ["RT"])
        P.act(lambda e: e.activation(out=gC[:], in_=cum[:, :, C - 1], func=AF.Exp, scale=-C0), r=["cum"], w=["gC"])
        P.act(lambda e: e.activation(out=t1[:], in_=cum[:], func=AF.Exp, scale=C0), r=["cum"], w=["t1"])
        tt("dve", t2[:], t2[:], t1[:], ALU.mult, ["sg", "t1"], ["sg"])
        tt("dve", t1[:], kF[:], t1[:], ALU.mult, ["kF", "t1"], ["t1"])
        P.act(lambda e: e.activation(out=B["BT"][:], in_=t2[:], func=AF.Identity), r=["sg"], w=["BT"])
        P.act(lambda e: e.activation(out=B["KK"][:], in_=t1[:], func=AF.Identity), r=["t1"], w=["KK"])
        tt("pool", B["BH"][:], t2[:], bc(gC), ALU.mult, ["sg", "gC"], ["BH"])
        tt("dve", B["KH"][:], t1[:], bc(gC), ALU.mult, ["t1", "gC"], ["KH"])
        if full:
            tt("dve", t0[:], rF[:], kF[:], ALU.mult, ["rF", "kF"], ["t0"])
            tt("dve", t0[:], t0[:], bc(hv["rkh"]), ALU.mult, ["t0", "rkh"], ["t0"])
            t0f_ = t0[:].rearrange("p h c -> p (h c)")
            for half in range(2):
                hs = slice(half * 512, (half + 1) * 512)
                P.pe(_mm(pb[2][0:64, :], ones1[:], t0f_[:, hs], True, True), r=["ones1", "t0"], w=[("pb", 2)])
                tt("dve", kap[:].rearrange("p h c -> p (h c)")[:, hs], pb[2][0:64, :],
                   vF[:].rearrange("p h c -> p (h c)")[:, hs], ALU.mult, [("pb", 2), "vF"], [("kap", half)])
        for src, dst in (("BH", "BHT"), ("KH", "KHT")):
            for g in range(2):
                for hh in range(8):
                    h = g * 8 + hh
                    P.pe(lambda e, hh=hh, h=h, src=src: e.transpose(trp[:, hh, :], B[src][:, h, :], ident[:]),
                         r=[src, "ident"], w=["trp"])
                P.act(lambda e, g=g, dst=dst: e.activation(out=B[dst][:, g * 8:(g + 1) * 8, :], in_=trp[:], func=AF.Identity),
                      r=["trp"], w=[(dst, g)])
        BS = ((0, 1, 2), (3, 4, 5))

        def score(lhs, rhs, dst, mask, neg, g, bank_i):
            bank = bk(bank_i)
            for hh in range(8):
                h = g * 8 + hh
                P.pe(_mm(bank[:, hh, :], B[lhs][:, h, :], B[rhs][:, h, :], True, True), r=[lhs, rhs], w=[("pb", bank_i)])
            P.dve(lambda e: e.scalar_tensor_tensor(out=grp(B[dst], g), in0=bank, scalar=(-1.0 if neg else 1.0),
                                                   in1=mask[:].unsqueeze(1).to_broadcast([64, 8, 64]),
                                                   op0=ALU.mult, op1=ALU.mult), r=[("pb", bank_i), kn(mask)], w=[(dst, g)])

        def NB(name, g):
            return B[f"{name}_{g}"]

        for g in range(2):
            score("KT", "BT", f"N0_{g}", mLs, True, g, BS[g][0])
            score("BT", "KT", f"Nt0_{g}", mUs, True, g, BS[g][1])
        for g in range(2):
            score("KK", "KT", "AkT", mUs, False, g, BS[g][2])
            if full:
                score("BT", "RT", "PbT", mUi, False, g, BS[g][0])
        for g in range(2):
            gs = slice(g * 8, (g + 1) * 8)
            if full:
                score("KK", "RT", "PkT", mUi, False, g, BS[g][1])
            tt("dve", B["Tt"][:, gs, :], NB("Nt0", g)[:], eye[:].unsqueeze(1).to_broadcast([64, 8, 64]), ALU.add,
               [(f"Nt0_{g}", g), "eye"], [("Tt", g)])
        cur = 0
        for lv in range(5):
            for g in range(2):
                gs = slice(g * 8, (g + 1) * 8)
                Nc, Ntc = NB(f"N{cur}", g), NB(f"Nt{cur}", g)
                Nn, Ntn = NB(f"N{1 - cur}", g), NB(f"Nt{1 - cur}", g)
                kNc, kNtc = (f"N{cur}_{g}", g), (f"Nt{cur}_{g}", g)
                kNn, kNtn = (f"N{1 - cur}_{g}", g), (f"Nt{1 - cur}_{g}", g)
                bA, bB, bCk = (bk(i) for i in BS[g])
                kA, kB, kC = (("pb", i) for i in BS[g])
                for hh in range(8):
                    P.pe(_mm(bA[:, hh, :], Ntc[:, hh, :], Nc[:, hh, :], True, True), r=[kNc, kNtc], w=[kA])
                P.act(lambda e, Nn=Nn, bA=bA: e.activation(out=Nn[:], in_=bA, func=AF.Identity), r=[kA], w=[kNn])
                if lv < 4:
                    for hh in range(8):
                        P.pe(_mm(bB[:, hh, :], Nc[:, hh, :], Ntc[:, hh, :], True, True), r=[kNc, kNtc], w=[kB])
                    P.act(lambda e, Ntn=Ntn, bB=bB: e.activation(out=Ntn[:], in_=bB, func=AF.Identity), r=[kB], w=[kNtn])
                for hh in range(8):
                    h = g * 8 + hh
                    P.pe(_mm(bCk[:, hh, :], Nn[:, hh, :], B["Tt"][:, h, :], True, True), r=[kNn, ("Tt", g)], w=[kC])
                tt("dve", B["Tt"][:, gs, :], B["Tt"][:, gs, :], bCk, ALU.add, [("Tt", g), kC], [("Tt", g)])
            cur = 1 - cur
        for g in range(2):
            gs = slice(g * 8, (g + 1) * 8)
            bZ = bk(BS[g][0]); kZ = ("pb", BS[g][0])
            for hh in range(8):
                h = g * 8 + hh
                vh = Vtm[:, h * 64:(h + 1) * 64]
                P.pe(_mm(bZ[:, hh, :], B["KT"][:, h, :], Mb[:, h, :], True, False), r=["KT", ("Mb", g)], w=[kZ])
                P.pe(_mm(bZ[:, hh, :], B["AkT"][:, h, :], vh, False, True), r=[("AkT", g), "Vtm"], w=[kZ])
            P.act(lambda e, bZ=bZ, gs=gs: e.activation(out=B["nZ"][:, gs, :], in_=bZ, func=AF.Identity, scale=-1.0),
                  r=[kZ], w=[("BH", g)])
        for g in range(2):
            gs = slice(g * 8, (g + 1) * 8)
            bU = bk(BS[g][1]); kU = ("pb", BS[g][1])
            for hh in range(8):
                h = g * 8 + hh
                P.pe(_mm(bU[:, hh, :], B["Tt"][:, h, :], B["nZ"][:, h, :], True, True), r=[("Tt", g), ("BH", g)], w=[kU])
            P.act(lambda e, bU=bU, gs=gs: e.activation(out=B["U"][:, gs, :], in_=bU, func=AF.Identity), r=[kU], w=[("KH", g)])
        for g in range(2):
            gs = slice(g * 8, (g + 1) * 8)
            bM = bk(BS[g][2]); kM = ("pb", BS[g][2])
            bY = bk(BS[g][0]); kY = ("pb", BS[g][0])
            for hh in (range(8) if full else ()):
                h = g * 8 + hh
                vh = Vtm[:, h * 64:(h + 1) * 64]
                P.pe(_mm(bY[:, hh, :], Mb[:, h, :], B["RT"][:, h, :], True, False), r=[("Mb", g), "RT"], w=[kY])
                P.pe(_mm(bY[:, hh, :], B["U"][:, h, :], B["PbT"][:, h, :], False, False), r=[("KH", g), ("PbT", g)], w=[kY])
                P.pe(_mm(bY[:, hh, :], vh, B["PkT"][:, h, :], False, True), r=["Vtm", ("PkT", g)], w=[kY])
            if full:
                P.act(lambda e, bY=bY, gs=gs: e.activation(out=F["cum"][:, gs, :], in_=bY, func=AF.Identity), r=[kY], w=[("cum", g)])
            for hh in range(8):
                h = g * 8 + hh
                vh = Vtm[:, h * 64:(h + 1) * 64]
                P.pe(_mm(bM[:, hh, :], B["BHT"][:, h, :], B["U"][:, h, :], True, False), r=[("BHT", g), ("KH", g)], w=[kM])
                P.pe(_mm(bM[:, hh, :], B["KHT"][:, h, :], vh, False, True), r=[("KHT", g), "Vtm"], w=[kM])
            tt("dve", M[:, gs, :], M[:, gs, :], gC[:, gs].unsqueeze(2).to_broadcast([64, 8, 64]), ALU.mult,
               [("M", g), "gC"], [("M", g)])
            tt("dve", M[:, gs, :], M[:, gs, :], bM, ALU.add, [("M", g), kM], [("M", g)])
            P.act(lambda e, gs=gs: e.activation(out=Mb[:, gs, :], in_=M[:, gs, :], func=AF.Identity), r=[("M", g)], w=[("Mb", g)])
    def late_segments(j):
        s = j % 2
        X = XT[s]
        rF, kF, vF, aF, sg, cum, kap, t0, t1 = (F[n] for n in ("rF", "kF", "vF", "aF", "sg", "cum", "kap", "t0", "t1"))
        t2 = sg
        Y = F["cum"]
        Yf = Y[:].rearrange("p h c -> p (h c)")
        t0f = t0[:].rearrange("p h c -> p (h c)"); t1f = t1[:].rearrange("p h c -> p (h c)"); t2f = t2[:].rearrange("p h c -> p (h c)")

        def l1():
            P.act(lambda e: e.activation(out=t0[:], in_=Y[:], func=AF.Square), r=["cum"], w=["t0"])
            for half in range(2):
                hs = slice(half * 512, (half + 1) * 512)
                P.pe(_mm(pb[4][0:64, :], ones64[:], Yf[:, hs], True, True), r=["ones64", "cum"], w=[("pb", 4)])
                P.pe(_mm(pb[5][0:64, :], ones64[:], t0f[:, hs], True, True), r=["ones64", "t0"], w=[("pb", 5)])
                P.act(lambda e, hs=hs: e.activation(out=t1f[:, hs], in_=pb[4][0:64, :], func=AF.Identity), r=[("pb", 4)], w=[("t1", half)])
                tt("dve", t2f[:, hs], t1f[:, hs], t1f[:, hs], ALU.mult, [("t1", half)], [("sg", half)])
                tt("dve", t2f[:, hs], pb[5][0:64, :], t2f[:, hs], ALU.subtract, [("pb", 5), ("sg", half)], [("sg", half)])
            P.dve(lambda e: e.tensor_scalar(out=t2[:], in0=t2[:], scalar1=0.0, scalar2=64e-5, op0=ALU.max, op1=ALU.add), r=["sg"], w=["sg"])
            P.act(lambda e: e.activation(out=t2[:], in_=t2[:], func=AF.Sqrt), r=["sg"], w=["sg"])
            P.dve(lambda e: e.reciprocal(out=t2[:], in_=t2[:]), r=["sg"], w=["sg"])

        def l2():
            tt("dve", Y[:], Y[:], t1[:], ALU.subtract, ["cum", "t1"], ["cum"])
            tt("dve", Y[:], Y[:], t2[:], ALU.mult, ["cum", "sg"], ["cum"])
            tt("pool", Y[:], Y[:], bc(hv["gng"]), ALU.mult, ["cum", "gng"], ["cum"])
            tt("pool", Y[:], Y[:], bc(hv["gnb"]), ALU.add, ["cum", "gnb"], ["cum"])
            tt("dve", Y[:], Y[:], kap[:], ALU.add, ["cum", "kap"], ["cum"])
            tt("dve", B["zB"][:], Y[:], F["gF"][:], ALU.mult, ["cum", "gF"], ["zB"])

        def l3():
            for db in range(KC):
                bi = 4 + db % 2
                dst = pb[bi][:, 0:TT]
                for h in range(H):
                    P.pe(_mm(dst, wo[:, h, db * 128:(db + 1) * 128], B["zB"][:, h, :], h == 0, h == H - 1),
                         r=[("wo", h // 2), "zB"], w=[("pb", bi)])
                P.dve(lambda e, db=db, dst=dst: e.scalar_tensor_tensor(out=xr_[:, db, :], in0=X[:, db, 1:TT + 1], scalar=float(ALPHA),
                                                                       in1=dst, op0=ALU.mult, op1=ALU.add),
                      r=[("XT", s), ("pb", bi)], w=[("xr_", db)])

        def l4():
            emit_ln(P, xr_, ("xr_",), KC, TT, ones1024, sq, mean_ps, msq_ps, mean_sb, var_sb, rstd_sb, yt,
                    lng, lnb, xr_, ("xr_",), LN_EPS, lkeys=("lng", "lnb", "ones1024"))
            P.dma("sp", lambda e: e.dma_start(out=out_v[:, :, j * TT:(j + 1) * TT], in_=xr_[:]), r=["xr_"], w=[ok + (j,)])
        return [l1, l2, l3, l4]

    load_x(0)
    if ntiles > 1:
        load_x(1)
    for seg in early_segments(0):
        seg()
    for j in range(ntiles):
        mid(j)
        E = early_segments(j + 1) if j + 1 < ntiles else []
        L = late_segments(j) if full else []
        for k in range(4):
            if k < len(E):
                E[k]()
            if k < len(L):
                L[k]()
        if j + 2 < ntiles:
            load_x(j + 2)
    P.dma("sp", lambda e: e.dma_start(out=io["M_out"], in_=M[:]), r=["M"], w=[io.get("Mout_key", ("M_out",))])


ODD_IN = dict(w_r=[D, D], w_k=[D, D], w_v=[D, D], w_o=[D, D], w1=[D, 64], a1=[D, 64], g1=[D, 160], w2=[64, D], a2=[64, D],
              g2=[160, D], mu=[128, 6, KC], w0h=[64, H], a0h=[64, H], kkh=[64, H], kah=[64, H], rkh=[64, H], gng=[64, H],
              gnb=[64, H], lng=[128, KC], lnb=[128, KC], mUs=[64, 64], mUi=[64, 64], mLs=[64, 64], eye=[64, 64],
              rmask_o=[64, H * C], id32_o=[64, 64], M=[64, H, 64])


def build_odd_nc(NT):
    nc = bass.Bass("TRN2", target_bir_lowering=False)
    io = {"xin": nc.dram_tensor("xin", [D, 1 + NT], F32, kind="ExternalInput").ap()}
    for k, shp in ODD_IN.items():
        io[k] = nc.dram_tensor(k, list(shp), F32, kind="ExternalInput").ap()
    io["out"] = nc.dram_tensor("out", [D, NT], F32, kind="ExternalOutput").ap()
    io["M_out"] = nc.dram_tensor("M_out", [64, H, 64], F32, kind="ExternalOutput").ap()
    P = Prog(nc)
    P.begin_phase()
    build_odd(nc, P, NT, io)
    P.end_phase()
    P.finish()
    return nc, P


def hvec(v):
    return np.ascontiguousarray(np.asarray(v, np.float32).reshape(H, 64).T)


def odd_inputs(xin_T, p, j, lng, lnb, M):
    i = np.arange(64)
    d = dict(xin=np.ascontiguousarray(xin_T, dtype=np.float32))
    for k in ("w_r", "w_k", "w_v", "w_o", "w1", "a1", "g1", "w2", "a2", "g2"):
        d[k] = np.ascontiguousarray(p["rw_" + k][j], dtype=np.float32)
    d["mu"] = np.ascontiguousarray(np.stack([vec_layout(p["rw_mu"][j][i6], KC) for i6 in range(6)], axis=1))
    d["w0h"] = hvec(p["rw_w0"][j]); d["a0h"] = hvec(p["rw_a0"][j]); d["kkh"] = hvec(p["rw_k_k"][j])
    d["kah"] = hvec(p["rw_k_a"][j]); d["rkh"] = hvec(p["rw_r_k"][j].reshape(-1)); d["gng"] = hvec(p["rw_gn_g"][j])
    d["gnb"] = hvec(p["rw_gn_b"][j]); d["lng"] = vec_layout(lng, KC); d["lnb"] = vec_layout(lnb, KC)
    d["mUs"] = (i[:, None] < i[None, :]).astype(np.float32); d["mUi"] = (i[:, None] <= i[None, :]).astype(np.float32)
    d["mLs"] = (i[:, None] > i[None, :]).astype(np.float32); d["eye"] = np.eye(64, dtype=np.float32)
    rm = np.ones((64, H * C), np.float32); rm[:, ::C] = 0.0
    d["rmask_o"] = rm; d["id32_o"] = np.eye(64, dtype=np.float32); d["M"] = np.ascontiguousarray(M, dtype=np.float32)
    return d


NCORES = 8
SEQ = 8192
NTC = SEQ // 2
TT_E = 256
TT_F = 256
HMAX = 30
PAIRS = [[0, 1], [2, 3], [4, 5], [6, 7]]

EVEN_L = dict(w_in=[D, 3072], w_out=[D, D], bin=[128, 24], convw=[128, 4 * 31], convb=[128, 4], clng=[128, 4],
              clnb=[128, 4], lb0=[128, 4], lb1=[128, 4], lbsel=[128, 1], ong=[128, 4], lng=[128, KC], lnb=[128, KC],
              bivbc=[128, 512])
EVEN_C = dict(mask=[128, 128], rmask=[128, TT_E], ident=[128, 128])
ODD_C = ("mUs", "mUi", "mLs", "eye", "rmask_o", "id32_o")
FFN_L = dict(w_up=[D, DFF], w_gate=[D, DFF], w_down=[DFF, D], cw=[128, FB * 3], cb=[128, FB], lng=[128, KC], lnb=[128, KC])


def _exchange(P, nc, tag, src_ap, src_key, rows, cols, bin_t, bout_t, dst_ap, dst_key, sel_ap, pairs=None):
    P.begin_phase(tag)
    np_ = min(rows, 128)
    nch = rows // np_
    t = P.sb([np_, nch, cols], F32, "xt")
    selt = P.sb([128, 1], F32, "sel")
    P.dma("sp", lambda e: e.dma_start(out=bin_t.ap(), in_=src_ap), r=[src_key], w=[(tag, "bin")])
    P.cc(lambda e: e.collective_compute("AllGather", ALU.bypass, replica_groups=(pairs or PAIRS),
                                        ins=[bin_t.ap().opt()], outs=[bout_t.ap().opt()]),
         r=[(tag, "bin")], w=[(tag, "bout")])
    P.dma("sp", lambda e: e.dma_start(out=t[:], in_=bout_t.ap()[0:rows, :].rearrange("(c p) t -> p c t", p=np_)),
          r=[(tag, "bout")], w=["xt"])
    P.dma("sp", lambda e: e.dma_start(out=selt[:], in_=sel_ap), w=["selt"])
    P.dve(lambda e: e.tensor_scalar(out=t[:], in0=t[:], scalar1=selt[0:np_, 0:1], scalar2=None, op0=ALU.mult),
          r=["xt", "selt"], w=["xt"])
    P.dma("sp", lambda e: e.dma_start(out=dst_ap.rearrange("(c p) t -> p c t", p=np_), in_=t[:]), r=["xt"], w=[dst_key])
    P.end_phase()


def build_fused_nc(NT=NTC, nlayers=DEPTH, ncores=NCORES):
    nc = bass.Bass("TRN2", target_bir_lowering=False)
    pairs = [[2 * i, 2 * i + 1] for i in range(ncores // 2)]

    def din(name, shape):
        return nc.dram_tensor(name, list(shape), F32, kind="ExternalInput").ap()

    def dint(name, shape):
        return nc.dram_tensor(name, list(shape), F32)

    xin = din("xin", [D, HMAX + NT])
    sel = din("sel", [128, 1])
    zst = din("zst", [128, 1024])
    out = nc.dram_tensor("out", [D, NT], F32, kind="ExternalOutput").ap()
    consts = {k: din(k, v) for k, v in EVEN_C.items()}
    for k in ODD_C:
        consts[k] = din(k, ODD_IN[k])
    slab = [dint(f"slab{i}", [D, HMAX + NT]) for i in range(2)]
    hx_in = dint("hx_in", [D, HMAX]); hx_out = dint("hx_out", [2 * D, HMAX])
    se_in = dint("se_in", [128, 512]); se_out = dint("se_out", [256, 512]); se_sel = dint("se_sel", [128, 512])
    so_in = dint("so_in", [64, 1024]); so_out = dint("so_out", [128, 1024]); so_sel = dint("so_sel", [64, 1024])
    dump = dint("dump", [128, 1024])
    P = Prog(nc)
    for layer in range(nlayers):
        src = xin if layer == 0 else slab[1].ap()
        skey = ("xin_ext",) if layer == 0 else ("slab", 1)
        if layer > 0:
            _exchange(P, nc, f"hx{layer}m_", slab[1].ap()[:, NT:NT + HMAX], ("slab", 1), D, HMAX, hx_in, hx_out,
                      slab[1].ap()[:, 0:HMAX], ("slab", 1, "halo"), sel, pairs)
        if layer % 2 == 0:
            io = {k: din(f"{k}_{layer}", v) for k, v in EVEN_L.items()}
            io.update(consts)
            io["hsc"] = sel
            io["xin"] = src
            io["xin_key"] = skey
            ioa = dict(io); ioa["S"] = zst[:, 0:512].rearrange("p (h v) -> p h v", h=4); ioa["S_key"] = ("zst",)
            ioa["out"] = slab[0].ap()[:, HMAX:HMAX + NT]; ioa["S_out"] = se_in.ap().rearrange("p (h v) -> p h v", h=4)
            ioa["Sout_key"] = ("se_in",)
            P.begin_phase(f"L{layer}a_"); build_even(nc, P, NT, TT_E, ioa, state_only=True); P.end_phase()
            _exchange(P, nc, f"sx{layer}_", se_in.ap(), ("se_in",), 128, 512, se_in, se_out, se_sel.ap(), ("se_sel",), sel, pairs)
            iob = dict(io); iob["S"] = se_sel.ap().rearrange("p (h v) -> p h v", h=4); iob["S_key"] = ("se_sel",)
            iob["out"] = slab[0].ap()[:, HMAX:HMAX + NT]; iob["out_key"] = ("slab", 0, "body")
            iob["S_out"] = dump.ap()[:, 0:512].rearrange("p (h v) -> p h v", h=4); iob["Sout_key"] = ("dump",)
            P.begin_phase(f"L{layer}b_"); build_even(nc, P, NT, TT_E, iob); P.end_phase()
        else:
            io = {k: din(f"{k}_{layer}", v) for k, v in ODD_IN.items() if k not in ODD_C and k != "M"}
            io.update(consts)
            io["xin"] = src[:, HMAX - 1:HMAX + NT]
            io["xin_key"] = skey
            ioa = dict(io); ioa["M"] = zst[0:64, :].rearrange("p (h v) -> p h v", h=H); ioa["M_key"] = ("zst",)
            ioa["out"] = slab[0].ap()[:, HMAX:HMAX + NT]; ioa["M_out"] = so_in.ap().rearrange("p (h v) -> p h v", h=H)
            ioa["Mout_key"] = ("so_in",)
            P.begin_phase(f"L{layer}a_"); build_odd(nc, P, NT, ioa, state_only=True); P.end_phase()
            _exchange(P, nc, f"sx{layer}_", so_in.ap(), ("so_in",), 64, 1024, so_in, so_out, so_sel.ap(), ("so_sel",), sel, pairs)
            iob = dict(io); iob["M"] = so_sel.ap().rearrange("p (h v) -> p h v", h=H); iob["M_key"] = ("so_sel",)
            iob["out"] = slab[0].ap()[:, HMAX:HMAX + NT]; iob["out_key"] = ("slab", 0, "body")
            iob["M_out"] = dump.ap()[0:64, :].rearrange("p (h v) -> p h v", h=H); iob["Mout_key"] = ("dump",)
            P.begin_phase(f"L{layer}b_"); build_odd(nc, P, NT, iob); P.end_phase()
        _exchange(P, nc, f"hx{layer}f_", slab[0].ap()[:, NT:NT + HMAX], ("slab", 0), D, HMAX, hx_in, hx_out,
                  slab[0].ap()[:, 0:HMAX], ("slab", 0, "halo"), sel, pairs)
        iof = {k: din(f"{k}_f{layer}", v) for k, v in FFN_L.items()}
        iof["xin"] = slab[0].ap()[:, HMAX - 2:HMAX + NT]; iof["xin_key"] = ("slab", 0)
        last = layer == nlayers - 1
        iof["out"] = out if last else slab[1].ap()[:, HMAX:HMAX + NT]
        iof["out_key"] = ("out_ext",) if last else ("slab", 1, "body")
        P.begin_phase(f"L{layer}f_"); build_ffn(nc, P, NT, TT_F, iof); P.end_phase()
    P.finish()
    return nc, P


def fused_inputs(inp, NT=NTC, nlayers=DEPTH, x_slabs=None):
    base = {}
    ec = even_consts(TT_E)
    base.update(ec)
    i = np.arange(64)
    base["mUs"] = (i[:, None] < i[None, :]).astype(np.float32); base["mUi"] = (i[:, None] <= i[None, :]).astype(np.float32)
    base["mLs"] = (i[:, None] > i[None, :]).astype(np.float32); base["eye"] = np.eye(64, dtype=np.float32)
    rm = np.ones((64, H * C), np.float32); rm[:, ::C] = 0.0
    base["rmask_o"] = rm; base["id32_o"] = np.eye(64, dtype=np.float32)
    base["zst"] = np.zeros((128, 1024), np.float32)
    dummy_x = np.zeros((D, 1), np.float32)
    for layer in range(nlayers):
        j = layer // 2
        if layer % 2 == 0:
            d = even_inputs(dummy_x, inp["ev_w_in"][j], inp["ev_b_in"][j], inp["ev_conv_w"][j], inp["ev_conv_b"][j],
                            inp["ev_cln_g"][j], inp["ev_cln_b"][j], inp["ev_lb_logits"], j, inp["ev_onorm_g"][j],
                            inp["ev_w_out"][j], inp["ln_mix_g"][layer], inp["ln_mix_b"][layer],
                            np.zeros((1,), np.float32), True, TT_E)
            for k in EVEN_L:
                base[f"{k}_{layer}"] = d[k]
        else:
            d = odd_inputs(dummy_x, inp, j, inp["ln_mix_g"][layer], inp["ln_mix_b"][layer], np.zeros((1,), np.float32))
            for k in ODD_IN:
                if k not in ODD_C and k != "M":
                    base[f"{k}_{layer}"] = d[k]
        d = ffn_inputs(dummy_x, inp["ff_w_up"][layer], inp["ff_w_gate"][layer], inp["ff_conv_w"][layer],
                       inp["ff_conv_b"][layer], inp["ff_w_down"][layer], inp["ln_ffn_g"][layer], inp["ln_ffn_b"][layer])
        for k in FFN_L:
            base[f"{k}_f{layer}"] = d[k]
    maps = []
    for c in range(len(x_slabs)):
        m = dict(base)
        m["xin"] = x_slabs[c]
        m["sel"] = np.full((128, 1), float(c % 2), np.float32)
        maps.append(m)
    return maps


_PROGS = {}


def kernel(**inp):
    inp = {k: np.asarray(v) for k, v in inp.items()}
    x = inp["x"].astype(np.float32)
    B = x.shape[0]
    slabs = []
    for c in range(NCORES):
        b, h = divmod(c, 2)
        xT = x[b].T
        if h == 0:
            sl = np.concatenate([np.zeros((D, HMAX), np.float32), xT[:, :NTC]], axis=1)
        else:
            sl = xT[:, NTC - HMAX:]
        slabs.append(np.ascontiguousarray(sl))
    if "fused" not in _PROGS:
        _PROGS["fused"] = build_fused_nc(NTC, DEPTH)[0]
    maps = fused_inputs(inp, NTC, DEPTH, slabs)
    res = run_bass_kernel_spmd(_PROGS["fused"], maps, core_ids=list(range(NCORES))).results
    out = np.empty((B, SEQ, D), np.float32)
    for c in range(NCORES):
        b, h = divmod(c, 2)
        out[b, h * NTC:(h + 1) * NTC, :] = res[c]["out"].T
    return out
```

```python
import contextlib
import os
import numpy as np
import concourse.bass as bass
import concourse.mybir as mybir
from concourse.bass_utils import run_bass_kernel_spmd

F32 = mybir.dt.float32
BF16 = mybir.dt.bfloat16
ALU = mybir.AluOpType
AF = mybir.ActivationFunctionType

D = 1024
KC = D // 128
DFF = 2816
FB = DFF // 128
DEPTH = 4
ALPHA = (2 * DEPTH) ** 0.25
LN_EPS = 1e-5


class _Rec:
    def __init__(self):
        self.call = None

    def __getattr__(self, name):
        def f(*a, **k):
            assert self.call is None, "op lambda must make exactly one engine call"
            self.call = (name, a, k)
            return self
        return f


class Prog:
    ENGS = ("pe", "act", "dve", "pool", "sp")
    EPOCH = 16000
    NEP = dict(pe=8, act=5, dve=6, pool=4, sp=1)
    NDMA = 24
    NCC = 16

    def __init__(self, nc):
        self.nc = nc
        self.ops = []
        self.last_w = {}
        self.readers = {}
        self.gstack = contextlib.ExitStack()
        self.stack = None
        self.ntile = 0
        self.known = set()
        self.children = {}
        self.bank_of = {}
        self.bank_last = {}
        self.prefix = ""
        self.phase_start = 0
        self.nphase = 0
        self.cnt = {e: 0 for e in self.ENGS}
        self.ndma = 0
        self.ncc = 0
        self.dma_prev = {}
        self.sems = {}
        g = self.gstack
        for e in self.ENGS:
            for ep in range(self.NEP[e]):
                self.sems[(e, ep)] = g.enter_context(nc.semaphore(f"s_{e}_{ep}"))
        for i in range(self.NDMA):
            self.sems[("dma", i)] = g.enter_context(nc.semaphore(f"s_dma_{i}"))
        for i in range(self.NCC):
            self.sems[("cc", i)] = g.enter_context(nc.semaphore(f"s_cc_{i}"))
        self.bar = g.enter_context(nc.semaphore("s_bar"))
        self.stats = dict(n_ops=0)

    def begin_phase(self, prefix=""):
        self.stack = contextlib.ExitStack()
        self.prefix = prefix
        self.phase_start = len(self.ops)
        self.bank_of = {}
        self.bank_last = {}

    def end_phase(self):
        self._emit_phase()
        self.stack.close()
        self.stack = None

    def finish(self):
        self.gstack.close()
        self.stats = dict(n_ops=len(self.ops), milestones=dict(self.cnt), ndma=self.ndma, ncc=self.ncc)

    def sb(self, shape, dt, name=None):
        self.ntile += 1
        name = "sb_" + self.prefix + (name or f"t{self.ntile}")
        return self.stack.enter_context(self.nc.sbuf_tensor(name, list(shape), dt))

    def ps(self, shape, dt=F32, name=None, keys=None):
        self.ntile += 1
        nm = name or f"p{self.ntile}"
        for k in (keys if keys is not None else [nm]):
            k = k if isinstance(k, tuple) else (k,)
            self.bank_of[k] = nm
        return self.stack.enter_context(self.nc.psum_tensor("ps_" + self.prefix + nm, list(shape), dt))

    def _banks(self, keys):
        out = set()
        for k in keys:
            for i in range(1, len(k) + 1):
                if k[:i] in self.bank_of:
                    out.add(self.bank_of[k[:i]])
        return out

    def _norm(self, k):
        k = k if isinstance(k, tuple) else (k,)
        if k not in self.known:
            self.known.add(k)
            for i in range(1, len(k)):
                self.children.setdefault(k[:i], set()).add(k)
        return k

    def _related(self, k):
        for i in range(1, len(k) + 1):
            yield k[:i]
        for ext in self.children.get(k, ()):
            yield ext

    def op(self, eng, fn, reads=(), writes=(), dma=False, cc=False):
        idx = len(self.ops)
        deps = set()
        reads = [self._norm(k) for k in reads]
        writes = [self._norm(k) for k in writes]
        for k in reads:
            for r in self._related(k):
                if r in self.last_w:
                    deps.add(self.last_w[r])
        for k in writes:
            for r in self._related(k):
                if r in self.last_w:
                    deps.add(self.last_w[r])
                for x in self.readers.get(r, ()):
                    deps.add(x)
        for k in writes:
            self.last_w[k] = idx
            self.readers[k] = []
            for ext in self.children.get(k, ()):
                self.last_w.pop(ext, None)
                self.readers[ext] = []
        for k in reads:
            self.readers.setdefault(k, []).append(idx)
        for bk in self._banks(reads + writes):
            bl = self.bank_last.setdefault(bk, {})
            for e2, i2 in bl.items():
                if e2 != eng:
                    deps.add(i2)
            bl[eng] = idx
        deps.discard(idx)
        deps = {d for d in deps if d >= self.phase_start}
        rec = _Rec()
        fn(rec)
        assert rec.call is not None
        self.ops.append(dict(eng=eng, call=rec.call, deps=deps, dma=dma or cc, cc=cc, stage=getattr(self, "stage", "")))
        return idx

    def pe(self, fn, r=(), w=()):
        return self.op("pe", fn, r, w)

    def act(self, fn, r=(), w=()):
        return self.op("act", fn, r, w)

    def dve(self, fn, r=(), w=()):
        return self.op("dve", fn, r, w)

    def pool(self, fn, r=(), w=()):
        return self.op("pool", fn, r, w)

    def dma(self, eng, fn, r=(), w=()):
        return self.op(eng, fn, r, w, dma=True)

    def cc(self, fn, r=(), w=()):
        return self.op("pool", fn, r, w, cc=True)

    def _emit_phase(self):
        nc = self.nc
        ops = self.ops
        lo = self.phase_start
        idxs = range(lo, len(ops))
        for i in idxs:
            o = ops[i]
            best = {}
            keep = set()
            for d in o["deps"]:
                p = ops[d]
                if p["dma"]:
                    keep.add(d)
                elif best.get(p["eng"], -1) < d:
                    best[p["eng"]] = d
            keep.update(best.values())
            o["deps"] = keep
        needed = set()
        for i in idxs:
            o = ops[i]
            for d in o["deps"]:
                p = ops[d]
                if p["dma"]:
                    needed.add(d)
                elif p["eng"] == o["eng"] and o["eng"] == "pe" and not o["dma"]:
                    continue
                else:
                    needed.add(d)
        per_eng = {e: [i for i in idxs if ops[i]["eng"] == e] for e in self.ENGS}
        last_compute = {}
        for e in self.ENGS:
            for i in reversed(per_eng[e]):
                if not ops[i]["dma"]:
                    last_compute[e] = i
                    needed.add(i)
                    break
        for i in idxs:
            o = ops[i]
            if o["cc"]:
                assert self.ncc < self.NCC
                o["sig"] = ("cc", self.ncc, 1)
                o["prev_same_slot"] = None
                self.ncc += 1
            elif o["dma"]:
                slot = self.ndma % self.NDMA
                o["sig"] = ("dma", slot, 16 * (self.ndma // self.NDMA + 1))
                o["prev_same_slot"] = self.dma_prev.get(slot)
                self.dma_prev[slot] = i
                self.ndma += 1
            elif i in needed:
                c = self.cnt[o["eng"]]
                assert c // self.EPOCH < self.NEP[o["eng"]], "out of semaphore epochs"
                o["sig"] = (o["eng"], c // self.EPOCH, c % self.EPOCH + 1)
                self.cnt[o["eng"]] = c + 1
            else:
                o["sig"] = None
        sems = self.sems
        self.nphase += 1
        nph = self.nphase
        bar = self.bar

        def run_engine(ename, eng):
            seen = {}

            def wait(sig):
                key = (sig[0], sig[1])
                if seen.get(key, 0) >= sig[2]:
                    return
                eng.wait_ge(sems[key], sig[2])
                seen[key] = sig[2]

            for i in per_eng[ename]:
                o = ops[i]
                best = {}
                for d in o["deps"]:
                    p = ops[d]
                    if (not p["dma"]) and p["eng"] == ename and ename == "pe" and not o["dma"]:
                        continue
                    sg = p["sig"]
                    kk = (sg[0], sg[1])
                    if best.get(kk, 0) < sg[2]:
                        best[kk] = sg[2]
                for kk in sorted(best):
                    wait((kk[0], kk[1], best[kk]))
                if o["dma"] and o["prev_same_slot"] is not None and o["prev_same_slot"] >= lo:
                    wait(ops[o["prev_same_slot"]]["sig"])
                call = o["call"]
                ins = getattr(eng, call[0])(*call[1], **call[2])
                sig = o["sig"]
                if sig is not None:
                    if sig[0] == "dma":
                        ins.then_inc(sems[("dma", sig[1])], 16)
                    elif sig[0] == "cc":
                        ins.then_inc(sems[("cc", sig[1])])
                    else:
                        ins.then_inc(sems[(sig[0], sig[1])], 1)
            for i in per_eng[ename]:
                if ops[i]["dma"]:
                    wait(ops[i]["sig"])
            if ename in last_compute:
                wait(ops[last_compute[ename]]["sig"])
            eng.sem_inc(bar, 1)
            eng.wait_ge(bar, len(self.ENGS) * nph)

        with nc.Block() as block:
            @block.tensor
            def _(e):
                run_engine("pe", e)

            @block.scalar
            def _(e):
                run_engine("act", e)

            @block.vector
            def _(e):
                run_engine("dve", e)

            @block.gpsimd
            def _(e):
                run_engine("pool", e)

            @block.sync
            def _(e):
                run_engine("sp", e)


class Stager:
    def __init__(self, P, width, nbuf=2):
        self.P = P
        self.bufs = [P.sb([128, width], F32, f"stage{i}") for i in range(nbuf)]
        self.n = 0

    def load(self, dst_ap, src_ap, n, dkey, eng=None, a=1):
        P = self.P
        i = self.n % len(self.bufs)
        eng = eng or ("pool", "act", "dve")[self.n % 3]
        self.n += 1
        st = self.bufs[i]
        sv = st[:, 0:n] if a == 1 else st[:, 0:n].rearrange("p (a d) -> p a d", a=a)
        P.dma("sp", lambda e: e.dma_start(out=sv, in_=src_ap), w=[("stage", i)])
        cast_op(P, eng, dst_ap, sv, [("stage", i)], [dkey])


def cast_op(P, eng, dst, src, r, w):
    if eng == "act":
        P.op("act", lambda e: e.activation(out=dst, in_=src, func=AF.Identity), r, w)
    else:
        P.op(eng, lambda e: e.tensor_copy(out=dst, in_=src), r, w)


def _mm(out, lhsT, rhs, start, stop):
    return lambda e: e.matmul(out, lhsT, rhs, start=start, stop=stop)


def build_ffn(nc, P, NT, TT, io):
    HALO = 2
    ntiles = NT // TT
    xk = io.get("xin_key", ("xin_ext",))
    ok = io.get("out_key", ("outdram",))
    xin = io["xin"]
    xin_v = xin.rearrange("(kc p) t -> p kc t", p=128)
    out_v = io["out"].rearrange("(kc p) t -> p kc t", p=128)

    wup = P.sb([128, KC, DFF], BF16, "wup")
    wgt = P.sb([128, KC, DFF], BF16, "wgt")
    wdn = P.sb([128, FB, D], BF16, "wdn")
    cw = P.sb([128, FB * 3], F32, "cw")
    cb = P.sb([128, FB], F32, "cb")
    lng = P.sb([128, KC], F32, "lng")
    lnb = P.sb([128, KC], F32, "lnb")
    ones = P.sb([128, 128], F32, "ones")
    carry = P.sb([128, FB, 2], F32, "carry")
    xbf = [P.sb([128, KC, TT], BF16, f"xbf{i}") for i in range(2)]
    xhalo = P.sb([128, KC, HALO], BF16, "xhalo")
    xres = [P.sb([128, KC, TT], F32, f"xres{i}") for i in range(2)]
    hT = P.sb([128, FB, TT], BF16, "hT")
    NQ = 3
    usb = [P.sb([128, TT + 2], F32, f"usb{i}") for i in range(NQ)]
    t1 = [P.sb([128, TT], F32, f"t1_{i}") for i in range(NQ)]
    gg = [P.sb([128, TT], F32, f"gg{i}") for i in range(NQ)]
    gsb = [P.sb([128, TT], BF16, f"gsb{i}") for i in range(NQ)]
    sq = [P.sb([128, TT], F32, f"sq{i}") for i in range(2)]
    mean_sb = P.sb([128, TT], F32, "mean_sb")
    var_sb = P.sb([128, TT], F32, "var_sb")
    rstd_sb = P.sb([128, TT], F32, "rstd_sb")
    yt = [P.sb([128, TT], F32, f"yt{i}") for i in range(2)]

    up_ps = [P.ps([128, TT], F32, f"up_ps{i}", keys=[("up_ps", i)]) for i in range(2)]
    gt_ps = [P.ps([128, TT], F32, f"gt_ps{i}", keys=[("gt_ps", i)]) for i in range(2)]
    o_ps = [P.ps([128, TT], F32, f"o_ps{i}", keys=[("o_ps", i)]) for i in range(2)]
    mean_ps = P.ps([128, TT], F32, "mean_ps")
    msq_ps = P.ps([128, TT], F32, "msq_ps")

    P.dma("sp", lambda e: e.dma_start(out=cw[:], in_=io["cw"]), w=["cw"])
    P.dma("sp", lambda e: e.dma_start(out=cb[:], in_=io["cb"]), w=["cb"])
    P.dma("sp", lambda e: e.dma_start(out=lng[:], in_=io["lng"]), w=["lng"])
    P.dma("sp", lambda e: e.dma_start(out=lnb[:], in_=io["lnb"]), w=["lnb"])
    P.pool(lambda e: e.memset(ones[:], 1.0 / D), w=["ones"])
    xh32 = P.sb([128, KC, HALO], F32, "xh32")
    P.dma("sp", lambda e: e.dma_start(out=xh32[:], in_=xin_v[:, :, 0:HALO]), r=[xk], w=["xh32"])
    P.pool(lambda e: e.tensor_copy(out=xhalo[:], in_=xh32[:]), r=["xh32"], w=["xhalo"])

    def load_xres(j):
        s = j % 2
        P.dma("sp", lambda e: e.dma_start(out=xres[s][:], in_=xin_v[:, :, HALO + j * TT:HALO + (j + 1) * TT]),
              r=[xk], w=[("xres", s)])
        P.pool(lambda e: e.tensor_copy(out=xbf[s][:], in_=xres[s][:]), r=[("xres", s)], w=[("xbf", s)])

    def load_x(j):
        pass

    load_xres(0)
    stg = Stager(P, 1024, nbuf=3)
    wupv = io["w_up"].rearrange("(kc p) f -> p kc f", p=128)
    wgtv = io["w_gate"].rearrange("(kc p) f -> p kc f", p=128)
    wdnv = io["w_down"].rearrange("(fb p) d -> p fb d", p=128)
    pieces = [(0, 1024), (1024, 1024), (2048, DFF - 2048)]
    for kc in range(KC):
        for (c0, cn) in pieces:
            stg.load(wup[:, kc, c0:c0 + cn], wupv[:, kc, c0:c0 + cn], cn, ("wup", kc, c0))
    for kc in range(KC):
        for (c0, cn) in pieces:
            stg.load(wgt[:, kc, c0:c0 + cn], wgtv[:, kc, c0:c0 + cn], cn, ("wgt", kc, c0))
    for fb in range(FB):
        stg.load(wdn[:, fb, :], wdnv[:, fb, :], D, ("wdn", fb // 2, fb % 2))

    hp = up_ps[1]
    for fb in range(FB):
        for kc in range(KC):
            P.pe(_mm(hp[:, fb * 2:fb * 2 + 2], wup[:, kc, fb * 128:(fb + 1) * 128], xhalo[:, kc, :],
                     kc == 0, kc == KC - 1),
                 r=[("wup", kc), "xhalo"], w=[("up_ps", 1)])
    P.act(lambda e: e.activation(out=carry[:].rearrange("p a b -> p (a b)"), in_=hp[:, 0:FB * 2], func=AF.Identity),
          r=[("up_ps", 1)], w=["carry"])

    nblk = 0
    for j in range(ntiles):
        s = j % 2
        if j + 1 < ntiles:
            load_x(j + 1)
        xb = xbf[s]
        for fb in range(FB):
            b = nblk % 2
            nblk += 1
            fsl = slice(fb * 128, (fb + 1) * 128)
            for kc in range(KC):
                P.pe(_mm(up_ps[b][:], wup[:, kc, fsl], xb[:, kc, :], kc == 0, kc == KC - 1),
                     r=[("wup", kc), ("xbf", s)], w=[("up_ps", b)])
            for kc in range(KC):
                P.pe(_mm(gt_ps[b][:], wgt[:, kc, fsl], xb[:, kc, :], kc == 0, kc == KC - 1),
                     r=[("wgt", kc), ("xbf", s)], w=[("gt_ps", b)])
            q = (nblk - 1) % NQ
            u = usb[q]
            P.pool(lambda e, u=u, fb=fb: e.tensor_copy(out=u[:, 0:2], in_=carry[:, fb, :]),
                   r=[("carry", fb)], w=[("usb", q)])
            P.act(lambda e, u=u, b=b: e.activation(out=u[:, 2:2 + TT], in_=up_ps[b][:], func=AF.Identity),
                  r=[("up_ps", b)], w=[("usb", q)])
            P.act(lambda e, q=q, b=b: e.activation(out=gsb[q][:], in_=gt_ps[b][:], func=AF.Identity),
                  r=[("gt_ps", b)], w=[("gsb", q)])
            P.pool(lambda e, u=u, fb=fb: e.tensor_copy(out=carry[:, fb, :], in_=u[:, TT:TT + 2]),
                   r=[("usb", q)], w=[("carry", fb)])
            tt = t1[q]
            P.dve(lambda e, u=u, tt=tt, fb=fb: e.tensor_scalar(
                out=tt[:], in0=u[:, 2:2 + TT], scalar1=cw[:, fb * 3 + 2:fb * 3 + 3], scalar2=cb[:, fb:fb + 1],
                op0=ALU.mult, op1=ALU.add), r=[("usb", q), "cw", "cb"], w=[("t1", q)])
            P.dve(lambda e, u=u, tt=tt, fb=fb: e.scalar_tensor_tensor(
                out=tt[:], in0=u[:, 1:1 + TT], scalar=cw[:, fb * 3 + 1:fb * 3 + 2], in1=tt[:],
                op0=ALU.mult, op1=ALU.add), r=[("usb", q), ("t1", q)], w=[("t1", q)])
            P.dve(lambda e, u=u, tt=tt, fb=fb: e.scalar_tensor_tensor(
                out=tt[:], in0=u[:, 0:TT], scalar=cw[:, fb * 3:fb * 3 + 1], in1=tt[:],
                op0=ALU.mult, op1=ALU.add), r=[("usb", q), ("t1", q)], w=[("t1", q)])
            g = gg[q]
            P.act(lambda e, g=g, tt=tt: e.activation(out=g[:], in_=tt[:], func=AF.Gelu),
                  r=[("t1", q)], w=[("gg", q)])
            P.dve(lambda e, g=g, q=q, fb=fb: e.tensor_tensor(out=hT[:, fb, :], in0=g[:], in1=gsb[q][:], op=ALU.mult),
                  r=[("gg", q), ("gsb", q)], w=[("hT", fb)])
        if j + 1 < ntiles:
            load_xres(j + 1)
        xr = xres[s]
        for db in range(KC):
            b = db % 2
            dsl = slice(db * 128, (db + 1) * 128)
            for fb in range(FB):
                P.pe(_mm(o_ps[b][:], wdn[:, fb, dsl], hT[:, fb, :], fb == 0, fb == FB - 1),
                     r=[("wdn", fb // 2), ("hT", fb)], w=[("o_ps", b)])
            P.dve(lambda e, xr=xr, db=db, b=b: e.scalar_tensor_tensor(
                out=xr[:, db, :], in0=xr[:, db, :], scalar=float(ALPHA), in1=o_ps[b][:],
                op0=ALU.mult, op1=ALU.add), r=[("xres", s, db), ("o_ps", b)], w=[("xres", s, db)])
        emit_ln(P, xr, ("xres", s), KC, TT, ones, sq, mean_ps, msq_ps, mean_sb, var_sb, rstd_sb, yt,
                lng, lnb, xr, ("xres", s), LN_EPS)
        P.dma("sp", lambda e, j=j, xr=xr: e.dma_start(out=out_v[:, :, j * TT:(j + 1) * TT], in_=xr[:]),
              r=[("xres", s)], w=[ok + (j,)])


def emit_ln(P, xr, xkey, nch, TT, ones, sq, mean_ps, msq_ps, mean_sb, var_sb, rstd_sb, yt, lng, lnb, osb, okey,
            eps, silu=False, lkeys=("lng", "lnb", "ones")):
    for c in range(nch):
        P.pe(_mm(mean_ps[:], ones[:], xr[:, c, :], c == 0, c == nch - 1),
             r=[lkeys[2], xkey + (c,)], w=["mean_ps"])
    for c in range(nch):
        b = c % 2
        P.act(lambda e, c=c, b=b: e.activation(out=sq[b][:], in_=xr[:, c, :], func=AF.Square),
              r=[xkey + (c,)], w=[("sq", b)])
        P.pe(_mm(msq_ps[:], ones[:], sq[b][:], c == 0, c == nch - 1),
             r=[lkeys[2], ("sq", b)], w=["msq_ps"])
    P.act(lambda e: e.activation(out=mean_sb[:], in_=mean_ps[:], func=AF.Identity), r=["mean_ps"], w=["mean_sb"])
    P.dve(lambda e: e.tensor_tensor(out=var_sb[:], in0=mean_sb[:], in1=mean_sb[:], op=ALU.mult),
          r=["mean_sb"], w=["var_sb"])
    P.dve(lambda e: e.tensor_tensor(out=var_sb[:], in0=msq_ps[:], in1=var_sb[:], op=ALU.subtract),
          r=["msq_ps", "var_sb"], w=["var_sb"])
    P.dve(lambda e: e.tensor_scalar(out=var_sb[:], in0=var_sb[:], scalar1=0.0, scalar2=float(eps),
                                    op0=ALU.max, op1=ALU.add), r=["var_sb"], w=["var_sb"])
    P.act(lambda e: e.activation(out=var_sb[:], in_=var_sb[:], func=AF.Sqrt), r=["var_sb"], w=["var_sb"])
    P.dve(lambda e: e.reciprocal(out=rstd_sb[:], in_=var_sb[:]), r=["var_sb"], w=["rstd_sb"])
    for c in range(nch):
        b = c % 2
        y = yt[b]
        P.dve(lambda e, y=y, c=c: e.tensor_tensor(out=y[:], in0=xr[:, c, :], in1=mean_sb[:], op=ALU.subtract),
              r=[xkey + (c,), "mean_sb"], w=[("yt", b)])
        P.dve(lambda e, y=y: e.tensor_tensor(out=y[:], in0=y[:], in1=rstd_sb[:], op=ALU.mult),
              r=[("yt", b), "rstd_sb"], w=[("yt", b)])
        P.act(lambda e, y=y, c=c: e.activation(out=osb[:, c, :], in_=y[:], func=(AF.Silu if silu else AF.Identity),
                                               scale=lng[:, c:c + 1], bias=lnb[:, c:c + 1]),
              r=[("yt", b), lkeys[0], lkeys[1]], w=[okey + (c,)])


def vec_layout(v, nb):
    return np.ascontiguousarray(np.asarray(v, np.float32).reshape(nb, 128).T)


def build_ffn_nc(NT, TT):
    nc = bass.Bass("TRN2", target_bir_lowering=False)
    io = {}
    io["xin"] = nc.dram_tensor("xin", [D, 2 + NT], F32, kind="ExternalInput").ap()
    io["w_up"] = nc.dram_tensor("w_up", [D, DFF], F32, kind="ExternalInput").ap()
    io["w_gate"] = nc.dram_tensor("w_gate", [D, DFF], F32, kind="ExternalInput").ap()
    io["w_down"] = nc.dram_tensor("w_down", [DFF, D], F32, kind="ExternalInput").ap()
    io["cw"] = nc.dram_tensor("cw", [128, FB * 3], F32, kind="ExternalInput").ap()
    io["cb"] = nc.dram_tensor("cb", [128, FB], F32, kind="ExternalInput").ap()
    io["lng"] = nc.dram_tensor("lng", [128, KC], F32, kind="ExternalInput").ap()
    io["lnb"] = nc.dram_tensor("lnb", [128, KC], F32, kind="ExternalInput").ap()
    io["out"] = nc.dram_tensor("out", [D, NT], F32, kind="ExternalOutput").ap()
    P = Prog(nc)
    P.begin_phase()
    build_ffn(nc, P, NT, TT, io)
    P.end_phase()
    P.finish()
    return nc, P


def ffn_inputs(xin_T, w_up, w_gate, conv_w, conv_b, w_down, g, b):
    cwl = np.stack([vec_layout(conv_w[t], FB) for t in range(3)], axis=-1).reshape(128, FB * 3)
    return dict(xin=np.ascontiguousarray(xin_T, dtype=np.float32),
                w_up=np.ascontiguousarray(w_up), w_gate=np.ascontiguousarray(w_gate),
                w_down=np.ascontiguousarray(w_down), cw=np.ascontiguousarray(cwl),
                cb=vec_layout(conv_b, FB), lng=vec_layout(g, KC), lnb=vec_layout(b, KC))


STAGE = 9
CH = 64
HAL_E = 30


def build_even(nc, P, NT, TT, io, state_only=False):
    HALO = HAL_E
    ntiles = NT // TT
    xk = io.get("xin_key", ("xin_ext",))
    ok = io.get("out_key", ("outdram",))
    full = not state_only
    nblk = TT // 128
    xin_v = io["xin"].rearrange("(kc p) t -> p kc t", p=128)
    out_v = io["out"].rearrange("(kc p) t -> p kc t", p=128)
    win = P.sb([128, KC, 3072], BF16, "win")
    wout = P.sb([128, KC, D], BF16, "wout")
    bin_ = P.sb([128, 24], F32, "bin")
    nbin = P.sb([128, 24], F32, "nbin")
    convw = P.sb([128, 4 * 31], F32, "convw")
    convb = P.sb([128, 4], F32, "convb")
    clng = P.sb([128, 4], F32, "clng")
    clnb = P.sb([128, 4], F32, "clnb")
    lb0 = P.sb([128, 4], F32, "lb0")
    lb1 = P.sb([128, 4], F32, "lb1")
    lbsel = P.sb([128, 1], F32, "lbsel")
    lb = P.sb([128, 4], F32, "lb")
    oml = P.sb([128, 4], F32, "oml")
    ong = P.sb([128, 4], F32, "ong")
    hsc = P.sb([128, 1], F32, "hsc")
    lng = P.sb([128, KC], F32, "lng")
    lnb = P.sb([128, KC], F32, "lnb")
    bivbc = P.sb([128, 512], F32, "bivbc")
    ones512 = P.sb([128, 128], F32, "ones512")
    ones128 = P.sb([128, 128], F32, "ones128")
    ones1024 = P.sb([128, 128], F32, "ones1024")
    ident = P.sb([128, 128], BF16, "ident")
    mask = P.sb([128, 128], F32, "mask")
    rmask = P.sb([128, TT], F32, "rmask")
    S = P.sb([128, 4, 128], F32, "S")
    Sbf = P.sb([128, 4, 128], BF16, "Sbf")
    xbf = [P.sb([128, KC, TT], BF16, f"xbf{i}") for i in range(2)]
    xhalo = P.sb([128, KC, HALO], BF16, "xhalo")
    xres = [P.sb([128, KC, TT], F32, f"xres{i}") for i in range(2)]
    glu = [P.sb([128, 4, HALO + TT], BF16, f"glu{i}") for i in range(2)]
    convd = P.sb([128, 4 * 31, 128], BF16, "convd") if not state_only else None
    sg = [P.sb([128, TT], F32, f"sg{i}") for i in range(2)]
    cacc = P.sb([128, 4, TT], F32, "cacc")
    cat = P.sb([128, KC, TT], BF16, "cat")
    qs = P.sb([128, TT], F32, "qs")
    sgp = P.sb([128, TT], F32, "sgp")
    sgn = P.sb([128, TT], F32, "sgn")
    logf = P.sb([128, TT], F32, "logf")
    cum = P.sb([128, TT], F32, "cum")
    eq = P.sb([128, TT], F32, "eq")
    en = P.sb([128, TT], F32, "en")
    gC = P.sb([128, 4, TT // CH], F32, "gC")
    qt = P.sb([128, 4, TT], BF16, "qt")
    kt = P.sb([128, 4, TT], F32, "kt")
    ktb = P.sb([128, 4, TT], BF16, "ktb")
    kh = P.sb([128, 4, TT], BF16, "kh")
    khT = P.sb([128, 4, 128], BF16, "khT")
    vtm = P.sb([128, 512], BF16, "vtm")
    pT = P.sb([128, 4, 128], BF16, "pT")
    osb = P.sb([128, 4, TT], F32, "osb")
    sqo = P.sb([128, TT], F32, "sqo")
    gsl = P.sb([128, TT], F32, "gsl")
    sq = [P.sb([128, TT], F32, f"sq{i}") for i in range(2)]
    mean_sb = P.sb([128, TT], F32, "mean_sb")
    var_sb = P.sb([128, TT], F32, "var_sb")
    rstd_sb = P.sb([128, TT], F32, "rstd_sb")
    yt = [P.sb([128, TT], F32, f"yt{i}") for i in range(2)]

    zps = [P.ps([128, 2, 256], F32, f"zps{i}", keys=[("zps", 2 * i), ("zps", 2 * i + 1)]) for i in range(2)]
    vtm_ps = P.ps([128, 512], F32, "vtm_ps")
    sc_ps = P.ps([128, 4, 128], F32, "sc_ps")
    o_ps = P.ps([128, 4, 128], F32, "o_ps")
    st_ps = P.ps([128, 4, 128], F32, "st_ps")
    tr_ps = P.ps([128, 4, 128], BF16, "tr_ps")
    stat_ps = P.ps([128, 2, 256], F32, "stat_ps", keys=["mean_ps", "msq_ps"])
    mean_ps = stat_ps[:, 0, 0:TT]
    msq_ps = stat_ps[:, 1, 0:TT]

    for nm, t in (("bin", bin_), ("convw", convw), ("convb", convb), ("clng", clng), ("clnb", clnb),
                  ("lb0", lb0), ("lb1", lb1), ("lbsel", lbsel), ("ong", ong), ("hsc", hsc), ("lng", lng),
                  ("lnb", lnb), ("bivbc", bivbc), ("mask", mask), ("rmask", rmask), ("S", S)):
        P.dma("sp", lambda e, t=t, nm=nm: e.dma_start(out=t[:], in_=io[nm]),
              r=([io.get("S_key", ("S_ext",))] if nm == "S" else []), w=[nm])
    id32 = P.sb([128, 128], F32, "id32")
    P.dma("sp", lambda e: e.dma_start(out=id32[:], in_=io["ident"]), w=["id32"])
    P.pool(lambda e: e.tensor_copy(out=ident[:], in_=id32[:]), r=["id32"], w=["ident"])
    if not state_only:
        for q_ in range(4 * 31):
            P.op(("pool", "dve")[q_ % 2], lambda e, q_=q_: e.tensor_scalar(
                out=convd[:, q_, :], in0=id32[:], scalar1=convw[:, q_:q_ + 1], scalar2=None, op0=ALU.mult),
                ["id32", "convw"], [("convd", q_)])
    P.pool(lambda e: e.memset(ones512[:], 1.0 / 512), w=["ones512"])
    P.pool(lambda e: e.memset(ones128[:], 1.0 / 128), w=["ones128"])
    P.pool(lambda e: e.memset(ones1024[:], 1.0 / D), w=["ones1024"])
    xh32 = P.sb([128, KC, HALO], F32, "xh32")
    P.dma("sp", lambda e: e.dma_start(out=xh32[:], in_=xin_v[:, :, 0:HALO]), r=[xk], w=["xh32"])
    P.pool(lambda e: e.tensor_copy(out=xhalo[:], in_=xh32[:]), r=["xh32"], w=["xhalo"])

    def load_xres(j):
        s = j % 2
        P.dma("sp", lambda e: e.dma_start(out=xres[s][:], in_=xin_v[:, :, HALO + j * TT:HALO + (j + 1) * TT]),
              r=[xk], w=[("xres", s)])
        P.pool(lambda e: e.tensor_copy(out=xbf[s][:], in_=xres[s][:]), r=[("xres", s)], w=[("xbf", s)])

    def load_x(j):
        pass

    load_xres(0)
    stg = Stager(P, 3072)
    winv = io["w_in"].rearrange("(kc p) f -> p kc f", p=128)
    woutv = io["w_out"].rearrange("(kc p) f -> p kc f", p=128)
    for kc in range(KC):
        stg.load(win[:, kc, :], winv[:, kc, :], 3072, ("win", kc))
    for kc in range(0, KC, 2):
        stg.load(wout[:, kc:kc + 2, :], woutv[:, kc:kc + 2, :], 2 * D, ("wout", kc // 2), a=2)
    P.dve(lambda e: e.tensor_scalar(out=nbin[:], in0=bin_[:], scalar1=-1.0, scalar2=None, op0=ALU.mult),
          r=["bin"], w=["nbin"])
    P.dve(lambda e: e.tensor_tensor(out=lb[:], in0=lb1[:], in1=lb0[:], op=ALU.subtract), r=["lb0", "lb1"], w=["lb"])
    P.act(lambda e: e.activation(out=lb[:], in_=lb[:], func=AF.Sigmoid), r=["lb"], w=["lb"])
    P.dve(lambda e: e.tensor_scalar(out=lb[:], in0=lb[:], scalar1=lbsel[:, 0:1], scalar2=None, op0=ALU.mult),
          r=["lb", "lbsel"], w=["lb"])
    P.dve(lambda e: e.tensor_scalar(out=oml[:], in0=lb[:], scalar1=-1.0, scalar2=1.0, op0=ALU.mult, op1=ALU.add),
          r=["lb"], w=["oml"])
    P.act(lambda e: e.activation(out=Sbf[:], in_=S[:], func=AF.Identity), r=["S"], w=["Sbf"])

    zcnt = [0]

    def proj(blk, rhs, n, rkeys):
        b = zcnt[0] % 4
        zcnt[0] += 1
        dst = zps[b // 2][:, b % 2, 0:n]
        for kc in range(KC):
            P.pe(_mm(dst, win[:, kc, blk * 128:(blk + 1) * 128], rhs[:, kc, :], kc == 0, kc == KC - 1),
                 r=[("win", kc)] + rkeys, w=[("zps", b)])
        return dst, ("zps", b)

    def glu_block(c, rhs, n, rkeys, dst_glu, dkey, col0, scale_ap=None):
        av, avk = proj(c, rhs, n, rkeys)
        ag, agk = proj(4 + c, rhs, n, rkeys)
        sgt = sg[c % 2]
        P.act(lambda e: e.activation(out=sgt[:, 0:n], in_=ag, func=AF.Sigmoid, bias=bin_[:, 4 + c:5 + c]),
              r=[agk, "bin"], w=[("sg", c % 2)])
        P.dve(lambda e: e.scalar_tensor_tensor(out=dst_glu[:, c, col0:col0 + n], in0=av, scalar=bin_[:, c:c + 1],
                                               in1=sgt[:, 0:n], op0=ALU.add, op1=ALU.mult),
              r=[avk, ("sg", c % 2), "bin"], w=[dkey + (c,)])
        if scale_ap is not None:
            P.dve(lambda e: e.tensor_scalar(out=dst_glu[:, c, col0:col0 + n], in0=dst_glu[:, c, col0:col0 + n],
                                            scalar1=scale_ap, scalar2=None, op0=ALU.mult),
                  r=[dkey + (c,), "hsc"], w=[dkey + (c,)])

    for c in (range(4) if full else ()):
        glu_block(c, xhalo, HALO, ["xhalo"], glu[1], ("glu", 1), TT, scale_ap=hsc[:, 0:1])

    for j in range(ntiles):
        s = j % 2
        if j + 1 < ntiles:
            load_x(j + 1)
        xb = xbf[s]
        G = glu[s]
        Gp = glu[1 - s]
        for c in (range(4) if full else ()):
            P.pool(lambda e, c=c, G=G, Gp=Gp: e.tensor_copy(out=G[:, c, 0:HALO], in_=Gp[:, c, TT:TT + HALO]),
                   r=[("glu", 1 - s, c)], w=[("glu", s, c)])
            glu_block(c, xb, TT, [("xbf", s)], G, ("glu", s), HALO)
            bq = zcnt[0] % 4
            zcnt[0] += 1
            cdst = zps[bq // 2][:, bq % 2, 0:TT]
            for tap in range(31):
                P.pe(_mm(cdst, convd[:, c * 31 + tap, :], G[:, c, tap:tap + TT], tap == 0, tap == 30),
                     r=[("convd", c * 31 + tap), ("glu", s, c)], w=[("zps", bq)])
            P.act(lambda e, c=c, cdst=cdst: e.activation(out=cacc[:, c, :], in_=cdst, func=AF.Identity,
                                                         bias=convb[:, c:c + 1]),
                  r=[("zps", bq), "convb"], w=[("cacc", c)])
        if full:
          emit_ln(P, cacc, ("cacc",), 4, TT, ones512, sq, mean_ps, msq_ps, mean_sb, var_sb, rstd_sb, yt,
                  clng, clnb, cat, ("cat",), LN_EPS, silu=True, lkeys=("clng", "clnb", "ones512"))
        for h in range(4):
            if full:
                qp, qk = proj(8 + h, xb, TT, [("xbf", s)])
                P.act(lambda e, qp=qp, h=h: e.activation(out=qs[:], in_=qp, func=AF.Silu, bias=bin_[:, 8 + h:9 + h]),
                      r=[qk, "bin"], w=["qs"])
            fp_, fk = proj(12 + h, xb, TT, [("xbf", s)])
            P.act(lambda e, fp_=fp_, h=h: e.activation(out=sgp[:], in_=fp_, func=AF.Sigmoid,
                                                       bias=bin_[:, 12 + h:13 + h]),
                  r=[fk, "bin"], w=["sgp"])
            P.act(lambda e, fp_=fp_, h=h: e.activation(out=sgn[:], in_=fp_, func=AF.Sigmoid, scale=-1.0,
                                                       bias=nbin[:, 12 + h:13 + h]),
                  r=[fk, "nbin"], w=["sgn"])
            P.dve(lambda e, h=h: e.tensor_scalar(out=logf[:], in0=sgp[:], scalar1=oml[:, h:h + 1],
                                                 scalar2=lb[:, h:h + 1], op0=ALU.mult, op1=ALU.add),
                  r=["sgp", "oml", "lb"], w=["logf"])
            P.act(lambda e: e.activation(out=logf[:], in_=logf[:], func=AF.Ln), r=["logf"], w=["logf"])
            P.dve(lambda e: e.tensor_tensor_scan(out=cum[:], data0=rmask[:], data1=logf[:], initial=0.0,
                                                 op0=ALU.mult, op1=ALU.add),
                  r=["rmask", "logf"], w=["cum"])
            if full:
                P.act(lambda e: e.activation(out=eq[:], in_=cum[:], func=AF.Exp), r=["cum"], w=["eq"])
            P.act(lambda e: e.activation(out=en[:], in_=cum[:], func=AF.Exp, scale=-1.0), r=["cum"], w=["en"])
            P.act(lambda e, h=h: e.activation(
                out=gC[:, h, :], in_=cum[:].rearrange("p (c t) -> p c t", t=CH)[:, :, CH - 1], func=AF.Exp),
                r=["cum"], w=[("gC", h)])
            if full:
                P.dve(lambda e, h=h: e.tensor_tensor(out=qt[:, h, :], in0=qs[:], in1=eq[:], op=ALU.mult),
                      r=["qs", "eq"], w=[("qt", h)])
            P.dve(lambda e, h=h: e.scalar_tensor_tensor(out=kt[:, h, :], in0=sgn[:], scalar=oml[:, h:h + 1],
                                                        in1=en[:], op0=ALU.mult, op1=ALU.mult),
                  r=["sgn", "en", "oml"], w=[("kt", h)])
            if full:
                P.act(lambda e, h=h: e.activation(out=ktb[:, h, :], in_=kt[:, h, :], func=AF.Identity),
                      r=[("kt", h)], w=[("ktb", h)])
            P.dve(lambda e, h=h: e.tensor_tensor(
                out=kh[:, h, :].rearrange("p (c t) -> p c t", t=CH),
                in0=kt[:, h, :].rearrange("p (c t) -> p c t", t=CH),
                in1=gC[:, h, :].unsqueeze(2).to_broadcast([128, TT // CH, CH]), op=ALU.mult),
                r=[("kt", h), ("gC", h)], w=[("kh", h)])
        for bi in range(nblk):
            tsl = slice(bi * 128, (bi + 1) * 128)
            for kc in range(KC):
                P.pe(_mm(vtm_ps[:], xb[:, kc, tsl], win[:, kc, 2048:2560], kc == 0, kc == KC - 1),
                     r=[("xbf", s), ("win", kc)], w=["vtm_ps"])
            P.dve(lambda e: e.tensor_tensor(out=vtm[:], in0=vtm_ps[:], in1=bivbc[:], op=ALU.add),
                  r=["vtm_ps", "bivbc"], w=["vtm"])
            for h in range(4):
                if full:
                    P.pe(_mm(sc_ps[:, h, :], ktb[:, h, tsl], qt[:, h, tsl], True, True),
                         r=[("ktb", h), ("qt", h)], w=[("sc_ps", h)])
                P.pe(lambda e, h=h, tsl=tsl: e.transpose(tr_ps[:, h, :], kh[:, h, tsl], ident[:]),
                     r=[("kh", h), "ident"], w=[("tr_ps", h)])
            if full:
                P.dve(lambda e: e.tensor_tensor(out=pT[:], in0=sc_ps[:],
                                                in1=mask[:].unsqueeze(1).to_broadcast([128, 4, 128]), op=ALU.mult),
                      r=["sc_ps", "mask"], w=["pT"])
            P.act(lambda e: e.activation(out=khT[:], in_=tr_ps[:], func=AF.Identity), r=["tr_ps"], w=["khT"])
            for h in (range(4) if full else ()):
                vh = vtm[:, h * 128:(h + 1) * 128]
                P.pe(_mm(o_ps[:, h, :], vh, pT[:, h, :], h == 0, False),
                     r=["vtm", "pT"], w=[("o_ps",)])
            for ci in range(2):
                csl = slice(ci * 64, (ci + 1) * 64)
                gcol = bi * 2 + ci
                for h in (range(4) if full else ()):
                    P.pe(_mm(o_ps[:, h, csl], Sbf[:, h, :], qt[:, h, bi * 128 + ci * 64:bi * 128 + (ci + 1) * 64],
                             False, ci == 1 and h == 3),
                         r=[("Sbf", h), ("qt", h)], w=[("o_ps",)])
                for h in range(4):
                    P.pe(_mm(st_ps[:, h, :], khT[csl, h, :], vtm[csl, h * 128:(h + 1) * 128], True, True),
                         r=["khT", "vtm"], w=[("st_ps", h)])
                for h in range(4):
                    P.dve(lambda e, h=h, gcol=gcol: e.scalar_tensor_tensor(
                        out=S[:, h, :], in0=S[:, h, :], scalar=gC[:, h, gcol:gcol + 1], in1=st_ps[:, h, :],
                        op0=ALU.mult, op1=ALU.add), r=[("S", h), ("gC", h), ("st_ps", h)], w=[("S", h)])
                    if full:
                        P.act(lambda e, h=h: e.activation(out=Sbf[:, h, :], in_=S[:, h, :], func=AF.Identity),
                              r=[("S", h)], w=[("Sbf", h)])
            if full:
                P.act(lambda e, tsl=tsl: e.activation(out=osb[:, :, tsl], in_=o_ps[:], func=AF.Identity),
                      r=["o_ps"], w=[("osb", bi)])
        for h in (range(4) if full else ()):
            if True:
                P.act(lambda e, h=h: e.activation(out=sqo[:], in_=osb[:, h, :], func=AF.Square), r=["osb"], w=["sqo"])
                P.pe(_mm(mean_ps, ones128[:], sqo[:], True, True), r=["ones128", "sqo"], w=["mean_ps"])
                P.dve(lambda e: e.tensor_scalar(out=var_sb[:], in0=mean_ps, scalar1=0.0, scalar2=1e-6,
                                                op0=ALU.max, op1=ALU.add), r=["mean_ps"], w=["var_sb"])
            if True:
                P.act(lambda e: e.activation(out=var_sb[:], in_=var_sb[:], func=AF.Sqrt), r=["var_sb"], w=["var_sb"])
                P.dve(lambda e: e.reciprocal(out=rstd_sb[:], in_=var_sb[:]), r=["var_sb"], w=["rstd_sb"])
            gp, gk = proj(20 + h, xb, TT, [("xbf", s)])
            P.act(lambda e, gp=gp, h=h: e.activation(out=gsl[:], in_=gp, func=AF.Silu, bias=bin_[:, 20 + h:21 + h]),
                  r=[gk, "bin"], w=["gsl"])
            if True:
                P.dve(lambda e, h=h: e.tensor_tensor(out=sqo[:], in0=osb[:, h, :], in1=rstd_sb[:], op=ALU.mult),
                      r=["osb", "rstd_sb"], w=["sqo"])
                P.dve(lambda e, h=h: e.scalar_tensor_tensor(out=cat[:, 4 + h, :], in0=sqo[:], scalar=ong[:, h:h + 1],
                                                            in1=gsl[:], op0=ALU.mult, op1=ALU.mult),
                      r=["sqo", "ong", "gsl"], w=[("cat", 4 + h)])
        if j + 1 < ntiles:
            load_xres(j + 1)
        xr = xres[s]
        if not full:
            continue
        for db in range(KC):
            b = zcnt[0] % 4
            zcnt[0] += 1
            dst = zps[b // 2][:, b % 2, 0:TT]
            for c in range(KC):
                P.pe(_mm(dst, wout[:, c, db * 128:(db + 1) * 128], cat[:, c, :], c == 0, c == KC - 1),
                     r=[("wout", c // 2), ("cat", c)], w=[("zps", b)])
            P.dve(lambda e, xr=xr, db=db, dst=dst: e.scalar_tensor_tensor(
                out=xr[:, db, :], in0=xr[:, db, :], scalar=float(ALPHA), in1=dst,
                op0=ALU.mult, op1=ALU.add), r=[("xres", s, db), ("zps", b)], w=[("xres", s, db)])
        emit_ln(P, xr, ("xres", s), KC, TT, ones1024, sq, mean_ps, msq_ps, mean_sb, var_sb, rstd_sb, yt,
                lng, lnb, xr, ("xres", s), LN_EPS, lkeys=("lng", "lnb", "ones1024"))
        P.dma("sp", lambda e, j=j, xr=xr: e.dma_start(out=out_v[:, :, j * TT:(j + 1) * TT], in_=xr[:]),
              r=[("xres", s)], w=[ok + (j,)])
    P.dma("sp", lambda e: e.dma_start(out=io["S_out"], in_=S[:]), r=["S"], w=[io.get("Sout_key", ("S_out",))])


def build_even_nc(NT, TT):
    nc = bass.Bass("TRN2", target_bir_lowering=False)
    io = {}

    def din(name, shape, dt=F32):
        io[name] = nc.dram_tensor(name, list(shape), dt, kind="ExternalInput").ap()

    din("xin", [D, HAL_E + NT])
    din("w_in", [D, 3072])
    din("w_out", [D, D])
    din("bin", [128, 24])
    din("convw", [128, 4 * 31])
    din("convb", [128, 4])
    din("clng", [128, 4])
    din("clnb", [128, 4])
    din("lb0", [128, 4])
    din("lb1", [128, 4])
    din("lbsel", [128, 1])
    din("ong", [128, 4])
    din("hsc", [128, 1])
    din("lng", [128, KC])
    din("lnb", [128, KC])
    din("bivbc", [128, 512])
    din("mask", [128, 128])
    din("rmask", [128, TT])
    din("ident", [128, 128])
    din("S", [128, 4, 128])
    io["out"] = nc.dram_tensor("out", [D, NT], F32, kind="ExternalOutput").ap()
    io["S_out"] = nc.dram_tensor("S_out", [128, 4, 128], F32, kind="ExternalOutput").ap()
    P = Prog(nc)
    P.begin_phase()
    build_even(nc, P, NT, TT, io)
    P.end_phase()
    P.finish()
    return nc, P


def even_consts(TT):
    i = np.arange(128)
    mask = ((i[:, None] // CH == i[None, :] // CH) & (i[:, None] <= i[None, :])).astype(np.float32)
    rmask = np.ones((128, TT), np.float32)
    rmask[:, ::CH] = 0.0
    return dict(mask=mask, rmask=rmask, ident=np.eye(128, dtype=np.float32))


def even_inputs(xin_T, w_in, b_in, conv_w, conv_b, cln_g, cln_b, lb_logits, j, onorm_g, w_out, g, b, S, first, TT):
    d = even_consts(TT)
    cwl = np.stack([vec_layout(conv_w[t], 4) for t in range(31)], axis=-1).reshape(128, 4 * 31)
    d.update(xin=np.ascontiguousarray(xin_T, dtype=np.float32), w_in=np.ascontiguousarray(w_in),
             w_out=np.ascontiguousarray(w_out), bin=vec_layout(b_in, 24), convw=np.ascontiguousarray(cwl),
             convb=vec_layout(conv_b, 4), clng=vec_layout(cln_g, 4), clnb=vec_layout(cln_b, 4),
             lb0=vec_layout(lb_logits[0], 4), lb1=vec_layout(lb_logits[1], 4),
             lbsel=np.full((128, 1), 1.0 if j == 1 else 0.0, np.float32),
             ong=vec_layout(onorm_g, 4), hsc=np.full((128, 1), 0.0 if first else 1.0, np.float32),
             lng=vec_layout(g, KC), lnb=vec_layout(b, KC),
             bivbc=np.ascontiguousarray(np.broadcast_to(np.asarray(b_in, np.float32)[2048:2560], (128, 512))),
             S=np.ascontiguousarray(S, dtype=np.float32))
    return d


C = 64
H = 16
C0 = float(np.exp(-0.5))


def build_odd(nc, P, NT, io, state_only=False):
    TT = C
    ntiles = NT // TT
    xk = io.get("xin_key", ("xin_ext",))
    ok = io.get("out_key", ("outdram",))
    full = not state_only
    xin_v = io["xin"].rearrange("(kc p) t -> p kc t", p=128)
    out_v = io["out"].rearrange("(kc p) t -> p kc t", p=128)
    sb, ps = P.sb, P.ps
    wr, wk, wv = (sb([128, KC, D], BF16, n) for n in ("wr", "wk", "wv"))
    wo = sb([64, H, D], BF16, "wo")
    w1 = sb([128, KC, 64], BF16, "w1"); a1 = sb([128, KC, 64], BF16, "a1"); g1 = sb([128, KC, 160], BF16, "g1")
    w2 = sb([64, D], BF16, "w2"); a2 = sb([64, D], BF16, "a2")
    g2a = sb([128, D], BF16, "g2a"); g2b = sb([32, D], BF16, "g2b")
    mu = sb([128, 6, KC], F32, "mu")
    hv = {n: sb([64, H], F32, n) for n in ("w0h", "a0h", "kkh", "kah", "rkh", "gng", "gnb")}
    omka = sb([64, H], F32, "omka")
    lng = sb([128, KC], F32, "lng"); lnb = sb([128, KC], F32, "lnb")
    mUs = sb([64, 64], F32, "mUs"); mUi = sb([64, 64], F32, "mUi"); mLs = sb([64, 64], F32, "mLs")
    eye = sb([64, 64], F32, "eye"); rmask = sb([64, H * C], F32, "rmask")
    id32 = sb([64, 64], F32, "id32"); ident = sb([64, 64], BF16, "ident")
    ones64 = sb([64, 64], F32, "ones64"); ones1 = sb([64, 64], F32, "ones1"); ones1024 = sb([128, 128], F32, "ones1024")
    M = sb([64, H, 64], F32, "M"); Mb = sb([64, H, 64], BF16, "Mb")
    XT = [sb([128, KC, TT + 1], F32, f"XT{i}") for i in range(2)]
    xx = sb([128, KC, TT], F32, "xx"); xt_ = sb([128, KC, TT], F32, "xt_")
    xm = [sb([128, KC, TT], BF16, f"xm{i}") for i in range(2)]
    F = {n: sb([64, H, C], F32, n) for n in ("rF", "kF", "vF", "aF", "gF", "sg", "cum", "kap", "t0", "t1")}
    F["sg"] = F["sg"]; F["cum"] = F["cum"]
    B = {n: sb([64, H, C], BF16, n) for n in ("KT", "BT", "KK", "RT", "BH", "KH", "Tt",
                                               "AkT", "PbT", "PkT", "zB", "BHT", "KHT")}
    for n in ("N0", "N1", "Nt0", "Nt1"):
        for g_ in range(2):
            B[f"{n}_{g_}"] = sb([64, 8, C], BF16, f"{n}_{g_}")
    B["nZ"] = B["BH"]; B["U"] = B["KH"]

    def hd(t, h):
        return t[:, h, :] if t.shape[1] == H else t[:, h % 8, :]

    def grp(t, g):
        return t[:, g * 8:(g + 1) * 8, :] if t.shape[1] == H else t[:]
    Vtm = sb([64, D], BF16, "Vtm")
    twB = sb([64, C], BF16, "twB"); taB = sb([64, C], BF16, "taB"); tgA = sb([128, C], BF16, "tgA"); tgB = sb([32, C], BF16, "tgB")
    gC = sb([64, H], F32, "gC")
    sq8 = sb([128, KC, TT], F32, "sq8")
    mean_sb = sb([128, TT], F32, "mean_sb"); var_sb = sb([128, TT], F32, "var_sb"); rstd_sb = sb([128, TT], F32, "rstd_sb")
    pb = [ps([128, 512], F32, f"pb{i}") for i in range(6)]
    trp = ps([64, 8, 64], BF16, "trp")
    stat = ps([128, 2, 256], F32, "stat", keys=["mean_ps", "msq_ps"])
    mean_ps = stat[:, 0, 0:TT]; msq_ps = stat[:, 1, 0:TT]

    names = {}
    for d_ in (F, B, hv):
        for n_, t_ in d_.items():
            names.setdefault(id(t_), n_)
    for n_, t_ in (("mUs", mUs), ("mUi", mUi), ("mLs", mLs), ("eye", eye), ("twB", twB), ("taB", taB), ("tgA", tgA), ("tgB", tgB)):
        names[id(t_)] = n_

    def kn(t):
        return names[id(t)]

    def bk(i):
        return pb[i][0:64, :].rearrange("p (h c) -> p h c", c=64)

    def ld(t, nm):
        P.dma("sp", lambda e: e.dma_start(out=t[:], in_=io[nm]), r=([io.get("M_key", ("M_ext",))] if nm == "M" else []), w=[nm])

    for nm, t in list(hv.items()) + [("mu", mu), ("lng", lng), ("lnb", lnb), ("mUs", mUs), ("mUi", mUi), ("mLs", mLs),
                                     ("eye", eye), ("rmask_o", rmask), ("id32_o", id32), ("M", M)]:
        ld(t, nm)
    P.pool(lambda e: e.tensor_copy(out=ident[:], in_=id32[:]), r=["id32_o"], w=["ident"])
    P.pool(lambda e: e.memset(ones64[:], 1.0 / 64), w=["ones64"])
    P.pool(lambda e: e.memset(ones1[:], 1.0), w=["ones1"])
    P.pool(lambda e: e.memset(ones1024[:], 1.0 / D), w=["ones1024"])
    P.dve(lambda e: e.tensor_scalar(out=omka[:], in0=hv["kah"][:], scalar1=-1.0, scalar2=1.0, op0=ALU.mult, op1=ALU.add),
          r=["kah"], w=["omka"])
    P.act(lambda e: e.activation(out=Mb[:], in_=M[:], func=AF.Identity), r=["M"], w=["Mb"])
    stg = Stager(P, 1024)
    for nm, t in (("w_r", wr), ("w_k", wk), ("w_v", wv)):
        v = io[nm].rearrange("(kc p) f -> p kc f", p=128)
        for kc in range(KC):
            stg.load(t[:, kc, :], v[:, kc, :], 1024, (nm, kc // 2))
    wov = io["w_o"].rearrange("(h p) d -> p h d", p=64)

    def load64(dst, src, n, key, rows=64):
        i = stg.n % 2
        stg.n += 1
        st = stg.bufs[i]
        P.dma("sp", lambda e: e.dma_start(out=st[0:rows, 0:n], in_=src), w=[("stage", i)])
        cast_op(P, ("pool", "act", "dve")[stg.n % 3], dst, st[0:rows, 0:n], [("stage", i)], [key])

    for h in range(H):
        load64(wo[:, h, :], wov[:, h, :], D, ("wo", h // 2))
    for nm, t, n in (("w1", w1, 64), ("a1", a1, 64)):
        v = io[nm].rearrange("(kc p) f -> p kc f", p=128)
        stg.load(t[:], v, KC * n, nm, a=KC)
    g1v = io["g1"].rearrange("(kc p) f -> p kc f", p=128)
    for q in range(2):
        stg.load(g1[:, q * 4:(q + 1) * 4, :], g1v[:, q * 4:(q + 1) * 4, :], 640, ("g1", q), a=4)
    load64(w2[:], io["w2"], D, "w2")
    load64(a2[:], io["a2"], D, "a2")
    stg.load(g2a[:], io["g2"][0:128, :], D, "g2a")
    load64(g2b[:], io["g2"][128:160, :], D, "g2b", rows=32)

    def bc(v):
        return v[:].unsqueeze(2).to_broadcast([64, H, C])

    def tt(eng, out, a, b, op, r, w):
        P.op(eng, lambda e: e.tensor_tensor(out=out, in0=a, in1=b, op=op), r, w)

    def headproj(w_t, wkey, xmi, dstF, post=None):
        for g in range(2):
            bank = bk(g)
            for hh in range(8):
                h = g * 8 + hh
                for kc in range(KC):
                    P.pe(_mm(bank[:, hh, :], w_t[:, kc, h * 64:(h + 1) * 64], xm[xmi][:, kc, :], kc == 0, kc == KC - 1),
                         r=[(wkey, kc // 2), ("xm", xmi)], w=[("pb", g)])
            P.act(lambda e, g=g, bank=bank: e.activation(out=dstF[:, g * 8:(g + 1) * 8, :], in_=bank, func=AF.Identity),
                  r=[("pb", g)], w=[(kn(dstF), g)])

    for i in range(6):
        P.bank_of[("pb", i)] = f"pb{i}"

    def load_x(j):
        s = j % 2
        P.dma("sp", lambda e: e.dma_start(out=XT[s][:], in_=xin_v[:, :, j * TT:j * TT + TT + 1]), r=[xk], w=[("XT", s)])

    xm4 = [xm[0], xm[1], P.sb([128, KC, TT], BF16, "xm2"), P.sb([128, KC, TT], BF16, "xm3")]
    xr_ = P.sb([128, KC, TT], F32, "xr_")

    def mixop(X, s, i, slot):
        P.dve(lambda e: e.tensor_tensor(out=xt_[:], in0=xx[:], in1=mu[:, i, :].unsqueeze(2).to_broadcast([128, KC, TT]),
                                        op=ALU.mult), r=["xx", "mu"], w=["xt_"])
        P.dve(lambda e: e.tensor_tensor(out=xm4[slot][:], in0=xt_[:], in1=X[:, :, 1:TT + 1], op=ALU.add),
              r=["xt_", ("XT", s)], w=[("xm", slot)])

    stg_tm = [stg.bufs[i][0:64, :].bitcast(BF16)[:, 0:D] for i in range(2)]

    def hproj_mm(w_t, wkey, slot, banks, dst_tm, tmk):
        for half in range(2):
            bi = banks[half]
            for kc in range(KC):
                P.pe(_mm(pb[bi][0:64, :], xm4[slot][:, kc, :], w_t[:, kc, half * 512:(half + 1) * 512], kc == 0, kc == KC - 1),
                     r=[("xm", slot), (wkey, kc // 2)], w=[("pb", bi)])
            P.act(lambda e, half=half, bi=bi: e.activation(out=dst_tm[:, half * 512:(half + 1) * 512], in_=pb[bi][0:64, :],
                                                           func=AF.Identity), r=[("pb", bi)], w=[tmk + (half,)])

    def hproj_tr(dst_tm, tmk, dstF):
        for g in range(2):
            for hh in range(8):
                h = g * 8 + hh
                P.pe(lambda e, hh=hh, h=h: e.transpose(trp[:, hh, :], dst_tm[:, h * 64:(h + 1) * 64], ident[:]),
                     r=[tmk + (g,), "ident"], w=["trp"])
            P.act(lambda e, g=g: e.activation(out=dstF[:, g * 8:(g + 1) * 8, :], in_=trp[:], func=AF.Identity),
                  r=["trp"], w=[(kn(dstF), g)])

    def early_segments(j):
        s = j % 2
        X = XT[s]

        def e1():
            tt("dve", xx[:], X[:, :, 0:TT], X[:, :, 1:TT + 1], ALU.subtract, [("XT", s)], ["xx"])
            if full:
                mixop(X, s, 0, 0)
            mixop(X, s, 2, 1)
            mixop(X, s, 3, 2)
            if full:
                hproj_mm(wr, "w_r", 0, (0, 1), stg_tm[1], ("stage", 1))
            hproj_mm(wk, "w_k", 1, (2, 3), stg_tm[0], ("stage", 0))

        def e2():
            if full:
                hproj_tr(stg_tm[1], ("stage", 1), F["rF"])
            hproj_mm(wv, "w_v", 2, (0, 1), Vtm, ("Vtm",))

        def e3():
            hproj_tr(stg_tm[0], ("stage", 0), F["kF"])
            hproj_tr(Vtm, ("Vtm",), F["vF"])

        def e4():
            for (i, l1, l1k, mid_, fn_) in ((1, w1, "w1", twB, AF.Tanh), (4, a1, "a1", taB, AF.Identity)):
                mixop(X, s, i, 3)
                for kc in range(KC):
                    P.pe(_mm(pb[3][0:64, 0:C], l1[:, kc, :], xm4[3][:, kc, :], kc == 0, kc == KC - 1),
                         r=[l1k, ("xm", 3)], w=[("pb", 3)])
                P.act(lambda e, mid_=mid_, fn_=fn_: e.activation(out=mid_[:], in_=pb[3][0:64, 0:C], func=fn_),
                      r=[("pb", 3)], w=[kn(mid_)])
            if full:
                mixop(X, s, 5, 3)
                for (dst, lo, n) in ((tgA, 0, 128), (tgB, 128, 32)):
                    for kc in range(KC):
                        P.pe(_mm(pb[3][0:n, 0:C], g1[:, kc, lo:lo + n], xm4[3][:, kc, :], kc == 0, kc == KC - 1),
                             r=["g1", ("xm", 3)], w=[("pb", 3)])
                    P.act(lambda e, dst=dst, n=n: e.activation(out=dst[:], in_=pb[3][0:n, 0:C], func=AF.Sigmoid),
                          r=[("pb", 3)], w=[kn(dst)])
        return [e1, e2, e3, e4]

    def lora2(l2, l2k, mid_, bias, outF, outfunc):
        for g in range(2):
            bank = bk(g)
            for hh in range(8):
                h = g * 8 + hh
                P.pe(_mm(bank[:, hh, :], l2[:, h * 64:(h + 1) * 64], mid_[:], True, True),
                     r=[l2k, kn(mid_)], w=[("pb", g)])
            tt("dve", outF[:, g * 8:(g + 1) * 8, :], bank, bias[:, g * 8:(g + 1) * 8].unsqueeze(2).to_broadcast([64, 8, C]),
               ALU.add, [("pb", g), kn(bias)], [(kn(outF), g)])
        P.act(lambda e: e.activation(out=outF[:], in_=outF[:], func=outfunc), r=[kn(outF)], w=[kn(outF)])

    def mid(j):
        rF, kF, vF, aF, sg, cum, kap, t0, t1 = (F[n] for n in ("rF", "kF", "vF", "aF", "sg", "cum", "kap", "t0", "t1"))
        t2 = sg
        tt("dve", kap[:], kF[:], bc(hv["kkh"]), ALU.mult, ["kF", "kkh"], ["kap"])
        lora2(w2, "w2", twB, hv["w0h"], F["sg"], AF.Sigmoid)
        P.act(lambda e: e.activation(out=t0[:], in_=kap[:], func=AF.Square), r=["kap"], w=["t0"])
        t0f = t0[:].rearrange("p h c -> p (h c)")
        for half in range(2):
            P.pe(_mm(pb[2][0:64, :], ones1[:], t0f[:, half * 512:(half + 1) * 512], True, True), r=["ones1", "t0"], w=[("pb", 2)])
            P.dve(lambda e, half=half: e.tensor_scalar(out=t1[:].rearrange("p h c -> p (h c)")[:, half * 512:(half + 1) * 512],
                                                       in0=pb[2][0:64, :], scalar1=1e-24, scalar2=None, op0=ALU.max),
                  r=[("pb", 2)], w=[("t1", half)])
        lora2(a2, "a2", taB, hv["a0h"], F["aF"], AF.Sigmoid)
        P.act(lambda e: e.activation(out=t1[:], in_=t1[:], func=AF.Sqrt), r=["t1"], w=["t1"])
        P.dve(lambda e: e.reciprocal(out=t1[:], in_=t1[:]), r=["t1"], w=["t1"])
        for g in (range(2) if full else ()):
            bank = bk(g)
            for hh in range(8):
                h = g * 8 + hh
                P.pe(_mm(bank[:, hh, :], g2a[:, h * 64:(h + 1) * 64], tgA[:], True, False), r=["g2a", kn(tgA)], w=[("pb", g)])
                P.pe(_mm(bank[:, hh, :], g2b[:, h * 64:(h + 1) * 64], tgB[:], False, True), r=["g2b", kn(tgB)], w=[("pb", g)])
            P.act(lambda e, g=g, bank=bank: e.activation(out=F["gF"][:, g * 8:(g + 1) * 8, :], in_=bank, func=AF.Identity),
                  r=[("pb", g)], w=[("gF", g)])
        tt("dve", kap[:], kap[:], t1[:], ALU.mult, ["kap", "t1"], ["kap"])
        tt("pool", t0[:], aF[:], bc(hv["kah"]), ALU.mult, ["aF", "kah"], ["t0"])
        tt("pool", t0[:], t0[:], bc(omka), ALU.add, ["t0", "omka"], ["t0"])
        tt("pool", kF[:], kF[:], t0[:], ALU.mult, ["kF", "t0"], ["kF"])
        P.dve(lambda e: e.tensor_tensor_scan(out=cum[:].rearrange("p h c -> p (h c)"), data0=rmask[:],
                                             data1=sg[:].rearrange("p h c -> p (h c)"), initial=0.0,
                                             op0=ALU.mult, op1=ALU.add), r=["rmask_o", "sg"], w=["cum"])
        tt("dve", t0[:], cum[:], sg[:], ALU.subtract, ["cum", "sg"], ["t0"])
        tt("dve", t2[:], kap[:], aF[:], ALU.mult, ["kap", "aF"], ["sg"])
        P.act(lambda e: e.activation(out=t0[:], in_=t0[:], func=AF.Exp, scale=-C0), r=["t0"], w=["t0"])
        tt("dve", B["KT"][:], kap[:], t0[:], ALU.mult, ["kap", "t0"], ["KT"])
        P.act(lambda e: e.activation(out=t0[:], in_=cum[:], func=AF.Exp, scale=-C0), r=["cum"], w=["t0"])
        if full:
            tt("dve", B["RT"][:], rF[:], t0[:], ALU.mult, ["rF", "t0"], ["RT"])
        P.act(lambda e: e.activation(out=gC[:], in_=cum[:, :, C - 1], func=AF.Exp, scale=-C0), r=["cum"], w=["gC"])
        P.act(lambda e: e.activation(out=t1[:], in_=cum[:], func=AF.Exp, scale=C0), r=["cum"], w=["t1"])
        tt("dve", t2[:], t2[:], t1[:], ALU.mult, ["sg", "t1"], ["sg"])
        tt("dve", t1[:], kF[:], t1[:], ALU.mult, ["kF", "t1"], ["t1"])
        P.act(lambda e: e.activation(out=B["BT"][:], in_=t2[:], func=AF.Identity), r=["sg"], w=["BT"])
        P.act(lambda e: e.activation(out=B["KK"][:], in_=t1[:], func=AF.Identity), r=["t1"], w=["KK"])
        tt("pool", B["BH"][:], t2[:], bc(gC), ALU.mult, ["sg", "gC"], ["BH"])
        tt("dve", B["KH"][:], t1[:], bc(gC), ALU.mult, ["t1", "gC"], ["KH"])
        if full:
            tt("dve", t0[:], rF[:], kF[:], ALU.mult, ["rF", "kF"], ["t0"])
            tt("dve", t0[:], t0[:], bc(hv["rkh"]), ALU.mult, ["t0", "rkh"], ["t0"])
            t0f_ = t0[:].rearrange("p h c -> p (h c)")
            for half in range(2):
                hs = slice(half * 512, (half + 1) * 512)
                P.pe(_mm(pb[2][0:64, :], ones1[:], t0f_[:, hs], True, True), r=["ones1", "t0"], w=[("pb", 2)])
                tt("dve", kap[:].rearrange("p h c -> p (h c)")[:, hs], pb[2][0:64, :],
                   vF[:].rearrange("p h c -> p (h c)")[:, hs], ALU.mult, [("pb", 2), "vF"], [("kap", half)])
        P.stage = f"mid{j}_tr"
        for src, dst in (("BH", "BHT"), ("KH", "KHT")):
            for g in range(2):
                for hh in range(8):
                    h = g * 8 + hh
                    P.pe(lambda e, hh=hh, h=h, src=src: e.transpose(trp[:, hh, :], B[src][:, h, :], ident[:]),
                         r=[src, "ident"], w=["trp"])
                P.act(lambda e, g=g, dst=dst: e.activation(out=B[dst][:, g * 8:(g + 1) * 8, :], in_=trp[:], func=AF.Identity),
                      r=["trp"], w=[(dst, g)])
        P.stage = f"mid{j}_scores"
        BS = ((0, 1, 2), (3, 4, 5))

        def score(lhs, rhs, dst, mask, neg, g, bank_i):
            bank = bk(bank_i)
            for hh in range(8):
                h = g * 8 + hh
                P.pe(_mm(bank[:, hh, :], B[lhs][:, h, :], B[rhs][:, h, :], True, True), r=[lhs, rhs], w=[("pb", bank_i)])
            P.dve(lambda e: e.scalar_tensor_tensor(out=grp(B[dst], g), in0=bank, scalar=(-1.0 if neg else 1.0),
                                                   in1=mask[:].unsqueeze(1).to_broadcast([64, 8, 64]),
                                                   op0=ALU.mult, op1=ALU.mult), r=[("pb", bank_i), kn(mask)], w=[(dst, g)])

        def NB(name, g):
            return B[f"{name}_{g}"]

        for g in range(2):
            score("KT", "BT", f"N0_{g}", mLs, True, g, BS[g][0])
            score("BT", "KT", f"Nt0_{g}", mUs, True, g, BS[g][1])
        for g in range(2):
            score("KK", "KT", "AkT", mUs, False, g, BS[g][2])
            if full:
                score("BT", "RT", "PbT", mUi, False, g, BS[g][0])
        for g in range(2):
            gs = slice(g * 8, (g + 1) * 8)
            if full:
                score("KK", "RT", "PkT", mUi, False, g, BS[g][1])
            tt("dve", B["Tt"][:, gs, :], NB("Nt0", g)[:], eye[:].unsqueeze(1).to_broadcast([64, 8, 64]), ALU.add,
               [(f"Nt0_{g}", g), "eye"], [("Tt", g)])
        cur = 0
        P.stage = f"mid{j}_dbl"
        for lv in range(5):
            for g in range(2):
                gs = slice(g * 8, (g + 1) * 8)
                Nc, Ntc = NB(f"N{cur}", g), NB(f"Nt{cur}", g)
                Nn, Ntn = NB(f"N{1 - cur}", g), NB(f"Nt{1 - cur}", g)
                kNc, kNtc = (f"N{cur}_{g}", g), (f"Nt{cur}_{g}", g)
                kNn, kNtn = (f"N{1 - cur}_{g}", g), (f"Nt{1 - cur}_{g}", g)
                bA, bB, bCk = (bk(i) for i in BS[g])
                kA, kB, kC = (("pb", i) for i in BS[g])
                for hh in range(8):
                    P.pe(_mm(bA[:, hh, :], Ntc[:, hh, :], Nc[:, hh, :], True, True), r=[kNc, kNtc], w=[kA])
                P.act(lambda e, Nn=Nn, bA=bA: e.activation(out=Nn[:], in_=bA, func=AF.Identity), r=[kA], w=[kNn])
                if lv < 4:
                    for hh in range(8):
                        P.pe(_mm(bB[:, hh, :], Nc[:, hh, :], Ntc[:, hh, :], True, True), r=[kNc, kNtc], w=[kB])
                    P.act(lambda e, Ntn=Ntn, bB=bB: e.activation(out=Ntn[:], in_=bB, func=AF.Identity), r=[kB], w=[kNtn])
                for hh in range(8):
                    h = g * 8 + hh
                    P.pe(_mm(bCk[:, hh, :], Nn[:, hh, :], B["Tt"][:, h, :], True, True), r=[kNn, ("Tt", g)], w=[kC])
                tt("dve", B["Tt"][:, gs, :], B["Tt"][:, gs, :], bCk, ALU.add, [("Tt", g), kC], [("Tt", g)])
            cur = 1 - cur
        P.stage = f"mid{j}_state"
        for g in range(2):
            gs = slice(g * 8, (g + 1) * 8)
            bZ = bk(BS[g][0]); kZ = ("pb", BS[g][0])
            for hh in range(8):
                h = g * 8 + hh
                vh = Vtm[:, h * 64:(h + 1) * 64]
                P.pe(_mm(bZ[:, hh, :], B["KT"][:, h, :], Mb[:, h, :], True, False), r=["KT", ("Mb", g)], w=[kZ])
                P.pe(_mm(bZ[:, hh, :], B["AkT"][:, h, :], vh, False, True), r=[("AkT", g), "Vtm"], w=[kZ])
            P.act(lambda e, bZ=bZ, gs=gs: e.activation(out=B["nZ"][:, gs, :], in_=bZ, func=AF.Identity, scale=-1.0),
                  r=[kZ], w=[("BH", g)])
        for g in range(2):
            gs = slice(g * 8, (g + 1) * 8)
            bU = bk(BS[g][1]); kU = ("pb", BS[g][1])
            for hh in range(8):
                h = g * 8 + hh
                P.pe(_mm(bU[:, hh, :], B["Tt"][:, h, :], B["nZ"][:, h, :], True, True), r=[("Tt", g), ("BH", g)], w=[kU])
            P.act(lambda e, bU=bU, gs=gs: e.activation(out=B["U"][:, gs, :], in_=bU, func=AF.Identity), r=[kU], w=[("KH", g)])
        for g in range(2):
            gs = slice(g * 8, (g + 1) * 8)
            bM = bk(BS[g][2]); kM = ("pb", BS[g][2])
            bY = bk(BS[g][0]); kY = ("pb", BS[g][0])
            for hh in (range(8) if full else ()):
                h = g * 8 + hh
                vh = Vtm[:, h * 64:(h + 1) * 64]
                P.pe(_mm(bY[:, hh, :], Mb[:, h, :], B["RT"][:, h, :], True, False), r=[("Mb", g), "RT"], w=[kY])
                P.pe(_mm(bY[:, hh, :], B["U"][:, h, :], B["PbT"][:, h, :], False, False), r=[("KH", g), ("PbT", g)], w=[kY])
                P.pe(_mm(bY[:, hh, :], vh, B["PkT"][:, h, :], False, True), r=["Vtm", ("PkT", g)], w=[kY])
            if full:
                P.act(lambda e, bY=bY, gs=gs: e.activation(out=F["cum"][:, gs, :], in_=bY, func=AF.Identity), r=[kY], w=[("cum", g)])
            for hh in range(8):
                h = g * 8 + hh
                vh = Vtm[:, h * 64:(h + 1) * 64]
                P.pe(_mm(bM[:, hh, :], B["BHT"][:, h, :], B["U"][:, h, :], True, False), r=[("BHT", g), ("KH", g)], w=[kM])
                P.pe(_mm(bM[:, hh, :], B["KHT"][:, h, :], vh, False, True), r=[("KHT", g), "Vtm"], w=[kM])
            tt("dve", M[:, gs, :], M[:, gs, :], gC[:, gs].unsqueeze(2).to_broadcast([64, 8, 64]), ALU.mult,
               [("M", g), "gC"], [("M", g)])
            tt("dve", M[:, gs, :], M[:, gs, :], bM, ALU.add, [("M", g), kM], [("M", g)])
            P.act(lambda e, gs=gs: e.activation(out=Mb[:, gs, :], in_=M[:, gs, :], func=AF.Identity), r=[("M", g)], w=[("Mb", g)])
    def late_segments(j):
        s = j % 2
        X = XT[s]
        rF, kF, vF, aF, sg, cum, kap, t0, t1 = (F[n] for n in ("rF", "kF", "vF", "aF", "sg", "cum", "kap", "t0", "t1"))
        t2 = sg
        Y = F["cum"]
        Yf = Y[:].rearrange("p h c -> p (h c)")
        t0f = t0[:].rearrange("p h c -> p (h c)"); t1f = t1[:].rearrange("p h c -> p (h c)"); t2f = t2[:].rearrange("p h c -> p (h c)")

        def l1():
            P.act(lambda e: e.activation(out=t0[:], in_=Y[:], func=AF.Square), r=["cum"], w=["t0"])
            for half in range(2):
                hs = slice(half * 512, (half + 1) * 512)
                P.pe(_mm(pb[4][0:64, :], ones64[:], Yf[:, hs], True, True), r=["ones64", "cum"], w=[("pb", 4)])
                P.pe(_mm(pb[5][0:64, :], ones64[:], t0f[:, hs], True, True), r=["ones64", "t0"], w=[("pb", 5)])
                P.act(lambda e, hs=hs: e.activation(out=t1f[:, hs], in_=pb[4][0:64, :], func=AF.Identity), r=[("pb", 4)], w=[("t1", half)])
                tt("dve", t2f[:, hs], t1f[:, hs], t1f[:, hs], ALU.mult, [("t1", half)], [("sg", half)])
                tt("dve", t2f[:, hs], pb[5][0:64, :], t2f[:, hs], ALU.subtract, [("pb", 5), ("sg", half)], [("sg", half)])
            P.dve(lambda e: e.tensor_scalar(out=t2[:], in0=t2[:], scalar1=0.0, scalar2=64e-5, op0=ALU.max, op1=ALU.add), r=["sg"], w=["sg"])
            P.act(lambda e: e.activation(out=t2[:], in_=t2[:], func=AF.Sqrt), r=["sg"], w=["sg"])
            P.dve(lambda e: e.reciprocal(out=t2[:], in_=t2[:]), r=["sg"], w=["sg"])

        def l2():
            tt("dve", Y[:], Y[:], t1[:], ALU.subtract, ["cum", "t1"], ["cum"])
            tt("dve", Y[:], Y[:], t2[:], ALU.mult, ["cum", "sg"], ["cum"])
            tt("pool", Y[:], Y[:], bc(hv["gng"]), ALU.mult, ["cum", "gng"], ["cum"])
            tt("pool", Y[:], Y[:], bc(hv["gnb"]), ALU.add, ["cum", "gnb"], ["cum"])
            tt("dve", Y[:], Y[:], kap[:], ALU.add, ["cum", "kap"], ["cum"])
            tt("dve", B["zB"][:], Y[:], F["gF"][:], ALU.mult, ["cum", "gF"], ["zB"])

        def l3():
            ob = pb[4][:, :].rearrange("p (c t) -> p c t", t=TT)
            for db in range(KC):
                for h in range(H):
                    P.pe(_mm(ob[:, db, :], wo[:, h, db * 128:(db + 1) * 128], B["zB"][:, h, :], h == 0, h == H - 1),
                         r=[("wo", h // 2), "zB"], w=[("pb", 4)])
            P.dve(lambda e: e.scalar_tensor_tensor(out=xr_[:], in0=X[:, :, 1:TT + 1], scalar=float(ALPHA), in1=ob,
                                                   op0=ALU.mult, op1=ALU.add), r=[("XT", s), ("pb", 4)], w=["xr_"])

        def l4():
            for c in range(KC):
                P.pe(_mm(mean_ps, ones1024[:], xr_[:, c, :], c == 0, c == KC - 1), r=["ones1024", "xr_"], w=["mean_ps"])
            P.act(lambda e: e.activation(out=sq8[:], in_=xr_[:], func=AF.Square), r=["xr_"], w=["sq8"])
            for c in range(KC):
                P.pe(_mm(msq_ps, ones1024[:], sq8[:, c, :], c == 0, c == KC - 1), r=["ones1024", "sq8"], w=["msq_ps"])
            P.act(lambda e: e.activation(out=mean_sb[:], in_=mean_ps, func=AF.Identity), r=["mean_ps"], w=["mean_sb"])
            tt("dve", var_sb[:], mean_sb[:], mean_sb[:], ALU.mult, ["mean_sb"], ["var_sb"])
            tt("dve", var_sb[:], msq_ps, var_sb[:], ALU.subtract, ["msq_ps", "var_sb"], ["var_sb"])
            P.dve(lambda e: e.tensor_scalar(out=var_sb[:], in0=var_sb[:], scalar1=0.0, scalar2=float(LN_EPS),
                                            op0=ALU.max, op1=ALU.add), r=["var_sb"], w=["var_sb"])
            P.act(lambda e: e.activation(out=var_sb[:], in_=var_sb[:], func=AF.Sqrt), r=["var_sb"], w=["var_sb"])
            P.dve(lambda e: e.reciprocal(out=rstd_sb[:], in_=var_sb[:]), r=["var_sb"], w=["rstd_sb"])
            tt("dve", xr_[:], xr_[:], mean_sb[:].unsqueeze(1).to_broadcast([128, KC, TT]), ALU.subtract, ["xr_", "mean_sb"], ["xr_"])
            tt("dve", xr_[:], xr_[:], rstd_sb[:].unsqueeze(1).to_broadcast([128, KC, TT]), ALU.mult, ["xr_", "rstd_sb"], ["xr_"])
            tt("dve", xr_[:], xr_[:], lng[:].unsqueeze(2).to_broadcast([128, KC, TT]), ALU.mult, ["xr_", "lng"], ["xr_"])
            tt("dve", xr_[:], xr_[:], lnb[:].unsqueeze(2).to_broadcast([128, KC, TT]), ALU.add, ["xr_", "lnb"], ["xr_"])
            P.dma("sp", lambda e: e.dma_start(out=out_v[:, :, j * TT:(j + 1) * TT], in_=xr_[:]), r=["xr_"], w=[ok + (j,)])
        return [l1, l2, l3, l4]

    load_x(0)
    if ntiles > 1:
        load_x(1)
    for seg in early_segments(0):
        seg()
    for j in range(ntiles):
        P.stage = f"mid{j}"
        mid(j)
        E = early_segments(j + 1) if j + 1 < ntiles else []
        L = late_segments(j) if full else []
        for k in range(4):
            if k < len(E):
                P.stage = f"early{j + 1}"
                E[k]()
            if k < len(L):
                P.stage = f"late{j}"
                L[k]()
        if j + 2 < ntiles:
            load_x(j + 2)
    P.dma("sp", lambda e: e.dma_start(out=io["M_out"], in_=M[:]), r=["M"], w=[io.get("Mout_key", ("M_out",))])


ODD_IN = dict(w_r=[D, D], w_k=[D, D], w_v=[D, D], w_o=[D, D], w1=[D, 64], a1=[D, 64], g1=[D, 160], w2=[64, D], a2=[64, D],
              g2=[160, D], mu=[128, 6, KC], w0h=[64, H], a0h=[64, H], kkh=[64, H], kah=[64, H], rkh=[64, H], gng=[64, H],
              gnb=[64, H], lng=[128, KC], lnb=[128, KC], mUs=[64, 64], mUi=[64, 64], mLs=[64, 64], eye=[64, 64],
              rmask_o=[64, H * C], id32_o=[64, 64], M=[64, H, 64])


def build_odd_nc(NT):
    nc = bass.Bass("TRN2", target_bir_lowering=False)
    io = {"xin": nc.dram_tensor("xin", [D, 1 + NT], F32, kind="ExternalInput").ap()}
    for k, shp in ODD_IN.items():
        io[k] = nc.dram_tensor(k, list(shp), F32, kind="ExternalInput").ap()
    io["out"] = nc.dram_tensor("out", [D, NT], F32, kind="ExternalOutput").ap()
    io["M_out"] = nc.dram_tensor("M_out", [64, H, 64], F32, kind="ExternalOutput").ap()
    P = Prog(nc)
    P.begin_phase()
    build_odd(nc, P, NT, io)
    P.end_phase()
    P.finish()
    return nc, P


def hvec(v):
    return np.ascontiguousarray(np.asarray(v, np.float32).reshape(H, 64).T)


def odd_inputs(xin_T, p, j, lng, lnb, M):
    i = np.arange(64)
    d = dict(xin=np.ascontiguousarray(xin_T, dtype=np.float32))
    for k in ("w_r", "w_k", "w_v", "w_o", "w1", "a1", "g1", "w2", "a2", "g2"):
        d[k] = np.ascontiguousarray(p["rw_" + k][j], dtype=np.float32)
    d["mu"] = np.ascontiguousarray(np.stack([vec_layout(p["rw_mu"][j][i6], KC) for i6 in range(6)], axis=1))
    d["w0h"] = hvec(p["rw_w0"][j]); d["a0h"] = hvec(p["rw_a0"][j]); d["kkh"] = hvec(p["rw_k_k"][j])
    d["kah"] = hvec(p["rw_k_a"][j]); d["rkh"] = hvec(p["rw_r_k"][j].reshape(-1)); d["gng"] = hvec(p["rw_gn_g"][j])
    d["gnb"] = hvec(p["rw_gn_b"][j]); d["lng"] = vec_layout(lng, KC); d["lnb"] = vec_layout(lnb, KC)
    d["mUs"] = (i[:, None] < i[None, :]).astype(np.float32); d["mUi"] = (i[:, None] <= i[None, :]).astype(np.float32)
    d["mLs"] = (i[:, None] > i[None, :]).astype(np.float32); d["eye"] = np.eye(64, dtype=np.float32)
    rm = np.ones((64, H * C), np.float32); rm[:, ::C] = 0.0
    d["rmask_o"] = rm; d["id32_o"] = np.eye(64, dtype=np.float32); d["M"] = np.ascontiguousarray(M, dtype=np.float32)
    return d


NCORES = 8
SEQ = 8192
NTC = SEQ // 2
TT_E = 256
TT_F = 256
HMAX = 30
PAIRS = [[0, 1], [2, 3], [4, 5], [6, 7]]

EVEN_L = dict(w_in=[D, 3072], w_out=[D, D], bin=[128, 24], convw=[128, 4 * 31], convb=[128, 4], clng=[128, 4],
              clnb=[128, 4], lb0=[128, 4], lb1=[128, 4], lbsel=[128, 1], ong=[128, 4], lng=[128, KC], lnb=[128, KC],
              bivbc=[128, 512])
EVEN_C = dict(mask=[128, 128], rmask=[128, TT_E], ident=[128, 128])
ODD_C = ("mUs", "mUi", "mLs", "eye", "rmask_o", "id32_o")
FFN_L = dict(w_up=[D, DFF], w_gate=[D, DFF], w_down=[DFF, D], cw=[128, FB * 3], cb=[128, FB], lng=[128, KC], lnb=[128, KC])


def _exchange(P, nc, tag, src_ap, src_key, rows, cols, bin_t, bout_t, dst_ap, dst_key, sel_ap, pairs=None):
    P.begin_phase(tag)
    np_ = min(rows, 128)
    nch = rows // np_
    t = P.sb([np_, nch, cols], F32, "xt")
    selt = P.sb([128, 1], F32, "sel")
    P.dma("sp", lambda e: e.dma_start(out=bin_t.ap(), in_=src_ap), r=[src_key], w=[(tag, "bin")])
    P.cc(lambda e: e.collective_compute("AllGather", ALU.bypass, replica_groups=(pairs or PAIRS),
                                        ins=[bin_t.ap().opt()], outs=[bout_t.ap().opt()]),
         r=[(tag, "bin")], w=[(tag, "bout")])
    P.dma("sp", lambda e: e.dma_start(out=t[:], in_=bout_t.ap()[0:rows, :].rearrange("(c p) t -> p c t", p=np_)),
          r=[(tag, "bout")], w=["xt"])
    P.dma("sp", lambda e: e.dma_start(out=selt[:], in_=sel_ap), w=["selt"])
    P.dve(lambda e: e.tensor_scalar(out=t[:], in0=t[:], scalar1=selt[0:np_, 0:1], scalar2=None, op0=ALU.mult),
          r=["xt", "selt"], w=["xt"])
    P.dma("sp", lambda e: e.dma_start(out=dst_ap.rearrange("(c p) t -> p c t", p=np_), in_=t[:]), r=["xt"], w=[dst_key])
    P.end_phase()


def build_fused_nc(NT=NTC, nlayers=DEPTH, ncores=NCORES):
    nc = bass.Bass("TRN2", target_bir_lowering=False)
    pairs = [[2 * i, 2 * i + 1] for i in range(ncores // 2)]

    def din(name, shape):
        return nc.dram_tensor(name, list(shape), F32, kind="ExternalInput").ap()

    def dint(name, shape):
        return nc.dram_tensor(name, list(shape), F32)

    xin = din("xin", [D, HMAX + NT])
    sel = din("sel", [128, 1])
    zst = din("zst", [128, 1024])
    out = nc.dram_tensor("out", [D, NT], F32, kind="ExternalOutput").ap()
    consts = {k: din(k, v) for k, v in EVEN_C.items()}
    for k in ODD_C:
        consts[k] = din(k, ODD_IN[k])
    slab = [dint(f"slab{i}", [D, HMAX + NT]) for i in range(2)]
    hx_in = dint("hx_in", [D, HMAX]); hx_out = dint("hx_out", [2 * D, HMAX])
    se_in = dint("se_in", [128, 512]); se_out = dint("se_out", [256, 512]); se_sel = dint("se_sel", [128, 512])
    so_in = dint("so_in", [64, 1024]); so_out = dint("so_out", [128, 1024]); so_sel = dint("so_sel", [64, 1024])
    dump = dint("dump", [128, 1024])
    P = Prog(nc)
    for layer in range(nlayers):
        src = xin if layer == 0 else slab[1].ap()
        skey = ("xin_ext",) if layer == 0 else ("slab", 1)
        if layer > 0:
            _exchange(P, nc, f"hx{layer}m_", slab[1].ap()[:, NT:NT + HMAX], ("slab", 1), D, HMAX, hx_in, hx_out,
                      slab[1].ap()[:, 0:HMAX], ("slab", 1, "halo"), sel, pairs)
        if layer % 2 == 0:
            io = {k: din(f"{k}_{layer}", v) for k, v in EVEN_L.items()}
            io.update(consts)
            io["hsc"] = sel
            io["xin"] = src
            io["xin_key"] = skey
            ioa = dict(io); ioa["S"] = zst[:, 0:512].rearrange("p (h v) -> p h v", h=4); ioa["S_key"] = ("zst",)
            ioa["out"] = slab[0].ap()[:, HMAX:HMAX + NT]; ioa["S_out"] = se_in.ap().rearrange("p (h v) -> p h v", h=4)
            ioa["Sout_key"] = ("se_in",)
            P.begin_phase(f"L{layer}a_"); build_even(nc, P, NT, TT_E, ioa, state_only=True); P.end_phase()
            _exchange(P, nc, f"sx{layer}_", se_in.ap(), ("se_in",), 128, 512, se_in, se_out, se_sel.ap(), ("se_sel",), sel, pairs)
            iob = dict(io); iob["S"] = se_sel.ap().rearrange("p (h v) -> p h v", h=4); iob["S_key"] = ("se_sel",)
            iob["out"] = slab[0].ap()[:, HMAX:HMAX + NT]; iob["out_key"] = ("slab", 0, "body")
            iob["S_out"] = dump.ap()[:, 0:512].rearrange("p (h v) -> p h v", h=4); iob["Sout_key"] = ("dump",)
            P.begin_phase(f"L{layer}b_"); build_even(nc, P, NT, TT_E, iob); P.end_phase()
        else:
            io = {k: din(f"{k}_{layer}", v) for k, v in ODD_IN.items() if k not in ODD_C and k != "M"}
            io.update(consts)
            io["xin"] = src[:, HMAX - 1:HMAX + NT]
            io["xin_key"] = skey
            ioa = dict(io); ioa["M"] = zst[0:64, :].rearrange("p (h v) -> p h v", h=H); ioa["M_key"] = ("zst",)
            ioa["out"] = slab[0].ap()[:, HMAX:HMAX + NT]; ioa["M_out"] = so_in.ap().rearrange("p (h v) -> p h v", h=H)
            ioa["Mout_key"] = ("so_in",)
            P.begin_phase(f"L{layer}a_"); build_odd(nc, P, NT, ioa, state_only=True); P.end_phase()
            _exchange(P, nc, f"sx{layer}_", so_in.ap(), ("so_in",), 64, 1024, so_in, so_out, so_sel.ap(), ("so_sel",), sel, pairs)
            iob = dict(io); iob["M"] = so_sel.ap().rearrange("p (h v) -> p h v", h=H); iob["M_key"] = ("so_sel",)
            iob["out"] = slab[0].ap()[:, HMAX:HMAX + NT]; iob["out_key"] = ("slab", 0, "body")
            iob["M_out"] = dump.ap()[0:64, :].rearrange("p (h v) -> p h v", h=H); iob["Mout_key"] = ("dump",)
            P.begin_phase(f"L{layer}b_"); build_odd(nc, P, NT, iob); P.end_phase()
        _exchange(P, nc, f"hx{layer}f_", slab[0].ap()[:, NT:NT + HMAX], ("slab", 0), D, HMAX, hx_in, hx_out,
                  slab[0].ap()[:, 0:HMAX], ("slab", 0, "halo"), sel, pairs)
        iof = {k: din(f"{k}_f{layer}", v) for k, v in FFN_L.items()}
        iof["xin"] = slab[0].ap()[:, HMAX - 2:HMAX + NT]; iof["xin_key"] = ("slab", 0)
        last = layer == nlayers - 1
        iof["out"] = out if last else slab[1].ap()[:, HMAX:HMAX + NT]
        iof["out_key"] = ("out_ext",) if last else ("slab", 1, "body")
        P.begin_phase(f"L{layer}f_"); build_ffn(nc, P, NT, TT_F, iof); P.end_phase()
    P.finish()
    return nc, P


def fused_inputs(inp, NT=NTC, nlayers=DEPTH, x_slabs=None):
    base = {}
    ec = even_consts(TT_E)
    base.update(ec)
    i = np.arange(64)
    base["mUs"] = (i[:, None] < i[None, :]).astype(np.float32); base["mUi"] = (i[:, None] <= i[None, :]).astype(np.float32)
    base["mLs"] = (i[:, None] > i[None, :]).astype(np.float32); base["eye"] = np.eye(64, dtype=np.float32)
    rm = np.ones((64, H * C), np.float32); rm[:, ::C] = 0.0
    base["rmask_o"] = rm; base["id32_o"] = np.eye(64, dtype=np.float32)
    base["zst"] = np.zeros((128, 1024), np.float32)
    dummy_x = np.zeros((D, 1), np.float32)
    for layer in range(nlayers):
        j = layer // 2
        if layer % 2 == 0:
            d = even_inputs(dummy_x, inp["ev_w_in"][j], inp["ev_b_in"][j], inp["ev_conv_w"][j], inp["ev_conv_b"][j],
                            inp["ev_cln_g"][j], inp["ev_cln_b"][j], inp["ev_lb_logits"], j, inp["ev_onorm_g"][j],
                            inp["ev_w_out"][j], inp["ln_mix_g"][layer], inp["ln_mix_b"][layer],
                            np.zeros((1,), np.float32), True, TT_E)
            for k in EVEN_L:
                base[f"{k}_{layer}"] = d[k]
        else:
            d = odd_inputs(dummy_x, inp, j, inp["ln_mix_g"][layer], inp["ln_mix_b"][layer], np.zeros((1,), np.float32))
            for k in ODD_IN:
                if k not in ODD_C and k != "M":
                    base[f"{k}_{layer}"] = d[k]
        d = ffn_inputs(dummy_x, inp["ff_w_up"][layer], inp["ff_w_gate"][layer], inp["ff_conv_w"][layer],
                       inp["ff_conv_b"][layer], inp["ff_w_down"][layer], inp["ln_ffn_g"][layer], inp["ln_ffn_b"][layer])
        for k in FFN_L:
            base[f"{k}_f{layer}"] = d[k]
    maps = []
    for c in range(len(x_slabs)):
        m = dict(base)
        m["xin"] = x_slabs[c]
        m["sel"] = np.full((128, 1), float(c % 2), np.float32)
        maps.append(m)
    return maps


_PROGS = {}


def kernel(**inp):
    inp = {k: np.asarray(v) for k, v in inp.items()}
    x = inp["x"].astype(np.float32)
    B = x.shape[0]
    slabs = []
    for c in range(NCORES):
        b, h = divmod(c, 2)
        xT = x[b].T
        if h == 0:
            sl = np.concatenate([np.zeros((D, HMAX), np.float32), xT[:, :NTC]], axis=1)
        else:
            sl = xT[:, NTC - HMAX:]
        slabs.append(np.ascontiguousarray(sl))
    if "fused" not in _PROGS:
        _PROGS["fused"] = build_fused_nc(NTC, DEPTH)[0]
    maps = fused_inputs(inp, NTC, DEPTH, slabs)
    res = run_bass_kernel_spmd(_PROGS["fused"], maps, core_ids=list(range(NCORES))).results
    out = np.empty((B, SEQ, D), np.float32)
    for c in range(NCORES):
        b, h = divmod(c, 2)
        out[b, h * NTC:(h + 1) * NTC, :] = res[c]["out"].T
    return out
```

```python
import contextlib
import os
import numpy as np
import concourse.bass as bass
import concourse.mybir as mybir
from concourse.bass_utils import run_bass_kernel_spmd

F32 = mybir.dt.float32
BF16 = mybir.dt.bfloat16
ALU = mybir.AluOpType
AF = mybir.ActivationFunctionType

D = 1024
KC = D // 128
DFF = 2816
FB = DFF // 128
DEPTH = 4
ALPHA = (2 * DEPTH) ** 0.25
LN_EPS = 1e-5


class _Rec:
    def __init__(self):
        self.call = None

    def __getattr__(self, name):
        def f(*a, **k):
            assert self.call is None, "op lambda must make exactly one engine call"
            self.call = (name, a, k)
            return self
        return f


class Prog:
    ENGS = ("pe", "act", "dve", "pool", "sp")
    EPOCH = 16000
    NEP = dict(pe=8, act=5, dve=6, pool=4, sp=1)
    NDMA = 24
    NCC = 16

    def __init__(self, nc):
        self.nc = nc
        self.ops = []
        self.last_w = {}
        self.readers = {}
        self.gstack = contextlib.ExitStack()
        self.stack = None
        self.ntile = 0
        self.known = set()
        self.children = {}
        self.bank_of = {}
        self.bank_last = {}
        self.prefix = ""
        self.phase_start = 0
        self.nphase = 0
        self.cnt = {e: 0 for e in self.ENGS}
        self.ndma = 0
        self.ncc = 0
        self.dma_prev = {}
        self.sems = {}
        g = self.gstack
        for e in self.ENGS:
            for ep in range(self.NEP[e]):
                self.sems[(e, ep)] = g.enter_context(nc.semaphore(f"s_{e}_{ep}"))
        for i in range(self.NDMA):
            self.sems[("dma", i)] = g.enter_context(nc.semaphore(f"s_dma_{i}"))
        for i in range(self.NCC):
            self.sems[("cc", i)] = g.enter_context(nc.semaphore(f"s_cc_{i}"))
        self.bar = g.enter_context(nc.semaphore("s_bar"))
        self.stats = dict(n_ops=0)

    def begin_phase(self, prefix=""):
        self.stack = contextlib.ExitStack()
        self.prefix = prefix
        self.phase_start = len(self.ops)
        self.bank_of = {}
        self.bank_last = {}

    def end_phase(self):
        self._emit_phase()
        self.stack.close()
        self.stack = None

    def finish(self):
        self.gstack.close()
        self.stats = dict(n_ops=len(self.ops), milestones=dict(self.cnt), ndma=self.ndma, ncc=self.ncc)

    def sb(self, shape, dt, name=None):
        self.ntile += 1
        name = "sb_" + self.prefix + (name or f"t{self.ntile}")
        return self.stack.enter_context(self.nc.sbuf_tensor(name, list(shape), dt))

    def ps(self, shape, dt=F32, name=None, keys=None):
        self.ntile += 1
        nm = name or f"p{self.ntile}"
        for k in (keys if keys is not None else [nm]):
            k = k if isinstance(k, tuple) else (k,)
            self.bank_of[k] = nm
        return self.stack.enter_context(self.nc.psum_tensor("ps_" + self.prefix + nm, list(shape), dt))

    def _banks(self, keys):
        out = set()
        for k in keys:
            for i in range(1, len(k) + 1):
                if k[:i] in self.bank_of:
                    out.add(self.bank_of[k[:i]])
        return out

    def _norm(self, k):
        k = k if isinstance(k, tuple) else (k,)
        if k not in self.known:
            self.known.add(k)
            for i in range(1, len(k)):
                self.children.setdefault(k[:i], set()).add(k)
        return k

    def _related(self, k):
        for i in range(1, len(k) + 1):
            yield k[:i]
        for ext in self.children.get(k, ()):
            yield ext

    def op(self, eng, fn, reads=(), writes=(), dma=False, cc=False):
        idx = len(self.ops)
        deps = set()
        reads = [self._norm(k) for k in reads]
        writes = [self._norm(k) for k in writes]
        for k in reads:
            for r in self._related(k):
                if r in self.last_w:
                    deps.add(self.last_w[r])
        for k in writes:
            for r in self._related(k):
                if r in self.last_w:
                    deps.add(self.last_w[r])
                for x in self.readers.get(r, ()):
                    deps.add(x)
        for k in writes:
            self.last_w[k] = idx
            self.readers[k] = []
            for ext in self.children.get(k, ()):
                self.last_w.pop(ext, None)
                self.readers[ext] = []
        for k in reads:
            self.readers.setdefault(k, []).append(idx)
        for bk in self._banks(reads + writes):
            bl = self.bank_last.setdefault(bk, {})
            for e2, i2 in bl.items():
                if e2 != eng:
                    deps.add(i2)
            bl[eng] = idx
        deps.discard(idx)
        deps = {d for d in deps if d >= self.phase_start}
        rec = _Rec()
        fn(rec)
        assert rec.call is not None
        self.ops.append(dict(eng=eng, call=rec.call, deps=deps, dma=dma or cc, cc=cc, stage=getattr(self, "stage", "")))
        return idx

    def pe(self, fn, r=(), w=()):
        return self.op("pe", fn, r, w)

    def act(self, fn, r=(), w=()):
        return self.op("act", fn, r, w)

    def dve(self, fn, r=(), w=()):
        return self.op("dve", fn, r, w)

    def pool(self, fn, r=(), w=()):
        return self.op("pool", fn, r, w)

    def dma(self, eng, fn, r=(), w=()):
        return self.op(eng, fn, r, w, dma=True)

    def cc(self, fn, r=(), w=()):
        return self.op("pool", fn, r, w, cc=True)

    def _emit_phase(self):
        nc = self.nc
        ops = self.ops
        lo = self.phase_start
        idxs = range(lo, len(ops))
        for i in idxs:
            o = ops[i]
            best = {}
            keep = set()
            for d in o["deps"]:
                p = ops[d]
                if p["dma"]:
                    keep.add(d)
                elif best.get(p["eng"], -1) < d:
                    best[p["eng"]] = d
            keep.update(best.values())
            o["deps"] = keep
        needed = set()
        for i in idxs:
            o = ops[i]
            for d in o["deps"]:
                p = ops[d]
                if p["dma"]:
                    needed.add(d)
                elif p["eng"] == o["eng"] and o["eng"] == "pe" and not o["dma"]:
                    continue
                else:
                    needed.add(d)
        per_eng = {e: [i for i in idxs if ops[i]["eng"] == e] for e in self.ENGS}
        last_compute = {}
        for e in self.ENGS:
            for i in reversed(per_eng[e]):
                if not ops[i]["dma"]:
                    last_compute[e] = i
                    needed.add(i)
                    break
        for i in idxs:
            o = ops[i]
            if o["cc"]:
                assert self.ncc < self.NCC
                o["sig"] = ("cc", self.ncc, 1)
                o["prev_same_slot"] = None
                self.ncc += 1
            elif o["dma"]:
                slot = self.ndma % self.NDMA
                o["sig"] = ("dma", slot, 16 * (self.ndma // self.NDMA + 1))
                o["prev_same_slot"] = self.dma_prev.get(slot)
                self.dma_prev[slot] = i
                self.ndma += 1
            elif i in needed:
                c = self.cnt[o["eng"]]
                assert c // self.EPOCH < self.NEP[o["eng"]], "out of semaphore epochs"
                o["sig"] = (o["eng"], c // self.EPOCH, c % self.EPOCH + 1)
                self.cnt[o["eng"]] = c + 1
            else:
                o["sig"] = None
        sems = self.sems
        self.nphase += 1
        nph = self.nphase
        bar = self.bar

        def run_engine(ename, eng):
            seen = {}

            def wait(sig):
                key = (sig[0], sig[1])
                if seen.get(key, 0) >= sig[2]:
                    return
                eng.wait_ge(sems[key], sig[2])
                seen[key] = sig[2]

            for i in per_eng[ename]:
                o = ops[i]
                best = {}
                for d in o["deps"]:
                    p = ops[d]
                    if (not p["dma"]) and p["eng"] == ename and ename == "pe" and not o["dma"]:
                        continue
                    sg = p["sig"]
                    kk = (sg[0], sg[1])
                    if best.get(kk, 0) < sg[2]:
                        best[kk] = sg[2]
                for kk in sorted(best):
                    wait((kk[0], kk[1], best[kk]))
                if o["dma"] and o["prev_same_slot"] is not None and o["prev_same_slot"] >= lo:
                    wait(ops[o["prev_same_slot"]]["sig"])
                call = o["call"]
                ins = getattr(eng, call[0])(*call[1], **call[2])
                sig = o["sig"]
                if sig is not None:
                    if sig[0] == "dma":
                        ins.then_inc(sems[("dma", sig[1])], 16)
                    elif sig[0] == "cc":
                        ins.then_inc(sems[("cc", sig[1])])
                    else:
                        ins.then_inc(sems[(sig[0], sig[1])], 1)
            for i in per_eng[ename]:
                if ops[i]["dma"]:
                    wait(ops[i]["sig"])
            if ename in last_compute:
                wait(ops[last_compute[ename]]["sig"])
            eng.sem_inc(bar, 1)
            eng.wait_ge(bar, len(self.ENGS) * nph)

        with nc.Block() as block:
            @block.tensor
            def _(e):
                run_engine("pe", e)

            @block.scalar
            def _(e):
                run_engine("act", e)

            @block.vector
            def _(e):
                run_engine("dve", e)

            @block.gpsimd
            def _(e):
                run_engine("pool", e)

            @block.sync
            def _(e):
                run_engine("sp", e)


class Stager:
    def __init__(self, P, width, nbuf=2):
        self.P = P
        self.bufs = [P.sb([128, width], F32, f"stage{i}") for i in range(nbuf)]
        self.n = 0

    def load(self, dst_ap, src_ap, n, dkey, eng=None, a=1):
        P = self.P
        i = self.n % len(self.bufs)
        eng = eng or ("pool", "act", "dve")[self.n % 3]
        self.n += 1
        st = self.bufs[i]
        sv = st[:, 0:n] if a == 1 else st[:, 0:n].rearrange("p (a d) -> p a d", a=a)
        P.dma("sp", lambda e: e.dma_start(out=sv, in_=src_ap), w=[("stage", i)])
        cast_op(P, eng, dst_ap, sv, [("stage", i)], [dkey])


def cast_op(P, eng, dst, src, r, w):
    if eng == "act":
        P.op("act", lambda e: e.activation(out=dst, in_=src, func=AF.Identity), r, w)
    else:
        P.op(eng, lambda e: e.tensor_copy(out=dst, in_=src), r, w)


def _mm(out, lhsT, rhs, start, stop):
    return lambda e: e.matmul(out, lhsT, rhs, start=start, stop=stop)


def build_ffn(nc, P, NT, TT, io):
    HALO = 2
    ntiles = NT // TT
    xk = io.get("xin_key", ("xin_ext",))
    ok = io.get("out_key", ("outdram",))
    xin = io["xin"]
    xin_v = xin.rearrange("(kc p) t -> p kc t", p=128)
    out_v = io["out"].rearrange("(kc p) t -> p kc t", p=128)

    wup = P.sb([128, KC, DFF], BF16, "wup")
    wgt = P.sb([128, KC, DFF], BF16, "wgt")
    wdn = P.sb([128, FB, D], BF16, "wdn")
    cw = P.sb([128, FB * 3], F32, "cw")
    cb = P.sb([128, FB], F32, "cb")
    lng = P.sb([128, KC], F32, "lng")
    lnb = P.sb([128, KC], F32, "lnb")
    ones = P.sb([128, 128], F32, "ones")
    carry = P.sb([128, FB, 2], F32, "carry")
    xbf = [P.sb([128, KC, TT], BF16, f"xbf{i}") for i in range(2)]
    xhalo = P.sb([128, KC, HALO], BF16, "xhalo")
    xres = [P.sb([128, KC, TT], F32, f"xres{i}") for i in range(2)]
    hT = P.sb([128, FB, TT], BF16, "hT")
    NQ = 3
    usb = [P.sb([128, TT + 2], F32, f"usb{i}") for i in range(NQ)]
    t1 = [P.sb([128, TT], F32, f"t1_{i}") for i in range(NQ)]
    gg = [P.sb([128, TT], F32, f"gg{i}") for i in range(NQ)]
    gsb = [P.sb([128, TT], BF16, f"gsb{i}") for i in range(NQ)]
    sq = [P.sb([128, TT], F32, f"sq{i}") for i in range(2)]
    mean_sb = P.sb([128, TT], F32, "mean_sb")
    var_sb = P.sb([128, TT], F32, "var_sb")
    rstd_sb = P.sb([128, TT], F32, "rstd_sb")
    yt = [P.sb([128, TT], F32, f"yt{i}") for i in range(2)]

    up_ps = [P.ps([128, TT], F32, f"up_ps{i}", keys=[("up_ps", i)]) for i in range(2)]
    gt_ps = [P.ps([128, TT], F32, f"gt_ps{i}", keys=[("gt_ps", i)]) for i in range(2)]
    o_ps = [P.ps([128, TT], F32, f"o_ps{i}", keys=[("o_ps", i)]) for i in range(2)]
    mean_ps = P.ps([128, TT], F32, "mean_ps")
    msq_ps = P.ps([128, TT], F32, "msq_ps")

    P.dma("sp", lambda e: e.dma_start(out=cw[:], in_=io["cw"]), w=["cw"])
    P.dma("sp", lambda e: e.dma_start(out=cb[:], in_=io["cb"]), w=["cb"])
    P.dma("sp", lambda e: e.dma_start(out=lng[:], in_=io["lng"]), w=["lng"])
    P.dma("sp", lambda e: e.dma_start(out=lnb[:], in_=io["lnb"]), w=["lnb"])
    P.pool(lambda e: e.memset(ones[:], 1.0 / D), w=["ones"])
    xh32 = P.sb([128, KC, HALO], F32, "xh32")
    P.dma("sp", lambda e: e.dma_start(out=xh32[:], in_=xin_v[:, :, 0:HALO]), r=[xk], w=["xh32"])
    P.pool(lambda e: e.tensor_copy(out=xhalo[:], in_=xh32[:]), r=["xh32"], w=["xhalo"])

    def load_xres(j):
        s = j % 2
        P.dma("sp", lambda e: e.dma_start(out=xres[s][:], in_=xin_v[:, :, HALO + j * TT:HALO + (j + 1) * TT]),
              r=[xk], w=[("xres", s)])
        P.pool(lambda e: e.tensor_copy(out=xbf[s][:], in_=xres[s][:]), r=[("xres", s)], w=[("xbf", s)])

    def load_x(j):
        pass

    load_xres(0)
    stg = Stager(P, 1024, nbuf=3)
    wupv = io["w_up"].rearrange("(kc p) f -> p kc f", p=128)
    wgtv = io["w_gate"].rearrange("(kc p) f -> p kc f", p=128)
    wdnv = io["w_down"].rearrange("(fb p) d -> p fb d", p=128)
    pieces = [(0, 1024), (1024, 1024), (2048, DFF - 2048)]
    for kc in range(KC):
        for (c0, cn) in pieces:
            stg.load(wup[:, kc, c0:c0 + cn], wupv[:, kc, c0:c0 + cn], cn, ("wup", kc, c0))
    for kc in range(KC):
        for (c0, cn) in pieces:
            stg.load(wgt[:, kc, c0:c0 + cn], wgtv[:, kc, c0:c0 + cn], cn, ("wgt", kc, c0))
    for fb in range(FB):
        stg.load(wdn[:, fb, :], wdnv[:, fb, :], D, ("wdn", fb // 2, fb % 2))

    hp = up_ps[1]
    for fb in range(FB):
        for kc in range(KC):
            P.pe(_mm(hp[:, fb * 2:fb * 2 + 2], wup[:, kc, fb * 128:(fb + 1) * 128], xhalo[:, kc, :],
                     kc == 0, kc == KC - 1),
                 r=[("wup", kc), "xhalo"], w=[("up_ps", 1)])
    P.act(lambda e: e.activation(out=carry[:].rearrange("p a b -> p (a b)"), in_=hp[:, 0:FB * 2], func=AF.Identity),
          r=[("up_ps", 1)], w=["carry"])

    nblk = 0
    for j in range(ntiles):
        s = j % 2
        if j + 1 < ntiles:
            load_x(j + 1)
        xb = xbf[s]
        for fb in range(FB):
            b = nblk % 2
            nblk += 1
            fsl = slice(fb * 128, (fb + 1) * 128)
            for kc in range(KC):
                P.pe(_mm(up_ps[b][:], wup[:, kc, fsl], xb[:, kc, :], kc == 0, kc == KC - 1),
                     r=[("wup", kc), ("xbf", s)], w=[("up_ps", b)])
            for kc in range(KC):
                P.pe(_mm(gt_ps[b][:], wgt[:, kc, fsl], xb[:, kc, :], kc == 0, kc == KC - 1),
                     r=[("wgt", kc), ("xbf", s)], w=[("gt_ps", b)])
            q = (nblk - 1) % NQ
            u = usb[q]
            P.pool(lambda e, u=u, fb=fb: e.tensor_copy(out=u[:, 0:2], in_=carry[:, fb, :]),
                   r=[("carry", fb)], w=[("usb", q)])
            P.act(lambda e, u=u, b=b: e.activation(out=u[:, 2:2 + TT], in_=up_ps[b][:], func=AF.Identity),
                  r=[("up_ps", b)], w=[("usb", q)])
            P.act(lambda e, q=q, b=b: e.activation(out=gsb[q][:], in_=gt_ps[b][:], func=AF.Identity),
                  r=[("gt_ps", b)], w=[("gsb", q)])
            P.pool(lambda e, u=u, fb=fb: e.tensor_copy(out=carry[:, fb, :], in_=u[:, TT:TT + 2]),
                   r=[("usb", q)], w=[("carry", fb)])
            tt = t1[q]
            P.dve(lambda e, u=u, tt=tt, fb=fb: e.tensor_scalar(
                out=tt[:], in0=u[:, 2:2 + TT], scalar1=cw[:, fb * 3 + 2:fb * 3 + 3], scalar2=cb[:, fb:fb + 1],
                op0=ALU.mult, op1=ALU.add), r=[("usb", q), "cw", "cb"], w=[("t1", q)])
            P.dve(lambda e, u=u, tt=tt, fb=fb: e.scalar_tensor_tensor(
                out=tt[:], in0=u[:, 1:1 + TT], scalar=cw[:, fb * 3 + 1:fb * 3 + 2], in1=tt[:],
                op0=ALU.mult, op1=ALU.add), r=[("usb", q), ("t1", q)], w=[("t1", q)])
            P.dve(lambda e, u=u, tt=tt, fb=fb: e.scalar_tensor_tensor(
                out=tt[:], in0=u[:, 0:TT], scalar=cw[:, fb * 3:fb * 3 + 1], in1=tt[:],
                op0=ALU.mult, op1=ALU.add), r=[("usb", q), ("t1", q)], w=[("t1", q)])
            g = gg[q]
            P.act(lambda e, g=g, tt=tt: e.activation(out=g[:], in_=tt[:], func=AF.Gelu),
                  r=[("t1", q)], w=[("gg", q)])
            P.dve(lambda e, g=g, q=q, fb=fb: e.tensor_tensor(out=hT[:, fb, :], in0=g[:], in1=gsb[q][:], op=ALU.mult),
                  r=[("gg", q), ("gsb", q)], w=[("hT", fb)])
        if j + 1 < ntiles:
            load_xres(j + 1)
        xr = xres[s]
        for db in range(KC):
            b = db % 2
            dsl = slice(db * 128, (db + 1) * 128)
            for fb in range(FB):
                P.pe(_mm(o_ps[b][:], wdn[:, fb, dsl], hT[:, fb, :], fb == 0, fb == FB - 1),
                     r=[("wdn", fb // 2), ("hT", fb)], w=[("o_ps", b)])
            P.dve(lambda e, xr=xr, db=db, b=b: e.scalar_tensor_tensor(
                out=xr[:, db, :], in0=xr[:, db, :], scalar=float(ALPHA), in1=o_ps[b][:],
                op0=ALU.mult, op1=ALU.add), r=[("xres", s, db), ("o_ps", b)], w=[("xres", s, db)])
        emit_ln(P, xr, ("xres", s), KC, TT, ones, sq, mean_ps, msq_ps, mean_sb, var_sb, rstd_sb, yt,
                lng, lnb, xr, ("xres", s), LN_EPS)
        P.dma("sp", lambda e, j=j, xr=xr: e.dma_start(out=out_v[:, :, j * TT:(j + 1) * TT], in_=xr[:]),
              r=[("xres", s)], w=[ok + (j,)])


def emit_ln(P, xr, xkey, nch, TT, ones, sq, mean_ps, msq_ps, mean_sb, var_sb, rstd_sb, yt, lng, lnb, osb, okey,
            eps, silu=False, lkeys=("lng", "lnb", "ones")):
    for c in range(nch):
        P.pe(_mm(mean_ps[:], ones[:], xr[:, c, :], c == 0, c == nch - 1),
             r=[lkeys[2], xkey + (c,)], w=["mean_ps"])
    for c in range(nch):
        b = c % 2
        P.act(lambda e, c=c, b=b: e.activation(out=sq[b][:], in_=xr[:, c, :], func=AF.Square),
              r=[xkey + (c,)], w=[("sq", b)])
        P.pe(_mm(msq_ps[:], ones[:], sq[b][:], c == 0, c == nch - 1),
             r=[lkeys[2], ("sq", b)], w=["msq_ps"])
    P.act(lambda e: e.activation(out=mean_sb[:], in_=mean_ps[:], func=AF.Identity), r=["mean_ps"], w=["mean_sb"])
    P.dve(lambda e: e.tensor_tensor(out=var_sb[:], in0=mean_sb[:], in1=mean_sb[:], op=ALU.mult),
          r=["mean_sb"], w=["var_sb"])
    P.dve(lambda e: e.tensor_tensor(out=var_sb[:], in0=msq_ps[:], in1=var_sb[:], op=ALU.subtract),
          r=["msq_ps", "var_sb"], w=["var_sb"])
    P.dve(lambda e: e.tensor_scalar(out=var_sb[:], in0=var_sb[:], scalar1=0.0, scalar2=float(eps),
                                    op0=ALU.max, op1=ALU.add), r=["var_sb"], w=["var_sb"])
    P.act(lambda e: e.activation(out=var_sb[:], in_=var_sb[:], func=AF.Sqrt), r=["var_sb"], w=["var_sb"])
    P.dve(lambda e: e.reciprocal(out=rstd_sb[:], in_=var_sb[:]), r=["var_sb"], w=["rstd_sb"])
    for c in range(nch):
        b = c % 2
        y = yt[b]
        P.dve(lambda e, y=y, c=c: e.tensor_tensor(out=y[:], in0=xr[:, c, :], in1=mean_sb[:], op=ALU.subtract),
              r=[xkey + (c,), "mean_sb"], w=[("yt", b)])
        P.dve(lambda e, y=y: e.tensor_tensor(out=y[:], in0=y[:], in1=rstd_sb[:], op=ALU.mult),
              r=[("yt", b), "rstd_sb"], w=[("yt", b)])
        P.act(lambda e, y=y, c=c: e.activation(out=osb[:, c, :], in_=y[:], func=(AF.Silu if silu else AF.Identity),
                                               scale=lng[:, c:c + 1], bias=lnb[:, c:c + 1]),
              r=[("yt", b), lkeys[0], lkeys[1]], w=[okey + (c,)])


def vec_layout(v, nb):
    return np.ascontiguousarray(np.asarray(v, np.float32).reshape(nb, 128).T)


def build_ffn_nc(NT, TT):
    nc = bass.Bass("TRN2", target_bir_lowering=False)
    io = {}
    io["xin"] = nc.dram_tensor("xin", [D, 2 + NT], F32, kind="ExternalInput").ap()
    io["w_up"] = nc.dram_tensor("w_up", [D, DFF], F32, kind="ExternalInput").ap()
    io["w_gate"] = nc.dram_tensor("w_gate", [D, DFF], F32, kind="ExternalInput").ap()
    io["w_down"] = nc.dram_tensor("w_down", [DFF, D], F32, kind="ExternalInput").ap()
    io["cw"] = nc.dram_tensor("cw", [128, FB * 3], F32, kind="ExternalInput").ap()
    io["cb"] = nc.dram_tensor("cb", [128, FB], F32, kind="ExternalInput").ap()
    io["lng"] = nc.dram_tensor("lng", [128, KC], F32, kind="ExternalInput").ap()
    io["lnb"] = nc.dram_tensor("lnb", [128, KC], F32, kind="ExternalInput").ap()
    io["out"] = nc.dram_tensor("out", [D, NT], F32, kind="ExternalOutput").ap()
    P = Prog(nc)
    P.begin_phase()
    build_ffn(nc, P, NT, TT, io)
    P.end_phase()
    P.finish()
    return nc, P


def ffn_inputs(xin_T, w_up, w_gate, conv_w, conv_b, w_down, g, b):
    cwl = np.stack([vec_layout(conv_w[t], FB) for t in range(3)], axis=-1).reshape(128, FB * 3)
    return dict(xin=np.ascontiguousarray(xin_T, dtype=np.float32),
                w_up=np.ascontiguousarray(w_up), w_gate=np.ascontiguousarray(w_gate),
                w_down=np.ascontiguousarray(w_down), cw=np.ascontiguousarray(cwl),
                cb=vec_layout(conv_b, FB), lng=vec_layout(g, KC), lnb=vec_layout(b, KC))


STAGE = 9
CH = 64
HAL_E = 30


def build_even(nc, P, NT, TT, io, state_only=False):
    HALO = HAL_E
    ntiles = NT // TT
    xk = io.get("xin_key", ("xin_ext",))
    ok = io.get("out_key", ("outdram",))
    full = not state_only
    nblk = TT // 128
    xin_v = io["xin"].rearrange("(kc p) t -> p kc t", p=128)
    out_v = io["out"].rearrange("(kc p) t -> p kc t", p=128)
    win = P.sb([128, KC, 3072], BF16, "win")
    wout = P.sb([128, KC, D], BF16, "wout")
    bin_ = P.sb([128, 24], F32, "bin")
    nbin = P.sb([128, 24], F32, "nbin")
    convw = P.sb([128, 4 * 31], F32, "convw")
    convb = P.sb([128, 4], F32, "convb")
    clng = P.sb([128, 4], F32, "clng")
    clnb = P.sb([128, 4], F32, "clnb")
    lb0 = P.sb([128, 4], F32, "lb0")
    lb1 = P.sb([128, 4], F32, "lb1")
    lbsel = P.sb([128, 1], F32, "lbsel")
    lb = P.sb([128, 4], F32, "lb")
    oml = P.sb([128, 4], F32, "oml")
    ong = P.sb([128, 4], F32, "ong")
    hsc = P.sb([128, 1], F32, "hsc")
    lng = P.sb([128, KC], F32, "lng")
    lnb = P.sb([128, KC], F32, "lnb")
    bivbc = P.sb([128, 512], F32, "bivbc")
    ones512 = P.sb([128, 128], F32, "ones512")
    ones128 = P.sb([128, 128], F32, "ones128")
    ones1024 = P.sb([128, 128], F32, "ones1024")
    ident = P.sb([128, 128], BF16, "ident")
    mask = P.sb([128, 128], F32, "mask")
    rmask = P.sb([128, TT], F32, "rmask")
    S = P.sb([128, 4, 128], F32, "S")
    Sbf = P.sb([128, 4, 128], BF16, "Sbf")
    xbf = [P.sb([128, KC, TT], BF16, f"xbf{i}") for i in range(2)]
    xhalo = P.sb([128, KC, HALO], BF16, "xhalo")
    xres = [P.sb([128, KC, TT], F32, f"xres{i}") for i in range(2)]
    glu = [P.sb([128, 4, HALO + TT], BF16, f"glu{i}") for i in range(2)]
    convd = P.sb([128, 4 * 31, 128], BF16, "convd") if not state_only else None
    sg = [P.sb([128, TT], F32, f"sg{i}") for i in range(2)]
    cacc = P.sb([128, 4, TT], F32, "cacc")
    cat = P.sb([128, KC, TT], BF16, "cat")
    qs = P.sb([128, TT], F32, "qs")
    sgp = P.sb([128, TT], F32, "sgp")
    sgn = P.sb([128, TT], F32, "sgn")
    logf = P.sb([128, TT], F32, "logf")
    cum = P.sb([128, TT], F32, "cum")
    eq = P.sb([128, TT], F32, "eq")
    en = P.sb([128, TT], F32, "en")
    gC = P.sb([128, 4, TT // CH], F32, "gC")
    qt = P.sb([128, 4, TT], BF16, "qt")
    kt = P.sb([128, 4, TT], F32, "kt")
    ktb = P.sb([128, 4, TT], BF16, "ktb")
    kh = P.sb([128, 4, TT], BF16, "kh")
    khT = P.sb([128, 4, 128], BF16, "khT")
    vtm = P.sb([128, 512], BF16, "vtm")
    pT = P.sb([128, 4, 128], BF16, "pT")
    osb = P.sb([128, 4, TT], F32, "osb")
    sqo = P.sb([128, TT], F32, "sqo")
    gsl = P.sb([128, TT], F32, "gsl")
    sq = [P.sb([128, TT], F32, f"sq{i}") for i in range(2)]
    mean_sb = P.sb([128, TT], F32, "mean_sb")
    var_sb = P.sb([128, TT], F32, "var_sb")
    rstd_sb = P.sb([128, TT], F32, "rstd_sb")
    yt = [P.sb([128, TT], F32, f"yt{i}") for i in range(2)]

    zps = [P.ps([128, 2, 256], F32, f"zps{i}", keys=[("zps", 2 * i), ("zps", 2 * i + 1)]) for i in range(2)]
    vtm_ps = P.ps([128, 512], F32, "vtm_ps")
    sc_ps = P.ps([128, 4, 128], F32, "sc_ps")
    o_ps = P.ps([128, 4, 128], F32, "o_ps")
    st_ps = P.ps([128, 4, 128], F32, "st_ps")
    tr_ps = P.ps([128, 4, 128], BF16, "tr_ps")
    stat_ps = P.ps([128, 2, 256], F32, "stat_ps", keys=["mean_ps", "msq_ps"])
    mean_ps = stat_ps[:, 0, 0:TT]
    msq_ps = stat_ps[:, 1, 0:TT]

    for nm, t in (("bin", bin_), ("convw", convw), ("convb", convb), ("clng", clng), ("clnb", clnb),
                  ("lb0", lb0), ("lb1", lb1), ("lbsel", lbsel), ("ong", ong), ("hsc", hsc), ("lng", lng),
                  ("lnb", lnb), ("bivbc", bivbc), ("mask", mask), ("rmask", rmask), ("S", S)):
        P.dma("sp", lambda e, t=t, nm=nm: e.dma_start(out=t[:], in_=io[nm]),
              r=([io.get("S_key", ("S_ext",))] if nm == "S" else []), w=[nm])
    id32 = P.sb([128, 128], F32, "id32")
    P.dma("sp", lambda e: e.dma_start(out=id32[:], in_=io["ident"]), w=["id32"])
    P.pool(lambda e: e.tensor_copy(out=ident[:], in_=id32[:]), r=["id32"], w=["ident"])
    if not state_only:
        for q_ in range(4 * 31):
            P.op(("pool", "dve")[q_ % 2], lambda e, q_=q_: e.tensor_scalar(
                out=convd[:, q_, :], in0=id32[:], scalar1=convw[:, q_:q_ + 1], scalar2=None, op0=ALU.mult),
                ["id32", "convw"], [("convd", q_)])
    P.pool(lambda e: e.memset(ones512[:], 1.0 / 512), w=["ones512"])
    P.pool(lambda e: e.memset(ones128[:], 1.0 / 128), w=["ones128"])
    P.pool(lambda e: e.memset(ones1024[:], 1.0 / D), w=["ones1024"])
    xh32 = P.sb([128, KC, HALO], F32, "xh32")
    P.dma("sp", lambda e: e.dma_start(out=xh32[:], in_=xin_v[:, :, 0:HALO]), r=[xk], w=["xh32"])
    P.pool(lambda e: e.tensor_copy(out=xhalo[:], in_=xh32[:]), r=["xh32"], w=["xhalo"])

    def load_xres(j):
        s = j % 2
        P.dma("sp", lambda e: e.dma_start(out=xres[s][:], in_=xin_v[:, :, HALO + j * TT:HALO + (j + 1) * TT]),
              r=[xk], w=[("xres", s)])
        P.pool(lambda e: e.tensor_copy(out=xbf[s][:], in_=xres[s][:]), r=[("xres", s)], w=[("xbf", s)])

    def load_x(j):
        pass

    load_xres(0)
    stg = Stager(P, 3072)
    winv = io["w_in"].rearrange("(kc p) f -> p kc f", p=128)
    woutv = io["w_out"].rearrange("(kc p) f -> p kc f", p=128)
    for kc in range(KC):
        stg.load(win[:, kc, :], winv[:, kc, :], 3072, ("win", kc))
    for kc in range(0, KC, 2):
        stg.load(wout[:, kc:kc + 2, :], woutv[:, kc:kc + 2, :], 2 * D, ("wout", kc // 2), a=2)
    P.dve(lambda e: e.tensor_scalar(out=nbin[:], in0=bin_[:], scalar1=-1.0, scalar2=None, op0=ALU.mult),
          r=["bin"], w=["nbin"])
    P.dve(lambda e: e.tensor_tensor(out=lb[:], in0=lb1[:], in1=lb0[:], op=ALU.subtract), r=["lb0", "lb1"], w=["lb"])
    P.act(lambda e: e.activation(out=lb[:], in_=lb[:], func=AF.Sigmoid), r=["lb"], w=["lb"])
    P.dve(lambda e: e.tensor_scalar(out=lb[:], in0=lb[:], scalar1=lbsel[:, 0:1], scalar2=None, op0=ALU.mult),
          r=["lb", "lbsel"], w=["lb"])
    P.dve(lambda e: e.tensor_scalar(out=oml[:], in0=lb[:], scalar1=-1.0, scalar2=1.0, op0=ALU.mult, op1=ALU.add),
          r=["lb"], w=["oml"])
    P.act(lambda e: e.activation(out=Sbf[:], in_=S[:], func=AF.Identity), r=["S"], w=["Sbf"])

    zcnt = [0]

    def proj(blk, rhs, n, rkeys):
        b = zcnt[0] % 4
        zcnt[0] += 1
        dst = zps[b // 2][:, b % 2, 0:n]
        for kc in range(KC):
            P.pe(_mm(dst, win[:, kc, blk * 128:(blk + 1) * 128], rhs[:, kc, :], kc == 0, kc == KC - 1),
                 r=[("win", kc)] + rkeys, w=[("zps", b)])
        return dst, ("zps", b)

    def glu_block(c, rhs, n, rkeys, dst_glu, dkey, col0, scale_ap=None):
        av, avk = proj(c, rhs, n, rkeys)
        ag, agk = proj(4 + c, rhs, n, rkeys)
        sgt = sg[c % 2]
        P.act(lambda e: e.activation(out=sgt[:, 0:n], in_=ag, func=AF.Sigmoid, bias=bin_[:, 4 + c:5 + c]),
              r=[agk, "bin"], w=[("sg", c % 2)])
        P.dve(lambda e: e.scalar_tensor_tensor(out=dst_glu[:, c, col0:col0 + n], in0=av, scalar=bin_[:, c:c + 1],
                                               in1=sgt[:, 0:n], op0=ALU.add, op1=ALU.mult),
              r=[avk, ("sg", c % 2), "bin"], w=[dkey + (c,)])
        if scale_ap is not None:
            P.dve(lambda e: e.tensor_scalar(out=dst_glu[:, c, col0:col0 + n], in0=dst_glu[:, c, col0:col0 + n],
                                            scalar1=scale_ap, scalar2=None, op0=ALU.mult),
                  r=[dkey + (c,), "hsc"], w=[dkey + (c,)])

    for c in (range(4) if full else ()):
        glu_block(c, xhalo, HALO, ["xhalo"], glu[1], ("glu", 1), TT, scale_ap=hsc[:, 0:1])

    for j in range(ntiles):
        s = j % 2
        if j + 1 < ntiles:
            load_x(j + 1)
        xb = xbf[s]
        G = glu[s]
        Gp = glu[1 - s]
        for c in (range(4) if full else ()):
            P.pool(lambda e, c=c, G=G, Gp=Gp: e.tensor_copy(out=G[:, c, 0:HALO], in_=Gp[:, c, TT:TT + HALO]),
                   r=[("glu", 1 - s, c)], w=[("glu", s, c)])
            glu_block(c, xb, TT, [("xbf", s)], G, ("glu", s), HALO)
            bq = zcnt[0] % 4
            zcnt[0] += 1
            cdst = zps[bq // 2][:, bq % 2, 0:TT]
            for tap in range(31):
                P.pe(_mm(cdst, convd[:, c * 31 + tap, :], G[:, c, tap:tap + TT], tap == 0, tap == 30),
                     r=[("convd", c * 31 + tap), ("glu", s, c)], w=[("zps", bq)])
            P.act(lambda e, c=c, cdst=cdst: e.activation(out=cacc[:, c, :], in_=cdst, func=AF.Identity,
                                                         bias=convb[:, c:c + 1]),
                  r=[("zps", bq), "convb"], w=[("cacc", c)])
        if full:
          emit_ln(P, cacc, ("cacc",), 4, TT, ones512, sq, mean_ps, msq_ps, mean_sb, var_sb, rstd_sb, yt,
                  clng, clnb, cat, ("cat",), LN_EPS, silu=True, lkeys=("clng", "clnb", "ones512"))
        for h in range(4):
            if full:
                qp, qk = proj(8 + h, xb, TT, [("xbf", s)])
                P.act(lambda e, qp=qp, h=h: e.activation(out=qs[:], in_=qp, func=AF.Silu, bias=bin_[:, 8 + h:9 + h]),
                      r=[qk, "bin"], w=["qs"])
            fp_, fk = proj(12 + h, xb, TT, [("xbf", s)])
            P.act(lambda e, fp_=fp_, h=h: e.activation(out=sgp[:], in_=fp_, func=AF.Sigmoid,
                                                       bias=bin_[:, 12 + h:13 + h]),
                  r=[fk, "bin"], w=["sgp"])
            P.act(lambda e, fp_=fp_, h=h: e.activation(out=sgn[:], in_=fp_, func=AF.Sigmoid, scale=-1.0,
                                                       bias=nbin[:, 12 + h:13 + h]),
                  r=[fk, "nbin"], w=["sgn"])
            P.dve(lambda e, h=h: e.tensor_scalar(out=logf[:], in0=sgp[:], scalar1=oml[:, h:h + 1],
                                                 scalar2=lb[:, h:h + 1], op0=ALU.mult, op1=ALU.add),
                  r=["sgp", "oml", "lb"], w=["logf"])
            P.act(lambda e: e.activation(out=logf[:], in_=logf[:], func=AF.Ln), r=["logf"], w=["logf"])
            P.dve(lambda e: e.tensor_tensor_scan(out=cum[:], data0=rmask[:], data1=logf[:], initial=0.0,
                                                 op0=ALU.mult, op1=ALU.add),
                  r=["rmask", "logf"], w=["cum"])
            if full:
                P.act(lambda e: e.activation(out=eq[:], in_=cum[:], func=AF.Exp), r=["cum"], w=["eq"])
            P.act(lambda e: e.activation(out=en[:], in_=cum[:], func=AF.Exp, scale=-1.0), r=["cum"], w=["en"])
            P.act(lambda e, h=h: e.activation(
                out=gC[:, h, :], in_=cum[:].rearrange("p (c t) -> p c t", t=CH)[:, :, CH - 1], func=AF.Exp),
                r=["cum"], w=[("gC", h)])
            if full:
                P.dve(lambda e, h=h: e.tensor_tensor(out=qt[:, h, :], in0=qs[:], in1=eq[:], op=ALU.mult),
                      r=["qs", "eq"], w=[("qt", h)])
            P.dve(lambda e, h=h: e.scalar_tensor_tensor(out=kt[:, h, :], in0=sgn[:], scalar=oml[:, h:h + 1],
                                                        in1=en[:], op0=ALU.mult, op1=ALU.mult),
                  r=["sgn", "en", "oml"], w=[("kt", h)])
            if full:
                P.act(lambda e, h=h: e.activation(out=ktb[:, h, :], in_=kt[:, h, :], func=AF.Identity),
                      r=[("kt", h)], w=[("ktb", h)])
            P.dve(lambda e, h=h: e.tensor_tensor(
                out=kh[:, h, :].rearrange("p (c t) -> p c t", t=CH),
                in0=kt[:, h, :].rearrange("p (c t) -> p c t", t=CH),
                in1=gC[:, h, :].unsqueeze(2).to_broadcast([128, TT // CH, CH]), op=ALU.mult),
                r=[("kt", h), ("gC", h)], w=[("kh", h)])
        for bi in range(nblk):
            tsl = slice(bi * 128, (bi + 1) * 128)
            for kc in range(KC):
                P.pe(_mm(vtm_ps[:], xb[:, kc, tsl], win[:, kc, 2048:2560], kc == 0, kc == KC - 1),
                     r=[("xbf", s), ("win", kc)], w=["vtm_ps"])
            P.dve(lambda e: e.tensor_tensor(out=vtm[:], in0=vtm_ps[:], in1=bivbc[:], op=ALU.add),
                  r=["vtm_ps", "bivbc"], w=["vtm"])
            for h in range(4):
                if full:
                    P.pe(_mm(sc_ps[:, h, :], ktb[:, h, tsl], qt[:, h, tsl], True, True),
                         r=[("ktb", h), ("qt", h)], w=[("sc_ps", h)])
                P.pe(lambda e, h=h, tsl=tsl: e.transpose(tr_ps[:, h, :], kh[:, h, tsl], ident[:]),
                     r=[("kh", h), "ident"], w=[("tr_ps", h)])
            if full:
                P.dve(lambda e: e.tensor_tensor(out=pT[:], in0=sc_ps[:],
                                                in1=mask[:].unsqueeze(1).to_broadcast([128, 4, 128]), op=ALU.mult),
                      r=["sc_ps", "mask"], w=["pT"])
            P.act(lambda e: e.activation(out=khT[:], in_=tr_ps[:], func=AF.Identity), r=["tr_ps"], w=["khT"])
            for h in (range(4) if full else ()):
                vh = vtm[:, h * 128:(h + 1) * 128]
                P.pe(_mm(o_ps[:, h, :], vh, pT[:, h, :], h == 0, False),
                     r=["vtm", "pT"], w=[("o_ps",)])
            for ci in range(2):
                csl = slice(ci * 64, (ci + 1) * 64)
                gcol = bi * 2 + ci
                for h in (range(4) if full else ()):
                    P.pe(_mm(o_ps[:, h, csl], Sbf[:, h, :], qt[:, h, bi * 128 + ci * 64:bi * 128 + (ci + 1) * 64],
                             False, ci == 1 and h == 3),
                         r=[("Sbf", h), ("qt", h)], w=[("o_ps",)])
                for h in range(4):
                    P.pe(_mm(st_ps[:, h, :], khT[csl, h, :], vtm[csl, h * 128:(h + 1) * 128], True, True),
                         r=["khT", "vtm"], w=[("st_ps", h)])
                for h in range(4):
                    P.dve(lambda e, h=h, gcol=gcol: e.scalar_tensor_tensor(
                        out=S[:, h, :], in0=S[:, h, :], scalar=gC[:, h, gcol:gcol + 1], in1=st_ps[:, h, :],
                        op0=ALU.mult, op1=ALU.add), r=[("S", h), ("gC", h), ("st_ps", h)], w=[("S", h)])
                    if full:
                        P.act(lambda e, h=h: e.activation(out=Sbf[:, h, :], in_=S[:, h, :], func=AF.Identity),
                              r=[("S", h)], w=[("Sbf", h)])
            if full:
                P.act(lambda e, tsl=tsl: e.activation(out=osb[:, :, tsl], in_=o_ps[:], func=AF.Identity),
                      r=["o_ps"], w=[("osb", bi)])
        for h in (range(4) if full else ()):
            if True:
                P.act(lambda e, h=h: e.activation(out=sqo[:], in_=osb[:, h, :], func=AF.Square), r=["osb"], w=["sqo"])
                P.pe(_mm(mean_ps, ones128[:], sqo[:], True, True), r=["ones128", "sqo"], w=["mean_ps"])
                P.dve(lambda e: e.tensor_scalar(out=var_sb[:], in0=mean_ps, scalar1=0.0, scalar2=1e-6,
                                                op0=ALU.max, op1=ALU.add), r=["mean_ps"], w=["var_sb"])
            if True:
                P.act(lambda e: e.activation(out=var_sb[:], in_=var_sb[:], func=AF.Sqrt), r=["var_sb"], w=["var_sb"])
                P.dve(lambda e: e.reciprocal(out=rstd_sb[:], in_=var_sb[:]), r=["var_sb"], w=["rstd_sb"])
            gp, gk = proj(20 + h, xb, TT, [("xbf", s)])
            P.act(lambda e, gp=gp, h=h: e.activation(out=gsl[:], in_=gp, func=AF.Silu, bias=bin_[:, 20 + h:21 + h]),
                  r=[gk, "bin"], w=["gsl"])
            if True:
                P.dve(lambda e, h=h: e.tensor_tensor(out=sqo[:], in0=osb[:, h, :], in1=rstd_sb[:], op=ALU.mult),
                      r=["osb", "rstd_sb"], w=["sqo"])
                P.dve(lambda e, h=h: e.scalar_tensor_tensor(out=cat[:, 4 + h, :], in0=sqo[:], scalar=ong[:, h:h + 1],
                                                            in1=gsl[:], op0=ALU.mult, op1=ALU.mult),
                      r=["sqo", "ong", "gsl"], w=[("cat", 4 + h)])
        if j + 1 < ntiles:
            load_xres(j + 1)
        xr = xres[s]
        if not full:
            continue
        for db in range(KC):
            b = zcnt[0] % 4
            zcnt[0] += 1
            dst = zps[b // 2][:, b % 2, 0:TT]
            for c in range(KC):
                P.pe(_mm(dst, wout[:, c, db * 128:(db + 1) * 128], cat[:, c, :], c == 0, c == KC - 1),
                     r=[("wout", c // 2), ("cat", c)], w=[("zps", b)])
            P.dve(lambda e, xr=xr, db=db, dst=dst: e.scalar_tensor_tensor(
                out=xr[:, db, :], in0=xr[:, db, :], scalar=float(ALPHA), in1=dst,
                op0=ALU.mult, op1=ALU.add), r=[("xres", s, db), ("zps", b)], w=[("xres", s, db)])
        emit_ln(P, xr, ("xres", s), KC, TT, ones1024, sq, mean_ps, msq_ps, mean_sb, var_sb, rstd_sb, yt,
                lng, lnb, xr, ("xres", s), LN_EPS, lkeys=("lng", "lnb", "ones1024"))
        P.dma("sp", lambda e, j=j, xr=xr: e.dma_start(out=out_v[:, :, j * TT:(j + 1) * TT], in_=xr[:]),
              r=[("xres", s)], w=[ok + (j,)])
    P.dma("sp", lambda e: e.dma_start(out=io["S_out"], in_=S[:]), r=["S"], w=[io.get("Sout_key", ("S_out",))])


def build_even_nc(NT, TT):
    nc = bass.Bass("TRN2", target_bir_lowering=False)
    io = {}

    def din(name, shape, dt=F32):
        io[name] = nc.dram_tensor(name, list(shape), dt, kind="ExternalInput").ap()

    din("xin", [D, HAL_E + NT])
    din("w_in", [D, 3072])
    din("w_out", [D, D])
    din("bin", [128, 24])
    din("convw", [128, 4 * 31])
    din("convb", [128, 4])
    din("clng", [128, 4])
    din("clnb", [128, 4])
    din("lb0", [128, 4])
    din("lb1", [128, 4])
    din("lbsel", [128, 1])
    din("ong", [128, 4])
    din("hsc", [128, 1])
    din("lng", [128, KC])
    din("lnb", [128, KC])
    din("bivbc", [128, 512])
    din("mask", [128, 128])
    din("rmask", [128, TT])
    din("ident", [128, 128])
    din("S", [128, 4, 128])
    io["out"] = nc.dram_tensor("out", [D, NT], F32, kind="ExternalOutput").ap()
    io["S_out"] = nc.dram_tensor("S_out", [128, 4, 128], F32, kind="ExternalOutput").ap()
    P = Prog(nc)
    P.begin_phase()
    build_even(nc, P, NT, TT, io)
    P.end_phase()
    P.finish()
    return nc, P


def even_consts(TT):
    i = np.arange(128)
    mask = ((i[:, None] // CH == i[None, :] // CH) & (i[:, None] <= i[None, :])).astype(np.float32)
    rmask = np.ones((128, TT), np.float32)
    rmask[:, ::CH] = 0.0
    return dict(mask=mask, rmask=rmask, ident=np.eye(128, dtype=np.float32))


def even_inputs(xin_T, w_in, b_in, conv_w, conv_b, cln_g, cln_b, lb_logits, j, onorm_g, w_out, g, b, S, first, TT):
    d = even_consts(TT)
    cwl = np.stack([vec_layout(conv_w[t], 4) for t in range(31)], axis=-1).reshape(128, 4 * 31)
    d.update(xin=np.ascontiguousarray(xin_T, dtype=np.float32), w_in=np.ascontiguousarray(w_in),
             w_out=np.ascontiguousarray(w_out), bin=vec_layout(b_in, 24), convw=np.ascontiguousarray(cwl),
             convb=vec_layout(conv_b, 4), clng=vec_layout(cln_g, 4), clnb=vec_layout(cln_b, 4),
             lb0=vec_layout(lb_logits[0], 4), lb1=vec_layout(lb_logits[1], 4),
             lbsel=np.full((128, 1), 1.0 if j == 1 else 0.0, np.float32),
             ong=vec_layout(onorm_g, 4), hsc=np.full((128, 1), 0.0 if first else 1.0, np.float32),
             lng=vec_layout(g, KC), lnb=vec_layout(b, KC),
             bivbc=np.ascontiguousarray(np.broadcast_to(np.asarray(b_in, np.float32)[2048:2560], (128, 512))),
             S=np.ascontiguousarray(S, dtype=np.float32))
    return d


C = 64
H = 16
ORDER_OVERRIDE = None
C0 = float(np.exp(-0.5))


def build_odd(nc, P, NT, io, state_only=False):
    TT = C
    ntiles = NT // TT
    xk = io.get("xin_key", ("xin_ext",))
    ok = io.get("out_key", ("outdram",))
    full = not state_only
    xin_v = io["xin"].rearrange("(kc p) t -> p kc t", p=128)
    out_v = io["out"].rearrange("(kc p) t -> p kc t", p=128)
    sb, ps = P.sb, P.ps
    wr, wk, wv = (sb([128, KC, D], BF16, n) for n in ("wr", "wk", "wv"))
    wo = sb([64, H, D], BF16, "wo")
    w1 = sb([128, KC, 64], BF16, "w1"); a1 = sb([128, KC, 64], BF16, "a1"); g1 = sb([128, KC, 160], BF16, "g1")
    w2 = sb([64, D], BF16, "w2"); a2 = sb([64, D], BF16, "a2")
    g2a = sb([128, D], BF16, "g2a"); g2b = sb([32, D], BF16, "g2b")
    mu = sb([128, 6, KC], F32, "mu")
    hv = {n: sb([64, H], F32, n) for n in ("w0h", "a0h", "kkh", "kah", "rkh", "gng", "gnb")}
    omka = sb([64, H], F32, "omka")
    lng = sb([128, KC], F32, "lng"); lnb = sb([128, KC], F32, "lnb")
    mUs = sb([64, 64], F32, "mUs"); mUi = sb([64, 64], F32, "mUi"); mLs = sb([64, 64], F32, "mLs")
    eye = sb([64, 64], F32, "eye"); rmask = sb([64, H * C], F32, "rmask")
    id32 = sb([64, 64], F32, "id32"); ident = sb([64, 64], BF16, "ident")
    ones64 = sb([64, 64], F32, "ones64"); ones1 = sb([64, 64], F32, "ones1"); ones1024 = sb([128, 128], F32, "ones1024")
    M = sb([64, H, 64], F32, "M"); Mb = sb([64, H, 64], BF16, "Mb")
    XT = [sb([128, KC, TT + 1], F32, f"XT{i}") for i in range(2)]
    xx = sb([128, KC, TT], F32, "xx"); xt_ = sb([128, KC, TT], F32, "xt_")
    xm = [sb([128, KC, TT], BF16, f"xm{i}") for i in range(2)]
    F = {n: sb([64, H, C], F32, n) for n in ("rF", "kF", "vF", "aF", "gF", "sg", "cum", "kap", "t0", "t1")}
    F["sg"] = F["sg"]; F["cum"] = F["cum"]
    B = {n: sb([64, H, C], BF16, n) for n in ("KT", "BT", "KK", "RT", "BH", "KH", "Tt",
                                               "AkT", "PbT", "PkT", "zB", "BHT", "KHT")}
    for n in ("N0", "N1", "Nt0", "Nt1"):
        for g_ in range(2):
            B[f"{n}_{g_}"] = sb([64, 8, C], BF16, f"{n}_{g_}")
    B["nZ"] = B["BH"]; B["U"] = B["KH"]

    def hd(t, h):
        return t[:, h, :] if t.shape[1] == H else t[:, h % 8, :]

    def grp(t, g):
        return t[:, g * 8:(g + 1) * 8, :] if t.shape[1] == H else t[:]
    Vtm = sb([64, D], BF16, "Vtm")
    twB = sb([64, C], BF16, "twB"); taB = sb([64, C], BF16, "taB"); tgA = sb([128, C], BF16, "tgA"); tgB = sb([32, C], BF16, "tgB")
    gC = sb([64, H], F32, "gC")
    sq8 = sb([128, KC, TT], F32, "sq8")
    mean_sb = sb([128, TT], F32, "mean_sb"); var_sb = sb([128, TT], F32, "var_sb"); rstd_sb = sb([128, TT], F32, "rstd_sb")
    pb = [ps([128, 512], F32, f"pb{i}") for i in range(6)]
    trp = ps([64, 8, 64], BF16, "trp")
    stat = ps([128, 2, 256], F32, "stat", keys=["mean_ps", "msq_ps"])
    mean_ps = stat[:, 0, 0:TT]; msq_ps = stat[:, 1, 0:TT]

    names = {}
    for d_ in (F, B, hv):
        for n_, t_ in d_.items():
            names.setdefault(id(t_), n_)
    for n_, t_ in (("mUs", mUs), ("mUi", mUi), ("mLs", mLs), ("eye", eye), ("twB", twB), ("taB", taB), ("tgA", tgA), ("tgB", tgB)):
        names[id(t_)] = n_

    def kn(t):
        return names[id(t)]

    def bk(i):
        return pb[i][0:64, :].rearrange("p (h c) -> p h c", c=64)

    def ld(t, nm):
        P.dma("sp", lambda e: e.dma_start(out=t[:], in_=io[nm]), r=([io.get("M_key", ("M_ext",))] if nm == "M" else []), w=[nm])

    for nm, t in list(hv.items()) + [("mu", mu), ("lng", lng), ("lnb", lnb), ("mUs", mUs), ("mUi", mUi), ("mLs", mLs),
                                     ("eye", eye), ("rmask_o", rmask), ("id32_o", id32), ("M", M)]:
        ld(t, nm)
    P.pool(lambda e: e.tensor_copy(out=ident[:], in_=id32[:]), r=["id32_o"], w=["ident"])
    P.pool(lambda e: e.memset(ones64[:], 1.0 / 64), w=["ones64"])
    P.pool(lambda e: e.memset(ones1[:], 1.0), w=["ones1"])
    P.pool(lambda e: e.memset(ones1024[:], 1.0 / D), w=["ones1024"])
    P.dve(lambda e: e.tensor_scalar(out=omka[:], in0=hv["kah"][:], scalar1=-1.0, scalar2=1.0, op0=ALU.mult, op1=ALU.add),
          r=["kah"], w=["omka"])
    P.act(lambda e: e.activation(out=Mb[:], in_=M[:], func=AF.Identity), r=["M"], w=["Mb"])
    stg = Stager(P, 1024)
    for nm, t in (("w_r", wr), ("w_k", wk), ("w_v", wv)):
        v = io[nm].rearrange("(kc p) f -> p kc f", p=128)
        for kc in range(KC):
            stg.load(t[:, kc, :], v[:, kc, :], 1024, (nm, kc // 2))
    wov = io["w_o"].rearrange("(h p) d -> p h d", p=64)

    def load64(dst, src, n, key, rows=64):
        i = stg.n % 2
        stg.n += 1
        st = stg.bufs[i]
        P.dma("sp", lambda e: e.dma_start(out=st[0:rows, 0:n], in_=src), w=[("stage", i)])
        cast_op(P, ("pool", "act", "dve")[stg.n % 3], dst, st[0:rows, 0:n], [("stage", i)], [key])

    for h in range(H):
        load64(wo[:, h, :], wov[:, h, :], D, ("wo", h // 2))
    for nm, t, n in (("w1", w1, 64), ("a1", a1, 64)):
        v = io[nm].rearrange("(kc p) f -> p kc f", p=128)
        stg.load(t[:], v, KC * n, nm, a=KC)
    g1v = io["g1"].rearrange("(kc p) f -> p kc f", p=128)
    for q in range(2):
        stg.load(g1[:, q * 4:(q + 1) * 4, :], g1v[:, q * 4:(q + 1) * 4, :], 640, ("g1", q), a=4)
    load64(w2[:], io["w2"], D, "w2")
    load64(a2[:], io["a2"], D, "a2")
    stg.load(g2a[:], io["g2"][0:128, :], D, "g2a")
    load64(g2b[:], io["g2"][128:160, :], D, "g2b", rows=32)

    def bc(v):
        return v[:].unsqueeze(2).to_broadcast([64, H, C])

    def tt(eng, out, a, b, op, r, w):
        P.op(eng, lambda e: e.tensor_tensor(out=out, in0=a, in1=b, op=op), r, w)

    def headproj(w_t, wkey, xmi, dstF, post=None):
        for g in range(2):
            bank = bk(g)
            for hh in range(8):
                h = g * 8 + hh
                for kc in range(KC):
                    P.pe(_mm(bank[:, hh, :], w_t[:, kc, h * 64:(h + 1) * 64], xm[xmi][:, kc, :], kc == 0, kc == KC - 1),
                         r=[(wkey, kc // 2), ("xm", xmi)], w=[("pb", g)])
            P.act(lambda e, g=g, bank=bank: e.activation(out=dstF[:, g * 8:(g + 1) * 8, :], in_=bank, func=AF.Identity),
                  r=[("pb", g)], w=[(kn(dstF), g)])

    for i in range(6):
        P.bank_of[("pb", i)] = f"pb{i}"

    def load_x(j):
        s = j % 2
        P.dma("sp", lambda e: e.dma_start(out=XT[s][:], in_=xin_v[:, :, j * TT:j * TT + TT + 1]), r=[xk], w=[("XT", s)])

    xm4 = [xm[0], xm[1], P.sb([128, KC, TT], BF16, "xm2"), P.sb([128, KC, TT], BF16, "xm3")]
    xr_ = P.sb([128, KC, TT], F32, "xr_")

    def mixop(X, s, i, slot):
        P.dve(lambda e: e.tensor_tensor(out=xt_[:], in0=xx[:], in1=mu[:, i, :].unsqueeze(2).to_broadcast([128, KC, TT]),
                                        op=ALU.mult), r=["xx", "mu"], w=["xt_"])
        P.dve(lambda e: e.tensor_tensor(out=xm4[slot][:], in0=xt_[:], in1=X[:, :, 1:TT + 1], op=ALU.add),
              r=["xt_", ("XT", s)], w=[("xm", slot)])

    stg_tm = [stg.bufs[i][0:64, :].bitcast(BF16)[:, 0:D] for i in range(2)]

    def hproj_mm(w_t, wkey, slot, banks, dst_tm, tmk):
        for half in range(2):
            bi = banks[half]
            for kc in range(KC):
                P.pe(_mm(pb[bi][0:64, :], xm4[slot][:, kc, :], w_t[:, kc, half * 512:(half + 1) * 512], kc == 0, kc == KC - 1),
                     r=[("xm", slot), (wkey, kc // 2)], w=[("pb", bi)])
            P.act(lambda e, half=half, bi=bi: e.activation(out=dst_tm[:, half * 512:(half + 1) * 512], in_=pb[bi][0:64, :],
                                                           func=AF.Identity), r=[("pb", bi)], w=[tmk + (half,)])

    def hproj_tr(dst_tm, tmk, dstF):
        for g in range(2):
            for hh in range(8):
                h = g * 8 + hh
                P.pe(lambda e, hh=hh, h=h: e.transpose(trp[:, hh, :], dst_tm[:, h * 64:(h + 1) * 64], ident[:]),
                     r=[tmk + (g,), "ident"], w=["trp"])
            P.act(lambda e, g=g: e.activation(out=dstF[:, g * 8:(g + 1) * 8, :], in_=trp[:], func=AF.Identity),
                  r=["trp"], w=[(kn(dstF), g)])

    def early_segments(j):
        s = j % 2
        X = XT[s]

        def e1():
            tt("dve", xx[:], X[:, :, 0:TT], X[:, :, 1:TT + 1], ALU.subtract, [("XT", s)], ["xx"])
            if full:
                mixop(X, s, 0, 0)
            mixop(X, s, 2, 1)
            mixop(X, s, 3, 2)
            if full:
                hproj_mm(wr, "w_r", 0, (0, 1), stg_tm[1], ("stage", 1))
            hproj_mm(wk, "w_k", 1, (2, 3), stg_tm[0], ("stage", 0))

        def e2():
            if full:
                hproj_tr(stg_tm[1], ("stage", 1), F["rF"])
            hproj_mm(wv, "w_v", 2, (0, 1), Vtm, ("Vtm",))

        def e3():
            hproj_tr(stg_tm[0], ("stage", 0), F["kF"])
            hproj_tr(Vtm, ("Vtm",), F["vF"])

        def e4():
            for (i, l1, l1k, mid_, fn_) in ((1, w1, "w1", twB, AF.Tanh), (4, a1, "a1", taB, AF.Identity)):
                mixop(X, s, i, 3)
                for kc in range(KC):
                    P.pe(_mm(pb[3][0:64, 0:C], l1[:, kc, :], xm4[3][:, kc, :], kc == 0, kc == KC - 1),
                         r=[l1k, ("xm", 3)], w=[("pb", 3)])
                P.act(lambda e, mid_=mid_, fn_=fn_: e.activation(out=mid_[:], in_=pb[3][0:64, 0:C], func=fn_),
                      r=[("pb", 3)], w=[kn(mid_)])
            if full:
                mixop(X, s, 5, 3)
                for (dst, lo, n) in ((tgA, 0, 128), (tgB, 128, 32)):
                    for kc in range(KC):
                        P.pe(_mm(pb[3][0:n, 0:C], g1[:, kc, lo:lo + n], xm4[3][:, kc, :], kc == 0, kc == KC - 1),
                             r=["g1", ("xm", 3)], w=[("pb", 3)])
                    P.act(lambda e, dst=dst, n=n: e.activation(out=dst[:], in_=pb[3][0:n, 0:C], func=AF.Sigmoid),
                          r=[("pb", 3)], w=[kn(dst)])
        return [e1, e2, e3, e4]

    def lora2(l2, l2k, mid_, bias, outF, outfunc):
        for g in range(2):
            bank = bk(g)
            for hh in range(8):
                h = g * 8 + hh
                P.pe(_mm(bank[:, hh, :], l2[:, h * 64:(h + 1) * 64], mid_[:], True, True),
                     r=[l2k, kn(mid_)], w=[("pb", g)])
            tt("dve", outF[:, g * 8:(g + 1) * 8, :], bank, bias[:, g * 8:(g + 1) * 8].unsqueeze(2).to_broadcast([64, 8, C]),
               ALU.add, [("pb", g), kn(bias)], [(kn(outF), g)])
        P.act(lambda e: e.activation(out=outF[:], in_=outF[:], func=outfunc), r=[kn(outF)], w=[kn(outF)])

    def mid(j):
        rF, kF, vF, aF, sg, cum, kap, t0, t1 = (F[n] for n in ("rF", "kF", "vF", "aF", "sg", "cum", "kap", "t0", "t1"))
        t2 = sg
        tt("dve", kap[:], kF[:], bc(hv["kkh"]), ALU.mult, ["kF", "kkh"], ["kap"])
        lora2(w2, "w2", twB, hv["w0h"], F["sg"], AF.Sigmoid)
        P.act(lambda e: e.activation(out=t0[:], in_=kap[:], func=AF.Square), r=["kap"], w=["t0"])
        t0f = t0[:].rearrange("p h c -> p (h c)")
        for half in range(2):
            P.pe(_mm(pb[2][0:64, :], ones1[:], t0f[:, half * 512:(half + 1) * 512], True, True), r=["ones1", "t0"], w=[("pb", 2)])
            P.dve(lambda e, half=half: e.tensor_scalar(out=t1[:].rearrange("p h c -> p (h c)")[:, half * 512:(half + 1) * 512],
                                                       in0=pb[2][0:64, :], scalar1=1e-24, scalar2=None, op0=ALU.max),
                  r=[("pb", 2)], w=[("t1", half)])
        lora2(a2, "a2", taB, hv["a0h"], F["aF"], AF.Sigmoid)
        P.act(lambda e: e.activation(out=t1[:], in_=t1[:], func=AF.Sqrt), r=["t1"], w=["t1"])
        P.dve(lambda e: e.reciprocal(out=t1[:], in_=t1[:]), r=["t1"], w=["t1"])
        for g in (range(2) if full else ()):
            bank = bk(g)
            for hh in range(8):
                h = g * 8 + hh
                P.pe(_mm(bank[:, hh, :], g2a[:, h * 64:(h + 1) * 64], tgA[:], True, False), r=["g2a", kn(tgA)], w=[("pb", g)])
                P.pe(_mm(bank[:, hh, :], g2b[:, h * 64:(h + 1) * 64], tgB[:], False, True), r=["g2b", kn(tgB)], w=[("pb", g)])
            P.act(lambda e, g=g, bank=bank: e.activation(out=F["gF"][:, g * 8:(g + 1) * 8, :], in_=bank, func=AF.Identity),
                  r=[("pb", g)], w=[("gF", g)])
        tt("dve", kap[:], kap[:], t1[:], ALU.mult, ["kap", "t1"], ["kap"])
        tt("pool", t0[:], aF[:], bc(hv["kah"]), ALU.mult, ["aF", "kah"], ["t0"])
        tt("pool", t0[:], t0[:], bc(omka), ALU.add, ["t0", "omka"], ["t0"])
        tt("pool", kF[:], kF[:], t0[:], ALU.mult, ["kF", "t0"], ["kF"])
        P.dve(lambda e: e.tensor_tensor_scan(out=cum[:].rearrange("p h c -> p (h c)"), data0=rmask[:],
                                             data1=sg[:].rearrange("p h c -> p (h c)"), initial=0.0,
                                             op0=ALU.mult, op1=ALU.add), r=["rmask_o", "sg"], w=["cum"])
        tt("dve", t0[:], cum[:], sg[:], ALU.subtract, ["cum", "sg"], ["t0"])
        tt("dve", t2[:], kap[:], aF[:], ALU.mult, ["kap", "aF"], ["sg"])
        P.act(lambda e: e.activation(out=t0[:], in_=t0[:], func=AF.Exp, scale=-C0), r=["t0"], w=["t0"])
        tt("dve", B["KT"][:], kap[:], t0[:], ALU.mult, ["kap", "t0"], ["KT"])
        P.act(lambda e: e.activation(out=t0[:], in_=cum[:], func=AF.Exp, scale=-C0), r=["cum"], w=["t0"])
        if full:
            tt("dve", B["RT"][:], rF[:], t0[:], ALU.mult, ["rF", "t0"], ["RT"])
        P.act(lambda e: e.activation(out=gC[:], in_=cum[:, :, C - 1], func=AF.Exp, scale=-C0), r=["cum"], w=["gC"])
        P.act(lambda e: e.activation(out=t1[:], in_=cum[:], func=AF.Exp, scale=C0), r=["cum"], w=["t1"])
        tt("dve", t2[:], t2[:], t1[:], ALU.mult, ["sg", "t1"], ["sg"])
        tt("dve", t1[:], kF[:], t1[:], ALU.mult, ["kF", "t1"], ["t1"])
        P.act(lambda e: e.activation(out=B["BT"][:], in_=t2[:], func=AF.Identity), r=["sg"], w=["BT"])
        P.act(lambda e: e.activation(out=B["KK"][:], in_=t1[:], func=AF.Identity), r=["t1"], w=["KK"])
        tt("pool", B["BH"][:], t2[:], bc(gC), ALU.mult, ["sg", "gC"], ["BH"])
        tt("dve", B["KH"][:], t1[:], bc(gC), ALU.mult, ["t1", "gC"], ["KH"])
        if full:
            tt("dve", t0[:], rF[:], kF[:], ALU.mult, ["rF", "kF"], ["t0"])
            tt("dve", t0[:], t0[:], bc(hv["rkh"]), ALU.mult, ["t0", "rkh"], ["t0"])
            t0f_ = t0[:].rearrange("p h c -> p (h c)")
            for half in range(2):
                hs = slice(half * 512, (half + 1) * 512)
                P.pe(_mm(pb[2][0:64, :], ones1[:], t0f_[:, hs], True, True), r=["ones1", "t0"], w=[("pb", 2)])
                tt("dve", kap[:].rearrange("p h c -> p (h c)")[:, hs], pb[2][0:64, :],
                   vF[:].rearrange("p h c -> p (h c)")[:, hs], ALU.mult, [("pb", 2), "vF"], [("kap", half)])
        P.stage = f"mid{j}_tr"
        for src, dst in (("BH", "BHT"), ("KH", "KHT")):
            for g in range(2):
                for hh in range(8):
                    h = g * 8 + hh
                    P.pe(lambda e, hh=hh, h=h, src=src: e.transpose(trp[:, hh, :], B[src][:, h, :], ident[:]),
                         r=[src, "ident"], w=["trp"])
                P.act(lambda e, g=g, dst=dst: e.activation(out=B[dst][:, g * 8:(g + 1) * 8, :], in_=trp[:], func=AF.Identity),
                      r=["trp"], w=[(dst, g)])
        P.stage = f"mid{j}_scores"
        BS = ((0, 1, 2), (3, 4, 5))

        def score(lhs, rhs, dst, mask, neg, g, bank_i):
            bank = bk(bank_i)
            for hh in range(8):
                h = g * 8 + hh
                P.pe(_mm(bank[:, hh, :], B[lhs][:, h, :], B[rhs][:, h, :], True, True), r=[lhs, rhs], w=[("pb", bank_i)])
            P.dve(lambda e: e.scalar_tensor_tensor(out=grp(B[dst], g), in0=bank, scalar=(-1.0 if neg else 1.0),
                                                   in1=mask[:].unsqueeze(1).to_broadcast([64, 8, 64]),
                                                   op0=ALU.mult, op1=ALU.mult), r=[("pb", bank_i), kn(mask)], w=[(dst, g)])

        def NB(name, g):
            return B[f"{name}_{g}"]

        for g in range(2):
            score("KT", "BT", f"N0_{g}", mLs, True, g, BS[g][0])
            score("BT", "KT", f"Nt0_{g}", mUs, True, g, BS[g][1])
        for g in range(2):
            score("KK", "KT", "AkT", mUs, False, g, BS[g][2])
            if full:
                score("BT", "RT", "PbT", mUi, False, g, BS[g][0])
        for g in range(2):
            gs = slice(g * 8, (g + 1) * 8)
            if full:
                score("KK", "RT", "PkT", mUi, False, g, BS[g][1])
            tt("dve", B["Tt"][:, gs, :], NB("Nt0", g)[:], eye[:].unsqueeze(1).to_broadcast([64, 8, 64]), ALU.add,
               [(f"Nt0_{g}", g), "eye"], [("Tt", g)])
        cur = 0
        P.stage = f"mid{j}_dbl"
        for lv in range(5):
            for g in range(2):
                gs = slice(g * 8, (g + 1) * 8)
                Nc, Ntc = NB(f"N{cur}", g), NB(f"Nt{cur}", g)
                Nn, Ntn = NB(f"N{1 - cur}", g), NB(f"Nt{1 - cur}", g)
                kNc, kNtc = (f"N{cur}_{g}", g), (f"Nt{cur}_{g}", g)
                kNn, kNtn = (f"N{1 - cur}_{g}", g), (f"Nt{1 - cur}_{g}", g)
                bA, bB, bCk = (bk(i) for i in BS[g])
                kA, kB, kC = (("pb", i) for i in BS[g])
                for hh in range(8):
                    P.pe(_mm(bA[:, hh, :], Ntc[:, hh, :], Nc[:, hh, :], True, True), r=[kNc, kNtc], w=[kA])
                P.act(lambda e, Nn=Nn, bA=bA: e.activation(out=Nn[:], in_=bA, func=AF.Identity), r=[kA], w=[kNn])
                if lv < 4:
                    for hh in range(8):
                        P.pe(_mm(bB[:, hh, :], Nc[:, hh, :], Ntc[:, hh, :], True, True), r=[kNc, kNtc], w=[kB])
                    P.act(lambda e, Ntn=Ntn, bB=bB: e.activation(out=Ntn[:], in_=bB, func=AF.Identity), r=[kB], w=[kNtn])
                for hh in range(8):
                    h = g * 8 + hh
                    P.pe(_mm(bCk[:, hh, :], Nn[:, hh, :], B["Tt"][:, h, :], True, True), r=[kNn, ("Tt", g)], w=[kC])
                tt("dve", B["Tt"][:, gs, :], B["Tt"][:, gs, :], bCk, ALU.add, [("Tt", g), kC], [("Tt", g)])
            cur = 1 - cur
        P.stage = f"mid{j}_state"
        for g in range(2):
            gs = slice(g * 8, (g + 1) * 8)
            bZ = bk(BS[g][0]); kZ = ("pb", BS[g][0])
            for hh in range(8):
                h = g * 8 + hh
                vh = Vtm[:, h * 64:(h + 1) * 64]
                P.pe(_mm(bZ[:, hh, :], B["KT"][:, h, :], Mb[:, h, :], True, False), r=["KT", ("Mb", g)], w=[kZ])
                P.pe(_mm(bZ[:, hh, :], B["AkT"][:, h, :], vh, False, True), r=[("AkT", g), "Vtm"], w=[kZ])
            P.act(lambda e, bZ=bZ, gs=gs: e.activation(out=B["nZ"][:, gs, :], in_=bZ, func=AF.Identity, scale=-1.0),
                  r=[kZ], w=[("BH", g)])
        for g in range(2):
            gs = slice(g * 8, (g + 1) * 8)
            bU = bk(BS[g][1]); kU = ("pb", BS[g][1])
            for hh in range(8):
                h = g * 8 + hh
                P.pe(_mm(bU[:, hh, :], B["Tt"][:, h, :], B["nZ"][:, h, :], True, True), r=[("Tt", g), ("BH", g)], w=[kU])
            P.act(lambda e, bU=bU, gs=gs: e.activation(out=B["U"][:, gs, :], in_=bU, func=AF.Identity), r=[kU], w=[("KH", g)])
        for g in range(2):
            gs = slice(g * 8, (g + 1) * 8)
            bM = bk(BS[g][2]); kM = ("pb", BS[g][2])
            bY = bk(BS[g][0]); kY = ("pb", BS[g][0])
            for hh in (range(8) if full else ()):
                h = g * 8 + hh
                vh = Vtm[:, h * 64:(h + 1) * 64]
                P.pe(_mm(bY[:, hh, :], Mb[:, h, :], B["RT"][:, h, :], True, False), r=[("Mb", g), "RT"], w=[kY])
                P.pe(_mm(bY[:, hh, :], B["U"][:, h, :], B["PbT"][:, h, :], False, False), r=[("KH", g), ("PbT", g)], w=[kY])
                P.pe(_mm(bY[:, hh, :], vh, B["PkT"][:, h, :], False, True), r=["Vtm", ("PkT", g)], w=[kY])
            if full:
                P.act(lambda e, bY=bY, gs=gs: e.activation(out=F["cum"][:, gs, :], in_=bY, func=AF.Identity), r=[kY], w=[("cum", g)])
            for hh in range(8):
                h = g * 8 + hh
                vh = Vtm[:, h * 64:(h + 1) * 64]
                P.pe(_mm(bM[:, hh, :], B["BHT"][:, h, :], B["U"][:, h, :], True, False), r=[("BHT", g), ("KH", g)], w=[kM])
                P.pe(_mm(bM[:, hh, :], B["KHT"][:, h, :], vh, False, True), r=[("KHT", g), "Vtm"], w=[kM])
            tt("dve", M[:, gs, :], M[:, gs, :], gC[:, gs].unsqueeze(2).to_broadcast([64, 8, 64]), ALU.mult,
               [("M", g), "gC"], [("M", g)])
            tt("dve", M[:, gs, :], M[:, gs, :], bM, ALU.add, [("M", g), kM], [("M", g)])
            P.act(lambda e, gs=gs: e.activation(out=Mb[:, gs, :], in_=M[:, gs, :], func=AF.Identity), r=[("M", g)], w=[("Mb", g)])
    def late_segments(j):
        s = j % 2
        X = XT[s]
        rF, kF, vF, aF, sg, cum, kap, t0, t1 = (F[n] for n in ("rF", "kF", "vF", "aF", "sg", "cum", "kap", "t0", "t1"))
        t2 = sg
        Y = F["cum"]
        Yf = Y[:].rearrange("p h c -> p (h c)")
        t0f = t0[:].rearrange("p h c -> p (h c)"); t1f = t1[:].rearrange("p h c -> p (h c)"); t2f = t2[:].rearrange("p h c -> p (h c)")

        def l1():
            P.act(lambda e: e.activation(out=t0[:], in_=Y[:], func=AF.Square), r=["cum"], w=["t0"])
            for half in range(2):
                hs = slice(half * 512, (half + 1) * 512)
                P.pe(_mm(pb[4][0:64, :], ones64[:], Yf[:, hs], True, True), r=["ones64", "cum"], w=[("pb", 4)])
                P.pe(_mm(pb[5][0:64, :], ones64[:], t0f[:, hs], True, True), r=["ones64", "t0"], w=[("pb", 5)])
                P.act(lambda e, hs=hs: e.activation(out=t1f[:, hs], in_=pb[4][0:64, :], func=AF.Identity), r=[("pb", 4)], w=[("t1", half)])
                tt("dve", t2f[:, hs], t1f[:, hs], t1f[:, hs], ALU.mult, [("t1", half)], [("sg", half)])
                tt("dve", t2f[:, hs], pb[5][0:64, :], t2f[:, hs], ALU.subtract, [("pb", 5), ("sg", half)], [("sg", half)])
            P.dve(lambda e: e.tensor_scalar(out=t2[:], in0=t2[:], scalar1=0.0, scalar2=64e-5, op0=ALU.max, op1=ALU.add), r=["sg"], w=["sg"])
            P.act(lambda e: e.activation(out=t2[:], in_=t2[:], func=AF.Sqrt), r=["sg"], w=["sg"])
            P.dve(lambda e: e.reciprocal(out=t2[:], in_=t2[:]), r=["sg"], w=["sg"])

        def l2():
            tt("dve", Y[:], Y[:], t1[:], ALU.subtract, ["cum", "t1"], ["cum"])
            tt("dve", Y[:], Y[:], t2[:], ALU.mult, ["cum", "sg"], ["cum"])
            tt("pool", Y[:], Y[:], bc(hv["gng"]), ALU.mult, ["cum", "gng"], ["cum"])
            tt("pool", Y[:], Y[:], bc(hv["gnb"]), ALU.add, ["cum", "gnb"], ["cum"])
            tt("dve", Y[:], Y[:], kap[:], ALU.add, ["cum", "kap"], ["cum"])
            tt("dve", B["zB"][:], Y[:], F["gF"][:], ALU.mult, ["cum", "gF"], ["zB"])

        def l3():
            ob = pb[4][:, :].rearrange("p (c t) -> p c t", t=TT)
            for db in range(KC):
                for h in range(H):
                    P.pe(_mm(ob[:, db, :], wo[:, h, db * 128:(db + 1) * 128], B["zB"][:, h, :], h == 0, h == H - 1),
                         r=[("wo", h // 2), "zB"], w=[("pb", 4)])
            P.dve(lambda e: e.scalar_tensor_tensor(out=xr_[:], in0=X[:, :, 1:TT + 1], scalar=float(ALPHA), in1=ob,
                                                   op0=ALU.mult, op1=ALU.add), r=[("XT", s), ("pb", 4)], w=["xr_"])

        def l4():
            for c in range(KC):
                P.pe(_mm(mean_ps, ones1024[:], xr_[:, c, :], c == 0, c == KC - 1), r=["ones1024", "xr_"], w=["mean_ps"])
            P.act(lambda e: e.activation(out=sq8[:], in_=xr_[:], func=AF.Square), r=["xr_"], w=["sq8"])
            for c in range(KC):
                P.pe(_mm(msq_ps, ones1024[:], sq8[:, c, :], c == 0, c == KC - 1), r=["ones1024", "sq8"], w=["msq_ps"])
            P.act(lambda e: e.activation(out=mean_sb[:], in_=mean_ps, func=AF.Identity), r=["mean_ps"], w=["mean_sb"])
            tt("dve", var_sb[:], mean_sb[:], mean_sb[:], ALU.mult, ["mean_sb"], ["var_sb"])
            tt("dve", var_sb[:], msq_ps, var_sb[:], ALU.subtract, ["msq_ps", "var_sb"], ["var_sb"])
            P.dve(lambda e: e.tensor_scalar(out=var_sb[:], in0=var_sb[:], scalar1=0.0, scalar2=float(LN_EPS),
                                            op0=ALU.max, op1=ALU.add), r=["var_sb"], w=["var_sb"])
            P.act(lambda e: e.activation(out=var_sb[:], in_=var_sb[:], func=AF.Sqrt), r=["var_sb"], w=["var_sb"])
            P.dve(lambda e: e.reciprocal(out=rstd_sb[:], in_=var_sb[:]), r=["var_sb"], w=["rstd_sb"])
            tt("dve", xr_[:], xr_[:], mean_sb[:].unsqueeze(1).to_broadcast([128, KC, TT]), ALU.subtract, ["xr_", "mean_sb"], ["xr_"])
            tt("dve", xr_[:], xr_[:], rstd_sb[:].unsqueeze(1).to_broadcast([128, KC, TT]), ALU.mult, ["xr_", "rstd_sb"], ["xr_"])
            tt("dve", xr_[:], xr_[:], lng[:].unsqueeze(2).to_broadcast([128, KC, TT]), ALU.mult, ["xr_", "lng"], ["xr_"])
            tt("dve", xr_[:], xr_[:], lnb[:].unsqueeze(2).to_broadcast([128, KC, TT]), ALU.add, ["xr_", "lnb"], ["xr_"])
            P.dma("sp", lambda e: e.dma_start(out=out_v[:, :, j * TT:(j + 1) * TT], in_=xr_[:]), r=["xr_"], w=[ok + (j,)])
        return [l1, l2, l3, l4]

    load_x(0)
    if ntiles > 1:
        load_x(1)
    for seg in early_segments(0):
        seg()
    for j in range(ntiles):
        P.stage = f"mid{j}"
        mid(j)
        E = early_segments(j + 1) if j + 1 < ntiles else []
        L = late_segments(j) if full else []
        order = [("L", 0), ("E", 0), ("L", 1), ("E", 1), ("L", 2), ("E", 2), ("L", 3), ("E", 3)]
        order = ORDER_OVERRIDE or order
        for kind, k in order:
            lst = L if kind == "L" else E
            if k < len(lst):
                P.stage = (f"late{j}" if kind == "L" else f"early{j + 1}")
                lst[k]()
        if j + 2 < ntiles:
            load_x(j + 2)
    P.dma("sp", lambda e: e.dma_start(out=io["M_out"], in_=M[:]), r=["M"], w=[io.get("Mout_key", ("M_out",))])


ODD_IN = dict(w_r=[D, D], w_k=[D, D], w_v=[D, D], w_o=[D, D], w1=[D, 64], a1=[D, 64], g1=[D, 160], w2=[64, D], a2=[64, D],
              g2=[160, D], mu=[128, 6, KC], w0h=[64, H], a0h=[64, H], kkh=[64, H], kah=[64, H], rkh=[64, H], gng=[64, H],
              gnb=[64, H], lng=[128, KC], lnb=[128, KC], mUs=[64, 64], mUi=[64, 64], mLs=[64, 64], eye=[64, 64],
              rmask_o=[64, H * C], id32_o=[64, 64], M=[64, H, 64])


def build_odd_nc(NT):
    nc = bass.Bass("TRN2", target_bir_lowering=False)
    io = {"xin": nc.dram_tensor("xin", [D, 1 + NT], F32, kind="ExternalInput").ap()}
    for k, shp in ODD_IN.items():
        io[k] = nc.dram_tensor(k, list(shp), F32, kind="ExternalInput").ap()
    io["out"] = nc.dram_tensor("out", [D, NT], F32, kind="ExternalOutput").ap()
    io["M_out"] = nc.dram_tensor("M_out", [64, H, 64], F32, kind="ExternalOutput").ap()
    P = Prog(nc)
    P.begin_phase()
    build_odd(nc, P, NT, io)
    P.end_phase()
    P.finish()
    return nc, P


def hvec(v):
    return np.ascontiguousarray(np.asarray(v, np.float32).reshape(H, 64).T)


def odd_inputs(xin_T, p, j, lng, lnb, M):
    i = np.arange(64)
    d = dict(xin=np.ascontiguousarray(xin_T, dtype=np.float32))
    for k in ("w_r", "w_k", "w_v", "w_o", "w1", "a1", "g1", "w2", "a2", "g2"):
        d[k] = np.ascontiguousarray(p["rw_" + k][j], dtype=np.float32)
    d["mu"] = np.ascontiguousarray(np.stack([vec_layout(p["rw_mu"][j][i6], KC) for i6 in range(6)], axis=1))
    d["w0h"] = hvec(p["rw_w0"][j]); d["a0h"] = hvec(p["rw_a0"][j]); d["kkh"] = hvec(p["rw_k_k"][j])
    d["kah"] = hvec(p["rw_k_a"][j]); d["rkh"] = hvec(p["rw_r_k"][j].reshape(-1)); d["gng"] = hvec(p["rw_gn_g"][j])
    d["gnb"] = hvec(p["rw_gn_b"][j]); d["lng"] = vec_layout(lng, KC); d["lnb"] = vec_layout(lnb, KC)
    d["mUs"] = (i[:, None] < i[None, :]).astype(np.float32); d["mUi"] = (i[:, None] <= i[None, :]).astype(np.float32)
    d["mLs"] = (i[:, None] > i[None, :]).astype(np.float32); d["eye"] = np.eye(64, dtype=np.float32)
    rm = np.ones((64, H * C), np.float32); rm[:, ::C] = 0.0
    d["rmask_o"] = rm; d["id32_o"] = np.eye(64, dtype=np.float32); d["M"] = np.ascontiguousarray(M, dtype=np.float32)
    return d


NCORES = 8
SEQ = 8192
NTC = SEQ // 2
TT_E = 256
TT_F = 256
HMAX = 30
PAIRS = [[0, 1], [2, 3], [4, 5], [6, 7]]

EVEN_L = dict(w_in=[D, 3072], w_out=[D, D], bin=[128, 24], convw=[128, 4 * 31], convb=[128, 4], clng=[128, 4],
              clnb=[128, 4], lb0=[128, 4], lb1=[128, 4], lbsel=[128, 1], ong=[128, 4], lng=[128, KC], lnb=[128, KC],
              bivbc=[128, 512])
EVEN_C = dict(mask=[128, 128], rmask=[128, TT_E], ident=[128, 128])
ODD_C = ("mUs", "mUi", "mLs", "eye", "rmask_o", "id32_o")
FFN_L = dict(w_up=[D, DFF], w_gate=[D, DFF], w_down=[DFF, D], cw=[128, FB * 3], cb=[128, FB], lng=[128, KC], lnb=[128, KC])


def _exchange(P, nc, tag, src_ap, src_key, rows, cols, bin_t, bout_t, dst_ap, dst_key, sel_ap, pairs=None):
    P.begin_phase(tag)
    np_ = min(rows, 128)
    nch = rows // np_
    t = P.sb([np_, nch, cols], F32, "xt")
    selt = P.sb([128, 1], F32, "sel")
    P.dma("sp", lambda e: e.dma_start(out=bin_t.ap(), in_=src_ap), r=[src_key], w=[(tag, "bin")])
    P.cc(lambda e: e.collective_compute("AllGather", ALU.bypass, replica_groups=(pairs or PAIRS),
                                        ins=[bin_t.ap().opt()], outs=[bout_t.ap().opt()]),
         r=[(tag, "bin")], w=[(tag, "bout")])
    P.dma("sp", lambda e: e.dma_start(out=t[:], in_=bout_t.ap()[0:rows, :].rearrange("(c p) t -> p c t", p=np_)),
          r=[(tag, "bout")], w=["xt"])
    P.dma("sp", lambda e: e.dma_start(out=selt[:], in_=sel_ap), w=["selt"])
    P.dve(lambda e: e.tensor_scalar(out=t[:], in0=t[:], scalar1=selt[0:np_, 0:1], scalar2=None, op0=ALU.mult),
          r=["xt", "selt"], w=["xt"])
    P.dma("sp", lambda e: e.dma_start(out=dst_ap.rearrange("(c p) t -> p c t", p=np_), in_=t[:]), r=["xt"], w=[dst_key])
    P.end_phase()


def build_fused_nc(NT=NTC, nlayers=DEPTH, ncores=NCORES):
    nc = bass.Bass("TRN2", target_bir_lowering=False)
    pairs = [[2 * i, 2 * i + 1] for i in range(ncores // 2)]

    def din(name, shape):
        return nc.dram_tensor(name, list(shape), F32, kind="ExternalInput").ap()

    def dint(name, shape):
        return nc.dram_tensor(name, list(shape), F32)

    xin = din("xin", [D, HMAX + NT])
    sel = din("sel", [128, 1])
    zst = din("zst", [128, 1024])
    out = nc.dram_tensor("out", [D, NT], F32, kind="ExternalOutput").ap()
    consts = {k: din(k, v) for k, v in EVEN_C.items()}
    for k in ODD_C:
        consts[k] = din(k, ODD_IN[k])
    slab = [dint(f"slab{i}", [D, HMAX + NT]) for i in range(2)]
    hx_in = dint("hx_in", [D, HMAX]); hx_out = dint("hx_out", [2 * D, HMAX])
    se_in = dint("se_in", [128, 512]); se_out = dint("se_out", [256, 512]); se_sel = dint("se_sel", [128, 512])
    so_in = dint("so_in", [64, 1024]); so_out = dint("so_out", [128, 1024]); so_sel = dint("so_sel", [64, 1024])
    dump = dint("dump", [128, 1024])
    P = Prog(nc)
    for layer in range(nlayers):
        src = xin if layer == 0 else slab[1].ap()
        skey = ("xin_ext",) if layer == 0 else ("slab", 1)
        if layer > 0:
            _exchange(P, nc, f"hx{layer}m_", slab[1].ap()[:, NT:NT + HMAX], ("slab", 1), D, HMAX, hx_in, hx_out,
                      slab[1].ap()[:, 0:HMAX], ("slab", 1, "halo"), sel, pairs)
        if layer % 2 == 0:
            io = {k: din(f"{k}_{layer}", v) for k, v in EVEN_L.items()}
            io.update(consts)
            io["hsc"] = sel
            io["xin"] = src
            io["xin_key"] = skey
            ioa = dict(io); ioa["S"] = zst[:, 0:512].rearrange("p (h v) -> p h v", h=4); ioa["S_key"] = ("zst",)
            ioa["out"] = slab[0].ap()[:, HMAX:HMAX + NT]; ioa["S_out"] = se_in.ap().rearrange("p (h v) -> p h v", h=4)
            ioa["Sout_key"] = ("se_in",)
            P.begin_phase(f"L{layer}a_"); build_even(nc, P, NT, TT_E, ioa, state_only=True); P.end_phase()
            _exchange(P, nc, f"sx{layer}_", se_in.ap(), ("se_in",), 128, 512, se_in, se_out, se_sel.ap(), ("se_sel",), sel, pairs)
            iob = dict(io); iob["S"] = se_sel.ap().rearrange("p (h v) -> p h v", h=4); iob["S_key"] = ("se_sel",)
            iob["out"] = slab[0].ap()[:, HMAX:HMAX + NT]; iob["out_key"] = ("slab", 0, "body")
            iob["S_out"] = dump.ap()[:, 0:512].rearrange("p (h v) -> p h v", h=4); iob["Sout_key"] = ("dump",)
            P.begin_phase(f"L{layer}b_"); build_even(nc, P, NT, TT_E, iob); P.end_phase()
        else:
            io = {k: din(f"{k}_{layer}", v) for k, v in ODD_IN.items() if k not in ODD_C and k != "M"}
            io.update(consts)
            io["xin"] = src[:, HMAX - 1:HMAX + NT]
            io["xin_key"] = skey
            ioa = dict(io); ioa["M"] = zst[0:64, :].rearrange("p (h v) -> p h v", h=H); ioa["M_key"] = ("zst",)
            ioa["out"] = slab[0].ap()[:, HMAX:HMAX + NT]; ioa["M_out"] = so_in.ap().rearrange("p (h v) -> p h v", h=H)
            ioa["Mout_key"] = ("so_in",)
            P.begin_phase(f"L{layer}a_"); build_odd(nc, P, NT, ioa, state_only=True); P.end_phase()
            _exchange(P, nc, f"sx{layer}_", so_in.ap(), ("so_in",), 64, 1024, so_in, so_out, so_sel.ap(), ("so_sel",), sel, pairs)
            iob = dict(io); iob["M"] = so_sel.ap().rearrange("p (h v) -> p h v", h=H); iob["M_key"] = ("so_sel",)
            iob["out"] = slab[0].ap()[:, HMAX:HMAX + NT]; iob["out_key"] = ("slab", 0, "body")
            iob["M_out"] = dump.ap()[0:64, :].rearrange("p (h v) -> p h v", h=H); iob["Mout_key"] = ("dump",)
            P.begin_phase(f"L{layer}b_"); build_odd(nc, P, NT, iob); P.end_phase()
        _exchange(P, nc, f"hx{layer}f_", slab[0].ap()[:, NT:NT + HMAX], ("slab", 0), D, HMAX, hx_in, hx_out,
                  slab[0].ap()[:, 0:HMAX], ("slab", 0, "halo"), sel, pairs)
        iof = {k: din(f"{k}_f{layer}", v) for k, v in FFN_L.items()}
        iof["xin"] = slab[0].ap()[:, HMAX - 2:HMAX + NT]; iof["xin_key"] = ("slab", 0)
        last = layer == nlayers - 1
        iof["out"] = out if last else slab[1].ap()[:, HMAX:HMAX + NT]
        iof["out_key"] = ("out_ext",) if last else ("slab", 1, "body")
        P.begin_phase(f"L{layer}f_"); build_ffn(nc, P, NT, TT_F, iof); P.end_phase()
    P.finish()
    return nc, P


def fused_inputs(inp, NT=NTC, nlayers=DEPTH, x_slabs=None):
    base = {}
    ec = even_consts(TT_E)
    base.update(ec)
    i = np.arange(64)
    base["mUs"] = (i[:, None] < i[None, :]).astype(np.float32); base["mUi"] = (i[:, None] <= i[None, :]).astype(np.float32)
    base["mLs"] = (i[:, None] > i[None, :]).astype(np.float32); base["eye"] = np.eye(64, dtype=np.float32)
    rm = np.ones((64, H * C), np.float32); rm[:, ::C] = 0.0
    base["rmask_o"] = rm; base["id32_o"] = np.eye(64, dtype=np.float32)
    base["zst"] = np.zeros((128, 1024), np.float32)
    dummy_x = np.zeros((D, 1), np.float32)
    for layer in range(nlayers):
        j = layer // 2
        if layer % 2 == 0:
            d = even_inputs(dummy_x, inp["ev_w_in"][j], inp["ev_b_in"][j], inp["ev_conv_w"][j], inp["ev_conv_b"][j],
                            inp["ev_cln_g"][j], inp["ev_cln_b"][j], inp["ev_lb_logits"], j, inp["ev_onorm_g"][j],
                            inp["ev_w_out"][j], inp["ln_mix_g"][layer], inp["ln_mix_b"][layer],
                            np.zeros((1,), np.float32), True, TT_E)
            for k in EVEN_L:
                base[f"{k}_{layer}"] = d[k]
        else:
            d = odd_inputs(dummy_x, inp, j, inp["ln_mix_g"][layer], inp["ln_mix_b"][layer], np.zeros((1,), np.float32))
            for k in ODD_IN:
                if k not in ODD_C and k != "M":
                    base[f"{k}_{layer}"] = d[k]
        d = ffn_inputs(dummy_x, inp["ff_w_up"][layer], inp["ff_w_gate"][layer], inp["ff_conv_w"][layer],
                       inp["ff_conv_b"][layer], inp["ff_w_down"][layer], inp["ln_ffn_g"][layer], inp["ln_ffn_b"][layer])
        for k in FFN_L:
            base[f"{k}_f{layer}"] = d[k]
    maps = []
    for c in range(len(x_slabs)):
        m = dict(base)
        m["xin"] = x_slabs[c]
        m["sel"] = np.full((128, 1), float(c % 2), np.float32)
        maps.append(m)
    return maps


_PROGS = {}


def kernel(**inp):
    inp = {k: np.asarray(v) for k, v in inp.items()}
    x = inp["x"].astype(np.float32)
    B = x.shape[0]
    slabs = []
    for c in range(NCORES):
        b, h = divmod(c, 2)
        xT = x[b].T
        if h == 0:
            sl = np.concatenate([np.zeros((D, HMAX), np.float32), xT[:, :NTC]], axis=1)
        else:
            sl = xT[:, NTC - HMAX:]
        slabs.append(np.ascontiguousarray(sl))
    if "fused" not in _PROGS:
        _PROGS["fused"] = build_fused_nc(NTC, DEPTH)[0]
    maps = fused_inputs(inp, NTC, DEPTH, slabs)
    res = run_bass_kernel_spmd(_PROGS["fused"], maps, core_ids=list(range(NCORES))).results
    out = np.empty((B, SEQ, D), np.float32)
    for c in range(NCORES):
        b, h = divmod(c, 2)
        out[b, h * NTC:(h + 1) * NTC, :] = res[c]["out"].T
    return out
```

```python
import contextlib
import os
import numpy as np
import concourse.bass as bass
import concourse.mybir as mybir
from concourse.bass_utils import run_bass_kernel_spmd

F32 = mybir.dt.float32
BF16 = mybir.dt.bfloat16
ALU = mybir.AluOpType
AF = mybir.ActivationFunctionType

D = 1024
KC = D // 128
DFF = 2816
FB = DFF // 128
DEPTH = 4
ALPHA = (2 * DEPTH) ** 0.25
LN_EPS = 1e-5


class _Rec:
    def __init__(self):
        self.call = None

    def __getattr__(self, name):
        def f(*a, **k):
            assert self.call is None, "op lambda must make exactly one engine call"
            self.call = (name, a, k)
            return self
        return f


class Prog:
    ENGS = ("pe", "act", "dve", "pool", "sp")
    EPOCH = 16000
    NEP = dict(pe=8, act=5, dve=6, pool=4, sp=1)
    NDMA = 24
    NCC = 16

    def __init__(self, nc):
        self.nc = nc
        self.ops = []
        self.last_w = {}
        self.readers = {}
        self.gstack = contextlib.ExitStack()
        self.stack = None
        self.ntile = 0
        self.known = set()
        self.children = {}
        self.bank_of = {}
        self.bank_last = {}
        self.prefix = ""
        self.phase_start = 0
        self.nphase = 0
        self.cnt = {e: 0 for e in self.ENGS}
        self.ndma = 0
        self.ncc = 0
        self.dma_prev = {}
        self.sems = {}
        g = self.gstack
        for e in self.ENGS:
            for ep in range(self.NEP[e]):
                self.sems[(e, ep)] = g.enter_context(nc.semaphore(f"s_{e}_{ep}"))
        for i in range(self.NDMA):
            self.sems[("dma", i)] = g.enter_context(nc.semaphore(f"s_dma_{i}"))
        for i in range(self.NCC):
            self.sems[("cc", i)] = g.enter_context(nc.semaphore(f"s_cc_{i}"))
        self.bar = g.enter_context(nc.semaphore("s_bar"))
        self.stats = dict(n_ops=0)

    def begin_phase(self, prefix=""):
        self.stack = contextlib.ExitStack()
        self.prefix = prefix
        self.phase_start = len(self.ops)
        self.bank_of = {}
        self.bank_last = {}

    def end_phase(self):
        self._emit_phase()
        self.stack.close()
        self.stack = None

    def finish(self):
        self.gstack.close()
        self.stats = dict(n_ops=len(self.ops), milestones=dict(self.cnt), ndma=self.ndma, ncc=self.ncc)

    def sb(self, shape, dt, name=None):
        self.ntile += 1
        name = "sb_" + self.prefix + (name or f"t{self.ntile}")
        return self.stack.enter_context(self.nc.sbuf_tensor(name, list(shape), dt))

    def ps(self, shape, dt=F32, name=None, keys=None):
        self.ntile += 1
        nm = name or f"p{self.ntile}"
        for k in (keys if keys is not None else [nm]):
            k = k if isinstance(k, tuple) else (k,)
            self.bank_of[k] = nm
        return self.stack.enter_context(self.nc.psum_tensor("ps_" + self.prefix + nm, list(shape), dt))

    def _banks(self, keys):
        out = set()
        for k in keys:
            for i in range(1, len(k) + 1):
                if k[:i] in self.bank_of:
                    out.add(self.bank_of[k[:i]])
        return out

    def _norm(self, k):
        k = k if isinstance(k, tuple) else (k,)
        if k not in self.known:
            self.known.add(k)
            for i in range(1, len(k)):
                self.children.setdefault(k[:i], set()).add(k)
        return k

    def _related(self, k):
        for i in range(1, len(k) + 1):
            yield k[:i]
        for ext in self.children.get(k, ()):
            yield ext

    def op(self, eng, fn, reads=(), writes=(), dma=False, cc=False):
        idx = len(self.ops)
        deps = set()
        reads = [self._norm(k) for k in reads]
        writes = [self._norm(k) for k in writes]
        for k in reads:
            for r in self._related(k):
                if r in self.last_w:
                    deps.add(self.last_w[r])
        for k in writes:
            for r in self._related(k):
                if r in self.last_w:
                    deps.add(self.last_w[r])
                for x in self.readers.get(r, ()):
                    deps.add(x)
        for k in writes:
            self.last_w[k] = idx
            self.readers[k] = []
            for ext in self.children.get(k, ()):
                self.last_w.pop(ext, None)
                self.readers[ext] = []
        for k in reads:
            self.readers.setdefault(k, []).append(idx)
        for bk in self._banks(reads + writes):
            bl = self.bank_last.setdefault(bk, {})
            for e2, i2 in bl.items():
                if e2 != eng:
                    deps.add(i2)
            bl[eng] = idx
        deps.discard(idx)
        deps = {d for d in deps if d >= self.phase_start}
        rec = _Rec()
        fn(rec)
        assert rec.call is not None
        self.ops.append(dict(eng=eng, call=rec.call, deps=deps, dma=dma or cc, cc=cc, stage=getattr(self, "stage", "")))
        return idx

    def pe(self, fn, r=(), w=()):
        return self.op("pe", fn, r, w)

    def act(self, fn, r=(), w=()):
        return self.op("act", fn, r, w)

    def dve(self, fn, r=(), w=()):
        return self.op("dve", fn, r, w)

    def pool(self, fn, r=(), w=()):
        return self.op("pool", fn, r, w)

    def dma(self, eng, fn, r=(), w=()):
        return self.op(eng, fn, r, w, dma=True)

    def cc(self, fn, r=(), w=()):
        return self.op("pool", fn, r, w, cc=True)

    def _emit_phase(self):
        nc = self.nc
        ops = self.ops
        lo = self.phase_start
        idxs = range(lo, len(ops))
        for i in idxs:
            o = ops[i]
            best = {}
            keep = set()
            for d in o["deps"]:
                p = ops[d]
                if p["dma"]:
                    keep.add(d)
                elif best.get(p["eng"], -1) < d:
                    best[p["eng"]] = d
            keep.update(best.values())
            o["deps"] = keep
        needed = set()
        for i in idxs:
            o = ops[i]
            for d in o["deps"]:
                p = ops[d]
                if p["dma"]:
                    needed.add(d)
                elif p["eng"] == o["eng"] and o["eng"] == "pe" and not o["dma"]:
                    continue
                else:
                    needed.add(d)
        per_eng = {e: [i for i in idxs if ops[i]["eng"] == e] for e in self.ENGS}
        last_compute = {}
        for e in self.ENGS:
            for i in reversed(per_eng[e]):
                if not ops[i]["dma"]:
                    last_compute[e] = i
                    needed.add(i)
                    break
        for i in idxs:
            o = ops[i]
            if o["cc"]:
                assert self.ncc < self.NCC
                o["sig"] = ("cc", self.ncc, 1)
                o["prev_same_slot"] = None
                self.ncc += 1
            elif o["dma"]:
                slot = self.ndma % self.NDMA
                o["sig"] = ("dma", slot, 16 * (self.ndma // self.NDMA + 1))
                o["prev_same_slot"] = self.dma_prev.get(slot)
                self.dma_prev[slot] = i
                self.ndma += 1
            elif i in needed:
                c = self.cnt[o["eng"]]
                assert c // self.EPOCH < self.NEP[o["eng"]], "out of semaphore epochs"
                o["sig"] = (o["eng"], c // self.EPOCH, c % self.EPOCH + 1)
                self.cnt[o["eng"]] = c + 1
            else:
                o["sig"] = None
        sems = self.sems
        self.nphase += 1
        nph = self.nphase
        bar = self.bar

        def run_engine(ename, eng):
            seen = {}

            def wait(sig):
                key = (sig[0], sig[1])
                if seen.get(key, 0) >= sig[2]:
                    return
                eng.wait_ge(sems[key], sig[2])
                seen[key] = sig[2]

            for i in per_eng[ename]:
                o = ops[i]
                best = {}
                for d in o["deps"]:
                    p = ops[d]
                    if (not p["dma"]) and p["eng"] == ename and ename == "pe" and not o["dma"]:
                        continue
                    sg = p["sig"]
                    kk = (sg[0], sg[1])
                    if best.get(kk, 0) < sg[2]:
                        best[kk] = sg[2]
                for kk in sorted(best):
                    wait((kk[0], kk[1], best[kk]))
                if o["dma"] and o["prev_same_slot"] is not None and o["prev_same_slot"] >= lo:
                    wait(ops[o["prev_same_slot"]]["sig"])
                call = o["call"]
                ins = getattr(eng, call[0])(*call[1], **call[2])
                sig = o["sig"]
                if sig is not None:
                    if sig[0] == "dma":
                        ins.then_inc(sems[("dma", sig[1])], 16)
                    elif sig[0] == "cc":
                        ins.then_inc(sems[("cc", sig[1])])
                    else:
                        ins.then_inc(sems[(sig[0], sig[1])], 1)
            for i in per_eng[ename]:
                if ops[i]["dma"]:
                    wait(ops[i]["sig"])
            if ename in last_compute:
                wait(ops[last_compute[ename]]["sig"])
            eng.sem_inc(bar, 1)
            eng.wait_ge(bar, len(self.ENGS) * nph)

        with nc.Block() as block:
            @block.tensor
            def _(e):
                run_engine("pe", e)

            @block.scalar
            def _(e):
                run_engine("act", e)

            @block.vector
            def _(e):
                run_engine("dve", e)

            @block.gpsimd
            def _(e):
                run_engine("pool", e)

            @block.sync
            def _(e):
                run_engine("sp", e)


class Stager:
    def __init__(self, P, width, nbuf=2):
        self.P = P
        self.bufs = [P.sb([128, width], F32, f"stage{i}") for i in range(nbuf)]
        self.n = 0

    def load(self, dst_ap, src_ap, n, dkey, eng=None, a=1):
        P = self.P
        i = self.n % len(self.bufs)
        eng = eng or ("pool", "act", "dve")[self.n % 3]
        self.n += 1
        st = self.bufs[i]
        sv = st[:, 0:n] if a == 1 else st[:, 0:n].rearrange("p (a d) -> p a d", a=a)
        P.dma("sp", lambda e: e.dma_start(out=sv, in_=src_ap), w=[("stage", i)])
        cast_op(P, eng, dst_ap, sv, [("stage", i)], [dkey])


def cast_op(P, eng, dst, src, r, w):
    if eng == "act":
        P.op("act", lambda e: e.activation(out=dst, in_=src, func=AF.Identity), r, w)
    else:
        P.op(eng, lambda e: e.tensor_copy(out=dst, in_=src), r, w)


def _mm(out, lhsT, rhs, start, stop):
    return lambda e: e.matmul(out, lhsT, rhs, start=start, stop=stop)


def build_ffn(nc, P, NT, TT, io):
    HALO = 2
    ntiles = NT // TT
    xk = io.get("xin_key", ("xin_ext",))
    ok = io.get("out_key", ("outdram",))
    xin = io["xin"]
    xin_v = xin.rearrange("(kc p) t -> p kc t", p=128)
    out_v = io["out"].rearrange("(kc p) t -> p kc t", p=128)

    wup = P.sb([128, KC, DFF], BF16, "wup")
    wgt = P.sb([128, KC, DFF], BF16, "wgt")
    wdn = P.sb([128, FB, D], BF16, "wdn")
    cw = P.sb([128, FB * 3], F32, "cw")
    cb = P.sb([128, FB], F32, "cb")
    lng = P.sb([128, KC], F32, "lng")
    lnb = P.sb([128, KC], F32, "lnb")
    ones = P.sb([128, 128], F32, "ones")
    carry = P.sb([128, FB, 2], F32, "carry")
    xbf = [P.sb([128, KC, TT], BF16, f"xbf{i}") for i in range(2)]
    xhalo = P.sb([128, KC, HALO], BF16, "xhalo")
    xres = [P.sb([128, KC, TT], F32, f"xres{i}") for i in range(2)]
    hT = P.sb([128, FB, TT], BF16, "hT")
    NQ = 3
    usb = [P.sb([128, TT + 2], F32, f"usb{i}") for i in range(NQ)]
    t1 = [P.sb([128, TT], F32, f"t1_{i}") for i in range(NQ)]
    gg = [P.sb([128, TT], F32, f"gg{i}") for i in range(NQ)]
    gsb = [P.sb([128, TT], BF16, f"gsb{i}") for i in range(NQ)]
    sq = [P.sb([128, TT], F32, f"sq{i}") for i in range(2)]
    mean_sb = P.sb([128, TT], F32, "mean_sb")
    var_sb = P.sb([128, TT], F32, "var_sb")
    rstd_sb = P.sb([128, TT], F32, "rstd_sb")
    yt = [P.sb([128, TT], F32, f"yt{i}") for i in range(2)]

    up_ps = [P.ps([128, TT], F32, f"up_ps{i}", keys=[("up_ps", i)]) for i in range(2)]
    gt_ps = [P.ps([128, TT], F32, f"gt_ps{i}", keys=[("gt_ps", i)]) for i in range(2)]
    o_ps = [P.ps([128, TT], F32, f"o_ps{i}", keys=[("o_ps", i)]) for i in range(2)]
    mean_ps = P.ps([128, TT], F32, "mean_ps")
    msq_ps = P.ps([128, TT], F32, "msq_ps")

    P.dma("sp", lambda e: e.dma_start(out=cw[:], in_=io["cw"]), w=["cw"])
    P.dma("sp", lambda e: e.dma_start(out=cb[:], in_=io["cb"]), w=["cb"])
    P.dma("sp", lambda e: e.dma_start(out=lng[:], in_=io["lng"]), w=["lng"])
    P.dma("sp", lambda e: e.dma_start(out=lnb[:], in_=io["lnb"]), w=["lnb"])
    P.pool(lambda e: e.memset(ones[:], 1.0 / D), w=["ones"])
    xh32 = P.sb([128, KC, HALO], F32, "xh32")
    P.dma("sp", lambda e: e.dma_start(out=xh32[:], in_=xin_v[:, :, 0:HALO]), r=[xk], w=["xh32"])
    P.pool(lambda e: e.tensor_copy(out=xhalo[:], in_=xh32[:]), r=["xh32"], w=["xhalo"])

    def load_xres(j):
        s = j % 2
        P.dma("sp", lambda e: e.dma_start(out=xres[s][:], in_=xin_v[:, :, HALO + j * TT:HALO + (j + 1) * TT]),
              r=[xk], w=[("xres", s)])
        P.pool(lambda e: e.tensor_copy(out=xbf[s][:], in_=xres[s][:]), r=[("xres", s)], w=[("xbf", s)])

    def load_x(j):
        pass

    load_xres(0)
    stg = Stager(P, 1024, nbuf=3)
    wupv = io["w_up"].rearrange("(kc p) f -> p kc f", p=128)
    wgtv = io["w_gate"].rearrange("(kc p) f -> p kc f", p=128)
    wdnv = io["w_down"].rearrange("(fb p) d -> p fb d", p=128)
    pieces = [(0, 1024), (1024, 1024), (2048, DFF - 2048)]
    for kc in range(KC):
        for (c0, cn) in pieces:
            stg.load(wup[:, kc, c0:c0 + cn], wupv[:, kc, c0:c0 + cn], cn, ("wup", kc, c0))
    for kc in range(KC):
        for (c0, cn) in pieces:
            stg.load(wgt[:, kc, c0:c0 + cn], wgtv[:, kc, c0:c0 + cn], cn, ("wgt", kc, c0))
    for fb in range(FB):
        stg.load(wdn[:, fb, :], wdnv[:, fb, :], D, ("wdn", fb // 2, fb % 2))

    hp = up_ps[1]
    for fb in range(FB):
        for kc in range(KC):
            P.pe(_mm(hp[:, fb * 2:fb * 2 + 2], wup[:, kc, fb * 128:(fb + 1) * 128], xhalo[:, kc, :],
                     kc == 0, kc == KC - 1),
                 r=[("wup", kc), "xhalo"], w=[("up_ps", 1)])
    P.act(lambda e: e.activation(out=carry[:].rearrange("p a b -> p (a b)"), in_=hp[:, 0:FB * 2], func=AF.Identity),
          r=[("up_ps", 1)], w=["carry"])

    nblk = 0
    for j in range(ntiles):
        s = j % 2
        if j + 1 < ntiles:
            load_x(j + 1)
        xb = xbf[s]
        for fb in range(FB):
            b = nblk % 2
            nblk += 1
            fsl = slice(fb * 128, (fb + 1) * 128)
            for kc in range(KC):
                P.pe(_mm(up_ps[b][:], wup[:, kc, fsl], xb[:, kc, :], kc == 0, kc == KC - 1),
                     r=[("wup", kc), ("xbf", s)], w=[("up_ps", b)])
            for kc in range(KC):
                P.pe(_mm(gt_ps[b][:], wgt[:, kc, fsl], xb[:, kc, :], kc == 0, kc == KC - 1),
                     r=[("wgt", kc), ("xbf", s)], w=[("gt_ps", b)])
            q = (nblk - 1) % NQ
            u = usb[q]
            P.pool(lambda e, u=u, fb=fb: e.tensor_copy(out=u[:, 0:2], in_=carry[:, fb, :]),
                   r=[("carry", fb)], w=[("usb", q)])
            P.act(lambda e, u=u, b=b: e.activation(out=u[:, 2:2 + TT], in_=up_ps[b][:], func=AF.Identity),
                  r=[("up_ps", b)], w=[("usb", q)])
            P.act(lambda e, q=q, b=b: e.activation(out=gsb[q][:], in_=gt_ps[b][:], func=AF.Identity),
                  r=[("gt_ps", b)], w=[("gsb", q)])
            P.pool(lambda e, u=u, fb=fb: e.tensor_copy(out=carry[:, fb, :], in_=u[:, TT:TT + 2]),
                   r=[("usb", q)], w=[("carry", fb)])
            tt = t1[q]
            P.dve(lambda e, u=u, tt=tt, fb=fb: e.tensor_scalar(
                out=tt[:], in0=u[:, 2:2 + TT], scalar1=cw[:, fb * 3 + 2:fb * 3 + 3], scalar2=cb[:, fb:fb + 1],
                op0=ALU.mult, op1=ALU.add), r=[("usb", q), "cw", "cb"], w=[("t1", q)])
            P.dve(lambda e, u=u, tt=tt, fb=fb: e.scalar_tensor_tensor(
                out=tt[:], in0=u[:, 1:1 + TT], scalar=cw[:, fb * 3 + 1:fb * 3 + 2], in1=tt[:],
                op0=ALU.mult, op1=ALU.add), r=[("usb", q), ("t1", q)], w=[("t1", q)])
            P.dve(lambda e, u=u, tt=tt, fb=fb: e.scalar_tensor_tensor(
                out=tt[:], in0=u[:, 0:TT], scalar=cw[:, fb * 3:fb * 3 + 1], in1=tt[:],
                op0=ALU.mult, op1=ALU.add), r=[("usb", q), ("t1", q)], w=[("t1", q)])
            g = gg[q]
            P.act(lambda e, g=g, tt=tt: e.activation(out=g[:], in_=tt[:], func=AF.Gelu),
                  r=[("t1", q)], w=[("gg", q)])
            P.dve(lambda e, g=g, q=q, fb=fb: e.tensor_tensor(out=hT[:, fb, :], in0=g[:], in1=gsb[q][:], op=ALU.mult),
                  r=[("gg", q), ("gsb", q)], w=[("hT", fb)])
        if j + 1 < ntiles:
            load_xres(j + 1)
        xr = xres[s]
        for db in range(KC):
            b = db % 2
            dsl = slice(db * 128, (db + 1) * 128)
            for fb in range(FB):
                P.pe(_mm(o_ps[b][:], wdn[:, fb, dsl], hT[:, fb, :], fb == 0, fb == FB - 1),
                     r=[("wdn", fb // 2), ("hT", fb)], w=[("o_ps", b)])
            P.dve(lambda e, xr=xr, db=db, b=b: e.scalar_tensor_tensor(
                out=xr[:, db, :], in0=xr[:, db, :], scalar=float(ALPHA), in1=o_ps[b][:],
                op0=ALU.mult, op1=ALU.add), r=[("xres", s, db), ("o_ps", b)], w=[("xres", s, db)])
        emit_ln(P, xr, ("xres", s), KC, TT, ones, sq, mean_ps, msq_ps, mean_sb, var_sb, rstd_sb, yt,
                lng, lnb, xr, ("xres", s), LN_EPS)
        P.dma("sp", lambda e, j=j, xr=xr: e.dma_start(out=out_v[:, :, j * TT:(j + 1) * TT], in_=xr[:]),
              r=[("xres", s)], w=[ok + (j,)])


def emit_ln(P, xr, xkey, nch, TT, ones, sq, mean_ps, msq_ps, mean_sb, var_sb, rstd_sb, yt, lng, lnb, osb, okey,
            eps, silu=False, lkeys=("lng", "lnb", "ones")):
    for c in range(nch):
        P.pe(_mm(mean_ps[:], ones[:], xr[:, c, :], c == 0, c == nch - 1),
             r=[lkeys[2], xkey + (c,)], w=["mean_ps"])
    for c in range(nch):
        b = c % 2
        P.act(lambda e, c=c, b=b: e.activation(out=sq[b][:], in_=xr[:, c, :], func=AF.Square),
              r=[xkey + (c,)], w=[("sq", b)])
        P.pe(_mm(msq_ps[:], ones[:], sq[b][:], c == 0, c == nch - 1),
             r=[lkeys[2], ("sq", b)], w=["msq_ps"])
    P.act(lambda e: e.activation(out=mean_sb[:], in_=mean_ps[:], func=AF.Identity), r=["mean_ps"], w=["mean_sb"])
    P.dve(lambda e: e.tensor_tensor(out=var_sb[:], in0=mean_sb[:], in1=mean_sb[:], op=ALU.mult),
          r=["mean_sb"], w=["var_sb"])
    P.dve(lambda e: e.tensor_tensor(out=var_sb[:], in0=msq_ps[:], in1=var_sb[:], op=ALU.subtract),
          r=["msq_ps", "var_sb"], w=["var_sb"])
    P.dve(lambda e: e.tensor_scalar(out=var_sb[:], in0=var_sb[:], scalar1=0.0, scalar2=float(eps),
                                    op0=ALU.max, op1=ALU.add), r=["var_sb"], w=["var_sb"])
    P.act(lambda e: e.activation(out=var_sb[:], in_=var_sb[:], func=AF.Ln), r=["var_sb"], w=["var_sb"])
    P.act(lambda e: e.activation(out=rstd_sb[:], in_=var_sb[:], func=AF.Exp, scale=-0.5), r=["var_sb"], w=["rstd_sb"])
    for c in range(nch):
        b = c % 2
        y = yt[b]
        P.dve(lambda e, y=y, c=c: e.tensor_tensor(out=y[:], in0=xr[:, c, :], in1=mean_sb[:], op=ALU.subtract),
              r=[xkey + (c,), "mean_sb"], w=[("yt", b)])
        P.dve(lambda e, y=y: e.tensor_tensor(out=y[:], in0=y[:], in1=rstd_sb[:], op=ALU.mult),
              r=[("yt", b), "rstd_sb"], w=[("yt", b)])
        P.act(lambda e, y=y, c=c: e.activation(out=osb[:, c, :], in_=y[:], func=(AF.Silu if silu else AF.Identity),
                                               scale=lng[:, c:c + 1], bias=lnb[:, c:c + 1]),
              r=[("yt", b), lkeys[0], lkeys[1]], w=[okey + (c,)])


def vec_layout(v, nb):
    return np.ascontiguousarray(np.asarray(v, np.float32).reshape(nb, 128).T)


def build_ffn_nc(NT, TT):
    nc = bass.Bass("TRN2", target_bir_lowering=False)
    io = {}
    io["xin"] = nc.dram_tensor("xin", [D, 2 + NT], F32, kind="ExternalInput").ap()
    io["w_up"] = nc.dram_tensor("w_up", [D, DFF], F32, kind="ExternalInput").ap()
    io["w_gate"] = nc.dram_tensor("w_gate", [D, DFF], F32, kind="ExternalInput").ap()
    io["w_down"] = nc.dram_tensor("w_down", [DFF, D], F32, kind="ExternalInput").ap()
    io["cw"] = nc.dram_tensor("cw", [128, FB * 3], F32, kind="ExternalInput").ap()
    io["cb"] = nc.dram_tensor("cb", [128, FB], F32, kind="ExternalInput").ap()
    io["lng"] = nc.dram_tensor("lng", [128, KC], F32, kind="ExternalInput").ap()
    io["lnb"] = nc.dram_tensor("lnb", [128, KC], F32, kind="ExternalInput").ap()
    io["out"] = nc.dram_tensor("out", [D, NT], F32, kind="ExternalOutput").ap()
    P = Prog(nc)
    P.begin_phase()
    build_ffn(nc, P, NT, TT, io)
    P.end_phase()
    P.finish()
    return nc, P


def ffn_inputs(xin_T, w_up, w_gate, conv_w, conv_b, w_down, g, b):
    cwl = np.stack([vec_layout(conv_w[t], FB) for t in range(3)], axis=-1).reshape(128, FB * 3)
    return dict(xin=np.ascontiguousarray(xin_T, dtype=np.float32),
                w_up=np.ascontiguousarray(w_up), w_gate=np.ascontiguousarray(w_gate),
                w_down=np.ascontiguousarray(w_down), cw=np.ascontiguousarray(cwl),
                cb=vec_layout(conv_b, FB), lng=vec_layout(g, KC), lnb=vec_layout(b, KC))


STAGE = 9
CH = 64
HAL_E = 30


def build_even(nc, P, NT, TT, io, state_only=False):
    HALO = HAL_E
    ntiles = NT // TT
    xk = io.get("xin_key", ("xin_ext",))
    ok = io.get("out_key", ("outdram",))
    full = not state_only
    nblk = TT // 128
    xin_v = io["xin"].rearrange("(kc p) t -> p kc t", p=128)
    out_v = io["out"].rearrange("(kc p) t -> p kc t", p=128)
    win = P.sb([128, KC, 3072], BF16, "win")
    wout = P.sb([128, KC, D], BF16, "wout")
    bin_ = P.sb([128, 24], F32, "bin")
    nbin = P.sb([128, 24], F32, "nbin")
    convw = P.sb([128, 4 * 31], F32, "convw")
    convb = P.sb([128, 4], F32, "convb")
    clng = P.sb([128, 4], F32, "clng")
    clnb = P.sb([128, 4], F32, "clnb")
    lb0 = P.sb([128, 4], F32, "lb0")
    lb1 = P.sb([128, 4], F32, "lb1")
    lbsel = P.sb([128, 1], F32, "lbsel")
    lb = P.sb([128, 4], F32, "lb")
    oml = P.sb([128, 4], F32, "oml")
    ong = P.sb([128, 4], F32, "ong")
    hsc = P.sb([128, 1], F32, "hsc")
    lng = P.sb([128, KC], F32, "lng")
    lnb = P.sb([128, KC], F32, "lnb")
    bivbc = P.sb([128, 512], F32, "bivbc")
    ones512 = P.sb([128, 128], F32, "ones512")
    ones128 = P.sb([128, 128], F32, "ones128")
    ones1024 = P.sb([128, 128], F32, "ones1024")
    ident = P.sb([128, 128], BF16, "ident")
    mask = P.sb([128, 128], F32, "mask")
    rmask = P.sb([128, TT], F32, "rmask")
    S = P.sb([128, 4, 128], F32, "S")
    Sbf = P.sb([128, 4, 128], BF16, "Sbf")
    xbf = [P.sb([128, KC, TT], BF16, f"xbf{i}") for i in range(2)]
    xhalo = P.sb([128, KC, HALO], BF16, "xhalo")
    xres = [P.sb([128, KC, TT], F32, f"xres{i}") for i in range(2)]
    glu = [P.sb([128, 4, HALO + TT], BF16, f"glu{i}") for i in range(2)]
    convd = P.sb([128, 4 * 31, 128], BF16, "convd") if not state_only else None
    sg = [P.sb([128, TT], F32, f"sg{i}") for i in range(2)]
    cacc = P.sb([128, 4, TT], F32, "cacc")
    cat = P.sb([128, KC, TT], BF16, "cat")
    qs = P.sb([128, TT], F32, "qs")
    sgp = P.sb([128, TT], F32, "sgp")
    sgn = P.sb([128, TT], F32, "sgn")
    logf = P.sb([128, TT], F32, "logf")
    cum = P.sb([128, TT], F32, "cum")
    eq = P.sb([128, TT], F32, "eq")
    en = P.sb([128, TT], F32, "en")
    gC = P.sb([128, 4, TT // CH], F32, "gC")
    qt = P.sb([128, 4, TT], BF16, "qt")
    kt = P.sb([128, 4, TT], F32, "kt")
    ktb = P.sb([128, 4, TT], BF16, "ktb")
    kh = P.sb([128, 4, TT], BF16, "kh")
    khT = P.sb([128, 4, 128], BF16, "khT")
    vtm = P.sb([128, 512], BF16, "vtm")
    pT = P.sb([128, 4, 128], BF16, "pT")
    osb = P.sb([128, 4, TT], F32, "osb")
    sqo = P.sb([128, TT], F32, "sqo")
    gsl = P.sb([128, TT], F32, "gsl")
    sq = [P.sb([128, TT], F32, f"sq{i}") for i in range(2)]
    mean_sb = P.sb([128, TT], F32, "mean_sb")
    var_sb = P.sb([128, TT], F32, "var_sb")
    rstd_sb = P.sb([128, TT], F32, "rstd_sb")
    yt = [P.sb([128, TT], F32, f"yt{i}") for i in range(2)]

    zps = [P.ps([128, 2, 256], F32, f"zps{i}", keys=[("zps", 2 * i), ("zps", 2 * i + 1)]) for i in range(2)]
    vtm_ps = P.ps([128, 512], F32, "vtm_ps")
    sc_ps = P.ps([128, 4, 128], F32, "sc_ps")
    o_ps = P.ps([128, 4, 128], F32, "o_ps")
    st_ps = P.ps([128, 4, 128], F32, "st_ps")
    tr_ps = P.ps([128, 4, 128], BF16, "tr_ps")
    stat_ps = P.ps([128, 2, 256], F32, "stat_ps", keys=["mean_ps", "msq_ps"])
    mean_ps = stat_ps[:, 0, 0:TT]
    msq_ps = stat_ps[:, 1, 0:TT]

    for nm, t in (("bin", bin_), ("convw", convw), ("convb", convb), ("clng", clng), ("clnb", clnb),
                  ("lb0", lb0), ("lb1", lb1), ("lbsel", lbsel), ("ong", ong), ("hsc", hsc), ("lng", lng),
                  ("lnb", lnb), ("bivbc", bivbc), ("mask", mask), ("rmask", rmask), ("S", S)):
        P.dma("sp", lambda e, t=t, nm=nm: e.dma_start(out=t[:], in_=io[nm]),
              r=([io.get("S_key", ("S_ext",))] if nm == "S" else []), w=[nm])
    id32 = P.sb([128, 128], F32, "id32")
    P.dma("sp", lambda e: e.dma_start(out=id32[:], in_=io["ident"]), w=["id32"])
    P.pool(lambda e: e.tensor_copy(out=ident[:], in_=id32[:]), r=["id32"], w=["ident"])
    if not state_only:
        for q_ in range(4 * 31):
            P.op(("pool", "dve")[q_ % 2], lambda e, q_=q_: e.tensor_scalar(
                out=convd[:, q_, :], in0=id32[:], scalar1=convw[:, q_:q_ + 1], scalar2=None, op0=ALU.mult),
                ["id32", "convw"], [("convd", q_)])
    P.pool(lambda e: e.memset(ones512[:], 1.0 / 512), w=["ones512"])
    P.pool(lambda e: e.memset(ones128[:], 1.0 / 128), w=["ones128"])
    P.pool(lambda e: e.memset(ones1024[:], 1.0 / D), w=["ones1024"])
    xh32 = P.sb([128, KC, HALO], F32, "xh32")
    P.dma("sp", lambda e: e.dma_start(out=xh32[:], in_=xin_v[:, :, 0:HALO]), r=[xk], w=["xh32"])
    P.pool(lambda e: e.tensor_copy(out=xhalo[:], in_=xh32[:]), r=["xh32"], w=["xhalo"])

    def load_xres(j):
        s = j % 2
        P.dma("sp", lambda e: e.dma_start(out=xres[s][:], in_=xin_v[:, :, HALO + j * TT:HALO + (j + 1) * TT]),
              r=[xk], w=[("xres", s)])
        P.pool(lambda e: e.tensor_copy(out=xbf[s][:], in_=xres[s][:]), r=[("xres", s)], w=[("xbf", s)])

    def load_x(j):
        pass

    load_xres(0)
    stg = Stager(P, 3072)
    winv = io["w_in"].rearrange("(kc p) f -> p kc f", p=128)
    woutv = io["w_out"].rearrange("(kc p) f -> p kc f", p=128)
    for kc in range(KC):
        stg.load(win[:, kc, :], winv[:, kc, :], 3072, ("win", kc))
    for kc in range(0, KC, 2):
        stg.load(wout[:, kc:kc + 2, :], woutv[:, kc:kc + 2, :], 2 * D, ("wout", kc // 2), a=2)
    P.dve(lambda e: e.tensor_scalar(out=nbin[:], in0=bin_[:], scalar1=-1.0, scalar2=None, op0=ALU.mult),
          r=["bin"], w=["nbin"])
    P.dve(lambda e: e.tensor_tensor(out=lb[:], in0=lb1[:], in1=lb0[:], op=ALU.subtract), r=["lb0", "lb1"], w=["lb"])
    P.act(lambda e: e.activation(out=lb[:], in_=lb[:], func=AF.Sigmoid), r=["lb"], w=["lb"])
    P.dve(lambda e: e.tensor_scalar(out=lb[:], in0=lb[:], scalar1=lbsel[:, 0:1], scalar2=None, op0=ALU.mult),
          r=["lb", "lbsel"], w=["lb"])
    P.dve(lambda e: e.tensor_scalar(out=oml[:], in0=lb[:], scalar1=-1.0, scalar2=1.0, op0=ALU.mult, op1=ALU.add),
          r=["lb"], w=["oml"])
    P.act(lambda e: e.activation(out=Sbf[:], in_=S[:], func=AF.Identity), r=["S"], w=["Sbf"])

    zcnt = [0]

    def proj(blk, rhs, n, rkeys):
        b = zcnt[0] % 4
        zcnt[0] += 1
        dst = zps[b // 2][:, b % 2, 0:n]
        for kc in range(KC):
            P.pe(_mm(dst, win[:, kc, blk * 128:(blk + 1) * 128], rhs[:, kc, :], kc == 0, kc == KC - 1),
                 r=[("win", kc)] + rkeys, w=[("zps", b)])
        return dst, ("zps", b)

    def glu_block(c, rhs, n, rkeys, dst_glu, dkey, col0, scale_ap=None):
        av, avk = proj(c, rhs, n, rkeys)
        ag, agk = proj(4 + c, rhs, n, rkeys)
        sgt = sg[c % 2]
        P.act(lambda e: e.activation(out=sgt[:, 0:n], in_=ag, func=AF.Sigmoid, bias=bin_[:, 4 + c:5 + c]),
              r=[agk, "bin"], w=[("sg", c % 2)])
        P.dve(lambda e: e.scalar_tensor_tensor(out=dst_glu[:, c, col0:col0 + n], in0=av, scalar=bin_[:, c:c + 1],
                                               in1=sgt[:, 0:n], op0=ALU.add, op1=ALU.mult),
              r=[avk, ("sg", c % 2), "bin"], w=[dkey + (c,)])
        if scale_ap is not None:
            P.dve(lambda e: e.tensor_scalar(out=dst_glu[:, c, col0:col0 + n], in0=dst_glu[:, c, col0:col0 + n],
                                            scalar1=scale_ap, scalar2=None, op0=ALU.mult),
                  r=[dkey + (c,), "hsc"], w=[dkey + (c,)])

    for c in (range(4) if full else ()):
        glu_block(c, xhalo, HALO, ["xhalo"], glu[1], ("glu", 1), TT, scale_ap=hsc[:, 0:1])

    for j in range(ntiles):
        s = j % 2
        if j + 1 < ntiles:
            load_x(j + 1)
        xb = xbf[s]
        G = glu[s]
        Gp = glu[1 - s]
        for c in (range(4) if full else ()):
            P.pool(lambda e, c=c, G=G, Gp=Gp: e.tensor_copy(out=G[:, c, 0:HALO], in_=Gp[:, c, TT:TT + HALO]),
                   r=[("glu", 1 - s, c)], w=[("glu", s, c)])
            glu_block(c, xb, TT, [("xbf", s)], G, ("glu", s), HALO)
            bq = zcnt[0] % 4
            zcnt[0] += 1
            cdst = zps[bq // 2][:, bq % 2, 0:TT]
            for tap in range(31):
                P.pe(_mm(cdst, convd[:, c * 31 + tap, :], G[:, c, tap:tap + TT], tap == 0, tap == 30),
                     r=[("convd", c * 31 + tap), ("glu", s, c)], w=[("zps", bq)])
            P.act(lambda e, c=c, cdst=cdst: e.activation(out=cacc[:, c, :], in_=cdst, func=AF.Identity,
                                                         bias=convb[:, c:c + 1]),
                  r=[("zps", bq), "convb"], w=[("cacc", c)])
        if full:
          emit_ln(P, cacc, ("cacc",), 4, TT, ones512, sq, mean_ps, msq_ps, mean_sb, var_sb, rstd_sb, yt,
                  clng, clnb, cat, ("cat",), LN_EPS, silu=True, lkeys=("clng", "clnb", "ones512"))
        for h in range(4):
            if full:
                qp, qk = proj(8 + h, xb, TT, [("xbf", s)])
                P.act(lambda e, qp=qp, h=h: e.activation(out=qs[:], in_=qp, func=AF.Silu, bias=bin_[:, 8 + h:9 + h]),
                      r=[qk, "bin"], w=["qs"])
            fp_, fk = proj(12 + h, xb, TT, [("xbf", s)])
            P.act(lambda e, fp_=fp_, h=h: e.activation(out=sgp[:], in_=fp_, func=AF.Sigmoid,
                                                       bias=bin_[:, 12 + h:13 + h]),
                  r=[fk, "bin"], w=["sgp"])
            P.act(lambda e, fp_=fp_, h=h: e.activation(out=sgn[:], in_=fp_, func=AF.Sigmoid, scale=-1.0,
                                                       bias=nbin[:, 12 + h:13 + h]),
                  r=[fk, "nbin"], w=["sgn"])
            P.dve(lambda e, h=h: e.tensor_scalar(out=logf[:], in0=sgp[:], scalar1=oml[:, h:h + 1],
                                                 scalar2=lb[:, h:h + 1], op0=ALU.mult, op1=ALU.add),
                  r=["sgp", "oml", "lb"], w=["logf"])
            P.act(lambda e: e.activation(out=logf[:], in_=logf[:], func=AF.Ln), r=["logf"], w=["logf"])
            P.dve(lambda e: e.tensor_tensor_scan(out=cum[:], data0=rmask[:], data1=logf[:], initial=0.0,
                                                 op0=ALU.mult, op1=ALU.add),
                  r=["rmask", "logf"], w=["cum"])
            if full:
                P.act(lambda e: e.activation(out=eq[:], in_=cum[:], func=AF.Exp), r=["cum"], w=["eq"])
            P.act(lambda e: e.activation(out=en[:], in_=cum[:], func=AF.Exp, scale=-1.0), r=["cum"], w=["en"])
            P.act(lambda e, h=h: e.activation(
                out=gC[:, h, :], in_=cum[:].rearrange("p (c t) -> p c t", t=CH)[:, :, CH - 1], func=AF.Exp),
                r=["cum"], w=[("gC", h)])
            if full:
                P.dve(lambda e, h=h: e.tensor_tensor(out=qt[:, h, :], in0=qs[:], in1=eq[:], op=ALU.mult),
                      r=["qs", "eq"], w=[("qt", h)])
            P.dve(lambda e, h=h: e.scalar_tensor_tensor(out=kt[:, h, :], in0=sgn[:], scalar=oml[:, h:h + 1],
                                                        in1=en[:], op0=ALU.mult, op1=ALU.mult),
                  r=["sgn", "en", "oml"], w=[("kt", h)])
            if full:
                P.act(lambda e, h=h: e.activation(out=ktb[:, h, :], in_=kt[:, h, :], func=AF.Identity),
                      r=[("kt", h)], w=[("ktb", h)])
            P.dve(lambda e, h=h: e.tensor_tensor(
                out=kh[:, h, :].rearrange("p (c t) -> p c t", t=CH),
                in0=kt[:, h, :].rearrange("p (c t) -> p c t", t=CH),
                in1=gC[:, h, :].unsqueeze(2).to_broadcast([128, TT // CH, CH]), op=ALU.mult),
                r=[("kt", h), ("gC", h)], w=[("kh", h)])
        for bi in range(nblk):
            tsl = slice(bi * 128, (bi + 1) * 128)
            for kc in range(KC):
                P.pe(_mm(vtm_ps[:], xb[:, kc, tsl], win[:, kc, 2048:2560], kc == 0, kc == KC - 1),
                     r=[("xbf", s), ("win", kc)], w=["vtm_ps"])
            P.dve(lambda e: e.tensor_tensor(out=vtm[:], in0=vtm_ps[:], in1=bivbc[:], op=ALU.add),
                  r=["vtm_ps", "bivbc"], w=["vtm"])
            for h in range(4):
                if full:
                    P.pe(_mm(sc_ps[:, h, :], ktb[:, h, tsl], qt[:, h, tsl], True, True),
                         r=[("ktb", h), ("qt", h)], w=[("sc_ps", h)])
                P.pe(lambda e, h=h, tsl=tsl: e.transpose(tr_ps[:, h, :], kh[:, h, tsl], ident[:]),
                     r=[("kh", h), "ident"], w=[("tr_ps", h)])
            if full:
                P.dve(lambda e: e.tensor_tensor(out=pT[:], in0=sc_ps[:],
                                                in1=mask[:].unsqueeze(1).to_broadcast([128, 4, 128]), op=ALU.mult),
                      r=["sc_ps", "mask"], w=["pT"])
            P.act(lambda e: e.activation(out=khT[:], in_=tr_ps[:], func=AF.Identity), r=["tr_ps"], w=["khT"])
            for h in (range(4) if full else ()):
                vh = vtm[:, h * 128:(h + 1) * 128]
                P.pe(_mm(o_ps[:, h, :], vh, pT[:, h, :], h == 0, False),
                     r=["vtm", "pT"], w=[("o_ps",)])
            for ci in range(2):
                csl = slice(ci * 64, (ci + 1) * 64)
                gcol = bi * 2 + ci
                for h in (range(4) if full else ()):
                    P.pe(_mm(o_ps[:, h, csl], Sbf[:, h, :], qt[:, h, bi * 128 + ci * 64:bi * 128 + (ci + 1) * 64],
                             False, ci == 1 and h == 3),
                         r=[("Sbf", h), ("qt", h)], w=[("o_ps",)])
                for h in range(4):
                    P.pe(_mm(st_ps[:, h, :], khT[csl, h, :], vtm[csl, h * 128:(h + 1) * 128], True, True),
                         r=["khT", "vtm"], w=[("st_ps", h)])
                for h in range(4):
                    P.dve(lambda e, h=h, gcol=gcol: e.scalar_tensor_tensor(
                        out=S[:, h, :], in0=S[:, h, :], scalar=gC[:, h, gcol:gcol + 1], in1=st_ps[:, h, :],
                        op0=ALU.mult, op1=ALU.add), r=[("S", h), ("gC", h), ("st_ps", h)], w=[("S", h)])
                    if full:
                        P.act(lambda e, h=h: e.activation(out=Sbf[:, h, :], in_=S[:, h, :], func=AF.Identity),
                              r=[("S", h)], w=[("Sbf", h)])
            if full:
                P.act(lambda e, tsl=tsl: e.activation(out=osb[:, :, tsl], in_=o_ps[:], func=AF.Identity),
                      r=["o_ps"], w=[("osb", bi)])
        for h in (range(4) if full else ()):
            if True:
                P.act(lambda e, h=h: e.activation(out=sqo[:], in_=osb[:, h, :], func=AF.Square), r=["osb"], w=["sqo"])
                P.pe(_mm(mean_ps, ones128[:], sqo[:], True, True), r=["ones128", "sqo"], w=["mean_ps"])
                P.dve(lambda e: e.tensor_scalar(out=var_sb[:], in0=mean_ps, scalar1=0.0, scalar2=1e-6,
                                                op0=ALU.max, op1=ALU.add), r=["mean_ps"], w=["var_sb"])
            if True:
                P.act(lambda e: e.activation(out=var_sb[:], in_=var_sb[:], func=AF.Ln), r=["var_sb"], w=["var_sb"])
                P.act(lambda e: e.activation(out=rstd_sb[:], in_=var_sb[:], func=AF.Exp, scale=-0.5), r=["var_sb"], w=["rstd_sb"])
            gp, gk = proj(20 + h, xb, TT, [("xbf", s)])
            P.act(lambda e, gp=gp, h=h: e.activation(out=gsl[:], in_=gp, func=AF.Silu, bias=bin_[:, 20 + h:21 + h]),
                  r=[gk, "bin"], w=["gsl"])
            if True:
                P.dve(lambda e, h=h: e.tensor_tensor(out=sqo[:], in0=osb[:, h, :], in1=rstd_sb[:], op=ALU.mult),
                      r=["osb", "rstd_sb"], w=["sqo"])
                P.dve(lambda e, h=h: e.scalar_tensor_tensor(out=cat[:, 4 + h, :], in0=sqo[:], scalar=ong[:, h:h + 1],
                                                            in1=gsl[:], op0=ALU.mult, op1=ALU.mult),
                      r=["sqo", "ong", "gsl"], w=[("cat", 4 + h)])
        if j + 1 < ntiles:
            load_xres(j + 1)
        xr = xres[s]
        if not full:
            continue
        for db in range(KC):
            b = zcnt[0] % 4
            zcnt[0] += 1
            dst = zps[b // 2][:, b % 2, 0:TT]
            for c in range(KC):
                P.pe(_mm(dst, wout[:, c, db * 128:(db + 1) * 128], cat[:, c, :], c == 0, c == KC - 1),
                     r=[("wout", c // 2), ("cat", c)], w=[("zps", b)])
            P.dve(lambda e, xr=xr, db=db, dst=dst: e.scalar_tensor_tensor(
                out=xr[:, db, :], in0=xr[:, db, :], scalar=float(ALPHA), in1=dst,
                op0=ALU.mult, op1=ALU.add), r=[("xres", s, db), ("zps", b)], w=[("xres", s, db)])
        emit_ln(P, xr, ("xres", s), KC, TT, ones1024, sq, mean_ps, msq_ps, mean_sb, var_sb, rstd_sb, yt,
                lng, lnb, xr, ("xres", s), LN_EPS, lkeys=("lng", "lnb", "ones1024"))
        P.dma("sp", lambda e, j=j, xr=xr: e.dma_start(out=out_v[:, :, j * TT:(j + 1) * TT], in_=xr[:]),
              r=[("xres", s)], w=[ok + (j,)])
    P.dma("sp", lambda e: e.dma_start(out=io["S_out"], in_=S[:]), r=["S"], w=[io.get("Sout_key", ("S_out",))])


def build_even_nc(NT, TT):
    nc = bass.Bass("TRN2", target_bir_lowering=False)
    io = {}

    def din(name, shape, dt=F32):
        io[name] = nc.dram_tensor(name, list(shape), dt, kind="ExternalInput").ap()

    din("xin", [D, HAL_E + NT])
    din("w_in", [D, 3072])
    din("w_out", [D, D])
    din("bin", [128, 24])
    din("convw", [128, 4 * 31])
    din("convb", [128, 4])
    din("clng", [128, 4])
    din("clnb", [128, 4])
    din("lb0", [128, 4])
    din("lb1", [128, 4])
    din("lbsel", [128, 1])
    din("ong", [128, 4])
    din("hsc", [128, 1])
    din("lng", [128, KC])
    din("lnb", [128, KC])
    din("bivbc", [128, 512])
    din("mask", [128, 128])
    din("rmask", [128, TT])
    din("ident", [128, 128])
    din("S", [128, 4, 128])
    io["out"] = nc.dram_tensor("out", [D, NT], F32, kind="ExternalOutput").ap()
    io["S_out"] = nc.dram_tensor("S_out", [128, 4, 128], F32, kind="ExternalOutput").ap()
    P = Prog(nc)
    P.begin_phase()
    build_even(nc, P, NT, TT, io)
    P.end_phase()
    P.finish()
    return nc, P


def even_consts(TT):
    i = np.arange(128)
    mask = ((i[:, None] // CH == i[None, :] // CH) & (i[:, None] <= i[None, :])).astype(np.float32)
    rmask = np.ones((128, TT), np.float32)
    rmask[:, ::CH] = 0.0
    return dict(mask=mask, rmask=rmask, ident=np.eye(128, dtype=np.float32))


def even_inputs(xin_T, w_in, b_in, conv_w, conv_b, cln_g, cln_b, lb_logits, j, onorm_g, w_out, g, b, S, first, TT):
    d = even_consts(TT)
    cwl = np.stack([vec_layout(conv_w[t], 4) for t in range(31)], axis=-1).reshape(128, 4 * 31)
    d.update(xin=np.ascontiguousarray(xin_T, dtype=np.float32), w_in=np.ascontiguousarray(w_in),
             w_out=np.ascontiguousarray(w_out), bin=vec_layout(b_in, 24), convw=np.ascontiguousarray(cwl),
             convb=vec_layout(conv_b, 4), clng=vec_layout(cln_g, 4), clnb=vec_layout(cln_b, 4),
             lb0=vec_layout(lb_logits[0], 4), lb1=vec_layout(lb_logits[1], 4),
             lbsel=np.full((128, 1), 1.0 if j == 1 else 0.0, np.float32),
             ong=vec_layout(onorm_g, 4), hsc=np.full((128, 1), 0.0 if first else 1.0, np.float32),
             lng=vec_layout(g, KC), lnb=vec_layout(b, KC),
             bivbc=np.ascontiguousarray(np.broadcast_to(np.asarray(b_in, np.float32)[2048:2560], (128, 512))),
             S=np.ascontiguousarray(S, dtype=np.float32))
    return d


C = 64
H = 16
ORDER_OVERRIDE = None
C0 = float(np.exp(-0.5))


def build_odd(nc, P, NT, io, state_only=False):
    TT = C
    ntiles = NT // TT
    xk = io.get("xin_key", ("xin_ext",))
    ok = io.get("out_key", ("outdram",))
    full = not state_only
    xin_v = io["xin"].rearrange("(kc p) t -> p kc t", p=128)
    out_v = io["out"].rearrange("(kc p) t -> p kc t", p=128)
    sb, ps = P.sb, P.ps
    wr, wk, wv = (sb([128, KC, D], BF16, n) for n in ("wr", "wk", "wv"))
    wo = sb([64, H, D], BF16, "wo")
    w1 = sb([128, KC, 64], BF16, "w1"); a1 = sb([128, KC, 64], BF16, "a1"); g1 = sb([128, KC, 160], BF16, "g1")
    w2 = sb([64, D], BF16, "w2"); a2 = sb([64, D], BF16, "a2")
    g2a = sb([128, D], BF16, "g2a"); g2b = sb([32, D], BF16, "g2b")
    mu = sb([128, 6, KC], F32, "mu")
    hv = {n: sb([64, H], F32, n) for n in ("w0h", "a0h", "kkh", "kah", "rkh", "gng", "gnb")}
    omka = sb([64, H], F32, "omka")
    lng = sb([128, KC], F32, "lng"); lnb = sb([128, KC], F32, "lnb")
    mUs = sb([64, 64], F32, "mUs"); mUi = sb([64, 64], F32, "mUi"); mLs = sb([64, 64], F32, "mLs")
    eye = sb([64, 64], F32, "eye"); rmask = sb([64, H * C], F32, "rmask")
    id32 = sb([64, 64], F32, "id32"); ident = sb([64, 64], BF16, "ident")
    ones64 = sb([64, 64], F32, "ones64"); ones1 = sb([64, 64], F32, "ones1"); ones1024 = sb([128, 128], F32, "ones1024")
    M = sb([64, H, 64], F32, "M"); Mb = sb([64, H, 64], BF16, "Mb")
    XT = [sb([128, KC, TT + 1], F32, f"XT{i}") for i in range(2)]
    xx = sb([128, KC, TT], F32, "xx"); xt_ = sb([128, KC, TT], F32, "xt_")
    xm = [sb([128, KC, TT], BF16, f"xm{i}") for i in range(2)]
    F = {n: sb([64, H, C], F32, n) for n in ("rF", "kF", "vF", "aF", "gF", "sg", "cum", "kap", "t0", "t1")}
    F["sg"] = F["sg"]; F["cum"] = F["cum"]
    B = {n: sb([64, H, C], BF16, n) for n in ("KT", "BT", "KK", "RT", "BH", "KH", "Tt",
                                               "AkT", "PbT", "PkT", "zB", "BHT", "KHT")}
    for n in ("N0", "N1", "Nt0", "Nt1"):
        for g_ in range(2):
            B[f"{n}_{g_}"] = sb([64, 8, C], BF16, f"{n}_{g_}")
    B["nZ"] = B["BH"]; B["U"] = B["KH"]

    def hd(t, h):
        return t[:, h, :] if t.shape[1] == H else t[:, h % 8, :]

    def grp(t, g):
        return t[:, g * 8:(g + 1) * 8, :] if t.shape[1] == H else t[:]
    Vtm = sb([64, D], BF16, "Vtm")
    twB = sb([64, C], BF16, "twB"); taB = sb([64, C], BF16, "taB"); tgA = sb([128, C], BF16, "tgA"); tgB = sb([32, C], BF16, "tgB")
    gC = sb([64, H], F32, "gC")
    sq8 = sb([128, KC, TT], F32, "sq8")
    mean_sb = sb([128, TT], F32, "mean_sb"); var_sb = sb([128, TT], F32, "var_sb"); rstd_sb = sb([128, TT], F32, "rstd_sb")
    pb = [ps([128, 512], F32, f"pb{i}") for i in range(6)]
    trp = ps([64, 8, 64], BF16, "trp")
    stat = ps([128, 2, 256], F32, "stat", keys=["mean_ps", "msq_ps"])
    mean_ps = stat[:, 0, 0:TT]; msq_ps = stat[:, 1, 0:TT]

    names = {}
    for d_ in (F, B, hv):
        for n_, t_ in d_.items():
            names.setdefault(id(t_), n_)
    for n_, t_ in (("mUs", mUs), ("mUi", mUi), ("mLs", mLs), ("eye", eye), ("twB", twB), ("taB", taB), ("tgA", tgA), ("tgB", tgB)):
        names[id(t_)] = n_

    def kn(t):
        return names[id(t)]

    def bk(i):
        return pb[i][0:64, :].rearrange("p (h c) -> p h c", c=64)

    def ld(t, nm):
        P.dma("sp", lambda e: e.dma_start(out=t[:], in_=io[nm]), r=([io.get("M_key", ("M_ext",))] if nm == "M" else []), w=[nm])

    for nm, t in list(hv.items()) + [("mu", mu), ("lng", lng), ("lnb", lnb), ("mUs", mUs), ("mUi", mUi), ("mLs", mLs),
                                     ("eye", eye), ("rmask_o", rmask), ("id32_o", id32), ("M", M)]:
        ld(t, nm)
    P.pool(lambda e: e.tensor_copy(out=ident[:], in_=id32[:]), r=["id32_o"], w=["ident"])
    P.pool(lambda e: e.memset(ones64[:], 1.0 / 64), w=["ones64"])
    P.pool(lambda e: e.memset(ones1[:], 1.0), w=["ones1"])
    P.pool(lambda e: e.memset(ones1024[:], 1.0 / D), w=["ones1024"])
    P.dve(lambda e: e.tensor_scalar(out=omka[:], in0=hv["kah"][:], scalar1=-1.0, scalar2=1.0, op0=ALU.mult, op1=ALU.add),
          r=["kah"], w=["omka"])
    P.act(lambda e: e.activation(out=Mb[:], in_=M[:], func=AF.Identity), r=["M"], w=["Mb"])
    stg = Stager(P, 1024)
    for nm, t in (("w_r", wr), ("w_k", wk), ("w_v", wv)):
        v = io[nm].rearrange("(kc p) f -> p kc f", p=128)
        for kc in range(KC):
            stg.load(t[:, kc, :], v[:, kc, :], 1024, (nm, kc // 2))
    wov = io["w_o"].rearrange("(h p) d -> p h d", p=64)

    def load64(dst, src, n, key, rows=64):
        i = stg.n % 2
        stg.n += 1
        st = stg.bufs[i]
        P.dma("sp", lambda e: e.dma_start(out=st[0:rows, 0:n], in_=src), w=[("stage", i)])
        cast_op(P, ("pool", "act", "dve")[stg.n % 3], dst, st[0:rows, 0:n], [("stage", i)], [key])

    for h in range(H):
        load64(wo[:, h, :], wov[:, h, :], D, ("wo", h // 2))
    for nm, t, n in (("w1", w1, 64), ("a1", a1, 64)):
        v = io[nm].rearrange("(kc p) f -> p kc f", p=128)
        stg.load(t[:], v, KC * n, nm, a=KC)
    g1v = io["g1"].rearrange("(kc p) f -> p kc f", p=128)
    for q in range(2):
        stg.load(g1[:, q * 4:(q + 1) * 4, :], g1v[:, q * 4:(q + 1) * 4, :], 640, ("g1", q), a=4)
    load64(w2[:], io["w2"], D, "w2")
    load64(a2[:], io["a2"], D, "a2")
    stg.load(g2a[:], io["g2"][0:128, :], D, "g2a")
    load64(g2b[:], io["g2"][128:160, :], D, "g2b", rows=32)

    def bc(v):
        return v[:].unsqueeze(2).to_broadcast([64, H, C])

    def tt(eng, out, a, b, op, r, w):
        P.op(eng, lambda e: e.tensor_tensor(out=out, in0=a, in1=b, op=op), r, w)

    def headproj(w_t, wkey, xmi, dstF, post=None):
        for g in range(2):
            bank = bk(g)
            for hh in range(8):
                h = g * 8 + hh
                for kc in range(KC):
                    P.pe(_mm(bank[:, hh, :], w_t[:, kc, h * 64:(h + 1) * 64], xm[xmi][:, kc, :], kc == 0, kc == KC - 1),
                         r=[(wkey, kc // 2), ("xm", xmi)], w=[("pb", g)])
            P.act(lambda e, g=g, bank=bank: e.activation(out=dstF[:, g * 8:(g + 1) * 8, :], in_=bank, func=AF.Identity),
                  r=[("pb", g)], w=[(kn(dstF), g)])

    for i in range(6):
        P.bank_of[("pb", i)] = f"pb{i}"

    def load_x(j):
        s = j % 2
        P.dma("sp", lambda e: e.dma_start(out=XT[s][:], in_=xin_v[:, :, j * TT:j * TT + TT + 1]), r=[xk], w=[("XT", s)])

    xm4 = [xm[0], xm[1], P.sb([128, KC, TT], BF16, "xm2"), P.sb([128, KC, TT], BF16, "xm3")]
    xr_ = P.sb([128, KC, TT], F32, "xr_")

    def mixop(X, s, i, slot):
        P.dve(lambda e: e.tensor_tensor(out=xt_[:], in0=xx[:], in1=mu[:, i, :].unsqueeze(2).to_broadcast([128, KC, TT]),
                                        op=ALU.mult), r=["xx", "mu"], w=["xt_"])
        P.dve(lambda e: e.tensor_tensor(out=xm4[slot][:], in0=xt_[:], in1=X[:, :, 1:TT + 1], op=ALU.add),
              r=["xt_", ("XT", s)], w=[("xm", slot)])

    stg_tm = [stg.bufs[i][0:64, :].bitcast(BF16)[:, 0:D] for i in range(2)]

    def hproj_mm(w_t, wkey, slot, banks, dst_tm, tmk):
        for half in range(2):
            bi = banks[half]
            for kc in range(KC):
                P.pe(_mm(pb[bi][0:64, :], xm4[slot][:, kc, :], w_t[:, kc, half * 512:(half + 1) * 512], kc == 0, kc == KC - 1),
                     r=[("xm", slot), (wkey, kc // 2)], w=[("pb", bi)])
            P.act(lambda e, half=half, bi=bi: e.activation(out=dst_tm[:, half * 512:(half + 1) * 512], in_=pb[bi][0:64, :],
                                                           func=AF.Identity), r=[("pb", bi)], w=[tmk + (half,)])

    def hproj_tr(dst_tm, tmk, dstF):
        for g in range(2):
            for hh in range(8):
                h = g * 8 + hh
                P.pe(lambda e, hh=hh, h=h: e.transpose(trp[:, hh, :], dst_tm[:, h * 64:(h + 1) * 64], ident[:]),
                     r=[tmk + (g,), "ident"], w=["trp"])
            P.act(lambda e, g=g: e.activation(out=dstF[:, g * 8:(g + 1) * 8, :], in_=trp[:], func=AF.Identity),
                  r=["trp"], w=[(kn(dstF), g)])

    def early_segments(j):
        s = j % 2
        X = XT[s]

        def e1():
            tt("dve", xx[:], X[:, :, 0:TT], X[:, :, 1:TT + 1], ALU.subtract, [("XT", s)], ["xx"])
            if full:
                mixop(X, s, 0, 0)
            mixop(X, s, 2, 1)
            mixop(X, s, 3, 2)
            if full:
                hproj_mm(wr, "w_r", 0, (0, 1), stg_tm[1], ("stage", 1))
            hproj_mm(wk, "w_k", 1, (2, 3), stg_tm[0], ("stage", 0))

        def e2():
            if full:
                hproj_tr(stg_tm[1], ("stage", 1), F["rF"])
            hproj_mm(wv, "w_v", 2, (0, 1), Vtm, ("Vtm",))

        def e3():
            hproj_tr(stg_tm[0], ("stage", 0), F["kF"])
            hproj_tr(Vtm, ("Vtm",), F["vF"])

        def e4():
            for (i, l1, l1k, mid_, fn_) in ((1, w1, "w1", twB, AF.Tanh), (4, a1, "a1", taB, AF.Identity)):
                mixop(X, s, i, 3)
                for kc in range(KC):
                    P.pe(_mm(pb[3][0:64, 0:C], l1[:, kc, :], xm4[3][:, kc, :], kc == 0, kc == KC - 1),
                         r=[l1k, ("xm", 3)], w=[("pb", 3)])
                P.act(lambda e, mid_=mid_, fn_=fn_: e.activation(out=mid_[:], in_=pb[3][0:64, 0:C], func=fn_),
                      r=[("pb", 3)], w=[kn(mid_)])
            if full:
                mixop(X, s, 5, 3)
                for (dst, lo, n) in ((tgA, 0, 128), (tgB, 128, 32)):
                    for kc in range(KC):
                        P.pe(_mm(pb[3][0:n, 0:C], g1[:, kc, lo:lo + n], xm4[3][:, kc, :], kc == 0, kc == KC - 1),
                             r=["g1", ("xm", 3)], w=[("pb", 3)])
                    P.act(lambda e, dst=dst, n=n: e.activation(out=dst[:], in_=pb[3][0:n, 0:C], func=AF.Sigmoid),
                          r=[("pb", 3)], w=[kn(dst)])
        return [e1, e2, e3, e4]

    def lora2(l2, l2k, mid_, bias, outF, outfunc):
        for g in range(2):
            bank = bk(g)
            for hh in range(8):
                h = g * 8 + hh
                P.pe(_mm(bank[:, hh, :], l2[:, h * 64:(h + 1) * 64], mid_[:], True, True),
                     r=[l2k, kn(mid_)], w=[("pb", g)])
            tt("dve", outF[:, g * 8:(g + 1) * 8, :], bank, bias[:, g * 8:(g + 1) * 8].unsqueeze(2).to_broadcast([64, 8, C]),
               ALU.add, [("pb", g), kn(bias)], [(kn(outF), g)])
        P.act(lambda e: e.activation(out=outF[:], in_=outF[:], func=outfunc), r=[kn(outF)], w=[kn(outF)])

    def mid(j):
        rF, kF, vF, aF, sg, cum, kap, t0, t1 = (F[n] for n in ("rF", "kF", "vF", "aF", "sg", "cum", "kap", "t0", "t1"))
        t2 = sg
        tt("dve", kap[:], kF[:], bc(hv["kkh"]), ALU.mult, ["kF", "kkh"], ["kap"])
        lora2(w2, "w2", twB, hv["w0h"], F["sg"], AF.Sigmoid)
        P.act(lambda e: e.activation(out=t0[:], in_=kap[:], func=AF.Square), r=["kap"], w=["t0"])
        t0f = t0[:].rearrange("p h c -> p (h c)")
        for half in range(2):
            P.pe(_mm(pb[2][0:64, :], ones1[:], t0f[:, half * 512:(half + 1) * 512], True, True), r=["ones1", "t0"], w=[("pb", 2)])
            P.dve(lambda e, half=half: e.tensor_scalar(out=t1[:].rearrange("p h c -> p (h c)")[:, half * 512:(half + 1) * 512],
                                                       in0=pb[2][0:64, :], scalar1=1e-18, scalar2=None, op0=ALU.max),
                  r=[("pb", 2)], w=[("t1", half)])
        lora2(a2, "a2", taB, hv["a0h"], F["aF"], AF.Sigmoid)
        P.act(lambda e: e.activation(out=t1[:], in_=t1[:], func=AF.Ln), r=["t1"], w=["t1"])
        P.act(lambda e: e.activation(out=t1[:], in_=t1[:], func=AF.Exp, scale=-0.5), r=["t1"], w=["t1"])
        for g in (range(2) if full else ()):
            bank = bk(g)
            for hh in range(8):
                h = g * 8 + hh
                P.pe(_mm(bank[:, hh, :], g2a[:, h * 64:(h + 1) * 64], tgA[:], True, False), r=["g2a", kn(tgA)], w=[("pb", g)])
                P.pe(_mm(bank[:, hh, :], g2b[:, h * 64:(h + 1) * 64], tgB[:], False, True), r=["g2b", kn(tgB)], w=[("pb", g)])
            P.act(lambda e, g=g, bank=bank: e.activation(out=F["gF"][:, g * 8:(g + 1) * 8, :], in_=bank, func=AF.Identity),
                  r=[("pb", g)], w=[("gF", g)])
        tt("dve", kap[:], kap[:], t1[:], ALU.mult, ["kap", "t1"], ["kap"])
        tt("pool", t0[:], aF[:], bc(hv["kah"]), ALU.mult, ["aF", "kah"], ["t0"])
        tt("pool", t0[:], t0[:], bc(omka), ALU.add, ["t0", "omka"], ["t0"])
        tt("pool", kF[:], kF[:], t0[:], ALU.mult, ["kF", "t0"], ["kF"])
        P.dve(lambda e: e.tensor_tensor_scan(out=cum[:].rearrange("p h c -> p (h c)"), data0=rmask[:],
                                             data1=sg[:].rearrange("p h c -> p (h c)"), initial=0.0,
                                             op0=ALU.mult, op1=ALU.add), r=["rmask_o", "sg"], w=["cum"])
        tt("dve", t0[:], cum[:], sg[:], ALU.subtract, ["cum", "sg"], ["t0"])
        tt("dve", t2[:], kap[:], aF[:], ALU.mult, ["kap", "aF"], ["sg"])
        P.act(lambda e: e.activation(out=t0[:], in_=t0[:], func=AF.Exp, scale=-C0), r=["t0"], w=["t0"])
        tt("dve", B["KT"][:], kap[:], t0[:], ALU.mult, ["kap", "t0"], ["KT"])
        P.act(lambda e: e.activation(out=t0[:], in_=cum[:], func=AF.Exp, scale=-C0), r=["cum"], w=["t0"])
        if full:
            tt("dve", B["RT"][:], rF[:], t0[:], ALU.mult, ["rF", "t0"], ["RT"])
        P.act(lambda e: e.activation(out=gC[:], in_=cum[:, :, C - 1], func=AF.Exp, scale=-C0), r=["cum"], w=["gC"])
        P.act(lambda e: e.activation(out=t1[:], in_=cum[:], func=AF.Exp, scale=C0), r=["cum"], w=["t1"])
        tt("dve", t2[:], t2[:], t1[:], ALU.mult, ["sg", "t1"], ["sg"])
        tt("dve", t1[:], kF[:], t1[:], ALU.mult, ["kF", "t1"], ["t1"])
        P.act(lambda e: e.activation(out=B["BT"][:], in_=t2[:], func=AF.Identity), r=["sg"], w=["BT"])
        P.act(lambda e: e.activation(out=B["KK"][:], in_=t1[:], func=AF.Identity), r=["t1"], w=["KK"])
        tt("pool", B["BH"][:], t2[:], bc(gC), ALU.mult, ["sg", "gC"], ["BH"])
        tt("dve", B["KH"][:], t1[:], bc(gC), ALU.mult, ["t1", "gC"], ["KH"])
        if full:
            tt("dve", t0[:], rF[:], kF[:], ALU.mult, ["rF", "kF"], ["t0"])
            tt("dve", t0[:], t0[:], bc(hv["rkh"]), ALU.mult, ["t0", "rkh"], ["t0"])
            t0f_ = t0[:].rearrange("p h c -> p (h c)")
            for half in range(2):
                hs = slice(half * 512, (half + 1) * 512)
                P.pe(_mm(pb[2][0:64, :], ones1[:], t0f_[:, hs], True, True), r=["ones1", "t0"], w=[("pb", 2)])
                tt("dve", kap[:].rearrange("p h c -> p (h c)")[:, hs], pb[2][0:64, :],
                   vF[:].rearrange("p h c -> p (h c)")[:, hs], ALU.mult, [("pb", 2), "vF"], [("kap", half)])
        P.stage = f"mid{j}_tr"
        for src, dst in (("BH", "BHT"), ("KH", "KHT")):
            for g in range(2):
                for hh in range(8):
                    h = g * 8 + hh
                    P.pe(lambda e, hh=hh, h=h, src=src: e.transpose(trp[:, hh, :], B[src][:, h, :], ident[:]),
                         r=[src, "ident"], w=["trp"])
                P.act(lambda e, g=g, dst=dst: e.activation(out=B[dst][:, g * 8:(g + 1) * 8, :], in_=trp[:], func=AF.Identity),
                      r=["trp"], w=[(dst, g)])
        P.stage = f"mid{j}_scores"
        BS = ((0, 1, 2), (3, 4, 5))

        def score(lhs, rhs, dst, mask, neg, g, bank_i):
            bank = bk(bank_i)
            for hh in range(8):
                h = g * 8 + hh
                P.pe(_mm(bank[:, hh, :], B[lhs][:, h, :], B[rhs][:, h, :], True, True), r=[lhs, rhs], w=[("pb", bank_i)])
            P.dve(lambda e: e.scalar_tensor_tensor(out=grp(B[dst], g), in0=bank, scalar=(-1.0 if neg else 1.0),
                                                   in1=mask[:].unsqueeze(1).to_broadcast([64, 8, 64]),
                                                   op0=ALU.mult, op1=ALU.mult), r=[("pb", bank_i), kn(mask)], w=[(dst, g)])

        def NB(name, g):
            return B[f"{name}_{g}"]

        for g in range(2):
            score("KT", "BT", f"N0_{g}", mLs, True, g, BS[g][0])
            score("BT", "KT", f"Nt0_{g}", mUs, True, g, BS[g][1])
        for g in range(2):
            score("KK", "KT", "AkT", mUs, False, g, BS[g][2])
            if full:
                score("BT", "RT", "PbT", mUi, False, g, BS[g][0])
        for g in range(2):
            gs = slice(g * 8, (g + 1) * 8)
            if full:
                score("KK", "RT", "PkT", mUi, False, g, BS[g][1])
            tt("dve", B["Tt"][:, gs, :], NB("Nt0", g)[:], eye[:].unsqueeze(1).to_broadcast([64, 8, 64]), ALU.add,
               [(f"Nt0_{g}", g), "eye"], [("Tt", g)])
        cur = 0
        P.stage = f"mid{j}_dbl"
        for lv in range(5):
            for g in range(2):
                gs = slice(g * 8, (g + 1) * 8)
                Nc, Ntc = NB(f"N{cur}", g), NB(f"Nt{cur}", g)
                Nn, Ntn = NB(f"N{1 - cur}", g), NB(f"Nt{1 - cur}", g)
                kNc, kNtc = (f"N{cur}_{g}", g), (f"Nt{cur}_{g}", g)
                kNn, kNtn = (f"N{1 - cur}_{g}", g), (f"Nt{1 - cur}_{g}", g)
                bA, bB, bCk = (bk(i) for i in BS[g])
                kA, kB, kC = (("pb", i) for i in BS[g])
                for hh in range(8):
                    P.pe(_mm(bA[:, hh, :], Ntc[:, hh, :], Nc[:, hh, :], True, True), r=[kNc, kNtc], w=[kA])
                P.act(lambda e, Nn=Nn, bA=bA: e.activation(out=Nn[:], in_=bA, func=AF.Identity), r=[kA], w=[kNn])
                if lv < 4:
                    for hh in range(8):
                        P.pe(_mm(bB[:, hh, :], Nc[:, hh, :], Ntc[:, hh, :], True, True), r=[kNc, kNtc], w=[kB])
                    P.act(lambda e, Ntn=Ntn, bB=bB: e.activation(out=Ntn[:], in_=bB, func=AF.Identity), r=[kB], w=[kNtn])
                for hh in range(8):
                    h = g * 8 + hh
                    P.pe(_mm(bCk[:, hh, :], Nn[:, hh, :], B["Tt"][:, h, :], True, True), r=[kNn, ("Tt", g)], w=[kC])
                tt("dve", B["Tt"][:, gs, :], B["Tt"][:, gs, :], bCk, ALU.add, [("Tt", g), kC], [("Tt", g)])
            cur = 1 - cur
        P.stage = f"mid{j}_state"
        for g in range(2):
            gs = slice(g * 8, (g + 1) * 8)
            bZ = bk(BS[g][0]); kZ = ("pb", BS[g][0])
            for hh in range(8):
                h = g * 8 + hh
                vh = Vtm[:, h * 64:(h + 1) * 64]
                P.pe(_mm(bZ[:, hh, :], B["KT"][:, h, :], Mb[:, h, :], True, False), r=["KT", ("Mb", g)], w=[kZ])
                P.pe(_mm(bZ[:, hh, :], B["AkT"][:, h, :], vh, False, True), r=[("AkT", g), "Vtm"], w=[kZ])
            P.act(lambda e, bZ=bZ, gs=gs: e.activation(out=B["nZ"][:, gs, :], in_=bZ, func=AF.Identity, scale=-1.0),
                  r=[kZ], w=[("BH", g)])
        for g in range(2):
            gs = slice(g * 8, (g + 1) * 8)
            bU = bk(BS[g][1]); kU = ("pb", BS[g][1])
            for hh in range(8):
                h = g * 8 + hh
                P.pe(_mm(bU[:, hh, :], B["Tt"][:, h, :], B["nZ"][:, h, :], True, True), r=[("Tt", g), ("BH", g)], w=[kU])
            P.act(lambda e, bU=bU, gs=gs: e.activation(out=B["U"][:, gs, :], in_=bU, func=AF.Identity), r=[kU], w=[("KH", g)])
        for g in range(2):
            gs = slice(g * 8, (g + 1) * 8)
            bM = bk(BS[g][2]); kM = ("pb", BS[g][2])
            bY = bk(BS[g][0]); kY = ("pb", BS[g][0])
            for hh in (range(8) if full else ()):
                h = g * 8 + hh
                vh = Vtm[:, h * 64:(h + 1) * 64]
                P.pe(_mm(bY[:, hh, :], Mb[:, h, :], B["RT"][:, h, :], True, False), r=[("Mb", g), "RT"], w=[kY])
                P.pe(_mm(bY[:, hh, :], B["U"][:, h, :], B["PbT"][:, h, :], False, False), r=[("KH", g), ("PbT", g)], w=[kY])
                P.pe(_mm(bY[:, hh, :], vh, B["PkT"][:, h, :], False, True), r=["Vtm", ("PkT", g)], w=[kY])
            if full:
                P.act(lambda e, bY=bY, gs=gs: e.activation(out=F["cum"][:, gs, :], in_=bY, func=AF.Identity), r=[kY], w=[("cum", g)])
            for hh in range(8):
                h = g * 8 + hh
                vh = Vtm[:, h * 64:(h + 1) * 64]
                P.pe(_mm(bM[:, hh, :], B["BHT"][:, h, :], B["U"][:, h, :], True, False), r=[("BHT", g), ("KH", g)], w=[kM])
                P.pe(_mm(bM[:, hh, :], B["KHT"][:, h, :], vh, False, True), r=[("KHT", g), "Vtm"], w=[kM])
            tt("dve", M[:, gs, :], M[:, gs, :], gC[:, gs].unsqueeze(2).to_broadcast([64, 8, 64]), ALU.mult,
               [("M", g), "gC"], [("M", g)])
            tt("dve", M[:, gs, :], M[:, gs, :], bM, ALU.add, [("M", g), kM], [("M", g)])
            P.act(lambda e, gs=gs: e.activation(out=Mb[:, gs, :], in_=M[:, gs, :], func=AF.Identity), r=[("M", g)], w=[("Mb", g)])
    def late_segments(j):
        s = j % 2
        X = XT[s]
        rF, kF, vF, aF, sg, cum, kap, t0, t1 = (F[n] for n in ("rF", "kF", "vF", "aF", "sg", "cum", "kap", "t0", "t1"))
        t2 = sg
        Y = F["cum"]
        Yf = Y[:].rearrange("p h c -> p (h c)")
        t0f = t0[:].rearrange("p h c -> p (h c)"); t1f = t1[:].rearrange("p h c -> p (h c)"); t2f = t2[:].rearrange("p h c -> p (h c)")

        def l1():
            P.act(lambda e: e.activation(out=t0[:], in_=Y[:], func=AF.Square), r=["cum"], w=["t0"])
            for half in range(2):
                hs = slice(half * 512, (half + 1) * 512)
                P.pe(_mm(pb[4][0:64, :], ones64[:], Yf[:, hs], True, True), r=["ones64", "cum"], w=[("pb", 4)])
                P.pe(_mm(pb[5][0:64, :], ones64[:], t0f[:, hs], True, True), r=["ones64", "t0"], w=[("pb", 5)])
                P.act(lambda e, hs=hs: e.activation(out=t1f[:, hs], in_=pb[4][0:64, :], func=AF.Identity), r=[("pb", 4)], w=[("t1", half)])
                tt("dve", t2f[:, hs], t1f[:, hs], t1f[:, hs], ALU.mult, [("t1", half)], [("sg", half)])
                tt("dve", t2f[:, hs], pb[5][0:64, :], t2f[:, hs], ALU.subtract, [("pb", 5), ("sg", half)], [("sg", half)])
            P.dve(lambda e: e.tensor_scalar(out=t2[:], in0=t2[:], scalar1=0.0, scalar2=64e-5, op0=ALU.max, op1=ALU.add), r=["sg"], w=["sg"])
            P.act(lambda e: e.activation(out=t2[:], in_=t2[:], func=AF.Ln), r=["sg"], w=["sg"])
            P.act(lambda e: e.activation(out=t2[:], in_=t2[:], func=AF.Exp, scale=-0.5), r=["sg"], w=["sg"])

        def l2():
            tt("dve", Y[:], Y[:], t1[:], ALU.subtract, ["cum", "t1"], ["cum"])
            tt("dve", Y[:], Y[:], t2[:], ALU.mult, ["cum", "sg"], ["cum"])
            tt("pool", Y[:], Y[:], bc(hv["gng"]), ALU.mult, ["cum", "gng"], ["cum"])
            tt("pool", Y[:], Y[:], bc(hv["gnb"]), ALU.add, ["cum", "gnb"], ["cum"])
            tt("dve", Y[:], Y[:], kap[:], ALU.add, ["cum", "kap"], ["cum"])
            tt("dve", B["zB"][:], Y[:], F["gF"][:], ALU.mult, ["cum", "gF"], ["zB"])

        def l3():
            ob = pb[4][:, :].rearrange("p (c t) -> p c t", t=TT)
            for db in range(KC):
                for h in range(H):
                    P.pe(_mm(ob[:, db, :], wo[:, h, db * 128:(db + 1) * 128], B["zB"][:, h, :], h == 0, h == H - 1),
                         r=[("wo", h // 2), "zB"], w=[("pb", 4)])
            P.dve(lambda e: e.scalar_tensor_tensor(out=xr_[:], in0=X[:, :, 1:TT + 1], scalar=float(ALPHA), in1=ob,
                                                   op0=ALU.mult, op1=ALU.add), r=[("XT", s), ("pb", 4)], w=["xr_"])

        def l4():
            for c in range(KC):
                P.pe(_mm(mean_ps, ones1024[:], xr_[:, c, :], c == 0, c == KC - 1), r=["ones1024", "xr_"], w=["mean_ps"])
            P.act(lambda e: e.activation(out=sq8[:], in_=xr_[:], func=AF.Square), r=["xr_"], w=["sq8"])
            for c in range(KC):
                P.pe(_mm(msq_ps, ones1024[:], sq8[:, c, :], c == 0, c == KC - 1), r=["ones1024", "sq8"], w=["msq_ps"])
            P.act(lambda e: e.activation(out=mean_sb[:], in_=mean_ps, func=AF.Identity), r=["mean_ps"], w=["mean_sb"])
            tt("dve", var_sb[:], mean_sb[:], mean_sb[:], ALU.mult, ["mean_sb"], ["var_sb"])
            tt("dve", var_sb[:], msq_ps, var_sb[:], ALU.subtract, ["msq_ps", "var_sb"], ["var_sb"])
            P.dve(lambda e: e.tensor_scalar(out=var_sb[:], in0=var_sb[:], scalar1=0.0, scalar2=float(LN_EPS),
                                            op0=ALU.max, op1=ALU.add), r=["var_sb"], w=["var_sb"])
            P.act(lambda e: e.activation(out=var_sb[:], in_=var_sb[:], func=AF.Ln), r=["var_sb"], w=["var_sb"])
            P.act(lambda e: e.activation(out=rstd_sb[:], in_=var_sb[:], func=AF.Exp, scale=-0.5), r=["var_sb"], w=["rstd_sb"])
            tt("dve", xr_[:], xr_[:], mean_sb[:].unsqueeze(1).to_broadcast([128, KC, TT]), ALU.subtract, ["xr_", "mean_sb"], ["xr_"])
            tt("dve", xr_[:], xr_[:], rstd_sb[:].unsqueeze(1).to_broadcast([128, KC, TT]), ALU.mult, ["xr_", "rstd_sb"], ["xr_"])
            tt("dve", xr_[:], xr_[:], lng[:].unsqueeze(2).to_broadcast([128, KC, TT]), ALU.mult, ["xr_", "lng"], ["xr_"])
            tt("dve", xr_[:], xr_[:], lnb[:].unsqueeze(2).to_broadcast([128, KC, TT]), ALU.add, ["xr_", "lnb"], ["xr_"])
            P.dma("sp", lambda e: e.dma_start(out=out_v[:, :, j * TT:(j + 1) * TT], in_=xr_[:]), r=["xr_"], w=[ok + (j,)])
        return [l1, l2, l3, l4]

    load_x(0)
    if ntiles > 1:
        load_x(1)
    for seg in early_segments(0):
        seg()
    for j in range(ntiles):
        P.stage = f"mid{j}"
        mid(j)
        E = early_segments(j + 1) if j + 1 < ntiles else []
        L = late_segments(j) if full else []
        order = [("L", 0), ("E", 0), ("L", 1), ("E", 1), ("L", 2), ("E", 2), ("L", 3), ("E", 3)]
        order = ORDER_OVERRIDE or order
        for kind, k in order:
            lst = L if kind == "L" else E
            if k < len(lst):
                P.stage = (f"late{j}" if kind == "L" else f"early{j + 1}")
                lst[k]()
        if j + 2 < ntiles:
            load_x(j + 2)
    P.dma("sp", lambda e: e.dma_start(out=io["M_out"], in_=M[:]), r=["M"], w=[io.get("Mout_key", ("M_out",))])


ODD_IN = dict(w_r=[D, D], w_k=[D, D], w_v=[D, D], w_o=[D, D], w1=[D, 64], a1=[D, 64], g1=[D, 160], w2=[64, D], a2=[64, D],
              g2=[160, D], mu=[128, 6, KC], w0h=[64, H], a0h=[64, H], kkh=[64, H], kah=[64, H], rkh=[64, H], gng=[64, H],
              gnb=[64, H], lng=[128, KC], lnb=[128, KC], mUs=[64, 64], mUi=[64, 64], mLs=[64, 64], eye=[64, 64],
              rmask_o=[64, H * C], id32_o=[64, 64], M=[64, H, 64])


def build_odd_nc(NT):
    nc = bass.Bass("TRN2", target_bir_lowering=False)
    io = {"xin": nc.dram_tensor("xin", [D, 1 + NT], F32, kind="ExternalInput").ap()}
    for k, shp in ODD_IN.items():
        io[k] = nc.dram_tensor(k, list(shp), F32, kind="ExternalInput").ap()
    io["out"] = nc.dram_tensor("out", [D, NT], F32, kind="ExternalOutput").ap()
    io["M_out"] = nc.dram_tensor("M_out", [64, H, 64], F32, kind="ExternalOutput").ap()
    P = Prog(nc)
    P.begin_phase()
    build_odd(nc, P, NT, io)
    P.end_phase()
    P.finish()
    return nc, P


def hvec(v):
    return np.ascontiguousarray(np.asarray(v, np.float32).reshape(H, 64).T)


def odd_inputs(xin_T, p, j, lng, lnb, M):
    i = np.arange(64)
    d = dict(xin=np.ascontiguousarray(xin_T, dtype=np.float32))
    for k in ("w_r", "w_k", "w_v", "w_o", "w1", "a1", "g1", "w2", "a2", "g2"):
        d[k] = np.ascontiguousarray(p["rw_" + k][j], dtype=np.float32)
    d["mu"] = np.ascontiguousarray(np.stack([vec_layout(p["rw_mu"][j][i6], KC) for i6 in range(6)], axis=1))
    d["w0h"] = hvec(p["rw_w0"][j]); d["a0h"] = hvec(p["rw_a0"][j]); d["kkh"] = hvec(p["rw_k_k"][j])
    d["kah"] = hvec(p["rw_k_a"][j]); d["rkh"] = hvec(p["rw_r_k"][j].reshape(-1)); d["gng"] = hvec(p["rw_gn_g"][j])
    d["gnb"] = hvec(p["rw_gn_b"][j]); d["lng"] = vec_layout(lng, KC); d["lnb"] = vec_layout(lnb, KC)
    d["mUs"] = (i[:, None] < i[None, :]).astype(np.float32); d["mUi"] = (i[:, None] <= i[None, :]).astype(np.float32)
    d["mLs"] = (i[:, None] > i[None, :]).astype(np.float32); d["eye"] = np.eye(64, dtype=np.float32)
    rm = np.ones((64, H * C), np.float32); rm[:, ::C] = 0.0
    d["rmask_o"] = rm; d["id32_o"] = np.eye(64, dtype=np.float32); d["M"] = np.ascontiguousarray(M, dtype=np.float32)
    return d


NCORES = 8
SEQ = 8192
NTC = SEQ // 2
TT_E = 256
TT_F = 256
HMAX = 30
PAIRS = [[0, 1], [2, 3], [4, 5], [6, 7]]

EVEN_L = dict(w_in=[D, 3072], w_out=[D, D], bin=[128, 24], convw=[128, 4 * 31], convb=[128, 4], clng=[128, 4],
              clnb=[128, 4], lb0=[128, 4], lb1=[128, 4], lbsel=[128, 1], ong=[128, 4], lng=[128, KC], lnb=[128, KC],
              bivbc=[128, 512])
EVEN_C = dict(mask=[128, 128], rmask=[128, TT_E], ident=[128, 128])
ODD_C = ("mUs", "mUi", "mLs", "eye", "rmask_o", "id32_o")
FFN_L = dict(w_up=[D, DFF], w_gate=[D, DFF], w_down=[DFF, D], cw=[128, FB * 3], cb=[128, FB], lng=[128, KC], lnb=[128, KC])


def _exchange(P, nc, tag, src_ap, src_key, rows, cols, bin_t, bout_t, dst_ap, dst_key, sel_ap, pairs=None):
    P.begin_phase(tag)
    np_ = min(rows, 128)
    nch = rows // np_
    t = P.sb([np_, nch, cols], F32, "xt")
    selt = P.sb([128, 1], F32, "sel")
    P.dma("sp", lambda e: e.dma_start(out=bin_t.ap(), in_=src_ap), r=[src_key], w=[(tag, "bin")])
    P.cc(lambda e: e.collective_compute("AllGather", ALU.bypass, replica_groups=(pairs or PAIRS),
                                        ins=[bin_t.ap().opt()], outs=[bout_t.ap().opt()]),
         r=[(tag, "bin")], w=[(tag, "bout")])
    P.dma("sp", lambda e: e.dma_start(out=t[:], in_=bout_t.ap()[0:rows, :].rearrange("(c p) t -> p c t", p=np_)),
          r=[(tag, "bout")], w=["xt"])
    P.dma("sp", lambda e: e.dma_start(out=selt[:], in_=sel_ap), w=["selt"])
    P.dve(lambda e: e.tensor_scalar(out=t[:], in0=t[:], scalar1=selt[0:np_, 0:1], scalar2=None, op0=ALU.mult),
          r=["xt", "selt"], w=["xt"])
    P.dma("sp", lambda e: e.dma_start(out=dst_ap.rearrange("(c p) t -> p c t", p=np_), in_=t[:]), r=["xt"], w=[dst_key])
    P.end_phase()


def build_fused_nc(NT=NTC, nlayers=DEPTH, ncores=NCORES):
    nc = bass.Bass("TRN2", target_bir_lowering=False)
    pairs = [[2 * i, 2 * i + 1] for i in range(ncores // 2)]

    def din(name, shape):
        return nc.dram_tensor(name, list(shape), F32, kind="ExternalInput").ap()

    def dint(name, shape):
        return nc.dram_tensor(name, list(shape), F32)

    xin = din("xin", [D, HMAX + NT])
    sel = din("sel", [128, 1])
    zst = din("zst", [128, 1024])
    out = nc.dram_tensor("out", [D, NT], F32, kind="ExternalOutput").ap()
    consts = {k: din(k, v) for k, v in EVEN_C.items()}
    for k in ODD_C:
        consts[k] = din(k, ODD_IN[k])
    slab = [dint(f"slab{i}", [D, HMAX + NT]) for i in range(2)]
    hx_in = dint("hx_in", [D, HMAX]); hx_out = dint("hx_out", [2 * D, HMAX])
    se_in = dint("se_in", [128, 512]); se_out = dint("se_out", [256, 512]); se_sel = dint("se_sel", [128, 512])
    so_in = dint("so_in", [64, 1024]); so_out = dint("so_out", [128, 1024]); so_sel = dint("so_sel", [64, 1024])
    dump = dint("dump", [128, 1024])
    P = Prog(nc)
    for layer in range(nlayers):
        src = xin if layer == 0 else slab[1].ap()
        skey = ("xin_ext",) if layer == 0 else ("slab", 1)
        if layer > 0:
            _exchange(P, nc, f"hx{layer}m_", slab[1].ap()[:, NT:NT + HMAX], ("slab", 1), D, HMAX, hx_in, hx_out,
                      slab[1].ap()[:, 0:HMAX], ("slab", 1, "halo"), sel, pairs)
        if layer % 2 == 0:
            io = {k: din(f"{k}_{layer}", v) for k, v in EVEN_L.items()}
            io.update(consts)
            io["hsc"] = sel
            io["xin"] = src
            io["xin_key"] = skey
            ioa = dict(io); ioa["S"] = zst[:, 0:512].rearrange("p (h v) -> p h v", h=4); ioa["S_key"] = ("zst",)
            ioa["out"] = slab[0].ap()[:, HMAX:HMAX + NT]; ioa["S_out"] = se_in.ap().rearrange("p (h v) -> p h v", h=4)
            ioa["Sout_key"] = ("se_in",)
            P.begin_phase(f"L{layer}a_"); build_even(nc, P, NT, TT_E, ioa, state_only=True); P.end_phase()
            _exchange(P, nc, f"sx{layer}_", se_in.ap(), ("se_in",), 128, 512, se_in, se_out, se_sel.ap(), ("se_sel",), sel, pairs)
            iob = dict(io); iob["S"] = se_sel.ap().rearrange("p (h v) -> p h v", h=4); iob["S_key"] = ("se_sel",)
            iob["out"] = slab[0].ap()[:, HMAX:HMAX + NT]; iob["out_key"] = ("slab", 0, "body")
            iob["S_out"] = dump.ap()[:, 0:512].rearrange("p (h v) -> p h v", h=4); iob["Sout_key"] = ("dump",)
            P.begin_phase(f"L{layer}b_"); build_even(nc, P, NT, TT_E, iob); P.end_phase()
        else:
            io = {k: din(f"{k}_{layer}", v) for k, v in ODD_IN.items() if k not in ODD_C and k != "M"}
            io.update(consts)
            io["xin"] = src[:, HMAX - 1:HMAX + NT]
            io["xin_key"] = skey
            ioa = dict(io); ioa["M"] = zst[0:64, :].rearrange("p (h v) -> p h v", h=H); ioa["M_key"] = ("zst",)
            ioa["out"] = slab[0].ap()[:, HMAX:HMAX + NT]; ioa["M_out"] = so_in.ap().rearrange("p (h v) -> p h v", h=H)
            ioa["Mout_key"] = ("so_in",)
            P.begin_phase(f"L{layer}a_"); build_odd(nc, P, NT, ioa, state_only=True); P.end_phase()
            _exchange(P, nc, f"sx{layer}_", so_in.ap(), ("so_in",), 64, 1024, so_in, so_out, so_sel.ap(), ("so_sel",), sel, pairs)
            iob = dict(io); iob["M"] = so_sel.ap().rearrange("p (h v) -> p h v", h=H); iob["M_key"] = ("so_sel",)
            iob["out"] = slab[0].ap()[:, HMAX:HMAX + NT]; iob["out_key"] = ("slab", 0, "body")
            iob["M_out"] = dump.ap()[0:64, :].rearrange("p (h v) -> p h v", h=H); iob["Mout_key"] = ("dump",)
            P.begin_phase(f"L{layer}b_"); build_odd(nc, P, NT, iob); P.end_phase()
        _exchange(P, nc, f"hx{layer}f_", slab[0].ap()[:, NT:NT + HMAX], ("slab", 0), D, HMAX, hx_in, hx_out,
                  slab[0].ap()[:, 0:HMAX], ("slab", 0, "halo"), sel, pairs)
        iof = {k: din(f"{k}_f{layer}", v) for k, v in FFN_L.items()}
        iof["xin"] = slab[0].ap()[:, HMAX - 2:HMAX + NT]; iof["xin_key"] = ("slab", 0)
        last = layer == nlayers - 1
        iof["out"] = out if last else slab[1].ap()[:, HMAX:HMAX + NT]
        iof["out_key"] = ("out_ext",) if last else ("slab", 1, "body")
        P.begin_phase(f"L{layer}f_"); build_ffn(nc, P, NT, TT_F, iof); P.end_phase()
    P.finish()
    return nc, P


def fused_inputs(inp, NT=NTC, nlayers=DEPTH, x_slabs=None):
    base = {}
    ec = even_consts(TT_E)
    base.update(ec)
    i = np.arange(64)
    base["mUs"] = (i[:, None] < i[None, :]).astype(np.float32); base["mUi"] = (i[:, None] <= i[None, :]).astype(np.float32)
    base["mLs"] = (i[:, None] > i[None, :]).astype(np.float32); base["eye"] = np.eye(64, dtype=np.float32)
    rm = np.ones((64, H * C), np.float32); rm[:, ::C] = 0.0
    base["rmask_o"] = rm; base["id32_o"] = np.eye(64, dtype=np.float32)
    base["zst"] = np.zeros((128, 1024), np.float32)
    dummy_x = np.zeros((D, 1), np.float32)
    for layer in range(nlayers):
        j = layer // 2
        if layer % 2 == 0:
            d = even_inputs(dummy_x, inp["ev_w_in"][j], inp["ev_b_in"][j], inp["ev_conv_w"][j], inp["ev_conv_b"][j],
                            inp["ev_cln_g"][j], inp["ev_cln_b"][j], inp["ev_lb_logits"], j, inp["ev_onorm_g"][j],
                            inp["ev_w_out"][j], inp["ln_mix_g"][layer], inp["ln_mix_b"][layer],
                            np.zeros((1,), np.float32), True, TT_E)
            for k in EVEN_L:
                base[f"{k}_{layer}"] = d[k]
        else:
            d = odd_inputs(dummy_x, inp, j, inp["ln_mix_g"][layer], inp["ln_mix_b"][layer], np.zeros((1,), np.float32))
            for k in ODD_IN:
                if k not in ODD_C and k != "M":
                    base[f"{k}_{layer}"] = d[k]
        d = ffn_inputs(dummy_x, inp["ff_w_up"][layer], inp["ff_w_gate"][layer], inp["ff_conv_w"][layer],
                       inp["ff_conv_b"][layer], inp["ff_w_down"][layer], inp["ln_ffn_g"][layer], inp["ln_ffn_b"][layer])
        for k in FFN_L:
            base[f"{k}_f{layer}"] = d[k]
    maps = []
    for c in range(len(x_slabs)):
        m = dict(base)
        m["xin"] = x_slabs[c]
        m["sel"] = np.full((128, 1), float(c % 2), np.float32)
        maps.append(m)
    return maps


_PROGS = {}


def kernel(**inp):
    inp = {k: np.asarray(v) for k, v in inp.items()}
    x = inp["x"].astype(np.float32)
    B = x.shape[0]
    slabs = []
    for c in range(NCORES):
        b, h = divmod(c, 2)
        xT = x[b].T
        if h == 0:
            sl = np.concatenate([np.zeros((D, HMAX), np.float32), xT[:, :NTC]], axis=1)
        else:
            sl = xT[:, NTC - HMAX:]
        slabs.append(np.ascontiguousarray(sl))
    if "fused" not in _PROGS:
        _PROGS["fused"] = build_fused_nc(NTC, DEPTH)[0]
    maps = fused_inputs(inp, NTC, DEPTH, slabs)
    res = run_bass_kernel_spmd(_PROGS["fused"], maps, core_ids=list(range(NCORES))).results
    out = np.empty((B, SEQ, D), np.float32)
    for c in range(NCORES):
        b, h = divmod(c, 2)
        out[b, h * NTC:(h + 1) * NTC, :] = res[c]["out"].T
    return out
```

```python
import contextlib
import os
import numpy as np
import concourse.bass as bass
import concourse.mybir as mybir
from concourse.bass_utils import run_bass_kernel_spmd

F32 = mybir.dt.float32
BF16 = mybir.dt.bfloat16
ALU = mybir.AluOpType
AF = mybir.ActivationFunctionType

D = 1024
KC = D // 128
DFF = 2816
FB = DFF // 128
DEPTH = 4
ALPHA = (2 * DEPTH) ** 0.25
LN_EPS = 1e-5


class _Rec:
    def __init__(self):
        self.call = None

    def __getattr__(self, name):
        def f(*a, **k):
            assert self.call is None, "op lambda must make exactly one engine call"
            self.call = (name, a, k)
            return self
        return f


class Prog:
    ENGS = ("pe", "act", "dve", "pool", "sp")
    EPOCH = 16000
    NEP = dict(pe=8, act=5, dve=6, pool=4, sp=1)
    NDMA = 24
    NCC = 16

    def __init__(self, nc):
        self.nc = nc
        self.ops = []
        self.last_w = {}
        self.readers = {}
        self.gstack = contextlib.ExitStack()
        self.stack = None
        self.ntile = 0
        self.known = set()
        self.children = {}
        self.bank_of = {}
        self.bank_last = {}
        self.prefix = ""
        self.phase_start = 0
        self.nphase = 0
        self.cnt = {e: 0 for e in self.ENGS}
        self.ndma = 0
        self.ncc = 0
        self.dma_prev = {}
        self.sems = {}
        g = self.gstack
        for e in self.ENGS:
            for ep in range(self.NEP[e]):
                self.sems[(e, ep)] = g.enter_context(nc.semaphore(f"s_{e}_{ep}"))
        for i in range(self.NDMA):
            self.sems[("dma", i)] = g.enter_context(nc.semaphore(f"s_dma_{i}"))
        for i in range(self.NCC):
            self.sems[("cc", i)] = g.enter_context(nc.semaphore(f"s_cc_{i}"))
        self.bar = g.enter_context(nc.semaphore("s_bar"))
        self.stats = dict(n_ops=0)

    def begin_phase(self, prefix=""):
        self.stack = contextlib.ExitStack()
        self.prefix = prefix
        self.phase_start = len(self.ops)
        self.bank_of = {}
        self.bank_last = {}

    def end_phase(self):
        self._emit_phase()
        self.stack.close()
        self.stack = None

    def finish(self):
        self.gstack.close()
        self.stats = dict(n_ops=len(self.ops), milestones=dict(self.cnt), ndma=self.ndma, ncc=self.ncc)

    def sb(self, shape, dt, name=None):
        self.ntile += 1
        name = "sb_" + self.prefix + (name or f"t{self.ntile}")
        return self.stack.enter_context(self.nc.sbuf_tensor(name, list(shape), dt))

    def ps(self, shape, dt=F32, name=None, keys=None):
        self.ntile += 1
        nm = name or f"p{self.ntile}"
        for k in (keys if keys is not None else [nm]):
            k = k if isinstance(k, tuple) else (k,)
            self.bank_of[k] = nm
        return self.stack.enter_context(self.nc.psum_tensor("ps_" + self.prefix + nm, list(shape), dt))

    def _banks(self, keys):
        out = set()
        for k in keys:
            for i in range(1, len(k) + 1):
                if k[:i] in self.bank_of:
                    out.add(self.bank_of[k[:i]])
        return out

    def _norm(self, k):
        k = k if isinstance(k, tuple) else (k,)
        if k not in self.known:
            self.known.add(k)
            for i in range(1, len(k)):
                self.children.setdefault(k[:i], set()).add(k)
        return k

    def _related(self, k):
        for i in range(1, len(k) + 1):
            yield k[:i]
        for ext in self.children.get(k, ()):
            yield ext

    def op(self, eng, fn, reads=(), writes=(), dma=False, cc=False):
        idx = len(self.ops)
        deps = set()
        reads = [self._norm(k) for k in reads]
        writes = [self._norm(k) for k in writes]
        for k in reads:
            for r in self._related(k):
                if r in self.last_w:
                    deps.add(self.last_w[r])
        for k in writes:
            for r in self._related(k):
                if r in self.last_w:
                    deps.add(self.last_w[r])
                for x in self.readers.get(r, ()):
                    deps.add(x)
        for k in writes:
            self.last_w[k] = idx
            self.readers[k] = []
            for ext in self.children.get(k, ()):
                self.last_w.pop(ext, None)
                self.readers[ext] = []
        for k in reads:
            self.readers.setdefault(k, []).append(idx)
        for bk in self._banks(reads + writes):
            bl = self.bank_last.setdefault(bk, {})
            for e2, i2 in bl.items():
                if e2 != eng:
                    deps.add(i2)
            bl[eng] = idx
        deps.discard(idx)
        deps = {d for d in deps if d >= self.phase_start}
        rec = _Rec()
        fn(rec)
        assert rec.call is not None
        self.ops.append(dict(eng=eng, call=rec.call, deps=deps, dma=dma or cc, cc=cc, stage=getattr(self, "stage", "")))
        return idx

    def pe(self, fn, r=(), w=()):
        return self.op("pe", fn, r, w)

    def act(self, fn, r=(), w=()):
        return self.op("act", fn, r, w)

    def dve(self, fn, r=(), w=()):
        return self.op("dve", fn, r, w)

    def pool(self, fn, r=(), w=()):
        return self.op("pool", fn, r, w)

    def dma(self, eng, fn, r=(), w=()):
        return self.op(eng, fn, r, w, dma=True)

    def cc(self, fn, r=(), w=()):
        return self.op("pool", fn, r, w, cc=True)

    def _emit_phase(self):
        nc = self.nc
        ops = self.ops
        lo = self.phase_start
        idxs = range(lo, len(ops))
        for i in idxs:
            o = ops[i]
            best = {}
            keep = set()
            for d in o["deps"]:
                p = ops[d]
                if p["dma"]:
                    keep.add(d)
                elif best.get(p["eng"], -1) < d:
                    best[p["eng"]] = d
            keep.update(best.values())
            o["deps"] = keep
        needed = set()
        for i in idxs:
            o = ops[i]
            for d in o["deps"]:
                p = ops[d]
                if p["dma"]:
                    needed.add(d)
                elif p["eng"] == o["eng"] and o["eng"] == "pe" and not o["dma"]:
                    continue
                else:
                    needed.add(d)
        per_eng = {e: [i for i in idxs if ops[i]["eng"] == e] for e in self.ENGS}
        last_compute = {}
        for e in self.ENGS:
            for i in reversed(per_eng[e]):
                if not ops[i]["dma"]:
                    last_compute[e] = i
                    needed.add(i)
                    break
        for i in idxs:
            o = ops[i]
            if o["cc"]:
                assert self.ncc < self.NCC
                o["sig"] = ("cc", self.ncc, 1)
                o["prev_same_slot"] = None
                self.ncc += 1
            elif o["dma"]:
                slot = self.ndma % self.NDMA
                o["sig"] = ("dma", slot, 16 * (self.ndma // self.NDMA + 1))
                o["prev_same_slot"] = self.dma_prev.get(slot)
                self.dma_prev[slot] = i
                self.ndma += 1
            elif i in needed:
                c = self.cnt[o["eng"]]
                assert c // self.EPOCH < self.NEP[o["eng"]], "out of semaphore epochs"
                o["sig"] = (o["eng"], c // self.EPOCH, c % self.EPOCH + 1)
                self.cnt[o["eng"]] = c + 1
            else:
                o["sig"] = None
        sems = self.sems
        self.nphase += 1
        nph = self.nphase
        bar = self.bar

        def run_engine(ename, eng):
            seen = {}

            def wait(sig):
                key = (sig[0], sig[1])
                if seen.get(key, 0) >= sig[2]:
                    return
                eng.wait_ge(sems[key], sig[2])
                seen[key] = sig[2]

            for i in per_eng[ename]:
                o = ops[i]
                best = {}
                for d in o["deps"]:
                    p = ops[d]
                    if (not p["dma"]) and p["eng"] == ename and ename == "pe" and not o["dma"]:
                        continue
                    sg = p["sig"]
                    kk = (sg[0], sg[1])
                    if best.get(kk, 0) < sg[2]:
                        best[kk] = sg[2]
                for kk in sorted(best):
                    wait((kk[0], kk[1], best[kk]))
                if o["dma"] and o["prev_same_slot"] is not None and o["prev_same_slot"] >= lo:
                    wait(ops[o["prev_same_slot"]]["sig"])
                call = o["call"]
                ins = getattr(eng, call[0])(*call[1], **call[2])
                sig = o["sig"]
                if sig is not None:
                    if sig[0] == "dma":
                        ins.then_inc(sems[("dma", sig[1])], 16)
                    elif sig[0] == "cc":
                        ins.then_inc(sems[("cc", sig[1])])
                    else:
                        ins.then_inc(sems[(sig[0], sig[1])], 1)
            for i in per_eng[ename]:
                if ops[i]["dma"]:
                    wait(ops[i]["sig"])
            if ename in last_compute:
                wait(ops[last_compute[ename]]["sig"])
            eng.sem_inc(bar, 1)
            eng.wait_ge(bar, len(self.ENGS) * nph)

        with nc.Block() as block:
            @block.tensor
            def _(e):
                run_engine("pe", e)

            @block.scalar
            def _(e):
                run_engine("act", e)

            @block.vector
            def _(e):
                run_engine("dve", e)

            @block.gpsimd
            def _(e):
                run_engine("pool", e)

            @block.sync
            def _(e):
                run_engine("sp", e)


class Stager:
    def __init__(self, P, width, nbuf=2):
        self.P = P
        self.bufs = [P.sb([128, width], F32, f"stage{i}") for i in range(nbuf)]
        self.n = 0

    def load(self, dst_ap, src_ap, n, dkey, eng=None, a=1):
        P = self.P
        i = self.n % len(self.bufs)
        eng = eng or ("pool", "act", "dve")[self.n % 3]
        self.n += 1
        st = self.bufs[i]
        sv = st[:, 0:n] if a == 1 else st[:, 0:n].rearrange("p (a d) -> p a d", a=a)
        P.dma("sp", lambda e: e.dma_start(out=sv, in_=src_ap), w=[("stage", i)])
        cast_op(P, eng, dst_ap, sv, [("stage", i)], [dkey])


def cast_op(P, eng, dst, src, r, w):
    if eng == "act":
        P.op("act", lambda e: e.activation(out=dst, in_=src, func=AF.Identity), r, w)
    else:
        P.op(eng, lambda e: e.tensor_copy(out=dst, in_=src), r, w)


def _mm(out, lhsT, rhs, start, stop):
    return lambda e: e.matmul(out, lhsT, rhs, start=start, stop=stop)


def build_ffn(nc, P, NT, TT, io):
    HALO = 2
    ntiles = NT // TT
    xk = io.get("xin_key", ("xin_ext",))
    ok = io.get("out_key", ("outdram",))
    xin = io["xin"]
    xin_v = xin.rearrange("(kc p) t -> p kc t", p=128)
    out_v = io["out"].rearrange("(kc p) t -> p kc t", p=128)

    wup = P.sb([128, KC, DFF], BF16, "wup")
    wgt = P.sb([128, KC, DFF], BF16, "wgt")
    wdn = P.sb([128, FB, D], BF16, "wdn")
    cw = P.sb([128, FB * 3], F32, "cw")
    cb = P.sb([128, FB], F32, "cb")
    lng = P.sb([128, KC], F32, "lng")
    lnb = P.sb([128, KC], F32, "lnb")
    ones = P.sb([128, 128], F32, "ones")
    carry = P.sb([128, FB, 2], F32, "carry")
    xbf = [P.sb([128, KC, TT], BF16, f"xbf{i}") for i in range(2)]
    xhalo = P.sb([128, KC, HALO], BF16, "xhalo")
    xres = [P.sb([128, KC, TT], F32, f"xres{i}") for i in range(2)]
    hT = P.sb([128, FB, TT], BF16, "hT")
    NQ = 3
    usb = [P.sb([128, TT + 2], F32, f"usb{i}") for i in range(NQ)]
    t1 = [P.sb([128, TT], F32, f"t1_{i}") for i in range(NQ)]
    gg = [P.sb([128, TT], F32, f"gg{i}") for i in range(NQ)]
    gsb = [P.sb([128, TT], BF16, f"gsb{i}") for i in range(NQ)]
    sq = [P.sb([128, TT], F32, f"sq{i}") for i in range(2)]
    mean_sb = P.sb([128, TT], F32, "mean_sb")
    var_sb = P.sb([128, TT], F32, "var_sb")
    rstd_sb = P.sb([128, TT], F32, "rstd_sb")
    yt = [P.sb([128, TT], F32, f"yt{i}") for i in range(2)]

    up_ps = [P.ps([128, TT], F32, f"up_ps{i}", keys=[("up_ps", i)]) for i in range(2)]
    gt_ps = [P.ps([128, TT], F32, f"gt_ps{i}", keys=[("gt_ps", i)]) for i in range(2)]
    o_ps = [P.ps([128, TT], F32, f"o_ps{i}", keys=[("o_ps", i)]) for i in range(2)]
    mean_ps = P.ps([128, TT], F32, "mean_ps")
    msq_ps = P.ps([128, TT], F32, "msq_ps")

    P.dma("sp", lambda e: e.dma_start(out=cw[:], in_=io["cw"]), w=["cw"])
    P.dma("sp", lambda e: e.dma_start(out=cb[:], in_=io["cb"]), w=["cb"])
    P.dma("sp", lambda e: e.dma_start(out=lng[:], in_=io["lng"]), w=["lng"])
    P.dma("sp", lambda e: e.dma_start(out=lnb[:], in_=io["lnb"]), w=["lnb"])
    P.pool(lambda e: e.memset(ones[:], 1.0 / D), w=["ones"])
    xh32 = P.sb([128, KC, HALO], F32, "xh32")
    P.dma("sp", lambda e: e.dma_start(out=xh32[:], in_=xin_v[:, :, 0:HALO]), r=[xk], w=["xh32"])
    P.pool(lambda e: e.tensor_copy(out=xhalo[:], in_=xh32[:]), r=["xh32"], w=["xhalo"])

    def load_xres(j):
        s = j % 2
        P.dma("sp", lambda e: e.dma_start(out=xres[s][:], in_=xin_v[:, :, HALO + j * TT:HALO + (j + 1) * TT]),
              r=[xk], w=[("xres", s)])
        P.pool(lambda e: e.tensor_copy(out=xbf[s][:], in_=xres[s][:]), r=[("xres", s)], w=[("xbf", s)])

    def load_x(j):
        pass

    load_xres(0)
    stg = Stager(P, 1024, nbuf=3)
    wupv = io["w_up"].rearrange("(kc p) f -> p kc f", p=128)
    wgtv = io["w_gate"].rearrange("(kc p) f -> p kc f", p=128)
    wdnv = io["w_down"].rearrange("(fb p) d -> p fb d", p=128)
    pieces = [(0, 1024), (1024, 1024), (2048, DFF - 2048)]
    for kc in range(KC):
        for (c0, cn) in pieces:
            stg.load(wup[:, kc, c0:c0 + cn], wupv[:, kc, c0:c0 + cn], cn, ("wup", kc, c0))
    for kc in range(KC):
        for (c0, cn) in pieces:
            stg.load(wgt[:, kc, c0:c0 + cn], wgtv[:, kc, c0:c0 + cn], cn, ("wgt", kc, c0))
    for fb in range(FB):
        stg.load(wdn[:, fb, :], wdnv[:, fb, :], D, ("wdn", fb // 2, fb % 2))

    hp = up_ps[1]
    for fb in range(FB):
        for kc in range(KC):
            P.pe(_mm(hp[:, fb * 2:fb * 2 + 2], wup[:, kc, fb * 128:(fb + 1) * 128], xhalo[:, kc, :],
                     kc == 0, kc == KC - 1),
                 r=[("wup", kc), "xhalo"], w=[("up_ps", 1)])
    P.act(lambda e: e.activation(out=carry[:].rearrange("p a b -> p (a b)"), in_=hp[:, 0:FB * 2], func=AF.Identity),
          r=[("up_ps", 1)], w=["carry"])

    nblk = 0
    for j in range(ntiles):
        s = j % 2
        if j + 1 < ntiles:
            load_x(j + 1)
        xb = xbf[s]
        for fb in range(FB):
            b = nblk % 2
            nblk += 1
            fsl = slice(fb * 128, (fb + 1) * 128)
            for kc in range(KC):
                P.pe(_mm(up_ps[b][:], wup[:, kc, fsl], xb[:, kc, :], kc == 0, kc == KC - 1),
                     r=[("wup", kc), ("xbf", s)], w=[("up_ps", b)])
            for kc in range(KC):
                P.pe(_mm(gt_ps[b][:], wgt[:, kc, fsl], xb[:, kc, :], kc == 0, kc == KC - 1),
                     r=[("wgt", kc), ("xbf", s)], w=[("gt_ps", b)])
            q = (nblk - 1) % NQ
            u = usb[q]
            P.pool(lambda e, u=u, fb=fb: e.tensor_copy(out=u[:, 0:2], in_=carry[:, fb, :]),
                   r=[("carry", fb)], w=[("usb", q)])
            P.act(lambda e, u=u, b=b: e.activation(out=u[:, 2:2 + TT], in_=up_ps[b][:], func=AF.Identity),
                  r=[("up_ps", b)], w=[("usb", q)])
            P.act(lambda e, q=q, b=b: e.activation(out=gsb[q][:], in_=gt_ps[b][:], func=AF.Identity),
                  r=[("gt_ps", b)], w=[("gsb", q)])
            P.pool(lambda e, u=u, fb=fb: e.tensor_copy(out=carry[:, fb, :], in_=u[:, TT:TT + 2]),
                   r=[("usb", q)], w=[("carry", fb)])
            tt = t1[q]
            P.dve(lambda e, u=u, tt=tt, fb=fb: e.tensor_scalar(
                out=tt[:], in0=u[:, 2:2 + TT], scalar1=cw[:, fb * 3 + 2:fb * 3 + 3], scalar2=cb[:, fb:fb + 1],
                op0=ALU.mult, op1=ALU.add), r=[("usb", q), "cw", "cb"], w=[("t1", q)])
            P.dve(lambda e, u=u, tt=tt, fb=fb: e.scalar_tensor_tensor(
                out=tt[:], in0=u[:, 1:1 + TT], scalar=cw[:, fb * 3 + 1:fb * 3 + 2], in1=tt[:],
                op0=ALU.mult, op1=ALU.add), r=[("usb", q), ("t1", q)], w=[("t1", q)])
            P.dve(lambda e, u=u, tt=tt, fb=fb: e.scalar_tensor_tensor(
                out=tt[:], in0=u[:, 0:TT], scalar=cw[:, fb * 3:fb * 3 + 1], in1=tt[:],
                op0=ALU.mult, op1=ALU.add), r=[("usb", q), ("t1", q)], w=[("t1", q)])
            g = gg[q]
            P.act(lambda e, g=g, tt=tt: e.activation(out=g[:], in_=tt[:], func=AF.Gelu),
                  r=[("t1", q)], w=[("gg", q)])
            P.dve(lambda e, g=g, q=q, fb=fb: e.tensor_tensor(out=hT[:, fb, :], in0=g[:], in1=gsb[q][:], op=ALU.mult),
                  r=[("gg", q), ("gsb", q)], w=[("hT", fb)])
        if j + 1 < ntiles:
            load_xres(j + 1)
        xr = xres[s]
        for db in range(KC):
            b = db % 2
            dsl = slice(db * 128, (db + 1) * 128)
            for fb in range(FB):
                P.pe(_mm(o_ps[b][:], wdn[:, fb, dsl], hT[:, fb, :], fb == 0, fb == FB - 1),
                     r=[("wdn", fb // 2), ("hT", fb)], w=[("o_ps", b)])
            P.dve(lambda e, xr=xr, db=db, b=b: e.scalar_tensor_tensor(
                out=xr[:, db, :], in0=xr[:, db, :], scalar=float(ALPHA), in1=o_ps[b][:],
                op0=ALU.mult, op1=ALU.add), r=[("xres", s, db), ("o_ps", b)], w=[("xres", s, db)])
        emit_ln(P, xr, ("xres", s), KC, TT, ones, sq, mean_ps, msq_ps, mean_sb, var_sb, rstd_sb, yt,
                lng, lnb, xr, ("xres", s), LN_EPS)
        P.dma("sp", lambda e, j=j, xr=xr: e.dma_start(out=out_v[:, :, j * TT:(j + 1) * TT], in_=xr[:]),
              r=[("xres", s)], w=[ok + (j,)])


def emit_ln(P, xr, xkey, nch, TT, ones, sq, mean_ps, msq_ps, mean_sb, var_sb, rstd_sb, yt, lng, lnb, osb, okey,
            eps, silu=False, lkeys=("lng", "lnb", "ones")):
    for c in range(nch):
        P.pe(_mm(mean_ps[:], ones[:], xr[:, c, :], c == 0, c == nch - 1),
             r=[lkeys[2], xkey + (c,)], w=["mean_ps"])
    for c in range(nch):
        b = c % 2
        P.act(lambda e, c=c, b=b: e.activation(out=sq[b][:], in_=xr[:, c, :], func=AF.Square),
              r=[xkey + (c,)], w=[("sq", b)])
        P.pe(_mm(msq_ps[:], ones[:], sq[b][:], c == 0, c == nch - 1),
             r=[lkeys[2], ("sq", b)], w=["msq_ps"])
    P.act(lambda e: e.activation(out=mean_sb[:], in_=mean_ps[:], func=AF.Identity), r=["mean_ps"], w=["mean_sb"])
    P.dve(lambda e: e.tensor_tensor(out=var_sb[:], in0=mean_sb[:], in1=mean_sb[:], op=ALU.mult),
          r=["mean_sb"], w=["var_sb"])
    P.dve(lambda e: e.tensor_tensor(out=var_sb[:], in0=msq_ps[:], in1=var_sb[:], op=ALU.subtract),
          r=["msq_ps", "var_sb"], w=["var_sb"])
    P.dve(lambda e: e.tensor_scalar(out=var_sb[:], in0=var_sb[:], scalar1=0.0, scalar2=float(eps),
                                    op0=ALU.max, op1=ALU.add), r=["var_sb"], w=["var_sb"])
    P.act(lambda e: e.activation(out=var_sb[:], in_=var_sb[:], func=AF.Ln), r=["var_sb"], w=["var_sb"])
    P.act(lambda e: e.activation(out=rstd_sb[:], in_=var_sb[:], func=AF.Exp, scale=-0.5), r=["var_sb"], w=["rstd_sb"])
    for c in range(nch):
        b = c % 2
        y = yt[b]
        P.dve(lambda e, y=y, c=c: e.tensor_tensor(out=y[:], in0=xr[:, c, :], in1=mean_sb[:], op=ALU.subtract),
              r=[xkey + (c,), "mean_sb"], w=[("yt", b)])
        P.dve(lambda e, y=y: e.tensor_tensor(out=y[:], in0=y[:], in1=rstd_sb[:], op=ALU.mult),
              r=[("yt", b), "rstd_sb"], w=[("yt", b)])
        P.act(lambda e, y=y, c=c: e.activation(out=osb[:, c, :], in_=y[:], func=(AF.Silu if silu else AF.Identity),
                                               scale=lng[:, c:c + 1], bias=lnb[:, c:c + 1]),
              r=[("yt", b), lkeys[0], lkeys[1]], w=[okey + (c,)])


def vec_layout(v, nb):
    return np.ascontiguousarray(np.asarray(v, np.float32).reshape(nb, 128).T)


def build_ffn_nc(NT, TT):
    nc = bass.Bass("TRN2", target_bir_lowering=False)
    io = {}
    io["xin"] = nc.dram_tensor("xin", [D, 2 + NT], F32, kind="ExternalInput").ap()
    io["w_up"] = nc.dram_tensor("w_up", [D, DFF], F32, kind="ExternalInput").ap()
    io["w_gate"] = nc.dram_tensor("w_gate", [D, DFF], F32, kind="ExternalInput").ap()
    io["w_down"] = nc.dram_tensor("w_down", [DFF, D], F32, kind="ExternalInput").ap()
    io["cw"] = nc.dram_tensor("cw", [128, FB * 3], F32, kind="ExternalInput").ap()
    io["cb"] = nc.dram_tensor("cb", [128, FB], F32, kind="ExternalInput").ap()
    io["lng"] = nc.dram_tensor("lng", [128, KC], F32, kind="ExternalInput").ap()
    io["lnb"] = nc.dram_tensor("lnb", [128, KC], F32, kind="ExternalInput").ap()
    io["out"] = nc.dram_tensor("out", [D, NT], F32, kind="ExternalOutput").ap()
    P = Prog(nc)
    P.begin_phase()
    build_ffn(nc, P, NT, TT, io)
    P.end_phase()
    P.finish()
    return nc, P


def ffn_inputs(xin_T, w_up, w_gate, conv_w, conv_b, w_down, g, b):
    cwl = np.stack([vec_layout(conv_w[t], FB) for t in range(3)], axis=-1).reshape(128, FB * 3)
    return dict(xin=np.ascontiguousarray(xin_T, dtype=np.float32),
                w_up=np.ascontiguousarray(w_up), w_gate=np.ascontiguousarray(w_gate),
                w_down=np.ascontiguousarray(w_down), cw=np.ascontiguousarray(cwl),
                cb=vec_layout(conv_b, FB), lng=vec_layout(g, KC), lnb=vec_layout(b, KC))


STAGE = 9
CH = 64
HAL_E = 30


def build_even(nc, P, NT, TT, io, state_only=False):
    HALO = HAL_E
    ntiles = NT // TT
    xk = io.get("xin_key", ("xin_ext",))
    ok = io.get("out_key", ("outdram",))
    full = not state_only
    nblk = TT // 128
    xin_v = io["xin"].rearrange("(kc p) t -> p kc t", p=128)
    out_v = io["out"].rearrange("(kc p) t -> p kc t", p=128)
    win = P.sb([128, KC, 3072], BF16, "win")
    wout = P.sb([128, KC, D], BF16, "wout")
    bin_ = P.sb([128, 24], F32, "bin")
    nbin = P.sb([128, 24], F32, "nbin")
    convw = P.sb([128, 4 * 31], F32, "convw")
    convb = P.sb([128, 4], F32, "convb")
    clng = P.sb([128, 4], F32, "clng")
    clnb = P.sb([128, 4], F32, "clnb")
    lb0 = P.sb([128, 4], F32, "lb0")
    lb1 = P.sb([128, 4], F32, "lb1")
    lbsel = P.sb([128, 1], F32, "lbsel")
    lb = P.sb([128, 4], F32, "lb")
    oml = P.sb([128, 4], F32, "oml")
    ong = P.sb([128, 4], F32, "ong")
    hsc = P.sb([128, 1], F32, "hsc")
    lng = P.sb([128, KC], F32, "lng")
    lnb = P.sb([128, KC], F32, "lnb")
    bivbc = P.sb([128, 512], F32, "bivbc")
    ones512 = P.sb([128, 128], F32, "ones512")
    ones128 = P.sb([128, 128], F32, "ones128")
    ones1024 = P.sb([128, 128], F32, "ones1024")
    ident = P.sb([128, 128], BF16, "ident")
    mask = P.sb([128, 128], F32, "mask")
    rmask = P.sb([128, TT], F32, "rmask")
    S = P.sb([128, 4, 128], F32, "S")
    Sbf = P.sb([128, 4, 128], BF16, "Sbf")
    xbf = [P.sb([128, KC, TT], BF16, f"xbf{i}") for i in range(2)]
    xhalo = P.sb([128, KC, HALO], BF16, "xhalo")
    xres = [P.sb([128, KC, TT], F32, f"xres{i}") for i in range(2)]
    glu = [P.sb([128, 4, HALO + TT], BF16, f"glu{i}") for i in range(2)]
    convd = P.sb([128, 4 * 31, 128], BF16, "convd") if not state_only else None
    sg = [P.sb([128, TT], F32, f"sg{i}") for i in range(2)]
    cacc = P.sb([128, 4, TT], F32, "cacc")
    cat = P.sb([128, KC, TT], BF16, "cat")
    qs = P.sb([128, TT], F32, "qs")
    sgp = P.sb([128, TT], F32, "sgp")
    sgn = P.sb([128, TT], F32, "sgn")
    logf = P.sb([128, TT], F32, "logf")
    cum = P.sb([128, TT], F32, "cum")
    eq = P.sb([128, TT], F32, "eq")
    en = P.sb([128, TT], F32, "en")
    gC = P.sb([128, 4, TT // CH], F32, "gC")
    qt = P.sb([128, 4, TT], BF16, "qt")
    kt = P.sb([128, 4, TT], F32, "kt")
    ktb = P.sb([128, 4, TT], BF16, "ktb")
    kh = P.sb([128, 4, TT], BF16, "kh")
    khT = P.sb([128, 4, 128], BF16, "khT")
    vtm = P.sb([128, 512], BF16, "vtm")
    pT = P.sb([128, 4, 128], BF16, "pT")
    osb = P.sb([128, 4, TT], F32, "osb")
    sqo = P.sb([128, TT], F32, "sqo")
    gsl = P.sb([128, TT], F32, "gsl")
    sq = [P.sb([128, TT], F32, f"sq{i}") for i in range(2)]
    mean_sb = P.sb([128, TT], F32, "mean_sb")
    var_sb = P.sb([128, TT], F32, "var_sb")
    rstd_sb = P.sb([128, TT], F32, "rstd_sb")
    yt = [P.sb([128, TT], F32, f"yt{i}") for i in range(2)]

    zps = [P.ps([128, 2, 256], F32, f"zps{i}", keys=[("zps", 2 * i), ("zps", 2 * i + 1)]) for i in range(2)]
    vtm_ps = P.ps([128, 512], F32, "vtm_ps")
    sc_ps = P.ps([128, 4, 128], F32, "sc_ps")
    o_ps = P.ps([128, 4, 128], F32, "o_ps")
    st_ps = P.ps([128, 4, 128], F32, "st_ps")
    tr_ps = P.ps([128, 4, 128], BF16, "tr_ps")
    stat_ps = P.ps([128, 2, 256], F32, "stat_ps", keys=["mean_ps", "msq_ps"])
    mean_ps = stat_ps[:, 0, 0:TT]
    msq_ps = stat_ps[:, 1, 0:TT]

    for nm, t in (("bin", bin_), ("convw", convw), ("convb", convb), ("clng", clng), ("clnb", clnb),
                  ("lb0", lb0), ("lb1", lb1), ("lbsel", lbsel), ("ong", ong), ("hsc", hsc), ("lng", lng),
                  ("lnb", lnb), ("bivbc", bivbc), ("mask", mask), ("rmask", rmask), ("S", S)):
        P.dma("sp", lambda e, t=t, nm=nm: e.dma_start(out=t[:], in_=io[nm]),
              r=([io.get("S_key", ("S_ext",))] if nm == "S" else []), w=[nm])
    id32 = P.sb([128, 128], F32, "id32")
    P.dma("sp", lambda e: e.dma_start(out=id32[:], in_=io["ident"]), w=["id32"])
    P.pool(lambda e: e.tensor_copy(out=ident[:], in_=id32[:]), r=["id32"], w=["ident"])
    if not state_only:
        for q_ in range(4 * 31):
            P.op(("pool", "dve")[q_ % 2], lambda e, q_=q_: e.tensor_scalar(
                out=convd[:, q_, :], in0=id32[:], scalar1=convw[:, q_:q_ + 1], scalar2=None, op0=ALU.mult),
                ["id32", "convw"], [("convd", q_)])
    P.pool(lambda e: e.memset(ones512[:], 1.0 / 512), w=["ones512"])
    P.pool(lambda e: e.memset(ones128[:], 1.0 / 128), w=["ones128"])
    P.pool(lambda e: e.memset(ones1024[:], 1.0 / D), w=["ones1024"])
    xh32 = P.sb([128, KC, HALO], F32, "xh32")
    P.dma("sp", lambda e: e.dma_start(out=xh32[:], in_=xin_v[:, :, 0:HALO]), r=[xk], w=["xh32"])
    P.pool(lambda e: e.tensor_copy(out=xhalo[:], in_=xh32[:]), r=["xh32"], w=["xhalo"])

    def load_xres(j):
        s = j % 2
        P.dma("sp", lambda e: e.dma_start(out=xres[s][:], in_=xin_v[:, :, HALO + j * TT:HALO + (j + 1) * TT]),
              r=[xk], w=[("xres", s)])
        P.pool(lambda e: e.tensor_copy(out=xbf[s][:], in_=xres[s][:]), r=[("xres", s)], w=[("xbf", s)])

    def load_x(j):
        pass

    load_xres(0)
    stg = Stager(P, 3072)
    winv = io["w_in"].rearrange("(kc p) f -> p kc f", p=128)
    woutv = io["w_out"].rearrange("(kc p) f -> p kc f", p=128)
    for kc in range(KC):
        stg.load(win[:, kc, :], winv[:, kc, :], 3072, ("win", kc))
    for kc in range(0, KC, 2):
        stg.load(wout[:, kc:kc + 2, :], woutv[:, kc:kc + 2, :], 2 * D, ("wout", kc // 2), a=2)
    P.dve(lambda e: e.tensor_scalar(out=nbin[:], in0=bin_[:], scalar1=-1.0, scalar2=None, op0=ALU.mult),
          r=["bin"], w=["nbin"])
    P.dve(lambda e: e.tensor_tensor(out=lb[:], in0=lb1[:], in1=lb0[:], op=ALU.subtract), r=["lb0", "lb1"], w=["lb"])
    P.act(lambda e: e.activation(out=lb[:], in_=lb[:], func=AF.Sigmoid), r=["lb"], w=["lb"])
    P.dve(lambda e: e.tensor_scalar(out=lb[:], in0=lb[:], scalar1=lbsel[:, 0:1], scalar2=None, op0=ALU.mult),
          r=["lb", "lbsel"], w=["lb"])
    P.dve(lambda e: e.tensor_scalar(out=oml[:], in0=lb[:], scalar1=-1.0, scalar2=1.0, op0=ALU.mult, op1=ALU.add),
          r=["lb"], w=["oml"])
    P.act(lambda e: e.activation(out=Sbf[:], in_=S[:], func=AF.Identity), r=["S"], w=["Sbf"])

    zcnt = [0]

    def proj(blk, rhs, n, rkeys):
        b = zcnt[0] % 4
        zcnt[0] += 1
        dst = zps[b // 2][:, b % 2, 0:n]
        for kc in range(KC):
            P.pe(_mm(dst, win[:, kc, blk * 128:(blk + 1) * 128], rhs[:, kc, :], kc == 0, kc == KC - 1),
                 r=[("win", kc)] + rkeys, w=[("zps", b)])
        return dst, ("zps", b)

    def glu_block(c, rhs, n, rkeys, dst_glu, dkey, col0, scale_ap=None):
        av, avk = proj(c, rhs, n, rkeys)
        ag, agk = proj(4 + c, rhs, n, rkeys)
        sgt = sg[c % 2]
        P.act(lambda e: e.activation(out=sgt[:, 0:n], in_=ag, func=AF.Sigmoid, bias=bin_[:, 4 + c:5 + c]),
              r=[agk, "bin"], w=[("sg", c % 2)])
        P.dve(lambda e: e.scalar_tensor_tensor(out=dst_glu[:, c, col0:col0 + n], in0=av, scalar=bin_[:, c:c + 1],
                                               in1=sgt[:, 0:n], op0=ALU.add, op1=ALU.mult),
              r=[avk, ("sg", c % 2), "bin"], w=[dkey + (c,)])
        if scale_ap is not None:
            P.dve(lambda e: e.tensor_scalar(out=dst_glu[:, c, col0:col0 + n], in0=dst_glu[:, c, col0:col0 + n],
                                            scalar1=scale_ap, scalar2=None, op0=ALU.mult),
                  r=[dkey + (c,), "hsc"], w=[dkey + (c,)])

    for c in (range(4) if full else ()):
        glu_block(c, xhalo, HALO, ["xhalo"], glu[1], ("glu", 1), TT, scale_ap=hsc[:, 0:1])

    for j in range(ntiles):
        s = j % 2
        if j + 1 < ntiles:
            load_x(j + 1)
        xb = xbf[s]
        G = glu[s]
        Gp = glu[1 - s]
        for c in (range(4) if full else ()):
            P.pool(lambda e, c=c, G=G, Gp=Gp: e.tensor_copy(out=G[:, c, 0:HALO], in_=Gp[:, c, TT:TT + HALO]),
                   r=[("glu", 1 - s, c)], w=[("glu", s, c)])
            glu_block(c, xb, TT, [("xbf", s)], G, ("glu", s), HALO)
            bq = zcnt[0] % 4
            zcnt[0] += 1
            cdst = zps[bq // 2][:, bq % 2, 0:TT]
            for tap in range(31):
                P.pe(_mm(cdst, convd[:, c * 31 + tap, :], G[:, c, tap:tap + TT], tap == 0, tap == 30),
                     r=[("convd", c * 31 + tap), ("glu", s, c)], w=[("zps", bq)])
            P.act(lambda e, c=c, cdst=cdst: e.activation(out=cacc[:, c, :], in_=cdst, func=AF.Identity,
                                                         bias=convb[:, c:c + 1]),
                  r=[("zps", bq), "convb"], w=[("cacc", c)])
        if full:
          emit_ln(P, cacc, ("cacc",), 4, TT, ones512, sq, mean_ps, msq_ps, mean_sb, var_sb, rstd_sb, yt,
                  clng, clnb, cat, ("cat",), LN_EPS, silu=True, lkeys=("clng", "clnb", "ones512"))
        for h in range(4):
            if full:
                qp, qk = proj(8 + h, xb, TT, [("xbf", s)])
                P.act(lambda e, qp=qp, h=h: e.activation(out=qs[:], in_=qp, func=AF.Silu, bias=bin_[:, 8 + h:9 + h]),
                      r=[qk, "bin"], w=["qs"])
            fp_, fk = proj(12 + h, xb, TT, [("xbf", s)])
            P.act(lambda e, fp_=fp_, h=h: e.activation(out=sgp[:], in_=fp_, func=AF.Sigmoid,
                                                       bias=bin_[:, 12 + h:13 + h]),
                  r=[fk, "bin"], w=["sgp"])
            P.act(lambda e, fp_=fp_, h=h: e.activation(out=sgn[:], in_=fp_, func=AF.Sigmoid, scale=-1.0,
                                                       bias=nbin[:, 12 + h:13 + h]),
                  r=[fk, "nbin"], w=["sgn"])
            P.dve(lambda e, h=h: e.tensor_scalar(out=logf[:], in0=sgp[:], scalar1=oml[:, h:h + 1],
                                                 scalar2=lb[:, h:h + 1], op0=ALU.mult, op1=ALU.add),
                  r=["sgp", "oml", "lb"], w=["logf"])
            P.act(lambda e: e.activation(out=logf[:], in_=logf[:], func=AF.Ln), r=["logf"], w=["logf"])
            P.dve(lambda e: e.tensor_tensor_scan(out=cum[:], data0=rmask[:], data1=logf[:], initial=0.0,
                                                 op0=ALU.mult, op1=ALU.add),
                  r=["rmask", "logf"], w=["cum"])
            if full:
                P.act(lambda e: e.activation(out=eq[:], in_=cum[:], func=AF.Exp), r=["cum"], w=["eq"])
            P.act(lambda e: e.activation(out=en[:], in_=cum[:], func=AF.Exp, scale=-1.0), r=["cum"], w=["en"])
            P.act(lambda e, h=h: e.activation(
                out=gC[:, h, :], in_=cum[:].rearrange("p (c t) -> p c t", t=CH)[:, :, CH - 1], func=AF.Exp),
                r=["cum"], w=[("gC", h)])
            if full:
                P.dve(lambda e, h=h: e.tensor_tensor(out=qt[:, h, :], in0=qs[:], in1=eq[:], op=ALU.mult),
                      r=["qs", "eq"], w=[("qt", h)])
            P.dve(lambda e, h=h: e.scalar_tensor_tensor(out=kt[:, h, :], in0=sgn[:], scalar=oml[:, h:h + 1],
                                                        in1=en[:], op0=ALU.mult, op1=ALU.mult),
                  r=["sgn", "en", "oml"], w=[("kt", h)])
            if full:
                P.act(lambda e, h=h: e.activation(out=ktb[:, h, :], in_=kt[:, h, :], func=AF.Identity),
                      r=[("kt", h)], w=[("ktb", h)])
            P.dve(lambda e, h=h: e.tensor_tensor(
                out=kh[:, h, :].rearrange("p (c t) -> p c t", t=CH),
                in0=kt[:, h, :].rearrange("p (c t) -> p c t", t=CH),
                in1=gC[:, h, :].unsqueeze(2).to_broadcast([128, TT // CH, CH]), op=ALU.mult),
                r=[("kt", h), ("gC", h)], w=[("kh", h)])
        for bi in range(nblk):
            tsl = slice(bi * 128, (bi + 1) * 128)
            for kc in range(KC):
                P.pe(_mm(vtm_ps[:], xb[:, kc, tsl], win[:, kc, 2048:2560], kc == 0, kc == KC - 1),
                     r=[("xbf", s), ("win", kc)], w=["vtm_ps"])
            P.dve(lambda e: e.tensor_tensor(out=vtm[:], in0=vtm_ps[:], in1=bivbc[:], op=ALU.add),
                  r=["vtm_ps", "bivbc"], w=["vtm"])
            for h in range(4):
                if full:
                    P.pe(_mm(sc_ps[:, h, :], ktb[:, h, tsl], qt[:, h, tsl], True, True),
                         r=[("ktb", h), ("qt", h)], w=[("sc_ps", h)])
                P.pe(lambda e, h=h, tsl=tsl: e.transpose(tr_ps[:, h, :], kh[:, h, tsl], ident[:]),
                     r=[("kh", h), "ident"], w=[("tr_ps", h)])
            if full:
                P.dve(lambda e: e.tensor_tensor(out=pT[:], in0=sc_ps[:],
                                                in1=mask[:].unsqueeze(1).to_broadcast([128, 4, 128]), op=ALU.mult),
                      r=["sc_ps", "mask"], w=["pT"])
            P.act(lambda e: e.activation(out=khT[:], in_=tr_ps[:], func=AF.Identity), r=["tr_ps"], w=["khT"])
            for h in (range(4) if full else ()):
                vh = vtm[:, h * 128:(h + 1) * 128]
                P.pe(_mm(o_ps[:, h, :], vh, pT[:, h, :], h == 0, False),
                     r=["vtm", "pT"], w=[("o_ps",)])
            for ci in range(2):
                csl = slice(ci * 64, (ci + 1) * 64)
                gcol = bi * 2 + ci
                for h in (range(4) if full else ()):
                    P.pe(_mm(o_ps[:, h, csl], Sbf[:, h, :], qt[:, h, bi * 128 + ci * 64:bi * 128 + (ci + 1) * 64],
                             False, ci == 1 and h == 3),
                         r=[("Sbf", h), ("qt", h)], w=[("o_ps",)])
                for h in range(4):
                    P.pe(_mm(st_ps[:, h, :], khT[csl, h, :], vtm[csl, h * 128:(h + 1) * 128], True, True),
                         r=["khT", "vtm"], w=[("st_ps", h)])
                for h in range(4):
                    P.dve(lambda e, h=h, gcol=gcol: e.scalar_tensor_tensor(
                        out=S[:, h, :], in0=S[:, h, :], scalar=gC[:, h, gcol:gcol + 1], in1=st_ps[:, h, :],
                        op0=ALU.mult, op1=ALU.add), r=[("S", h), ("gC", h), ("st_ps", h)], w=[("S", h)])
                    if full:
                        P.act(lambda e, h=h: e.activation(out=Sbf[:, h, :], in_=S[:, h, :], func=AF.Identity),
                              r=[("S", h)], w=[("Sbf", h)])
            if full:
                P.act(lambda e, tsl=tsl: e.activation(out=osb[:, :, tsl], in_=o_ps[:], func=AF.Identity),
                      r=["o_ps"], w=[("osb", bi)])
        for h in (range(4) if full else ()):
            if True:
                P.act(lambda e, h=h: e.activation(out=sqo[:], in_=osb[:, h, :], func=AF.Square), r=["osb"], w=["sqo"])
                P.pe(_mm(mean_ps, ones128[:], sqo[:], True, True), r=["ones128", "sqo"], w=["mean_ps"])
                P.dve(lambda e: e.tensor_scalar(out=var_sb[:], in0=mean_ps, scalar1=0.0, scalar2=1e-6,
                                                op0=ALU.max, op1=ALU.add), r=["mean_ps"], w=["var_sb"])
            if True:
                P.act(lambda e: e.activation(out=var_sb[:], in_=var_sb[:], func=AF.Ln), r=["var_sb"], w=["var_sb"])
                P.act(lambda e: e.activation(out=rstd_sb[:], in_=var_sb[:], func=AF.Exp, scale=-0.5), r=["var_sb"], w=["rstd_sb"])
            gp, gk = proj(20 + h, xb, TT, [("xbf", s)])
            P.act(lambda e, gp=gp, h=h: e.activation(out=gsl[:], in_=gp, func=AF.Silu, bias=bin_[:, 20 + h:21 + h]),
                  r=[gk, "bin"], w=["gsl"])
            if True:
                P.dve(lambda e, h=h: e.tensor_tensor(out=sqo[:], in0=osb[:, h, :], in1=rstd_sb[:], op=ALU.mult),
                      r=["osb", "rstd_sb"], w=["sqo"])
                P.dve(lambda e, h=h: e.scalar_tensor_tensor(out=cat[:, 4 + h, :], in0=sqo[:], scalar=ong[:, h:h + 1],
                                                            in1=gsl[:], op0=ALU.mult, op1=ALU.mult),
                      r=["sqo", "ong", "gsl"], w=[("cat", 4 + h)])
        if j + 1 < ntiles:
            load_xres(j + 1)
        xr = xres[s]
        if not full:
            continue
        for db in range(KC):
            b = zcnt[0] % 4
            zcnt[0] += 1
            dst = zps[b // 2][:, b % 2, 0:TT]
            for c in range(KC):
                P.pe(_mm(dst, wout[:, c, db * 128:(db + 1) * 128], cat[:, c, :], c == 0, c == KC - 1),
                     r=[("wout", c // 2), ("cat", c)], w=[("zps", b)])
            P.dve(lambda e, xr=xr, db=db, dst=dst: e.scalar_tensor_tensor(
                out=xr[:, db, :], in0=xr[:, db, :], scalar=float(ALPHA), in1=dst,
                op0=ALU.mult, op1=ALU.add), r=[("xres", s, db), ("zps", b)], w=[("xres", s, db)])
        emit_ln(P, xr, ("xres", s), KC, TT, ones1024, sq, mean_ps, msq_ps, mean_sb, var_sb, rstd_sb, yt,
                lng, lnb, xr, ("xres", s), LN_EPS, lkeys=("lng", "lnb", "ones1024"))
        P.dma("sp", lambda e, j=j, xr=xr: e.dma_start(out=out_v[:, :, j * TT:(j + 1) * TT], in_=xr[:]),
              r=[("xres", s)], w=[ok + (j,)])
    P.dma("sp", lambda e: e.dma_start(out=io["S_out"], in_=S[:]), r=["S"], w=[io.get("Sout_key", ("S_out",))])


def build_even_nc(NT, TT):
    nc = bass.Bass("TRN2", target_bir_lowering=False)
    io = {}

    def din(name, shape, dt=F32):
        io[name] = nc.dram_tensor(name, list(shape), dt, kind="ExternalInput").ap()

    din("xin", [D, HAL_E + NT])
    din("w_in", [D, 3072])
    din("w_out", [D, D])
    din("bin", [128, 24])
    din("convw", [128, 4 * 31])
    din("convb", [128, 4])
    din("clng", [128, 4])
    din("clnb", [128, 4])
    din("lb0", [128, 4])
    din("lb1", [128, 4])
    din("lbsel", [128, 1])
    din("ong", [128, 4])
    din("hsc", [128, 1])
    din("lng", [128, KC])
    din("lnb", [128, KC])
    din("bivbc", [128, 512])
    din("mask", [128, 128])
    din("rmask", [128, TT])
    din("ident", [128, 128])
    din("S", [128, 4, 128])
    io["out"] = nc.dram_tensor("out", [D, NT], F32, kind="ExternalOutput").ap()
    io["S_out"] = nc.dram_tensor("S_out", [128, 4, 128], F32, kind="ExternalOutput").ap()
    P = Prog(nc)
    P.begin_phase()
    build_even(nc, P, NT, TT, io)
    P.end_phase()
    P.finish()
    return nc, P


def even_consts(TT):
    i = np.arange(128)
    mask = ((i[:, None] // CH == i[None, :] // CH) & (i[:, None] <= i[None, :])).astype(np.float32)
    rmask = np.ones((128, TT), np.float32)
    rmask[:, ::CH] = 0.0
    return dict(mask=mask, rmask=rmask, ident=np.eye(128, dtype=np.float32))


def even_inputs(xin_T, w_in, b_in, conv_w, conv_b, cln_g, cln_b, lb_logits, j, onorm_g, w_out, g, b, S, first, TT):
    d = even_consts(TT)
    cwl = np.stack([vec_layout(conv_w[t], 4) for t in range(31)], axis=-1).reshape(128, 4 * 31)
    d.update(xin=np.ascontiguousarray(xin_T, dtype=np.float32), w_in=np.ascontiguousarray(w_in),
             w_out=np.ascontiguousarray(w_out), bin=vec_layout(b_in, 24), convw=np.ascontiguousarray(cwl),
             convb=vec_layout(conv_b, 4), clng=vec_layout(cln_g, 4), clnb=vec_layout(cln_b, 4),
             lb0=vec_layout(lb_logits[0], 4), lb1=vec_layout(lb_logits[1], 4),
             lbsel=np.full((128, 1), 1.0 if j == 1 else 0.0, np.float32),
             ong=vec_layout(onorm_g, 4), hsc=np.full((128, 1), 0.0 if first else 1.0, np.float32),
             lng=vec_layout(g, KC), lnb=vec_layout(b, KC),
             bivbc=np.ascontiguousarray(np.broadcast_to(np.asarray(b_in, np.float32)[2048:2560], (128, 512))),
             S=np.ascontiguousarray(S, dtype=np.float32))
    return d


C = 64
H = 16
ORDER_OVERRIDE = None
C0 = float(np.exp(-0.5))


def build_odd(nc, P, NT, io, state_only=False):
    TT = C
    ntiles = NT // TT
    xk = io.get("xin_key", ("xin_ext",))
    ok = io.get("out_key", ("outdram",))
    full = not state_only
    xin_v = io["xin"].rearrange("(kc p) t -> p kc t", p=128)
    out_v = io["out"].rearrange("(kc p) t -> p kc t", p=128)
    sb, ps = P.sb, P.ps
    wr, wk, wv = (sb([128, KC, D], BF16, n) for n in ("wr", "wk", "wv"))
    wo = sb([64, H, D], BF16, "wo")
    w1 = sb([128, KC, 64], BF16, "w1"); a1 = sb([128, KC, 64], BF16, "a1"); g1 = sb([128, KC, 160], BF16, "g1")
    w2 = sb([64, D], BF16, "w2"); a2 = sb([64, D], BF16, "a2")
    g2a = sb([128, D], BF16, "g2a"); g2b = sb([32, D], BF16, "g2b")
    mu = sb([128, 6, KC], F32, "mu")
    hv = {n: sb([64, H], F32, n) for n in ("w0h", "a0h", "kkh", "kah", "rkh", "gng", "gnb")}
    omka = sb([64, H], F32, "omka")
    lng = sb([128, KC], F32, "lng"); lnb = sb([128, KC], F32, "lnb")
    mUs = sb([64, 64], F32, "mUs"); mUi = sb([64, 64], F32, "mUi"); mLs = sb([64, 64], F32, "mLs")
    eye = sb([64, 64], F32, "eye"); rmask = sb([64, H * C], F32, "rmask")
    id32 = sb([64, 64], F32, "id32"); ident = sb([64, 64], BF16, "ident")
    ones64 = sb([64, 64], F32, "ones64"); ones1 = sb([64, 64], F32, "ones1"); ones1024 = sb([128, 128], F32, "ones1024")
    M = sb([64, H, 64], F32, "M"); Mb = sb([64, H, 64], BF16, "Mb")
    XT = [sb([128, KC, TT + 1], F32, f"XT{i}") for i in range(2)]
    xx = sb([128, KC, TT], F32, "xx"); xt_ = sb([128, KC, TT], F32, "xt_")
    xm = [sb([128, KC, TT], BF16, f"xm{i}") for i in range(2)]
    F = {n: sb([64, H, C], F32, n) for n in ("rF", "kF", "vF", "aF", "gF", "sg", "cum", "kap", "t0", "t1")}
    F["sg"] = F["sg"]; F["cum"] = F["cum"]
    B = {n: sb([64, H, C], BF16, n) for n in ("KT", "BT", "KK", "RT", "BH", "KH", "Tt",
                                               "AkT", "PbT", "PkT", "zB", "BHT", "KHT")}
    for n in ("N0", "N1", "Nt0", "Nt1"):
        for g_ in range(2):
            B[f"{n}_{g_}"] = sb([64, 8, C], BF16, f"{n}_{g_}")
    B["nZ"] = B["BH"]; B["U"] = B["KH"]

    def hd(t, h):
        return t[:, h, :] if t.shape[1] == H else t[:, h % 8, :]

    def grp(t, g):
        return t[:, g * 8:(g + 1) * 8, :] if t.shape[1] == H else t[:]
    Vtm = sb([64, D], BF16, "Vtm")
    twB = sb([64, C], BF16, "twB"); taB = sb([64, C], BF16, "taB"); tgA = sb([128, C], BF16, "tgA"); tgB = sb([32, C], BF16, "tgB")
    gC = sb([64, H], F32, "gC")
    sq8 = sb([128, KC, TT], F32, "sq8")
    mean_sb = sb([128, TT], F32, "mean_sb"); var_sb = sb([128, TT], F32, "var_sb"); rstd_sb = sb([128, TT], F32, "rstd_sb")
    pb = [ps([128, 512], F32, f"pb{i}") for i in range(6)]
    trp = ps([64, 8, 64], BF16, "trp")
    stat = ps([128, 2, 256], F32, "stat", keys=["mean_ps", "msq_ps"])
    mean_ps = stat[:, 0, 0:TT]; msq_ps = stat[:, 1, 0:TT]

    names = {}
    for d_ in (F, B, hv):
        for n_, t_ in d_.items():
            names.setdefault(id(t_), n_)
    for n_, t_ in (("mUs", mUs), ("mUi", mUi), ("mLs", mLs), ("eye", eye), ("twB", twB), ("taB", taB), ("tgA", tgA), ("tgB", tgB)):
        names[id(t_)] = n_

    def kn(t):
        return names[id(t)]

    def bk(i):
        return pb[i][0:64, :].rearrange("p (h c) -> p h c", c=64)

    def ld(t, nm):
        P.dma("sp", lambda e: e.dma_start(out=t[:], in_=io[nm]), r=([io.get("M_key", ("M_ext",))] if nm == "M" else []), w=[nm])

    for nm, t in list(hv.items()) + [("mu", mu), ("lng", lng), ("lnb", lnb), ("mUs", mUs), ("mUi", mUi), ("mLs", mLs),
                                     ("eye", eye), ("rmask_o", rmask), ("id32_o", id32), ("M", M)]:
        ld(t, nm)
    P.pool(lambda e: e.tensor_copy(out=ident[:], in_=id32[:]), r=["id32_o"], w=["ident"])
    P.pool(lambda e: e.memset(ones64[:], 1.0 / 64), w=["ones64"])
    P.pool(lambda e: e.memset(ones1[:], 1.0), w=["ones1"])
    P.pool(lambda e: e.memset(ones1024[:], 1.0 / D), w=["ones1024"])
    P.dve(lambda e: e.tensor_scalar(out=omka[:], in0=hv["kah"][:], scalar1=-1.0, scalar2=1.0, op0=ALU.mult, op1=ALU.add),
          r=["kah"], w=["omka"])
    P.act(lambda e: e.activation(out=Mb[:], in_=M[:], func=AF.Identity), r=["M"], w=["Mb"])
    stg = Stager(P, 1024)
    for nm, t in (("w_r", wr), ("w_k", wk), ("w_v", wv)):
        v = io[nm].rearrange("(kc p) f -> p kc f", p=128)
        for kc in range(KC):
            stg.load(t[:, kc, :], v[:, kc, :], 1024, (nm, kc // 2))
    wov = io["w_o"].rearrange("(h p) d -> p h d", p=64)

    def load64(dst, src, n, key, rows=64):
        i = stg.n % 2
        stg.n += 1
        st = stg.bufs[i]
        P.dma("sp", lambda e: e.dma_start(out=st[0:rows, 0:n], in_=src), w=[("stage", i)])
        cast_op(P, ("pool", "act", "dve")[stg.n % 3], dst, st[0:rows, 0:n], [("stage", i)], [key])

    for h in range(H):
        load64(wo[:, h, :], wov[:, h, :], D, ("wo", h // 2))
    for nm, t, n in (("w1", w1, 64), ("a1", a1, 64)):
        v = io[nm].rearrange("(kc p) f -> p kc f", p=128)
        stg.load(t[:], v, KC * n, nm, a=KC)
    g1v = io["g1"].rearrange("(kc p) f -> p kc f", p=128)
    for q in range(2):
        stg.load(g1[:, q * 4:(q + 1) * 4, :], g1v[:, q * 4:(q + 1) * 4, :], 640, ("g1", q), a=4)
    load64(w2[:], io["w2"], D, "w2")
    load64(a2[:], io["a2"], D, "a2")
    stg.load(g2a[:], io["g2"][0:128, :], D, "g2a")
    load64(g2b[:], io["g2"][128:160, :], D, "g2b", rows=32)

    def bc(v):
        return v[:].unsqueeze(2).to_broadcast([64, H, C])

    def tt(eng, out, a, b, op, r, w):
        P.op(eng, lambda e: e.tensor_tensor(out=out, in0=a, in1=b, op=op), r, w)

    def headproj(w_t, wkey, xmi, dstF, post=None):
        for g in range(2):
            bank = bk(g)
            for hh in range(8):
                h = g * 8 + hh
                for kc in range(KC):
                    P.pe(_mm(bank[:, hh, :], w_t[:, kc, h * 64:(h + 1) * 64], xm[xmi][:, kc, :], kc == 0, kc == KC - 1),
                         r=[(wkey, kc // 2), ("xm", xmi)], w=[("pb", g)])
            P.act(lambda e, g=g, bank=bank: e.activation(out=dstF[:, g * 8:(g + 1) * 8, :], in_=bank, func=AF.Identity),
                  r=[("pb", g)], w=[(kn(dstF), g)])

    for i in range(6):
        P.bank_of[("pb", i)] = f"pb{i}"

    def load_x(j):
        s = j % 2
        P.dma("sp", lambda e: e.dma_start(out=XT[s][:], in_=xin_v[:, :, j * TT:j * TT + TT + 1]), r=[xk], w=[("XT", s)])

    xm4 = [xm[0], xm[1], P.sb([128, KC, TT], BF16, "xm2"), P.sb([128, KC, TT], BF16, "xm3")]
    xr_ = P.sb([128, KC, TT], F32, "xr_")

    def mixop(X, s, i, slot):
        P.dve(lambda e: e.tensor_tensor(out=xt_[:], in0=xx[:], in1=mu[:, i, :].unsqueeze(2).to_broadcast([128, KC, TT]),
                                        op=ALU.mult), r=["xx", "mu"], w=["xt_"])
        P.dve(lambda e: e.tensor_tensor(out=xm4[slot][:], in0=xt_[:], in1=X[:, :, 1:TT + 1], op=ALU.add),
              r=["xt_", ("XT", s)], w=[("xm", slot)])

    stg_tm = [stg.bufs[i][0:64, :].bitcast(BF16)[:, 0:D] for i in range(2)]

    def hproj_mm(w_t, wkey, slot, banks, dst_tm, tmk):
        for half in range(2):
            bi = banks[half]
            for kc in range(KC):
                P.pe(_mm(pb[bi][0:64, :], xm4[slot][:, kc, :], w_t[:, kc, half * 512:(half + 1) * 512], kc == 0, kc == KC - 1),
                     r=[("xm", slot), (wkey, kc // 2)], w=[("pb", bi)])
            P.act(lambda e, half=half, bi=bi: e.activation(out=dst_tm[:, half * 512:(half + 1) * 512], in_=pb[bi][0:64, :],
                                                           func=AF.Identity), r=[("pb", bi)], w=[tmk + (half,)])

    def hproj_tr(dst_tm, tmk, dstF):
        for g in range(2):
            for hh in range(8):
                h = g * 8 + hh
                P.pe(lambda e, hh=hh, h=h: e.transpose(trp[:, hh, :], dst_tm[:, h * 64:(h + 1) * 64], ident[:]),
                     r=[tmk + (g,), "ident"], w=["trp"])
            P.act(lambda e, g=g: e.activation(out=dstF[:, g * 8:(g + 1) * 8, :], in_=trp[:], func=AF.Identity),
                  r=["trp"], w=[(kn(dstF), g)])

    def early_segments(j):
        s = j % 2
        X = XT[s]

        def e1():
            tt("dve", xx[:], X[:, :, 0:TT], X[:, :, 1:TT + 1], ALU.subtract, [("XT", s)], ["xx"])
            if full:
                mixop(X, s, 0, 0)
            mixop(X, s, 2, 1)
            mixop(X, s, 3, 2)
            if full:
                hproj_mm(wr, "w_r", 0, (0, 1), stg_tm[1], ("stage", 1))
            hproj_mm(wk, "w_k", 1, (2, 3), stg_tm[0], ("stage", 0))

        def e2():
            if full:
                hproj_tr(stg_tm[1], ("stage", 1), F["rF"])
            hproj_mm(wv, "w_v", 2, (0, 1), Vtm, ("Vtm",))

        def e3():
            hproj_tr(stg_tm[0], ("stage", 0), F["kF"])
            hproj_tr(Vtm, ("Vtm",), F["vF"])

        def e4():
            for (i, l1, l1k, mid_, fn_) in ((1, w1, "w1", twB, AF.Tanh), (4, a1, "a1", taB, AF.Identity)):
                mixop(X, s, i, 3)
                for kc in range(KC):
                    P.pe(_mm(pb[3][0:64, 0:C], l1[:, kc, :], xm4[3][:, kc, :], kc == 0, kc == KC - 1),
                         r=[l1k, ("xm", 3)], w=[("pb", 3)])
                P.act(lambda e, mid_=mid_, fn_=fn_: e.activation(out=mid_[:], in_=pb[3][0:64, 0:C], func=fn_),
                      r=[("pb", 3)], w=[kn(mid_)])
            if full:
                mixop(X, s, 5, 3)
                for (dst, lo, n) in ((tgA, 0, 128), (tgB, 128, 32)):
                    for kc in range(KC):
                        P.pe(_mm(pb[3][0:n, 0:C], g1[:, kc, lo:lo + n], xm4[3][:, kc, :], kc == 0, kc == KC - 1),
                             r=["g1", ("xm", 3)], w=[("pb", 3)])
                    P.act(lambda e, dst=dst, n=n: e.activation(out=dst[:], in_=pb[3][0:n, 0:C], func=AF.Sigmoid),
                          r=[("pb", 3)], w=[kn(dst)])
        return [e1, e2, e3, e4]

    def lora2(l2, l2k, mid_, bias, outF, outfunc):
        for g in range(2):
            bank = bk(g)
            for hh in range(8):
                h = g * 8 + hh
                P.pe(_mm(bank[:, hh, :], l2[:, h * 64:(h + 1) * 64], mid_[:], True, True),
                     r=[l2k, kn(mid_)], w=[("pb", g)])
            tt("dve", outF[:, g * 8:(g + 1) * 8, :], bank, bias[:, g * 8:(g + 1) * 8].unsqueeze(2).to_broadcast([64, 8, C]),
               ALU.add, [("pb", g), kn(bias)], [(kn(outF), g)])
        P.act(lambda e: e.activation(out=outF[:], in_=outF[:], func=outfunc), r=[kn(outF)], w=[kn(outF)])

    def mid(j):
        rF, kF, vF, aF, sg, cum, kap, t0, t1 = (F[n] for n in ("rF", "kF", "vF", "aF", "sg", "cum", "kap", "t0", "t1"))
        t2 = sg
        tt("dve", kap[:], kF[:], bc(hv["kkh"]), ALU.mult, ["kF", "kkh"], ["kap"])
        lora2(w2, "w2", twB, hv["w0h"], F["sg"], AF.Sigmoid)
        P.act(lambda e: e.activation(out=t0[:], in_=kap[:], func=AF.Square), r=["kap"], w=["t0"])
        t0f = t0[:].rearrange("p h c -> p (h c)")
        for half in range(2):
            P.pe(_mm(pb[2][0:64, :], ones1[:], t0f[:, half * 512:(half + 1) * 512], True, True), r=["ones1", "t0"], w=[("pb", 2)])
            P.dve(lambda e, half=half: e.tensor_scalar(out=t1[:].rearrange("p h c -> p (h c)")[:, half * 512:(half + 1) * 512],
                                                       in0=pb[2][0:64, :], scalar1=1e-18, scalar2=None, op0=ALU.max),
                  r=[("pb", 2)], w=[("t1", half)])
        lora2(a2, "a2", taB, hv["a0h"], F["aF"], AF.Sigmoid)
        P.act(lambda e: e.activation(out=t1[:], in_=t1[:], func=AF.Ln), r=["t1"], w=["t1"])
        P.act(lambda e: e.activation(out=t1[:], in_=t1[:], func=AF.Exp, scale=-0.5), r=["t1"], w=["t1"])
        for g in (range(2) if full else ()):
            bank = bk(g)
            for hh in range(8):
                h = g * 8 + hh
                P.pe(_mm(bank[:, hh, :], g2a[:, h * 64:(h + 1) * 64], tgA[:], True, False), r=["g2a", kn(tgA)], w=[("pb", g)])
                P.pe(_mm(bank[:, hh, :], g2b[:, h * 64:(h + 1) * 64], tgB[:], False, True), r=["g2b", kn(tgB)], w=[("pb", g)])
            P.act(lambda e, g=g, bank=bank: e.activation(out=F["gF"][:, g * 8:(g + 1) * 8, :], in_=bank, func=AF.Identity),
                  r=[("pb", g)], w=[("gF", g)])
        tt("dve", kap[:], kap[:], t1[:], ALU.mult, ["kap", "t1"], ["kap"])
        tt("pool", t0[:], aF[:], bc(hv["kah"]), ALU.mult, ["aF", "kah"], ["t0"])
        tt("pool", t0[:], t0[:], bc(omka), ALU.add, ["t0", "omka"], ["t0"])
        tt("pool", kF[:], kF[:], t0[:], ALU.mult, ["kF", "t0"], ["kF"])
        P.dve(lambda e: e.tensor_tensor_scan(out=cum[:].rearrange("p h c -> p (h c)"), data0=rmask[:],
                                             data1=sg[:].rearrange("p h c -> p (h c)"), initial=0.0,
                                             op0=ALU.mult, op1=ALU.add), r=["rmask_o", "sg"], w=["cum"])
        tt("dve", t0[:], cum[:], sg[:], ALU.subtract, ["cum", "sg"], ["t0"])
        tt("dve", t2[:], kap[:], aF[:], ALU.mult, ["kap", "aF"], ["sg"])
        P.act(lambda e: e.activation(out=t0[:], in_=t0[:], func=AF.Exp, scale=-C0), r=["t0"], w=["t0"])
        tt("dve", B["KT"][:], kap[:], t0[:], ALU.mult, ["kap", "t0"], ["KT"])
        P.act(lambda e: e.activation(out=t0[:], in_=cum[:], func=AF.Exp, scale=-C0), r=["cum"], w=["t0"])
        if full:
            tt("dve", B["RT"][:], rF[:], t0[:], ALU.mult, ["rF", "t0"], ["RT"])
        P.act(lambda e: e.activation(out=gC[:], in_=cum[:, :, C - 1], func=AF.Exp, scale=-C0), r=["cum"], w=["gC"])
        P.act(lambda e: e.activation(out=t1[:], in_=cum[:], func=AF.Exp, scale=C0), r=["cum"], w=["t1"])
        tt("dve", t2[:], t2[:], t1[:], ALU.mult, ["sg", "t1"], ["sg"])
        tt("dve", t1[:], kF[:], t1[:], ALU.mult, ["kF", "t1"], ["t1"])
        P.act(lambda e: e.activation(out=B["BT"][:], in_=t2[:], func=AF.Identity), r=["sg"], w=["BT"])
        P.act(lambda e: e.activation(out=B["KK"][:], in_=t1[:], func=AF.Identity), r=["t1"], w=["KK"])
        tt("dve", B["BH"][:], t2[:], bc(gC), ALU.mult, ["sg", "gC"], ["BH"])
        tt("dve", B["KH"][:], t1[:], bc(gC), ALU.mult, ["t1", "gC"], ["KH"])
        if full:
            tt("dve", t0[:], rF[:], kF[:], ALU.mult, ["rF", "kF"], ["t0"])
            tt("dve", t0[:], t0[:], bc(hv["rkh"]), ALU.mult, ["t0", "rkh"], ["t0"])
            t0f_ = t0[:].rearrange("p h c -> p (h c)")
            for half in range(2):
                hs = slice(half * 512, (half + 1) * 512)
                P.pe(_mm(pb[2][0:64, :], ones1[:], t0f_[:, hs], True, True), r=["ones1", "t0"], w=[("pb", 2)])
                tt("dve", kap[:].rearrange("p h c -> p (h c)")[:, hs], pb[2][0:64, :],
                   vF[:].rearrange("p h c -> p (h c)")[:, hs], ALU.mult, [("pb", 2), "vF"], [("kap", half)])
        P.stage = f"mid{j}_tr"
        for src, dst in (("BH", "BHT"), ("KH", "KHT")):
            for g in range(2):
                for hh in range(8):
                    h = g * 8 + hh
                    P.pe(lambda e, hh=hh, h=h, src=src: e.transpose(trp[:, hh, :], B[src][:, h, :], ident[:]),
                         r=[src, "ident"], w=["trp"])
                P.act(lambda e, g=g, dst=dst: e.activation(out=B[dst][:, g * 8:(g + 1) * 8, :], in_=trp[:], func=AF.Identity),
                      r=["trp"], w=[(dst, g)])
        P.stage = f"mid{j}_scores"
        BS = ((0, 1, 2), (3, 4, 5))

        def score(lhs, rhs, dst, mask, neg, g, bank_i):
            bank = bk(bank_i)
            for hh in range(8):
                h = g * 8 + hh
                P.pe(_mm(bank[:, hh, :], B[lhs][:, h, :], B[rhs][:, h, :], True, True), r=[lhs, rhs], w=[("pb", bank_i)])
            P.dve(lambda e: e.scalar_tensor_tensor(out=grp(B[dst], g), in0=bank, scalar=(-1.0 if neg else 1.0),
                                                   in1=mask[:].unsqueeze(1).to_broadcast([64, 8, 64]),
                                                   op0=ALU.mult, op1=ALU.mult), r=[("pb", bank_i), kn(mask)], w=[(dst, g)])

        def NB(name, g):
            return B[f"{name}_{g}"]

        for g in range(2):
            score("KT", "BT", f"N0_{g}", mLs, True, g, BS[g][0])
            score("BT", "KT", f"Nt0_{g}", mUs, True, g, BS[g][1])
        for g in range(2):
            score("KK", "KT", "AkT", mUs, False, g, BS[g][2])
            if full:
                score("BT", "RT", "PbT", mUi, False, g, BS[g][0])
        for g in range(2):
            gs = slice(g * 8, (g + 1) * 8)
            if full:
                score("KK", "RT", "PkT", mUi, False, g, BS[g][1])
            tt("dve", B["Tt"][:, gs, :], NB("Nt0", g)[:], eye[:].unsqueeze(1).to_broadcast([64, 8, 64]), ALU.add,
               [(f"Nt0_{g}", g), "eye"], [("Tt", g)])
        cur = 0
        P.stage = f"mid{j}_dbl"
        for lv in range(5):
            for g in range(2):
                gs = slice(g * 8, (g + 1) * 8)
                Nc, Ntc = NB(f"N{cur}", g), NB(f"Nt{cur}", g)
                Nn, Ntn = NB(f"N{1 - cur}", g), NB(f"Nt{1 - cur}", g)
                kNc, kNtc = (f"N{cur}_{g}", g), (f"Nt{cur}_{g}", g)
                kNn, kNtn = (f"N{1 - cur}_{g}", g), (f"Nt{1 - cur}_{g}", g)
                bA, bB, bCk = (bk(i) for i in BS[g])
                kA, kB, kC = (("pb", i) for i in BS[g])
                for hh in range(8):
                    P.pe(_mm(bA[:, hh, :], Ntc[:, hh, :], Nc[:, hh, :], True, True), r=[kNc, kNtc], w=[kA])
                P.act(lambda e, Nn=Nn, bA=bA: e.activation(out=Nn[:], in_=bA, func=AF.Identity), r=[kA], w=[kNn])
                if lv < 4:
                    for hh in range(8):
                        P.pe(_mm(bB[:, hh, :], Nc[:, hh, :], Ntc[:, hh, :], True, True), r=[kNc, kNtc], w=[kB])
                    P.act(lambda e, Ntn=Ntn, bB=bB: e.activation(out=Ntn[:], in_=bB, func=AF.Identity), r=[kB], w=[kNtn])
                for hh in range(8):
                    h = g * 8 + hh
                    P.pe(_mm(bCk[:, hh, :], Nn[:, hh, :], B["Tt"][:, h, :], True, True), r=[kNn, ("Tt", g)], w=[kC])
                tt("dve", B["Tt"][:, gs, :], B["Tt"][:, gs, :], bCk, ALU.add, [("Tt", g), kC], [("Tt", g)])
            cur = 1 - cur
        P.stage = f"mid{j}_state"
        for g in range(2):
            gs = slice(g * 8, (g + 1) * 8)
            bZ = bk(BS[g][0]); kZ = ("pb", BS[g][0])
            for hh in range(8):
                h = g * 8 + hh
                vh = Vtm[:, h * 64:(h + 1) * 64]
                P.pe(_mm(bZ[:, hh, :], B["KT"][:, h, :], Mb[:, h, :], True, False), r=["KT", ("Mb", g)], w=[kZ])
                P.pe(_mm(bZ[:, hh, :], B["AkT"][:, h, :], vh, False, True), r=[("AkT", g), "Vtm"], w=[kZ])
            P.act(lambda e, bZ=bZ, gs=gs: e.activation(out=B["nZ"][:, gs, :], in_=bZ, func=AF.Identity, scale=-1.0),
                  r=[kZ], w=[("BH", g)])
        for g in range(2):
            gs = slice(g * 8, (g + 1) * 8)
            bU = bk(BS[g][1]); kU = ("pb", BS[g][1])
            for hh in range(8):
                h = g * 8 + hh
                P.pe(_mm(bU[:, hh, :], B["Tt"][:, h, :], B["nZ"][:, h, :], True, True), r=[("Tt", g), ("BH", g)], w=[kU])
            P.act(lambda e, bU=bU, gs=gs: e.activation(out=B["U"][:, gs, :], in_=bU, func=AF.Identity), r=[kU], w=[("KH", g)])
        for g in range(2):
            gs = slice(g * 8, (g + 1) * 8)
            bM = bk(BS[g][2]); kM = ("pb", BS[g][2])
            bY = bk(BS[g][0]); kY = ("pb", BS[g][0])
            for hh in (range(8) if full else ()):
                h = g * 8 + hh
                vh = Vtm[:, h * 64:(h + 1) * 64]
                P.pe(_mm(bY[:, hh, :], Mb[:, h, :], B["RT"][:, h, :], True, False), r=[("Mb", g), "RT"], w=[kY])
                P.pe(_mm(bY[:, hh, :], B["U"][:, h, :], B["PbT"][:, h, :], False, False), r=[("KH", g), ("PbT", g)], w=[kY])
                P.pe(_mm(bY[:, hh, :], vh, B["PkT"][:, h, :], False, True), r=["Vtm", ("PkT", g)], w=[kY])
            if full:
                P.act(lambda e, bY=bY, gs=gs: e.activation(out=F["cum"][:, gs, :], in_=bY, func=AF.Identity), r=[kY], w=[("cum", g)])
            for hh in range(8):
                h = g * 8 + hh
                vh = Vtm[:, h * 64:(h + 1) * 64]
                P.pe(_mm(bM[:, hh, :], B["BHT"][:, h, :], B["U"][:, h, :], True, False), r=[("BHT", g), ("KH", g)], w=[kM])
                P.pe(_mm(bM[:, hh, :], B["KHT"][:, h, :], vh, False, True), r=[("KHT", g), "Vtm"], w=[kM])
            tt("dve", M[:, gs, :], M[:, gs, :], gC[:, gs].unsqueeze(2).to_broadcast([64, 8, 64]), ALU.mult,
               [("M", g), "gC"], [("M", g)])
            tt("dve", M[:, gs, :], M[:, gs, :], bM, ALU.add, [("M", g), kM], [("M", g)])
            P.act(lambda e, gs=gs: e.activation(out=Mb[:, gs, :], in_=M[:, gs, :], func=AF.Identity), r=[("M", g)], w=[("Mb", g)])
    def late_segments(j):
        s = j % 2
        X = XT[s]
        rF, kF, vF, aF, sg, cum, kap, t0, t1 = (F[n] for n in ("rF", "kF", "vF", "aF", "sg", "cum", "kap", "t0", "t1"))
        t2 = sg
        Y = F["cum"]
        Yf = Y[:].rearrange("p h c -> p (h c)")
        t0f = t0[:].rearrange("p h c -> p (h c)"); t1f = t1[:].rearrange("p h c -> p (h c)"); t2f = t2[:].rearrange("p h c -> p (h c)")

        def l1():
            P.act(lambda e: e.activation(out=t0[:], in_=Y[:], func=AF.Square), r=["cum"], w=["t0"])
            for half in range(2):
                hs = slice(half * 512, (half + 1) * 512)
                P.pe(_mm(pb[4][0:64, :], ones64[:], Yf[:, hs], True, True), r=["ones64", "cum"], w=[("pb", 4)])
                P.pe(_mm(pb[5][0:64, :], ones64[:], t0f[:, hs], True, True), r=["ones64", "t0"], w=[("pb", 5)])
                P.act(lambda e, hs=hs: e.activation(out=t1f[:, hs], in_=pb[4][0:64, :], func=AF.Identity), r=[("pb", 4)], w=[("t1", half)])
                tt("dve", t2f[:, hs], t1f[:, hs], t1f[:, hs], ALU.mult, [("t1", half)], [("sg", half)])
                tt("dve", t2f[:, hs], pb[5][0:64, :], t2f[:, hs], ALU.subtract, [("pb", 5), ("sg", half)], [("sg", half)])
            P.dve(lambda e: e.tensor_scalar(out=t2[:], in0=t2[:], scalar1=0.0, scalar2=64e-5, op0=ALU.max, op1=ALU.add), r=["sg"], w=["sg"])
            P.act(lambda e: e.activation(out=t2[:], in_=t2[:], func=AF.Ln), r=["sg"], w=["sg"])
            P.act(lambda e: e.activation(out=t2[:], in_=t2[:], func=AF.Exp, scale=-0.5), r=["sg"], w=["sg"])

        def l2():
            tt("dve", Y[:], Y[:], t1[:], ALU.subtract, ["cum", "t1"], ["cum"])
            tt("dve", Y[:], Y[:], t2[:], ALU.mult, ["cum", "sg"], ["cum"])
            tt("dve", Y[:], Y[:], bc(hv["gng"]), ALU.mult, ["cum", "gng"], ["cum"])
            tt("dve", Y[:], Y[:], bc(hv["gnb"]), ALU.add, ["cum", "gnb"], ["cum"])
            tt("dve", Y[:], Y[:], kap[:], ALU.add, ["cum", "kap"], ["cum"])
            tt("dve", B["zB"][:], Y[:], F["gF"][:], ALU.mult, ["cum", "gF"], ["zB"])

        def l3():
            ob = pb[4][:, :].rearrange("p (c t) -> p c t", t=TT)
            for db in range(KC):
                for h in range(H):
                    P.pe(_mm(ob[:, db, :], wo[:, h, db * 128:(db + 1) * 128], B["zB"][:, h, :], h == 0, h == H - 1),
                         r=[("wo", h // 2), "zB"], w=[("pb", 4)])
            P.dve(lambda e: e.scalar_tensor_tensor(out=xr_[:], in0=X[:, :, 1:TT + 1], scalar=float(ALPHA), in1=ob,
                                                   op0=ALU.mult, op1=ALU.add), r=[("XT", s), ("pb", 4)], w=["xr_"])

        def l4():
            for c in range(KC):
                P.pe(_mm(mean_ps, ones1024[:], xr_[:, c, :], c == 0, c == KC - 1), r=["ones1024", "xr_"], w=["mean_ps"])
            P.act(lambda e: e.activation(out=sq8[:], in_=xr_[:], func=AF.Square), r=["xr_"], w=["sq8"])
            for c in range(KC):
                P.pe(_mm(msq_ps, ones1024[:], sq8[:, c, :], c == 0, c == KC - 1), r=["ones1024", "sq8"], w=["msq_ps"])
            P.act(lambda e: e.activation(out=mean_sb[:], in_=mean_ps, func=AF.Identity), r=["mean_ps"], w=["mean_sb"])
            tt("dve", var_sb[:], mean_sb[:], mean_sb[:], ALU.mult, ["mean_sb"], ["var_sb"])
            tt("dve", var_sb[:], msq_ps, var_sb[:], ALU.subtract, ["msq_ps", "var_sb"], ["var_sb"])
            P.dve(lambda e: e.tensor_scalar(out=var_sb[:], in0=var_sb[:], scalar1=0.0, scalar2=float(LN_EPS),
                                            op0=ALU.max, op1=ALU.add), r=["var_sb"], w=["var_sb"])
            P.act(lambda e: e.activation(out=var_sb[:], in_=var_sb[:], func=AF.Ln), r=["var_sb"], w=["var_sb"])
            P.act(lambda e: e.activation(out=rstd_sb[:], in_=var_sb[:], func=AF.Exp, scale=-0.5), r=["var_sb"], w=["rstd_sb"])
            tt("dve", xr_[:], xr_[:], mean_sb[:].unsqueeze(1).to_broadcast([128, KC, TT]), ALU.subtract, ["xr_", "mean_sb"], ["xr_"])
            tt("dve", xr_[:], xr_[:], rstd_sb[:].unsqueeze(1).to_broadcast([128, KC, TT]), ALU.mult, ["xr_", "rstd_sb"], ["xr_"])
            tt("dve", xr_[:], xr_[:], lng[:].unsqueeze(2).to_broadcast([128, KC, TT]), ALU.mult, ["xr_", "lng"], ["xr_"])
            tt("dve", xr_[:], xr_[:], lnb[:].unsqueeze(2).to_broadcast([128, KC, TT]), ALU.add, ["xr_", "lnb"], ["xr_"])
            P.dma("sp", lambda e: e.dma_start(out=out_v[:, :, j * TT:(j + 1) * TT], in_=xr_[:]), r=["xr_"], w=[ok + (j,)])
        return [l1, l2, l3, l4]

    load_x(0)
    if ntiles > 1:
        load_x(1)
    for seg in early_segments(0):
        seg()
    for j in range(ntiles):
        P.stage = f"mid{j}"
        mid(j)
        E = early_segments(j + 1) if j + 1 < ntiles else []
        L = late_segments(j) if full else []
        order = [("L", 0), ("E", 0), ("L", 1), ("E", 1), ("L", 2), ("E", 2), ("L", 3), ("E", 3)]
        order = ORDER_OVERRIDE or order
        for kind, k in order:
            lst = L if kind == "L" else E
            if k < len(lst):
                P.stage = (f"late{j}" if kind == "L" else f"early{j + 1}")
                lst[k]()
        if j + 2 < ntiles:
            load_x(j + 2)
    P.dma("sp", lambda e: e.dma_start(out=io["M_out"], in_=M[:]), r=["M"], w=[io.get("Mout_key", ("M_out",))])


ODD_IN = dict(w_r=[D, D], w_k=[D, D], w_v=[D, D], w_o=[D, D], w1=[D, 64], a1=[D, 64], g1=[D, 160], w2=[64, D], a2=[64, D],
              g2=[160, D], mu=[128, 6, KC], w0h=[64, H], a0h=[64, H], kkh=[64, H], kah=[64, H], rkh=[64, H], gng=[64, H],
              gnb=[64, H], lng=[128, KC], lnb=[128, KC], mUs=[64, 64], mUi=[64, 64], mLs=[64, 64], eye=[64, 64],
              rmask_o=[64, H * C], id32_o=[64, 64], M=[64, H, 64])


def build_odd_nc(NT):
    nc = bass.Bass("TRN2", target_bir_lowering=False)
    io = {"xin": nc.dram_tensor("xin", [D, 1 + NT], F32, kind="ExternalInput").ap()}
    for k, shp in ODD_IN.items():
        io[k] = nc.dram_tensor(k, list(shp), F32, kind="ExternalInput").ap()
    io["out"] = nc.dram_tensor("out", [D, NT], F32, kind="ExternalOutput").ap()
    io["M_out"] = nc.dram_tensor("M_out", [64, H, 64], F32, kind="ExternalOutput").ap()
    P = Prog(nc)
    P.begin_phase()
    build_odd(nc, P, NT, io)
    P.end_phase()
    P.finish()
    return nc, P


def hvec(v):
    return np.ascontiguousarray(np.asarray(v, np.float32).reshape(H, 64).T)


def odd_inputs(xin_T, p, j, lng, lnb, M):
    i = np.arange(64)
    d = dict(xin=np.ascontiguousarray(xin_T, dtype=np.float32))
    for k in ("w_r", "w_k", "w_v", "w_o", "w1", "a1", "g1", "w2", "a2", "g2"):
        d[k] = np.ascontiguousarray(p["rw_" + k][j], dtype=np.float32)
    d["mu"] = np.ascontiguousarray(np.stack([vec_layout(p["rw_mu"][j][i6], KC) for i6 in range(6)], axis=1))
    d["w0h"] = hvec(p["rw_w0"][j]); d["a0h"] = hvec(p["rw_a0"][j]); d["kkh"] = hvec(p["rw_k_k"][j])
    d["kah"] = hvec(p["rw_k_a"][j]); d["rkh"] = hvec(p["rw_r_k"][j].reshape(-1)); d["gng"] = hvec(p["rw_gn_g"][j])
    d["gnb"] = hvec(p["rw_gn_b"][j]); d["lng"] = vec_layout(lng, KC); d["lnb"] = vec_layout(lnb, KC)
    d["mUs"] = (i[:, None] < i[None, :]).astype(np.float32); d["mUi"] = (i[:, None] <= i[None, :]).astype(np.float32)
    d["mLs"] = (i[:, None] > i[None, :]).astype(np.float32); d["eye"] = np.eye(64, dtype=np.float32)
    rm = np.ones((64, H * C), np.float32); rm[:, ::C] = 0.0
    d["rmask_o"] = rm; d["id32_o"] = np.eye(64, dtype=np.float32); d["M"] = np.ascontiguousarray(M, dtype=np.float32)
    return d


NCORES = 8
SEQ = 8192
NTC = SEQ // 2
TT_E = 256
TT_F = 256
HMAX = 30
PAIRS = [[0, 1], [2, 3], [4, 5], [6, 7]]

EVEN_L = dict(w_in=[D, 3072], w_out=[D, D], bin=[128, 24], convw=[128, 4 * 31], convb=[128, 4], clng=[128, 4],
              clnb=[128, 4], lb0=[128, 4], lb1=[128, 4], lbsel=[128, 1], ong=[128, 4], lng=[128, KC], lnb=[128, KC],
              bivbc=[128, 512])
EVEN_C = dict(mask=[128, 128], rmask=[128, TT_E], ident=[128, 128])
ODD_C = ("mUs", "mUi", "mLs", "eye", "rmask_o", "id32_o")
FFN_L = dict(w_up=[D, DFF], w_gate=[D, DFF], w_down=[DFF, D], cw=[128, FB * 3], cb=[128, FB], lng=[128, KC], lnb=[128, KC])


def _exchange(P, nc, tag, src_ap, src_key, rows, cols, bin_t, bout_t, dst_ap, dst_key, sel_ap, pairs=None):
    P.begin_phase(tag)
    np_ = min(rows, 128)
    nch = rows // np_
    t = P.sb([np_, nch, cols], F32, "xt")
    selt = P.sb([128, 1], F32, "sel")
    P.dma("sp", lambda e: e.dma_start(out=bin_t.ap(), in_=src_ap), r=[src_key], w=[(tag, "bin")])
    P.cc(lambda e: e.collective_compute("AllGather", ALU.bypass, replica_groups=(pairs or PAIRS),
                                        ins=[bin_t.ap().opt()], outs=[bout_t.ap().opt()]),
         r=[(tag, "bin")], w=[(tag, "bout")])
    P.dma("sp", lambda e: e.dma_start(out=t[:], in_=bout_t.ap()[0:rows, :].rearrange("(c p) t -> p c t", p=np_)),
          r=[(tag, "bout")], w=["xt"])
    P.dma("sp", lambda e: e.dma_start(out=selt[:], in_=sel_ap), w=["selt"])
    P.dve(lambda e: e.tensor_scalar(out=t[:], in0=t[:], scalar1=selt[0:np_, 0:1], scalar2=None, op0=ALU.mult),
          r=["xt", "selt"], w=["xt"])
    P.dma("sp", lambda e: e.dma_start(out=dst_ap.rearrange("(c p) t -> p c t", p=np_), in_=t[:]), r=["xt"], w=[dst_key])
    P.end_phase()


def build_fused_nc(NT=NTC, nlayers=DEPTH, ncores=NCORES):
    nc = bass.Bass("TRN2", target_bir_lowering=False)
    pairs = [[2 * i, 2 * i + 1] for i in range(ncores // 2)]

    def din(name, shape):
        return nc.dram_tensor(name, list(shape), F32, kind="ExternalInput").ap()

    def dint(name, shape):
        return nc.dram_tensor(name, list(shape), F32)

    xin = din("xin", [D, HMAX + NT])
    sel = din("sel", [128, 1])
    zst = din("zst", [128, 1024])
    out = nc.dram_tensor("out", [D, NT], F32, kind="ExternalOutput").ap()
    consts = {k: din(k, v) for k, v in EVEN_C.items()}
    for k in ODD_C:
        consts[k] = din(k, ODD_IN[k])
    slab = [dint(f"slab{i}", [D, HMAX + NT]) for i in range(2)]
    hx_in = dint("hx_in", [D, HMAX]); hx_out = dint("hx_out", [2 * D, HMAX])
    se_in = dint("se_in", [128, 512]); se_out = dint("se_out", [256, 512]); se_sel = dint("se_sel", [128, 512])
    so_in = dint("so_in", [64, 1024]); so_out = dint("so_out", [128, 1024]); so_sel = dint("so_sel", [64, 1024])
    dump = dint("dump", [128, 1024])
    P = Prog(nc)
    for layer in range(nlayers):
        src = xin if layer == 0 else slab[1].ap()
        skey = ("xin_ext",) if layer == 0 else ("slab", 1)
        if layer > 0:
            _exchange(P, nc, f"hx{layer}m_", slab[1].ap()[:, NT:NT + HMAX], ("slab", 1), D, HMAX, hx_in, hx_out,
                      slab[1].ap()[:, 0:HMAX], ("slab", 1, "halo"), sel, pairs)
        if layer % 2 == 0:
            io = {k: din(f"{k}_{layer}", v) for k, v in EVEN_L.items()}
            io.update(consts)
            io["hsc"] = sel
            io["xin"] = src
            io["xin_key"] = skey
            ioa = dict(io); ioa["S"] = zst[:, 0:512].rearrange("p (h v) -> p h v", h=4); ioa["S_key"] = ("zst",)
            ioa["out"] = slab[0].ap()[:, HMAX:HMAX + NT]; ioa["S_out"] = se_in.ap().rearrange("p (h v) -> p h v", h=4)
            ioa["Sout_key"] = ("se_in",)
            P.begin_phase(f"L{layer}a_"); build_even(nc, P, NT, TT_E, ioa, state_only=True); P.end_phase()
            _exchange(P, nc, f"sx{layer}_", se_in.ap(), ("se_in",), 128, 512, se_in, se_out, se_sel.ap(), ("se_sel",), sel, pairs)
            iob = dict(io); iob["S"] = se_sel.ap().rearrange("p (h v) -> p h v", h=4); iob["S_key"] = ("se_sel",)
            iob["out"] = slab[0].ap()[:, HMAX:HMAX + NT]; iob["out_key"] = ("slab", 0, "body")
            iob["S_out"] = dump.ap()[:, 0:512].rearrange("p (h v) -> p h v", h=4); iob["Sout_key"] = ("dump",)
            P.begin_phase(f"L{layer}b_"); build_even(nc, P, NT, TT_E, iob); P.end_phase()
        else:
            io = {k: din(f"{k}_{layer}", v) for k, v in ODD_IN.items() if k not in ODD_C and k != "M"}
            io.update(consts)
            io["xin"] = src[:, HMAX - 1:HMAX + NT]
            io["xin_key"] = skey
            ioa = dict(io); ioa["M"] = zst[0:64, :].rearrange("p (h v) -> p h v", h=H); ioa["M_key"] = ("zst",)
            ioa["out"] = slab[0].ap()[:, HMAX:HMAX + NT]; ioa["M_out"] = so_in.ap().rearrange("p (h v) -> p h v", h=H)
            ioa["Mout_key"] = ("so_in",)
            P.begin_phase(f"L{layer}a_"); build_odd(nc, P, NT, ioa, state_only=True); P.end_phase()
            _exchange(P, nc, f"sx{layer}_", so_in.ap(), ("so_in",), 64, 1024, so_in, so_out, so_sel.ap(), ("so_sel",), sel, pairs)
            iob = dict(io); iob["M"] = so_sel.ap().rearrange("p (h v) -> p h v", h=H); iob["M_key"] = ("so_sel",)
            iob["out"] = slab[0].ap()[:, HMAX:HMAX + NT]; iob["out_key"] = ("slab", 0, "body")
            iob["M_out"] = dump.ap()[0:64, :].rearrange("p (h v) -> p h v", h=H); iob["Mout_key"] = ("dump",)
            P.begin_phase(f"L{layer}b_"); build_odd(nc, P, NT, iob); P.end_phase()
        _exchange(P, nc, f"hx{layer}f_", slab[0].ap()[:, NT:NT + HMAX], ("slab", 0), D, HMAX, hx_in, hx_out,
                  slab[0].ap()[:, 0:HMAX], ("slab", 0, "halo"), sel, pairs)
        iof = {k: din(f"{k}_f{layer}", v) for k, v in FFN_L.items()}
        iof["xin"] = slab[0].ap()[:, HMAX - 2:HMAX + NT]; iof["xin_key"] = ("slab", 0)
        last = layer == nlayers - 1
        iof["out"] = out if last else slab[1].ap()[:, HMAX:HMAX + NT]
        iof["out_key"] = ("out_ext",) if last else ("slab", 1, "body")
        P.begin_phase(f"L{layer}f_"); build_ffn(nc, P, NT, TT_F, iof); P.end_phase()
    P.finish()
    return nc, P


def fused_inputs(inp, NT=NTC, nlayers=DEPTH, x_slabs=None):
    base = {}
    ec = even_consts(TT_E)
    base.update(ec)
    i = np.arange(64)
    base["mUs"] = (i[:, None] < i[None, :]).astype(np.float32); base["mUi"] = (i[:, None] <= i[None, :]).astype(np.float32)
    base["mLs"] = (i[:, None] > i[None, :]).astype(np.float32); base["eye"] = np.eye(64, dtype=np.float32)
    rm = np.ones((64, H * C), np.float32); rm[:, ::C] = 0.0
    base["rmask_o"] = rm; base["id32_o"] = np.eye(64, dtype=np.float32)
    base["zst"] = np.zeros((128, 1024), np.float32)
    dummy_x = np.zeros((D, 1), np.float32)
    for layer in range(nlayers):
        j = layer // 2
        if layer % 2 == 0:
            d = even_inputs(dummy_x, inp["ev_w_in"][j], inp["ev_b_in"][j], inp["ev_conv_w"][j], inp["ev_conv_b"][j],
                            inp["ev_cln_g"][j], inp["ev_cln_b"][j], inp["ev_lb_logits"], j, inp["ev_onorm_g"][j],
                            inp["ev_w_out"][j], inp["ln_mix_g"][layer], inp["ln_mix_b"][layer],
                            np.zeros((1,), np.float32), True, TT_E)
            for k in EVEN_L:
                base[f"{k}_{layer}"] = d[k]
        else:
            d = odd_inputs(dummy_x, inp, j, inp["ln_mix_g"][layer], inp["ln_mix_b"][layer], np.zeros((1,), np.float32))
            for k in ODD_IN:
                if k not in ODD_C and k != "M":
                    base[f"{k}_{layer}"] = d[k]
        d = ffn_inputs(dummy_x, inp["ff_w_up"][layer], inp["ff_w_gate"][layer], inp["ff_conv_w"][layer],
                       inp["ff_conv_b"][layer], inp["ff_w_down"][layer], inp["ln_ffn_g"][layer], inp["ln_ffn_b"][layer])
        for k in FFN_L:
            base[f"{k}_f{layer}"] = d[k]
    maps = []
    for c in range(len(x_slabs)):
        m = dict(base)
        m["xin"] = x_slabs[c]
        m["sel"] = np.full((128, 1), float(c % 2), np.float32)
        maps.append(m)
    return maps


_PROGS = {}


def kernel(**inp):
    inp = {k: np.asarray(v) for k, v in inp.items()}
    x = inp["x"].astype(np.float32)
    B = x.shape[0]
    slabs = []
    for c in range(NCORES):
        b, h = divmod(c, 2)
        xT = x[b].T
        if h == 0:
            sl = np.concatenate([np.zeros((D, HMAX), np.float32), xT[:, :NTC]], axis=1)
        else:
            sl = xT[:, NTC - HMAX:]
        slabs.append(np.ascontiguousarray(sl))
    if "fused" not in _PROGS:
        _PROGS["fused"] = build_fused_nc(NTC, DEPTH)[0]
    maps = fused_inputs(inp, NTC, DEPTH, slabs)
    res = run_bass_kernel_spmd(_PROGS["fused"], maps, core_ids=list(range(NCORES))).results
    out = np.empty((B, SEQ, D), np.float32)
    for c in range(NCORES):
        b, h = divmod(c, 2)
        out[b, h * NTC:(h + 1) * NTC, :] = res[c]["out"].T
    return out
```

```python
import contextlib
import os
import numpy as np
import concourse.bass as bass
import concourse.mybir as mybir
from concourse.bass_utils import run_bass_kernel_spmd

F32 = mybir.dt.float32
BF16 = mybir.dt.bfloat16
ALU = mybir.AluOpType
AF = mybir.ActivationFunctionType

D = 1024
KC = D // 128
DFF = 2816
FB = DFF // 128
DEPTH = 4
ALPHA = (2 * DEPTH) ** 0.25
LN_EPS = 1e-5


class _Rec:
    def __init__(self):
        self.call = None

    def __getattr__(self, name):
        def f(*a, **k):
            assert self.call is None, "op lambda must make exactly one engine call"
            self.call = (name, a, k)
            return self
        return f


class Prog:
    ENGS = ("pe", "act", "dve", "pool", "sp")
    EPOCH = 16000
    NEP = dict(pe=8, act=5, dve=6, pool=4, sp=1)
    NDMA = 24
    NCC = 16

    def __init__(self, nc):
        self.nc = nc
        self.ops = []
        self.last_w = {}
        self.readers = {}
        self.gstack = contextlib.ExitStack()
        self.stack = None
        self.ntile = 0
        self.known = set()
        self.children = {}
        self.bank_of = {}
        self.bank_last = {}
        self.prefix = ""
        self.phase_start = 0
        self.nphase = 0
        self.cnt = {e: 0 for e in self.ENGS}
        self.ndma = 0
        self.ncc = 0
        self.dma_prev = {}
        self.sems = {}
        g = self.gstack
        for e in self.ENGS:
            for ep in range(self.NEP[e]):
                self.sems[(e, ep)] = g.enter_context(nc.semaphore(f"s_{e}_{ep}"))
        for i in range(self.NDMA):
            self.sems[("dma", i)] = g.enter_context(nc.semaphore(f"s_dma_{i}"))
        for i in range(self.NCC):
            self.sems[("cc", i)] = g.enter_context(nc.semaphore(f"s_cc_{i}"))
        self.bar = g.enter_context(nc.semaphore("s_bar"))
        self.stats = dict(n_ops=0)

    def begin_phase(self, prefix=""):
        self.stack = contextlib.ExitStack()
        self.prefix = prefix
        self.phase_start = len(self.ops)
        self.bank_of = {}
        self.bank_last = {}

    def end_phase(self):
        self._emit_phase()
        self.stack.close()
        self.stack = None

    def finish(self):
        self.gstack.close()
        self.stats = dict(n_ops=len(self.ops), milestones=dict(self.cnt), ndma=self.ndma, ncc=self.ncc)

    def sb(self, shape, dt, name=None):
        self.ntile += 1
        name = "sb_" + self.prefix + (name or f"t{self.ntile}")
        return self.stack.enter_context(self.nc.sbuf_tensor(name, list(shape), dt))

    def ps(self, shape, dt=F32, name=None, keys=None):
        self.ntile += 1
        nm = name or f"p{self.ntile}"
        for k in (keys if keys is not None else [nm]):
            k = k if isinstance(k, tuple) else (k,)
            self.bank_of[k] = nm
        return self.stack.enter_context(self.nc.psum_tensor("ps_" + self.prefix + nm, list(shape), dt))

    def _banks(self, keys):
        out = set()
        for k in keys:
            for i in range(1, len(k) + 1):
                if k[:i] in self.bank_of:
                    out.add(self.bank_of[k[:i]])
        return out

    def _norm(self, k):
        k = k if isinstance(k, tuple) else (k,)
        if k not in self.known:
            self.known.add(k)
            for i in range(1, len(k)):
                self.children.setdefault(k[:i], set()).add(k)
        return k

    def _related(self, k):
        for i in range(1, len(k) + 1):
            yield k[:i]
        for ext in self.children.get(k, ()):
            yield ext

    def op(self, eng, fn, reads=(), writes=(), dma=False, cc=False):
        idx = len(self.ops)
        deps = set()
        reads = [self._norm(k) for k in reads]
        writes = [self._norm(k) for k in writes]
        for k in reads:
            for r in self._related(k):
                if r in self.last_w:
                    deps.add(self.last_w[r])
        for k in writes:
            for r in self._related(k):
                if r in self.last_w:
                    deps.add(self.last_w[r])
                for x in self.readers.get(r, ()):
                    deps.add(x)
        for k in writes:
            self.last_w[k] = idx
            self.readers[k] = []
            for ext in self.children.get(k, ()):
                self.last_w.pop(ext, None)
                self.readers[ext] = []
        for k in reads:
            self.readers.setdefault(k, []).append(idx)
        for bk in self._banks(reads + writes):
            bl = self.bank_last.setdefault(bk, {})
            for e2, i2 in bl.items():
                if e2 != eng:
                    deps.add(i2)
            bl[eng] = idx
        deps.discard(idx)
        deps = {d for d in deps if d >= self.phase_start}
        rec = _Rec()
        fn(rec)
        assert rec.call is not None
        self.ops.append(dict(eng=eng, call=rec.call, deps=deps, dma=dma or cc, cc=cc, stage=getattr(self, "stage", "")))
        return idx

    def pe(self, fn, r=(), w=()):
        return self.op("pe", fn, r, w)

    def act(self, fn, r=(), w=()):
        return self.op("act", fn, r, w)

    def dve(self, fn, r=(), w=()):
        return self.op("dve", fn, r, w)

    def pool(self, fn, r=(), w=()):
        return self.op("pool", fn, r, w)

    def dma(self, eng, fn, r=(), w=()):
        return self.op(eng, fn, r, w, dma=True)

    def cc(self, fn, r=(), w=()):
        return self.op("pool", fn, r, w, cc=True)

    def _emit_phase(self):
        nc = self.nc
        ops = self.ops
        lo = self.phase_start
        idxs = range(lo, len(ops))
        for i in idxs:
            o = ops[i]
            best = {}
            keep = set()
            for d in o["deps"]:
                p = ops[d]
                if p["dma"]:
                    keep.add(d)
                elif best.get(p["eng"], -1) < d:
                    best[p["eng"]] = d
            keep.update(best.values())
            o["deps"] = keep
        needed = set()
        for i in idxs:
            o = ops[i]
            for d in o["deps"]:
                p = ops[d]
                if p["dma"]:
                    needed.add(d)
                elif p["eng"] == o["eng"] and o["eng"] == "pe" and not o["dma"]:
                    continue
                else:
                    needed.add(d)
        per_eng = {e: [i for i in idxs if ops[i]["eng"] == e] for e in self.ENGS}
        last_compute = {}
        for e in self.ENGS:
            for i in reversed(per_eng[e]):
                if not ops[i]["dma"]:
                    last_compute[e] = i
                    needed.add(i)
                    break
        for i in idxs:
            o = ops[i]
            if o["cc"]:
                assert self.ncc < self.NCC
                o["sig"] = ("cc", self.ncc, 1)
                o["prev_same_slot"] = None
                self.ncc += 1
            elif o["dma"]:
                slot = self.ndma % self.NDMA
                o["sig"] = ("dma", slot, 16 * (self.ndma // self.NDMA + 1))
                o["prev_same_slot"] = self.dma_prev.get(slot)
                self.dma_prev[slot] = i
                self.ndma += 1
            elif i in needed:
                c = self.cnt[o["eng"]]
                assert c // self.EPOCH < self.NEP[o["eng"]], "out of semaphore epochs"
                o["sig"] = (o["eng"], c // self.EPOCH, c % self.EPOCH + 1)
                self.cnt[o["eng"]] = c + 1
            else:
                o["sig"] = None
        sems = self.sems
        self.nphase += 1
        nph = self.nphase
        bar = self.bar

        def run_engine(ename, eng):
            seen = {}

            def wait(sig):
                key = (sig[0], sig[1])
                if seen.get(key, 0) >= sig[2]:
                    return
                eng.wait_ge(sems[key], sig[2])
                seen[key] = sig[2]

            for i in per_eng[ename]:
                o = ops[i]
                best = {}
                for d in o["deps"]:
                    p = ops[d]
                    if (not p["dma"]) and p["eng"] == ename and ename == "pe" and not o["dma"]:
                        continue
                    sg = p["sig"]
                    kk = (sg[0], sg[1])
                    if best.get(kk, 0) < sg[2]:
                        best[kk] = sg[2]
                for kk in sorted(best):
                    wait((kk[0], kk[1], best[kk]))
                if o["dma"] and o["prev_same_slot"] is not None and o["prev_same_slot"] >= lo:
                    wait(ops[o["prev_same_slot"]]["sig"])
                call = o["call"]
                ins = getattr(eng, call[0])(*call[1], **call[2])
                sig = o["sig"]
                if sig is not None:
                    if sig[0] == "dma":
                        ins.then_inc(sems[("dma", sig[1])], 16)
                    elif sig[0] == "cc":
                        ins.then_inc(sems[("cc", sig[1])])
                    else:
                        ins.then_inc(sems[(sig[0], sig[1])], 1)
            for i in per_eng[ename]:
                if ops[i]["dma"]:
                    wait(ops[i]["sig"])
            if ename in last_compute:
                wait(ops[last_compute[ename]]["sig"])
            eng.sem_inc(bar, 1)
            eng.wait_ge(bar, len(self.ENGS) * nph)

        with nc.Block() as block:
            @block.tensor
            def _(e):
                run_engine("pe", e)

            @block.scalar
            def _(e):
                run_engine("act", e)

            @block.vector
            def _(e):
                run_engine("dve", e)

            @block.gpsimd
            def _(e):
                run_engine("pool", e)

            @block.sync
            def _(e):
                run_engine("sp", e)


class Stager:
    def __init__(self, P, width, nbuf=2):
        self.P = P
        self.bufs = [P.sb([128, width], F32, f"stage{i}") for i in range(nbuf)]
        self.n = 0

    def load(self, dst_ap, src_ap, n, dkey, eng=None, a=1):
        P = self.P
        i = self.n % len(self.bufs)
        eng = eng or ("act", "dve")[self.n % 2]
        self.n += 1
        st = self.bufs[i]
        sv = st[:, 0:n] if a == 1 else st[:, 0:n].rearrange("p (a d) -> p a d", a=a)
        P.dma("sp", lambda e: e.dma_start(out=sv, in_=src_ap), w=[("stage", i)])
        cast_op(P, eng, dst_ap, sv, [("stage", i)], [dkey])


def cast_op(P, eng, dst, src, r, w):
    if eng == "act":
        P.op("act", lambda e: e.activation(out=dst, in_=src, func=AF.Identity), r, w)
    else:
        P.op(eng, lambda e: e.tensor_copy(out=dst, in_=src), r, w)


def _mm(out, lhsT, rhs, start, stop):
    return lambda e: e.matmul(out, lhsT, rhs, start=start, stop=stop)


def build_ffn(nc, P, NT, TT, io):
    HALO = 2
    ntiles = NT // TT
    xk = io.get("xin_key", ("xin_ext",))
    ok = io.get("out_key", ("outdram",))
    xin = io["xin"]
    xin_v = xin.rearrange("(kc p) t -> p kc t", p=128)
    out_v = io["out"].rearrange("(kc p) t -> p kc t", p=128)

    wup = P.sb([128, KC, DFF], BF16, "wup")
    wgt = P.sb([128, KC, DFF], BF16, "wgt")
    wdn = P.sb([128, FB, D], BF16, "wdn")
    cw = P.sb([128, FB * 3], F32, "cw")
    cb = P.sb([128, FB], F32, "cb")
    lng = P.sb([128, KC], F32, "lng")
    lnb = P.sb([128, KC], F32, "lnb")
    ones = P.sb([128, 128], F32, "ones")
    carry = P.sb([128, FB, 2], F32, "carry")
    xbf = [P.sb([128, KC, TT], BF16, f"xbf{i}") for i in range(2)]
    xhalo = P.sb([128, KC, HALO], BF16, "xhalo")
    xres = [P.sb([128, KC, TT], F32, f"xres{i}") for i in range(2)]
    hT = P.sb([128, FB, TT], BF16, "hT")
    NQ = 3
    usb = [P.sb([128, TT + 2], F32, f"usb{i}") for i in range(NQ)]
    t1 = [P.sb([128, TT], F32, f"t1_{i}") for i in range(NQ)]
    gg = [P.sb([128, TT], F32, f"gg{i}") for i in range(NQ)]
    gsb = [P.sb([128, TT], BF16, f"gsb{i}") for i in range(NQ)]
    sq = [P.sb([128, TT], F32, f"sq{i}") for i in range(2)]
    mean_sb = P.sb([128, TT], F32, "mean_sb")
    var_sb = P.sb([128, TT], F32, "var_sb")
    rstd_sb = P.sb([128, TT], F32, "rstd_sb")
    yt = [P.sb([128, TT], F32, f"yt{i}") for i in range(2)]

    up_ps = [P.ps([128, TT], F32, f"up_ps{i}", keys=[("up_ps", i)]) for i in range(2)]
    gt_ps = [P.ps([128, TT], F32, f"gt_ps{i}", keys=[("gt_ps", i)]) for i in range(2)]
    o_ps = [P.ps([128, TT], F32, f"o_ps{i}", keys=[("o_ps", i)]) for i in range(2)]
    mean_ps = P.ps([128, TT], F32, "mean_ps")
    msq_ps = P.ps([128, TT], F32, "msq_ps")

    P.dma("sp", lambda e: e.dma_start(out=cw[:], in_=io["cw"]), w=["cw"])
    P.dma("sp", lambda e: e.dma_start(out=cb[:], in_=io["cb"]), w=["cb"])
    P.dma("sp", lambda e: e.dma_start(out=lng[:], in_=io["lng"]), w=["lng"])
    P.dma("sp", lambda e: e.dma_start(out=lnb[:], in_=io["lnb"]), w=["lnb"])
    P.pool(lambda e: e.memset(ones[:], 1.0 / D), w=["ones"])
    xh32 = P.sb([128, KC, HALO], F32, "xh32")
    P.dma("sp", lambda e: e.dma_start(out=xh32[:], in_=xin_v[:, :, 0:HALO]), r=[xk], w=["xh32"])
    P.pool(lambda e: e.tensor_copy(out=xhalo[:], in_=xh32[:]), r=["xh32"], w=["xhalo"])

    def load_xres(j):
        s = j % 2
        P.dma("sp", lambda e: e.dma_start(out=xres[s][:], in_=xin_v[:, :, HALO + j * TT:HALO + (j + 1) * TT]),
              r=[xk], w=[("xres", s)])
        P.pool(lambda e: e.tensor_copy(out=xbf[s][:], in_=xres[s][:]), r=[("xres", s)], w=[("xbf", s)])

    def load_x(j):
        pass

    load_xres(0)
    stg = Stager(P, 1024, nbuf=3)
    wupv = io["w_up"].rearrange("(kc p) f -> p kc f", p=128)
    wgtv = io["w_gate"].rearrange("(kc p) f -> p kc f", p=128)
    wdnv = io["w_down"].rearrange("(fb p) d -> p fb d", p=128)
    pieces = [(0, 1024), (1024, 1024), (2048, DFF - 2048)]
    for kc in range(KC):
        for (c0, cn) in pieces:
            stg.load(wup[:, kc, c0:c0 + cn], wupv[:, kc, c0:c0 + cn], cn, ("wup", kc, c0))
    for kc in range(KC):
        for (c0, cn) in pieces:
            stg.load(wgt[:, kc, c0:c0 + cn], wgtv[:, kc, c0:c0 + cn], cn, ("wgt", kc, c0))
    for fb in range(FB):
        stg.load(wdn[:, fb, :], wdnv[:, fb, :], D, ("wdn", fb // 2, fb % 2))

    hp = up_ps[1]
    for fb in range(FB):
        for kc in range(KC):
            P.pe(_mm(hp[:, fb * 2:fb * 2 + 2], wup[:, kc, fb * 128:(fb + 1) * 128], xhalo[:, kc, :],
                     kc == 0, kc == KC - 1),
                 r=[("wup", kc), "xhalo"], w=[("up_ps", 1)])
    P.act(lambda e: e.activation(out=carry[:].rearrange("p a b -> p (a b)"), in_=hp[:, 0:FB * 2], func=AF.Identity),
          r=[("up_ps", 1)], w=["carry"])

    nblk = 0
    for j in range(ntiles):
        s = j % 2
        if j + 1 < ntiles:
            load_x(j + 1)
        xb = xbf[s]
        for fb in range(FB):
            b = nblk % 2
            nblk += 1
            fsl = slice(fb * 128, (fb + 1) * 128)
            for kc in range(KC):
                P.pe(_mm(up_ps[b][:], wup[:, kc, fsl], xb[:, kc, :], kc == 0, kc == KC - 1),
                     r=[("wup", kc), ("xbf", s)], w=[("up_ps", b)])
            for kc in range(KC):
                P.pe(_mm(gt_ps[b][:], wgt[:, kc, fsl], xb[:, kc, :], kc == 0, kc == KC - 1),
                     r=[("wgt", kc), ("xbf", s)], w=[("gt_ps", b)])
            q = (nblk - 1) % NQ
            u = usb[q]
            P.pool(lambda e, u=u, fb=fb: e.tensor_copy(out=u[:, 0:2], in_=carry[:, fb, :]),
                   r=[("carry", fb)], w=[("usb", q)])
            P.act(lambda e, u=u, b=b: e.activation(out=u[:, 2:2 + TT], in_=up_ps[b][:], func=AF.Identity),
                  r=[("up_ps", b)], w=[("usb", q)])
            P.act(lambda e, q=q, b=b: e.activation(out=gsb[q][:], in_=gt_ps[b][:], func=AF.Identity),
                  r=[("gt_ps", b)], w=[("gsb", q)])
            P.pool(lambda e, u=u, fb=fb: e.tensor_copy(out=carry[:, fb, :], in_=u[:, TT:TT + 2]),
                   r=[("usb", q)], w=[("carry", fb)])
            tt = t1[q]
            P.dve(lambda e, u=u, tt=tt, fb=fb: e.tensor_scalar(
                out=tt[:], in0=u[:, 2:2 + TT], scalar1=cw[:, fb * 3 + 2:fb * 3 + 3], scalar2=cb[:, fb:fb + 1],
                op0=ALU.mult, op1=ALU.add), r=[("usb", q), "cw", "cb"], w=[("t1", q)])
            P.dve(lambda e, u=u, tt=tt, fb=fb: e.scalar_tensor_tensor(
                out=tt[:], in0=u[:, 1:1 + TT], scalar=cw[:, fb * 3 + 1:fb * 3 + 2], in1=tt[:],
                op0=ALU.mult, op1=ALU.add), r=[("usb", q), ("t1", q)], w=[("t1", q)])
            P.dve(lambda e, u=u, tt=tt, fb=fb: e.scalar_tensor_tensor(
                out=tt[:], in0=u[:, 0:TT], scalar=cw[:, fb * 3:fb * 3 + 1], in1=tt[:],
                op0=ALU.mult, op1=ALU.add), r=[("usb", q), ("t1", q)], w=[("t1", q)])
            g = gg[q]
            P.act(lambda e, g=g, tt=tt: e.activation(out=g[:], in_=tt[:], func=AF.Gelu),
                  r=[("t1", q)], w=[("gg", q)])
            P.dve(lambda e, g=g, q=q, fb=fb: e.tensor_tensor(out=hT[:, fb, :], in0=g[:], in1=gsb[q][:], op=ALU.mult),
                  r=[("gg", q), ("gsb", q)], w=[("hT", fb)])
        if j + 1 < ntiles:
            load_xres(j + 1)
        xr = xres[s]
        for db in range(KC):
            b = db % 2
            dsl = slice(db * 128, (db + 1) * 128)
            for fb in range(FB):
                P.pe(_mm(o_ps[b][:], wdn[:, fb, dsl], hT[:, fb, :], fb == 0, fb == FB - 1),
                     r=[("wdn", fb // 2), ("hT", fb)], w=[("o_ps", b)])
            P.dve(lambda e, xr=xr, db=db, b=b: e.scalar_tensor_tensor(
                out=xr[:, db, :], in0=xr[:, db, :], scalar=float(ALPHA), in1=o_ps[b][:],
                op0=ALU.mult, op1=ALU.add), r=[("xres", s, db), ("o_ps", b)], w=[("xres", s, db)])
        emit_ln(P, xr, ("xres", s), KC, TT, ones, sq, mean_ps, msq_ps, mean_sb, var_sb, rstd_sb, yt,
                lng, lnb, xr, ("xres", s), LN_EPS)
        P.dma("sp", lambda e, j=j, xr=xr: e.dma_start(out=out_v[:, :, j * TT:(j + 1) * TT], in_=xr[:]),
              r=[("xres", s)], w=[ok + (j,)])


def emit_ln(P, xr, xkey, nch, TT, ones, sq, mean_ps, msq_ps, mean_sb, var_sb, rstd_sb, yt, lng, lnb, osb, okey,
            eps, silu=False, lkeys=("lng", "lnb", "ones")):
    for c in range(nch):
        P.pe(_mm(mean_ps[:], ones[:], xr[:, c, :], c == 0, c == nch - 1),
             r=[lkeys[2], xkey + (c,)], w=["mean_ps"])
    for c in range(nch):
        b = c % 2
        P.act(lambda e, c=c, b=b: e.activation(out=sq[b][:], in_=xr[:, c, :], func=AF.Square),
              r=[xkey + (c,)], w=[("sq", b)])
        P.pe(_mm(msq_ps[:], ones[:], sq[b][:], c == 0, c == nch - 1),
             r=[lkeys[2], ("sq", b)], w=["msq_ps"])
    P.act(lambda e: e.activation(out=mean_sb[:], in_=mean_ps[:], func=AF.Identity), r=["mean_ps"], w=["mean_sb"])
    P.dve(lambda e: e.tensor_tensor(out=var_sb[:], in0=mean_sb[:], in1=mean_sb[:], op=ALU.mult),
          r=["mean_sb"], w=["var_sb"])
    P.dve(lambda e: e.tensor_tensor(out=var_sb[:], in0=msq_ps[:], in1=var_sb[:], op=ALU.subtract),
          r=["msq_ps", "var_sb"], w=["var_sb"])
    P.dve(lambda e: e.tensor_scalar(out=var_sb[:], in0=var_sb[:], scalar1=0.0, scalar2=float(eps),
                                    op0=ALU.max, op1=ALU.add), r=["var_sb"], w=["var_sb"])
    P.act(lambda e: e.activation(out=var_sb[:], in_=var_sb[:], func=AF.Ln), r=["var_sb"], w=["var_sb"])
    P.act(lambda e: e.activation(out=rstd_sb[:], in_=var_sb[:], func=AF.Exp, scale=-0.5), r=["var_sb"], w=["rstd_sb"])
    for c in range(nch):
        b = c % 2
        y = yt[b]
        P.dve(lambda e, y=y, c=c: e.tensor_tensor(out=y[:], in0=xr[:, c, :], in1=mean_sb[:], op=ALU.subtract),
              r=[xkey + (c,), "mean_sb"], w=[("yt", b)])
        P.dve(lambda e, y=y: e.tensor_tensor(out=y[:], in0=y[:], in1=rstd_sb[:], op=ALU.mult),
              r=[("yt", b), "rstd_sb"], w=[("yt", b)])
        P.act(lambda e, y=y, c=c: e.activation(out=osb[:, c, :], in_=y[:], func=(AF.Silu if silu else AF.Identity),
                                               scale=lng[:, c:c + 1], bias=lnb[:, c:c + 1]),
              r=[("yt", b), lkeys[0], lkeys[1]], w=[okey + (c,)])


def vec_layout(v, nb):
    return np.ascontiguousarray(np.asarray(v, np.float32).reshape(nb, 128).T)


def build_ffn_nc(NT, TT):
    nc = bass.Bass("TRN2", target_bir_lowering=False)
    io = {}
    io["xin"] = nc.dram_tensor("xin", [D, 2 + NT], F32, kind="ExternalInput").ap()
    io["w_up"] = nc.dram_tensor("w_up", [D, DFF], F32, kind="ExternalInput").ap()
    io["w_gate"] = nc.dram_tensor("w_gate", [D, DFF], F32, kind="ExternalInput").ap()
    io["w_down"] = nc.dram_tensor("w_down", [DFF, D], F32, kind="ExternalInput").ap()
    io["cw"] = nc.dram_tensor("cw", [128, FB * 3], F32, kind="ExternalInput").ap()
    io["cb"] = nc.dram_tensor("cb", [128, FB], F32, kind="ExternalInput").ap()
    io["lng"] = nc.dram_tensor("lng", [128, KC], F32, kind="ExternalInput").ap()
    io["lnb"] = nc.dram_tensor("lnb", [128, KC], F32, kind="ExternalInput").ap()
    io["out"] = nc.dram_tensor("out", [D, NT], F32, kind="ExternalOutput").ap()
    P = Prog(nc)
    P.begin_phase()
    build_ffn(nc, P, NT, TT, io)
    P.end_phase()
    P.finish()
    return nc, P


def ffn_inputs(xin_T, w_up, w_gate, conv_w, conv_b, w_down, g, b):
    cwl = np.stack([vec_layout(conv_w[t], FB) for t in range(3)], axis=-1).reshape(128, FB * 3)
    return dict(xin=np.ascontiguousarray(xin_T, dtype=np.float32),
                w_up=np.ascontiguousarray(w_up), w_gate=np.ascontiguousarray(w_gate),
                w_down=np.ascontiguousarray(w_down), cw=np.ascontiguousarray(cwl),
                cb=vec_layout(conv_b, FB), lng=vec_layout(g, KC), lnb=vec_layout(b, KC))


STAGE = 9
CH = 64
HAL_E = 30


def build_even(nc, P, NT, TT, io, state_only=False):
    HALO = HAL_E
    ntiles = NT // TT
    xk = io.get("xin_key", ("xin_ext",))
    ok = io.get("out_key", ("outdram",))
    full = not state_only
    nblk = TT // 128
    xin_v = io["xin"].rearrange("(kc p) t -> p kc t", p=128)
    out_v = io["out"].rearrange("(kc p) t -> p kc t", p=128)
    win = P.sb([128, KC, 3072], BF16, "win")
    wout = P.sb([128, KC, D], BF16, "wout")
    bin_ = P.sb([128, 24], F32, "bin")
    nbin = P.sb([128, 24], F32, "nbin")
    convw = P.sb([128, 4 * 31], F32, "convw")
    convb = P.sb([128, 4], F32, "convb")
    clng = P.sb([128, 4], F32, "clng")
    clnb = P.sb([128, 4], F32, "clnb")
    lb0 = P.sb([128, 4], F32, "lb0")
    lb1 = P.sb([128, 4], F32, "lb1")
    lbsel = P.sb([128, 1], F32, "lbsel")
    lb = P.sb([128, 4], F32, "lb")
    oml = P.sb([128, 4], F32, "oml")
    ong = P.sb([128, 4], F32, "ong")
    hsc = P.sb([128, 1], F32, "hsc")
    lng = P.sb([128, KC], F32, "lng")
    lnb = P.sb([128, KC], F32, "lnb")
    bivbc = P.sb([128, 512], F32, "bivbc")
    ones512 = P.sb([128, 128], F32, "ones512")
    ones128 = P.sb([128, 128], F32, "ones128")
    ones1024 = P.sb([128, 128], F32, "ones1024")
    ident = P.sb([128, 128], BF16, "ident")
    mask = P.sb([128, 128], F32, "mask")
    rmask = P.sb([128, TT], F32, "rmask")
    S = P.sb([128, 4, 128], F32, "S")
    Sbf = P.sb([128, 4, 128], BF16, "Sbf")
    xbf = [P.sb([128, KC, TT], BF16, f"xbf{i}") for i in range(2)]
    xhalo = P.sb([128, KC, HALO], BF16, "xhalo")
    xres = [P.sb([128, KC, TT], F32, f"xres{i}") for i in range(2)]
    glu = [P.sb([128, 4, HALO + TT], BF16, f"glu{i}") for i in range(2)]
    convd = P.sb([128, 4 * 31, 128], BF16, "convd") if not state_only else None
    sg = [P.sb([128, TT], F32, f"sg{i}") for i in range(2)]
    cacc = P.sb([128, 4, TT], F32, "cacc")
    cat = P.sb([128, KC, TT], BF16, "cat")
    qs = P.sb([128, TT], F32, "qs")
    sgp = P.sb([128, TT], F32, "sgp")
    sgn = P.sb([128, TT], F32, "sgn")
    logf = P.sb([128, TT], F32, "logf")
    cum = P.sb([128, TT], F32, "cum")
    eq = P.sb([128, TT], F32, "eq")
    en = P.sb([128, TT], F32, "en")
    gC = P.sb([128, 4, TT // CH], F32, "gC")
    qt = P.sb([128, 4, TT], BF16, "qt")
    kt = P.sb([128, 4, TT], F32, "kt")
    ktb = P.sb([128, 4, TT], BF16, "ktb")
    kh = P.sb([128, 4, TT], BF16, "kh")
    khT = P.sb([128, 4, 128], BF16, "khT")
    vtm = P.sb([128, 512], BF16, "vtm")
    pT = P.sb([128, 4, 128], BF16, "pT")
    osb = P.sb([128, 4, TT], F32, "osb")
    sqo = P.sb([128, TT], F32, "sqo")
    gsl = P.sb([128, TT], F32, "gsl")
    sq = [P.sb([128, TT], F32, f"sq{i}") for i in range(2)]
    mean_sb = P.sb([128, TT], F32, "mean_sb")
    var_sb = P.sb([128, TT], F32, "var_sb")
    rstd_sb = P.sb([128, TT], F32, "rstd_sb")
    yt = [P.sb([128, TT], F32, f"yt{i}") for i in range(2)]

    zps = [P.ps([128, 2, 256], F32, f"zps{i}", keys=[("zps", 2 * i), ("zps", 2 * i + 1)]) for i in range(2)]
    vtm_ps = P.ps([128, 512], F32, "vtm_ps")
    sc_ps = P.ps([128, 4, 128], F32, "sc_ps")
    o_ps = P.ps([128, 4, 128], F32, "o_ps")
    st_ps = P.ps([128, 4, 128], F32, "st_ps")
    tr_ps = P.ps([128, 4, 128], BF16, "tr_ps")
    stat_ps = P.ps([128, 2, 256], F32, "stat_ps", keys=["mean_ps", "msq_ps"])
    mean_ps = stat_ps[:, 0, 0:TT]
    msq_ps = stat_ps[:, 1, 0:TT]

    for nm, t in (("bin", bin_), ("convw", convw), ("convb", convb), ("clng", clng), ("clnb", clnb),
                  ("lb0", lb0), ("lb1", lb1), ("lbsel", lbsel), ("ong", ong), ("hsc", hsc), ("lng", lng),
                  ("lnb", lnb), ("bivbc", bivbc), ("mask", mask), ("rmask", rmask), ("S", S)):
        P.dma("sp", lambda e, t=t, nm=nm: e.dma_start(out=t[:], in_=io[nm]),
              r=([io.get("S_key", ("S_ext",))] if nm == "S" else []), w=[nm])
    id32 = P.sb([128, 128], F32, "id32")
    P.dma("sp", lambda e: e.dma_start(out=id32[:], in_=io["ident"]), w=["id32"])
    P.pool(lambda e: e.tensor_copy(out=ident[:], in_=id32[:]), r=["id32"], w=["ident"])
    if not state_only:
        for q_ in range(4 * 31):
            P.op(("pool", "dve")[q_ % 2], lambda e, q_=q_: e.tensor_scalar(
                out=convd[:, q_, :], in0=id32[:], scalar1=convw[:, q_:q_ + 1], scalar2=None, op0=ALU.mult),
                ["id32", "convw"], [("convd", q_)])
    P.pool(lambda e: e.memset(ones512[:], 1.0 / 512), w=["ones512"])
    P.pool(lambda e: e.memset(ones128[:], 1.0 / 128), w=["ones128"])
    P.pool(lambda e: e.memset(ones1024[:], 1.0 / D), w=["ones1024"])
    xh32 = P.sb([128, KC, HALO], F32, "xh32")
    P.dma("sp", lambda e: e.dma_start(out=xh32[:], in_=xin_v[:, :, 0:HALO]), r=[xk], w=["xh32"])
    P.pool(lambda e: e.tensor_copy(out=xhalo[:], in_=xh32[:]), r=["xh32"], w=["xhalo"])

    def load_xres(j):
        s = j % 2
        P.dma("sp", lambda e: e.dma_start(out=xres[s][:], in_=xin_v[:, :, HALO + j * TT:HALO + (j + 1) * TT]),
              r=[xk], w=[("xres", s)])
        P.pool(lambda e: e.tensor_copy(out=xbf[s][:], in_=xres[s][:]), r=[("xres", s)], w=[("xbf", s)])

    def load_x(j):
        pass

    load_xres(0)
    stg = Stager(P, 3072)
    winv = io["w_in"].rearrange("(kc p) f -> p kc f", p=128)
    woutv = io["w_out"].rearrange("(kc p) f -> p kc f", p=128)
    for kc in range(KC):
        stg.load(win[:, kc, :], winv[:, kc, :], 3072, ("win", kc))
    for kc in range(0, KC, 2):
        stg.load(wout[:, kc:kc + 2, :], woutv[:, kc:kc + 2, :], 2 * D, ("wout", kc // 2), a=2)
    P.dve(lambda e: e.tensor_scalar(out=nbin[:], in0=bin_[:], scalar1=-1.0, scalar2=None, op0=ALU.mult),
          r=["bin"], w=["nbin"])
    P.dve(lambda e: e.tensor_tensor(out=lb[:], in0=lb1[:], in1=lb0[:], op=ALU.subtract), r=["lb0", "lb1"], w=["lb"])
    P.act(lambda e: e.activation(out=lb[:], in_=lb[:], func=AF.Sigmoid), r=["lb"], w=["lb"])
    P.dve(lambda e: e.tensor_scalar(out=lb[:], in0=lb[:], scalar1=lbsel[:, 0:1], scalar2=None, op0=ALU.mult),
          r=["lb", "lbsel"], w=["lb"])
    P.dve(lambda e: e.tensor_scalar(out=oml[:], in0=lb[:], scalar1=-1.0, scalar2=1.0, op0=ALU.mult, op1=ALU.add),
          r=["lb"], w=["oml"])
    P.act(lambda e: e.activation(out=Sbf[:], in_=S[:], func=AF.Identity), r=["S"], w=["Sbf"])

    zcnt = [0]

    def proj(blk, rhs, n, rkeys):
        b = zcnt[0] % 4
        zcnt[0] += 1
        dst = zps[b // 2][:, b % 2, 0:n]
        for kc in range(KC):
            P.pe(_mm(dst, win[:, kc, blk * 128:(blk + 1) * 128], rhs[:, kc, :], kc == 0, kc == KC - 1),
                 r=[("win", kc)] + rkeys, w=[("zps", b)])
        return dst, ("zps", b)

    def glu_block(c, rhs, n, rkeys, dst_glu, dkey, col0, scale_ap=None):
        av, avk = proj(c, rhs, n, rkeys)
        ag, agk = proj(4 + c, rhs, n, rkeys)
        sgt = sg[c % 2]
        P.act(lambda e: e.activation(out=sgt[:, 0:n], in_=ag, func=AF.Sigmoid, bias=bin_[:, 4 + c:5 + c]),
              r=[agk, "bin"], w=[("sg", c % 2)])
        P.dve(lambda e: e.scalar_tensor_tensor(out=dst_glu[:, c, col0:col0 + n], in0=av, scalar=bin_[:, c:c + 1],
                                               in1=sgt[:, 0:n], op0=ALU.add, op1=ALU.mult),
              r=[avk, ("sg", c % 2), "bin"], w=[dkey + (c,)])
        if scale_ap is not None:
            P.dve(lambda e: e.tensor_scalar(out=dst_glu[:, c, col0:col0 + n], in0=dst_glu[:, c, col0:col0 + n],
                                            scalar1=scale_ap, scalar2=None, op0=ALU.mult),
                  r=[dkey + (c,), "hsc"], w=[dkey + (c,)])

    for c in (range(4) if full else ()):
        glu_block(c, xhalo, HALO, ["xhalo"], glu[1], ("glu", 1), TT, scale_ap=hsc[:, 0:1])

    for j in range(ntiles):
        s = j % 2
        if j + 1 < ntiles:
            load_x(j + 1)
        xb = xbf[s]
        G = glu[s]
        Gp = glu[1 - s]
        for c in (range(4) if full else ()):
            P.pool(lambda e, c=c, G=G, Gp=Gp: e.tensor_copy(out=G[:, c, 0:HALO], in_=Gp[:, c, TT:TT + HALO]),
                   r=[("glu", 1 - s, c)], w=[("glu", s, c)])
            glu_block(c, xb, TT, [("xbf", s)], G, ("glu", s), HALO)
            bq = zcnt[0] % 4
            zcnt[0] += 1
            cdst = zps[bq // 2][:, bq % 2, 0:TT]
            for tap in range(31):
                P.pe(_mm(cdst, convd[:, c * 31 + tap, :], G[:, c, tap:tap + TT], tap == 0, tap == 30),
                     r=[("convd", c * 31 + tap), ("glu", s, c)], w=[("zps", bq)])
            P.act(lambda e, c=c, cdst=cdst: e.activation(out=cacc[:, c, :], in_=cdst, func=AF.Identity,
                                                         bias=convb[:, c:c + 1]),
                  r=[("zps", bq), "convb"], w=[("cacc", c)])
        if full:
          emit_ln(P, cacc, ("cacc",), 4, TT, ones512, sq, mean_ps, msq_ps, mean_sb, var_sb, rstd_sb, yt,
                  clng, clnb, cat, ("cat",), LN_EPS, silu=True, lkeys=("clng", "clnb", "ones512"))
        for h in range(4):
            if full:
                qp, qk = proj(8 + h, xb, TT, [("xbf", s)])
                P.act(lambda e, qp=qp, h=h: e.activation(out=qs[:], in_=qp, func=AF.Silu, bias=bin_[:, 8 + h:9 + h]),
                      r=[qk, "bin"], w=["qs"])
            fp_, fk = proj(12 + h, xb, TT, [("xbf", s)])
            P.act(lambda e, fp_=fp_, h=h: e.activation(out=sgp[:], in_=fp_, func=AF.Sigmoid,
                                                       bias=bin_[:, 12 + h:13 + h]),
                  r=[fk, "bin"], w=["sgp"])
            P.act(lambda e, fp_=fp_, h=h: e.activation(out=sgn[:], in_=fp_, func=AF.Sigmoid, scale=-1.0,
                                                       bias=nbin[:, 12 + h:13 + h]),
                  r=[fk, "nbin"], w=["sgn"])
            P.dve(lambda e, h=h: e.tensor_scalar(out=logf[:], in0=sgp[:], scalar1=oml[:, h:h + 1],
                                                 scalar2=lb[:, h:h + 1], op0=ALU.mult, op1=ALU.add),
                  r=["sgp", "oml", "lb"], w=["logf"])
            P.act(lambda e: e.activation(out=logf[:], in_=logf[:], func=AF.Ln), r=["logf"], w=["logf"])
            P.dve(lambda e: e.tensor_tensor_scan(out=cum[:], data0=rmask[:], data1=logf[:], initial=0.0,
                                                 op0=ALU.mult, op1=ALU.add),
                  r=["rmask", "logf"], w=["cum"])
            if full:
                P.act(lambda e: e.activation(out=eq[:], in_=cum[:], func=AF.Exp), r=["cum"], w=["eq"])
            P.act(lambda e: e.activation(out=en[:], in_=cum[:], func=AF.Exp, scale=-1.0), r=["cum"], w=["en"])
            P.act(lambda e, h=h: e.activation(
                out=gC[:, h, :], in_=cum[:].rearrange("p (c t) -> p c t", t=CH)[:, :, CH - 1], func=AF.Exp),
                r=["cum"], w=[("gC", h)])
            if full:
                P.dve(lambda e, h=h: e.tensor_tensor(out=qt[:, h, :], in0=qs[:], in1=eq[:], op=ALU.mult),
                      r=["qs", "eq"], w=[("qt", h)])
            P.dve(lambda e, h=h: e.scalar_tensor_tensor(out=kt[:, h, :], in0=sgn[:], scalar=oml[:, h:h + 1],
                                                        in1=en[:], op0=ALU.mult, op1=ALU.mult),
                  r=["sgn", "en", "oml"], w=[("kt", h)])
            if full:
                P.act(lambda e, h=h: e.activation(out=ktb[:, h, :], in_=kt[:, h, :], func=AF.Identity),
                      r=[("kt", h)], w=[("ktb", h)])
            P.dve(lambda e, h=h: e.tensor_tensor(
                out=kh[:, h, :].rearrange("p (c t) -> p c t", t=CH),
                in0=kt[:, h, :].rearrange("p (c t) -> p c t", t=CH),
                in1=gC[:, h, :].unsqueeze(2).to_broadcast([128, TT // CH, CH]), op=ALU.mult),
                r=[("kt", h), ("gC", h)], w=[("kh", h)])
        for bi in range(nblk):
            tsl = slice(bi * 128, (bi + 1) * 128)
            for kc in range(KC):
                P.pe(_mm(vtm_ps[:], xb[:, kc, tsl], win[:, kc, 2048:2560], kc == 0, kc == KC - 1),
                     r=[("xbf", s), ("win", kc)], w=["vtm_ps"])
            P.dve(lambda e: e.tensor_tensor(out=vtm[:], in0=vtm_ps[:], in1=bivbc[:], op=ALU.add),
                  r=["vtm_ps", "bivbc"], w=["vtm"])
            for h in range(4):
                if full:
                    P.pe(_mm(sc_ps[:, h, :], ktb[:, h, tsl], qt[:, h, tsl], True, True),
                         r=[("ktb", h), ("qt", h)], w=[("sc_ps", h)])
                P.pe(lambda e, h=h, tsl=tsl: e.transpose(tr_ps[:, h, :], kh[:, h, tsl], ident[:]),
                     r=[("kh", h), "ident"], w=[("tr_ps", h)])
            if full:
                P.dve(lambda e: e.tensor_tensor(out=pT[:], in0=sc_ps[:],
                                                in1=mask[:].unsqueeze(1).to_broadcast([128, 4, 128]), op=ALU.mult),
                      r=["sc_ps", "mask"], w=["pT"])
            P.act(lambda e: e.activation(out=khT[:], in_=tr_ps[:], func=AF.Identity), r=["tr_ps"], w=["khT"])
            for h in (range(4) if full else ()):
                vh = vtm[:, h * 128:(h + 1) * 128]
                P.pe(_mm(o_ps[:, h, :], vh, pT[:, h, :], h == 0, False),
                     r=["vtm", "pT"], w=[("o_ps",)])
            for ci in range(2):
                csl = slice(ci * 64, (ci + 1) * 64)
                gcol = bi * 2 + ci
                for h in (range(4) if full else ()):
                    P.pe(_mm(o_ps[:, h, csl], Sbf[:, h, :], qt[:, h, bi * 128 + ci * 64:bi * 128 + (ci + 1) * 64],
                             False, ci == 1 and h == 3),
                         r=[("Sbf", h), ("qt", h)], w=[("o_ps",)])
                for h in range(4):
                    P.pe(_mm(st_ps[:, h, :], khT[csl, h, :], vtm[csl, h * 128:(h + 1) * 128], True, True),
                         r=["khT", "vtm"], w=[("st_ps", h)])
                for h in range(4):
                    P.dve(lambda e, h=h, gcol=gcol: e.scalar_tensor_tensor(
                        out=S[:, h, :], in0=S[:, h, :], scalar=gC[:, h, gcol:gcol + 1], in1=st_ps[:, h, :],
                        op0=ALU.mult, op1=ALU.add), r=[("S", h), ("gC", h), ("st_ps", h)], w=[("S", h)])
                    if full:
                        P.act(lambda e, h=h: e.activation(out=Sbf[:, h, :], in_=S[:, h, :], func=AF.Identity),
                              r=[("S", h)], w=[("Sbf", h)])
            if full:
                P.act(lambda e, tsl=tsl: e.activation(out=osb[:, :, tsl], in_=o_ps[:], func=AF.Identity),
                      r=["o_ps"], w=[("osb", bi)])
        for h in (range(4) if full else ()):
            if True:
                P.act(lambda e, h=h: e.activation(out=sqo[:], in_=osb[:, h, :], func=AF.Square), r=["osb"], w=["sqo"])
                P.pe(_mm(mean_ps, ones128[:], sqo[:], True, True), r=["ones128", "sqo"], w=["mean_ps"])
                P.dve(lambda e: e.tensor_scalar(out=var_sb[:], in0=mean_ps, scalar1=0.0, scalar2=1e-6,
                                                op0=ALU.max, op1=ALU.add), r=["mean_ps"], w=["var_sb"])
            if True:
                P.act(lambda e: e.activation(out=var_sb[:], in_=var_sb[:], func=AF.Ln), r=["var_sb"], w=["var_sb"])
                P.act(lambda e: e.activation(out=rstd_sb[:], in_=var_sb[:], func=AF.Exp, scale=-0.5), r=["var_sb"], w=["rstd_sb"])
            gp, gk = proj(20 + h, xb, TT, [("xbf", s)])
            P.act(lambda e, gp=gp, h=h: e.activation(out=gsl[:], in_=gp, func=AF.Silu, bias=bin_[:, 20 + h:21 + h]),
                  r=[gk, "bin"], w=["gsl"])
            if True:
                P.dve(lambda e, h=h: e.tensor_tensor(out=sqo[:], in0=osb[:, h, :], in1=rstd_sb[:], op=ALU.mult),
                      r=["osb", "rstd_sb"], w=["sqo"])
                P.dve(lambda e, h=h: e.scalar_tensor_tensor(out=cat[:, 4 + h, :], in0=sqo[:], scalar=ong[:, h:h + 1],
                                                            in1=gsl[:], op0=ALU.mult, op1=ALU.mult),
                      r=["sqo", "ong", "gsl"], w=[("cat", 4 + h)])
        if j + 1 < ntiles:
            load_xres(j + 1)
        xr = xres[s]
        if not full:
            continue
        for db in range(KC):
            b = zcnt[0] % 4
            zcnt[0] += 1
            dst = zps[b // 2][:, b % 2, 0:TT]
            for c in range(KC):
                P.pe(_mm(dst, wout[:, c, db * 128:(db + 1) * 128], cat[:, c, :], c == 0, c == KC - 1),
                     r=[("wout", c // 2), ("cat", c)], w=[("zps", b)])
            P.dve(lambda e, xr=xr, db=db, dst=dst: e.scalar_tensor_tensor(
                out=xr[:, db, :], in0=xr[:, db, :], scalar=float(ALPHA), in1=dst,
                op0=ALU.mult, op1=ALU.add), r=[("xres", s, db), ("zps", b)], w=[("xres", s, db)])
        emit_ln(P, xr, ("xres", s), KC, TT, ones1024, sq, mean_ps, msq_ps, mean_sb, var_sb, rstd_sb, yt,
                lng, lnb, xr, ("xres", s), LN_EPS, lkeys=("lng", "lnb", "ones1024"))
        P.dma("sp", lambda e, j=j, xr=xr: e.dma_start(out=out_v[:, :, j * TT:(j + 1) * TT], in_=xr[:]),
              r=[("xres", s)], w=[ok + (j,)])
    P.dma("sp", lambda e: e.dma_start(out=io["S_out"], in_=S[:]), r=["S"], w=[io.get("Sout_key", ("S_out",))])


def build_even_nc(NT, TT):
    nc = bass.Bass("TRN2", target_bir_lowering=False)
    io = {}

    def din(name, shape, dt=F32):
        io[name] = nc.dram_tensor(name, list(shape), dt, kind="ExternalInput").ap()

    din("xin", [D, HAL_E + NT])
    din("w_in", [D, 3072])
    din("w_out", [D, D])
    din("bin", [128, 24])
    din("convw", [128, 4 * 31])
    din("convb", [128, 4])
    din("clng", [128, 4])
    din("clnb", [128, 4])
    din("lb0", [128, 4])
    din("lb1", [128, 4])
    din("lbsel", [128, 1])
    din("ong", [128, 4])
    din("hsc", [128, 1])
    din("lng", [128, KC])
    din("lnb", [128, KC])
    din("bivbc", [128, 512])
    din("mask", [128, 128])
    din("rmask", [128, TT])
    din("ident", [128, 128])
    din("S", [128, 4, 128])
    io["out"] = nc.dram_tensor("out", [D, NT], F32, kind="ExternalOutput").ap()
    io["S_out"] = nc.dram_tensor("S_out", [128, 4, 128], F32, kind="ExternalOutput").ap()
    P = Prog(nc)
    P.begin_phase()
    build_even(nc, P, NT, TT, io)
    P.end_phase()
    P.finish()
    return nc, P


def even_consts(TT):
    i = np.arange(128)
    mask = ((i[:, None] // CH == i[None, :] // CH) & (i[:, None] <= i[None, :])).astype(np.float32)
    rmask = np.ones((128, TT), np.float32)
    rmask[:, ::CH] = 0.0
    return dict(mask=mask, rmask=rmask, ident=np.eye(128, dtype=np.float32))


def even_inputs(xin_T, w_in, b_in, conv_w, conv_b, cln_g, cln_b, lb_logits, j, onorm_g, w_out, g, b, S, first, TT):
    d = even_consts(TT)
    cwl = np.stack([vec_layout(conv_w[t], 4) for t in range(31)], axis=-1).reshape(128, 4 * 31)
    d.update(xin=np.ascontiguousarray(xin_T, dtype=np.float32), w_in=np.ascontiguousarray(w_in),
             w_out=np.ascontiguousarray(w_out), bin=vec_layout(b_in, 24), convw=np.ascontiguousarray(cwl),
             convb=vec_layout(conv_b, 4), clng=vec_layout(cln_g, 4), clnb=vec_layout(cln_b, 4),
             lb0=vec_layout(lb_logits[0], 4), lb1=vec_layout(lb_logits[1], 4),
             lbsel=np.full((128, 1), 1.0 if j == 1 else 0.0, np.float32),
             ong=vec_layout(onorm_g, 4), hsc=np.full((128, 1), 0.0 if first else 1.0, np.float32),
             lng=vec_layout(g, KC), lnb=vec_layout(b, KC),
             bivbc=np.ascontiguousarray(np.broadcast_to(np.asarray(b_in, np.float32)[2048:2560], (128, 512))),
             S=np.ascontiguousarray(S, dtype=np.float32))
    return d


C = 64
H = 16
ORDER_OVERRIDE = None
C0 = float(np.exp(-0.5))


def build_odd(nc, P, NT, io, state_only=False):
    TT = C
    ntiles = NT // TT
    xk = io.get("xin_key", ("xin_ext",))
    ok = io.get("out_key", ("outdram",))
    full = not state_only
    xin_v = io["xin"].rearrange("(kc p) t -> p kc t", p=128)
    out_v = io["out"].rearrange("(kc p) t -> p kc t", p=128)
    sb, ps = P.sb, P.ps
    wr, wk, wv = (sb([128, KC, D], BF16, n) for n in ("wr", "wk", "wv"))
    wo = sb([64, H, D], BF16, "wo")
    w1 = sb([128, KC, 64], BF16, "w1"); a1 = sb([128, KC, 64], BF16, "a1"); g1 = sb([128, KC, 160], BF16, "g1")
    w2 = sb([64, D], BF16, "w2"); a2 = sb([64, D], BF16, "a2")
    g2a = sb([128, D], BF16, "g2a"); g2b = sb([32, D], BF16, "g2b")
    mu = sb([128, 6, KC], F32, "mu")
    hv = {n: sb([64, H], F32, n) for n in ("w0h", "a0h", "kkh", "kah", "rkh", "gng", "gnb")}
    omka = sb([64, H], F32, "omka")
    lng = sb([128, KC], F32, "lng"); lnb = sb([128, KC], F32, "lnb")
    mUs = sb([64, 64], F32, "mUs"); mUi = sb([64, 64], F32, "mUi"); mLs = sb([64, 64], F32, "mLs")
    eye = sb([64, 64], F32, "eye"); rmask = sb([64, H * C], F32, "rmask")
    id32 = sb([64, 64], F32, "id32"); ident = sb([64, 64], BF16, "ident")
    ones64 = sb([64, 64], F32, "ones64"); ones1 = sb([64, 64], F32, "ones1"); ones1024 = sb([128, 128], F32, "ones1024")
    M = sb([64, H, 64], F32, "M"); Mb = sb([64, H, 64], BF16, "Mb")
    XT = [sb([128, KC, TT + 1], F32, f"XT{i}") for i in range(2)]
    xx = sb([128, KC, TT], F32, "xx"); xt_ = sb([128, KC, TT], F32, "xt_")
    xm = [sb([128, KC, TT], BF16, f"xm{i}") for i in range(2)]
    F = {n: sb([64, H, C], F32, n) for n in ("rF", "kF", "vF", "aF", "gF", "sg", "cum", "kap", "t0", "t1")}
    F["sg"] = F["sg"]; F["cum"] = F["cum"]
    B = {n: sb([64, H, C], BF16, n) for n in ("KT", "BT", "KK", "RT", "BH", "KH", "Tt",
                                               "AkT", "PbT", "PkT", "zB", "BHT", "KHT")}
    for n in ("N0", "N1", "Nt0", "Nt1"):
        for g_ in range(2):
            B[f"{n}_{g_}"] = sb([64, 8, C], BF16, f"{n}_{g_}")
    B["nZ"] = B["BH"]; B["U"] = B["KH"]

    def hd(t, h):
        return t[:, h, :] if t.shape[1] == H else t[:, h % 8, :]

    def grp(t, g):
        return t[:, g * 8:(g + 1) * 8, :] if t.shape[1] == H else t[:]
    Vtm = sb([64, D], BF16, "Vtm")
    twB = sb([64, C], BF16, "twB"); taB = sb([64, C], BF16, "taB"); tgA = sb([128, C], BF16, "tgA"); tgB = sb([32, C], BF16, "tgB")
    gC = sb([64, H], F32, "gC")
    sq8 = sb([128, KC, TT], F32, "sq8")
    mean_sb = sb([128, TT], F32, "mean_sb"); var_sb = sb([128, TT], F32, "var_sb"); rstd_sb = sb([128, TT], F32, "rstd_sb")
    pb = [ps([128, 512], F32, f"pb{i}") for i in range(6)]
    trp = ps([64, 8, 64], BF16, "trp")
    stat = ps([128, 2, 256], F32, "stat", keys=["mean_ps", "msq_ps"])
    mean_ps = stat[:, 0, 0:TT]; msq_ps = stat[:, 1, 0:TT]

    names = {}
    for d_ in (F, B, hv):
        for n_, t_ in d_.items():
            names.setdefault(id(t_), n_)
    for n_, t_ in (("mUs", mUs), ("mUi", mUi), ("mLs", mLs), ("eye", eye), ("twB", twB), ("taB", taB), ("tgA", tgA), ("tgB", tgB)):
        names[id(t_)] = n_

    def kn(t):
        return names[id(t)]

    def bk(i):
        return pb[i][0:64, :].rearrange("p (h c) -> p h c", c=64)

    def ld(t, nm):
        P.dma("sp", lambda e: e.dma_start(out=t[:], in_=io[nm]), r=([io.get("M_key", ("M_ext",))] if nm == "M" else []), w=[nm])

    for nm, t in list(hv.items()) + [("mu", mu), ("lng", lng), ("lnb", lnb), ("mUs", mUs), ("mUi", mUi), ("mLs", mLs),
                                     ("eye", eye), ("rmask_o", rmask), ("id32_o", id32), ("M", M)]:
        ld(t, nm)
    P.pool(lambda e: e.tensor_copy(out=ident[:], in_=id32[:]), r=["id32_o"], w=["ident"])
    P.pool(lambda e: e.memset(ones64[:], 1.0 / 64), w=["ones64"])
    P.pool(lambda e: e.memset(ones1[:], 1.0), w=["ones1"])
    P.pool(lambda e: e.memset(ones1024[:], 1.0 / D), w=["ones1024"])
    P.dve(lambda e: e.tensor_scalar(out=omka[:], in0=hv["kah"][:], scalar1=-1.0, scalar2=1.0, op0=ALU.mult, op1=ALU.add),
          r=["kah"], w=["omka"])
    P.act(lambda e: e.activation(out=Mb[:], in_=M[:], func=AF.Identity), r=["M"], w=["Mb"])
    stg = Stager(P, 1024)
    for nm, t in (("w_r", wr), ("w_k", wk), ("w_v", wv)):
        v = io[nm].rearrange("(kc p) f -> p kc f", p=128)
        for kc in range(KC):
            stg.load(t[:, kc, :], v[:, kc, :], 1024, (nm, kc // 2))
    wov = io["w_o"].rearrange("(h p) d -> p h d", p=64)

    def load64(dst, src, n, key, rows=64):
        i = stg.n % 2
        stg.n += 1
        st = stg.bufs[i]
        P.dma("sp", lambda e: e.dma_start(out=st[0:rows, 0:n], in_=src), w=[("stage", i)])
        cast_op(P, ("act", "dve")[stg.n % 2], dst, st[0:rows, 0:n], [("stage", i)], [key])

    for h in range(H):
        load64(wo[:, h, :], wov[:, h, :], D, ("wo", h // 2))
    for nm, t, n in (("w1", w1, 64), ("a1", a1, 64)):
        v = io[nm].rearrange("(kc p) f -> p kc f", p=128)
        stg.load(t[:], v, KC * n, nm, a=KC)
    g1v = io["g1"].rearrange("(kc p) f -> p kc f", p=128)
    for q in range(2):
        stg.load(g1[:, q * 4:(q + 1) * 4, :], g1v[:, q * 4:(q + 1) * 4, :], 640, ("g1", q), a=4)
    load64(w2[:], io["w2"], D, "w2")
    load64(a2[:], io["a2"], D, "a2")
    stg.load(g2a[:], io["g2"][0:128, :], D, "g2a")
    load64(g2b[:], io["g2"][128:160, :], D, "g2b", rows=32)

    def bc(v):
        return v[:].unsqueeze(2).to_broadcast([64, H, C])

    def tt(eng, out, a, b, op, r, w):
        P.op(eng, lambda e: e.tensor_tensor(out=out, in0=a, in1=b, op=op), r, w)

    def headproj(w_t, wkey, xmi, dstF, post=None):
        for g in range(2):
            bank = bk(g)
            for hh in range(8):
                h = g * 8 + hh
                for kc in range(KC):
                    P.pe(_mm(bank[:, hh, :], w_t[:, kc, h * 64:(h + 1) * 64], xm[xmi][:, kc, :], kc == 0, kc == KC - 1),
                         r=[(wkey, kc // 2), ("xm", xmi)], w=[("pb", g)])
            P.act(lambda e, g=g, bank=bank: e.activation(out=dstF[:, g * 8:(g + 1) * 8, :], in_=bank, func=AF.Identity),
                  r=[("pb", g)], w=[(kn(dstF), g)])

    for i in range(6):
        P.bank_of[("pb", i)] = f"pb{i}"

    def load_x(j):
        s = j % 2
        P.dma("sp", lambda e: e.dma_start(out=XT[s][:], in_=xin_v[:, :, j * TT:j * TT + TT + 1]), r=[xk], w=[("XT", s)])

    xm4 = [xm[0], xm[1], P.sb([128, KC, TT], BF16, "xm2"), P.sb([128, KC, TT], BF16, "xm3")]
    xr_ = P.sb([128, KC, TT], F32, "xr_")

    def mixop(X, s, i, slot):
        P.dve(lambda e: e.tensor_tensor(out=xt_[:], in0=xx[:], in1=mu[:, i, :].unsqueeze(2).to_broadcast([128, KC, TT]),
                                        op=ALU.mult), r=["xx", "mu"], w=["xt_"])
        P.dve(lambda e: e.tensor_tensor(out=xm4[slot][:], in0=xt_[:], in1=X[:, :, 1:TT + 1], op=ALU.add),
              r=["xt_", ("XT", s)], w=[("xm", slot)])

    stg_tm = [stg.bufs[i][0:64, :].bitcast(BF16)[:, 0:D] for i in range(2)]

    def hproj_mm(w_t, wkey, slot, banks, dst_tm, tmk):
        for half in range(2):
            bi = banks[half]
            for kc in range(KC):
                P.pe(_mm(pb[bi][0:64, :], xm4[slot][:, kc, :], w_t[:, kc, half * 512:(half + 1) * 512], kc == 0, kc == KC - 1),
                     r=[("xm", slot), (wkey, kc // 2)], w=[("pb", bi)])
            P.act(lambda e, half=half, bi=bi: e.activation(out=dst_tm[:, half * 512:(half + 1) * 512], in_=pb[bi][0:64, :],
                                                           func=AF.Identity), r=[("pb", bi)], w=[tmk + (half,)])

    def hproj_tr(dst_tm, tmk, dstF):
        for g in range(2):
            for hh in range(8):
                h = g * 8 + hh
                P.pe(lambda e, hh=hh, h=h: e.transpose(trp[:, hh, :], dst_tm[:, h * 64:(h + 1) * 64], ident[:]),
                     r=[tmk + (g,), "ident"], w=["trp"])
            P.act(lambda e, g=g: e.activation(out=dstF[:, g * 8:(g + 1) * 8, :], in_=trp[:], func=AF.Identity),
                  r=["trp"], w=[(kn(dstF), g)])

    def early_segments(j):
        s = j % 2
        X = XT[s]

        def e1():
            tt("dve", xx[:], X[:, :, 0:TT], X[:, :, 1:TT + 1], ALU.subtract, [("XT", s)], ["xx"])
            if full:
                mixop(X, s, 0, 0)
            mixop(X, s, 2, 1)
            mixop(X, s, 3, 2)
            if full:
                hproj_mm(wr, "w_r", 0, (0, 1), stg_tm[1], ("stage", 1))
            hproj_mm(wk, "w_k", 1, (2, 3), stg_tm[0], ("stage", 0))

        def e2():
            if full:
                hproj_tr(stg_tm[1], ("stage", 1), F["rF"])
            hproj_mm(wv, "w_v", 2, (0, 1), Vtm, ("Vtm",))

        def e3():
            hproj_tr(stg_tm[0], ("stage", 0), F["kF"])
            hproj_tr(Vtm, ("Vtm",), F["vF"])

        def e4():
            for (i, l1, l1k, mid_, fn_) in ((1, w1, "w1", twB, AF.Tanh), (4, a1, "a1", taB, AF.Identity)):
                mixop(X, s, i, 3)
                for kc in range(KC):
                    P.pe(_mm(pb[3][0:64, 0:C], l1[:, kc, :], xm4[3][:, kc, :], kc == 0, kc == KC - 1),
                         r=[l1k, ("xm", 3)], w=[("pb", 3)])
                P.act(lambda e, mid_=mid_, fn_=fn_: e.activation(out=mid_[:], in_=pb[3][0:64, 0:C], func=fn_),
                      r=[("pb", 3)], w=[kn(mid_)])
            if full:
                mixop(X, s, 5, 3)
                for (dst, lo, n) in ((tgA, 0, 128), (tgB, 128, 32)):
                    for kc in range(KC):
                        P.pe(_mm(pb[3][0:n, 0:C], g1[:, kc, lo:lo + n], xm4[3][:, kc, :], kc == 0, kc == KC - 1),
                             r=["g1", ("xm", 3)], w=[("pb", 3)])
                    P.act(lambda e, dst=dst, n=n: e.activation(out=dst[:], in_=pb[3][0:n, 0:C], func=AF.Sigmoid),
                          r=[("pb", 3)], w=[kn(dst)])
        return [e1, e2, e3, e4]

    def lora2(l2, l2k, mid_, bias, outF, outfunc):
        for g in range(2):
            bank = bk(g)
            for hh in range(8):
                h = g * 8 + hh
                P.pe(_mm(bank[:, hh, :], l2[:, h * 64:(h + 1) * 64], mid_[:], True, True),
                     r=[l2k, kn(mid_)], w=[("pb", g)])
            tt("dve", outF[:, g * 8:(g + 1) * 8, :], bank, bias[:, g * 8:(g + 1) * 8].unsqueeze(2).to_broadcast([64, 8, C]),
               ALU.add, [("pb", g), kn(bias)], [(kn(outF), g)])
        P.act(lambda e: e.activation(out=outF[:], in_=outF[:], func=outfunc), r=[kn(outF)], w=[kn(outF)])

    def mid(j):
        rF, kF, vF, aF, sg, cum, kap, t0, t1 = (F[n] for n in ("rF", "kF", "vF", "aF", "sg", "cum", "kap", "t0", "t1"))
        t2 = sg
        tt("dve", kap[:], kF[:], bc(hv["kkh"]), ALU.mult, ["kF", "kkh"], ["kap"])
        lora2(w2, "w2", twB, hv["w0h"], F["sg"], AF.Sigmoid)
        P.act(lambda e: e.activation(out=t0[:], in_=kap[:], func=AF.Square), r=["kap"], w=["t0"])
        t0f = t0[:].rearrange("p h c -> p (h c)")
        for half in range(2):
            P.pe(_mm(pb[2][0:64, :], ones1[:], t0f[:, half * 512:(half + 1) * 512], True, True), r=["ones1", "t0"], w=[("pb", 2)])
            P.dve(lambda e, half=half: e.tensor_scalar(out=t1[:].rearrange("p h c -> p (h c)")[:, half * 512:(half + 1) * 512],
                                                       in0=pb[2][0:64, :], scalar1=1e-18, scalar2=None, op0=ALU.max),
                  r=[("pb", 2)], w=[("t1", half)])
        lora2(a2, "a2", taB, hv["a0h"], F["aF"], AF.Sigmoid)
        P.act(lambda e: e.activation(out=t1[:], in_=t1[:], func=AF.Ln), r=["t1"], w=["t1"])
        P.act(lambda e: e.activation(out=t1[:], in_=t1[:], func=AF.Exp, scale=-0.5), r=["t1"], w=["t1"])
        for g in (range(2) if full else ()):
            bank = bk(g)
            for hh in range(8):
                h = g * 8 + hh
                P.pe(_mm(bank[:, hh, :], g2a[:, h * 64:(h + 1) * 64], tgA[:], True, False), r=["g2a", kn(tgA)], w=[("pb", g)])
                P.pe(_mm(bank[:, hh, :], g2b[:, h * 64:(h + 1) * 64], tgB[:], False, True), r=["g2b", kn(tgB)], w=[("pb", g)])
            P.act(lambda e, g=g, bank=bank: e.activation(out=F["gF"][:, g * 8:(g + 1) * 8, :], in_=bank, func=AF.Identity),
                  r=[("pb", g)], w=[("gF", g)])
        tt("dve", kap[:], kap[:], t1[:], ALU.mult, ["kap", "t1"], ["kap"])
        tt("pool", t0[:], aF[:], bc(hv["kah"]), ALU.mult, ["aF", "kah"], ["t0"])
        tt("pool", t0[:], t0[:], bc(omka), ALU.add, ["t0", "omka"], ["t0"])
        tt("pool", kF[:], kF[:], t0[:], ALU.mult, ["kF", "t0"], ["kF"])
        P.dve(lambda e: e.tensor_tensor_scan(out=cum[:].rearrange("p h c -> p (h c)"), data0=rmask[:],
                                             data1=sg[:].rearrange("p h c -> p (h c)"), initial=0.0,
                                             op0=ALU.mult, op1=ALU.add), r=["rmask_o", "sg"], w=["cum"])
        tt("dve", t0[:], cum[:], sg[:], ALU.subtract, ["cum", "sg"], ["t0"])
        tt("dve", t2[:], kap[:], aF[:], ALU.mult, ["kap", "aF"], ["sg"])
        P.act(lambda e: e.activation(out=t0[:], in_=t0[:], func=AF.Exp, scale=-C0), r=["t0"], w=["t0"])
        tt("dve", B["KT"][:], kap[:], t0[:], ALU.mult, ["kap", "t0"], ["KT"])
        P.act(lambda e: e.activation(out=t0[:], in_=cum[:], func=AF.Exp, scale=-C0), r=["cum"], w=["t0"])
        if full:
            tt("dve", B["RT"][:], rF[:], t0[:], ALU.mult, ["rF", "t0"], ["RT"])
        P.act(lambda e: e.activation(out=gC[:], in_=cum[:, :, C - 1], func=AF.Exp, scale=-C0), r=["cum"], w=["gC"])
        P.act(lambda e: e.activation(out=t1[:], in_=cum[:], func=AF.Exp, scale=C0), r=["cum"], w=["t1"])
        tt("dve", t2[:], t2[:], t1[:], ALU.mult, ["sg", "t1"], ["sg"])
        tt("dve", t1[:], kF[:], t1[:], ALU.mult, ["kF", "t1"], ["t1"])
        P.act(lambda e: e.activation(out=B["BT"][:], in_=t2[:], func=AF.Identity), r=["sg"], w=["BT"])
        P.act(lambda e: e.activation(out=B["KK"][:], in_=t1[:], func=AF.Identity), r=["t1"], w=["KK"])
        tt("dve", B["BH"][:], t2[:], bc(gC), ALU.mult, ["sg", "gC"], ["BH"])
        tt("dve", B["KH"][:], t1[:], bc(gC), ALU.mult, ["t1", "gC"], ["KH"])
        if full:
            tt("dve", t0[:], rF[:], kF[:], ALU.mult, ["rF", "kF"], ["t0"])
            tt("dve", t0[:], t0[:], bc(hv["rkh"]), ALU.mult, ["t0", "rkh"], ["t0"])
            t0f_ = t0[:].rearrange("p h c -> p (h c)")
            for half in range(2):
                hs = slice(half * 512, (half + 1) * 512)
                P.pe(_mm(pb[2][0:64, :], ones1[:], t0f_[:, hs], True, True), r=["ones1", "t0"], w=[("pb", 2)])
                tt("dve", kap[:].rearrange("p h c -> p (h c)")[:, hs], pb[2][0:64, :],
                   vF[:].rearrange("p h c -> p (h c)")[:, hs], ALU.mult, [("pb", 2), "vF"], [("kap", half)])
        P.stage = f"mid{j}_tr"
        for src, dst in (("BH", "BHT"), ("KH", "KHT")):
            for g in range(2):
                for hh in range(8):
                    h = g * 8 + hh
                    P.pe(lambda e, hh=hh, h=h, src=src: e.transpose(trp[:, hh, :], B[src][:, h, :], ident[:]),
                         r=[src, "ident"], w=["trp"])
                P.act(lambda e, g=g, dst=dst: e.activation(out=B[dst][:, g * 8:(g + 1) * 8, :], in_=trp[:], func=AF.Identity),
                      r=["trp"], w=[(dst, g)])
        P.stage = f"mid{j}_scores"
        BS = ((0, 1, 2), (3, 4, 5))

        def score(lhs, rhs, dst, mask, neg, g, bank_i):
            bank = bk(bank_i)
            for hh in range(8):
                h = g * 8 + hh
                P.pe(_mm(bank[:, hh, :], B[lhs][:, h, :], B[rhs][:, h, :], True, True), r=[lhs, rhs], w=[("pb", bank_i)])
            P.dve(lambda e: e.scalar_tensor_tensor(out=grp(B[dst], g), in0=bank, scalar=(-1.0 if neg else 1.0),
                                                   in1=mask[:].unsqueeze(1).to_broadcast([64, 8, 64]),
                                                   op0=ALU.mult, op1=ALU.mult), r=[("pb", bank_i), kn(mask)], w=[(dst, g)])

        def NB(name, g):
            return B[f"{name}_{g}"]

        for g in range(2):
            score("KT", "BT", f"N0_{g}", mLs, True, g, BS[g][0])
            score("BT", "KT", f"Nt0_{g}", mUs, True, g, BS[g][1])
        for g in range(2):
            score("KK", "KT", "AkT", mUs, False, g, BS[g][2])
            if full:
                score("BT", "RT", "PbT", mUi, False, g, BS[g][0])
        for g in range(2):
            gs = slice(g * 8, (g + 1) * 8)
            if full:
                score("KK", "RT", "PkT", mUi, False, g, BS[g][1])
            tt("dve", B["Tt"][:, gs, :], NB("Nt0", g)[:], eye[:].unsqueeze(1).to_broadcast([64, 8, 64]), ALU.add,
               [(f"Nt0_{g}", g), "eye"], [("Tt", g)])
        cur = 0
        P.stage = f"mid{j}_dbl"
        for lv in range(5):
            for g in range(2):
                gs = slice(g * 8, (g + 1) * 8)
                Nc, Ntc = NB(f"N{cur}", g), NB(f"Nt{cur}", g)
                Nn, Ntn = NB(f"N{1 - cur}", g), NB(f"Nt{1 - cur}", g)
                kNc, kNtc = (f"N{cur}_{g}", g), (f"Nt{cur}_{g}", g)
                kNn, kNtn = (f"N{1 - cur}_{g}", g), (f"Nt{1 - cur}_{g}", g)
                bA, bB, bCk = (bk(i) for i in BS[g])
                kA, kB, kC = (("pb", i) for i in BS[g])
                for hh in range(8):
                    P.pe(_mm(bA[:, hh, :], Ntc[:, hh, :], Nc[:, hh, :], True, True), r=[kNc, kNtc], w=[kA])
                P.act(lambda e, Nn=Nn, bA=bA: e.activation(out=Nn[:], in_=bA, func=AF.Identity), r=[kA], w=[kNn])
                if lv < 4:
                    for hh in range(8):
                        P.pe(_mm(bB[:, hh, :], Nc[:, hh, :], Ntc[:, hh, :], True, True), r=[kNc, kNtc], w=[kB])
                    P.act(lambda e, Ntn=Ntn, bB=bB: e.activation(out=Ntn[:], in_=bB, func=AF.Identity), r=[kB], w=[kNtn])
                for hh in range(8):
                    h = g * 8 + hh
                    P.pe(_mm(bCk[:, hh, :], Nn[:, hh, :], B["Tt"][:, h, :], True, True), r=[kNn, ("Tt", g)], w=[kC])
                tt("dve", B["Tt"][:, gs, :], B["Tt"][:, gs, :], bCk, ALU.add, [("Tt", g), kC], [("Tt", g)])
            cur = 1 - cur
        P.stage = f"mid{j}_state"
        for g in range(2):
            gs = slice(g * 8, (g + 1) * 8)
            bZ = bk(BS[g][0]); kZ = ("pb", BS[g][0])
            for hh in range(8):
                h = g * 8 + hh
                vh = Vtm[:, h * 64:(h + 1) * 64]
                P.pe(_mm(bZ[:, hh, :], B["KT"][:, h, :], Mb[:, h, :], True, False), r=["KT", ("Mb", g)], w=[kZ])
                P.pe(_mm(bZ[:, hh, :], B["AkT"][:, h, :], vh, False, True), r=[("AkT", g), "Vtm"], w=[kZ])
            P.act(lambda e, bZ=bZ, gs=gs: e.activation(out=B["nZ"][:, gs, :], in_=bZ, func=AF.Identity, scale=-1.0),
                  r=[kZ], w=[("BH", g)])
        for g in range(2):
            gs = slice(g * 8, (g + 1) * 8)
            bU = bk(BS[g][1]); kU = ("pb", BS[g][1])
            for hh in range(8):
                h = g * 8 + hh
                P.pe(_mm(bU[:, hh, :], B["Tt"][:, h, :], B["nZ"][:, h, :], True, True), r=[("Tt", g), ("BH", g)], w=[kU])
            P.act(lambda e, bU=bU, gs=gs: e.activation(out=B["U"][:, gs, :], in_=bU, func=AF.Identity), r=[kU], w=[("KH", g)])
        for g in range(2):
            gs = slice(g * 8, (g + 1) * 8)
            bM = bk(BS[g][2]); kM = ("pb", BS[g][2])
            bY = bk(BS[g][0]); kY = ("pb", BS[g][0])
            for hh in (range(8) if full else ()):
                h = g * 8 + hh
                vh = Vtm[:, h * 64:(h + 1) * 64]
                P.pe(_mm(bY[:, hh, :], Mb[:, h, :], B["RT"][:, h, :], True, False), r=[("Mb", g), "RT"], w=[kY])
                P.pe(_mm(bY[:, hh, :], B["U"][:, h, :], B["PbT"][:, h, :], False, False), r=[("KH", g), ("PbT", g)], w=[kY])
                P.pe(_mm(bY[:, hh, :], vh, B["PkT"][:, h, :], False, True), r=["Vtm", ("PkT", g)], w=[kY])
            if full:
                P.act(lambda e, bY=bY, gs=gs: e.activation(out=F["cum"][:, gs, :], in_=bY, func=AF.Identity), r=[kY], w=[("cum", g)])
            for hh in range(8):
                h = g * 8 + hh
                vh = Vtm[:, h * 64:(h + 1) * 64]
                P.pe(_mm(bM[:, hh, :], B["BHT"][:, h, :], B["U"][:, h, :], True, False), r=[("BHT", g), ("KH", g)], w=[kM])
                P.pe(_mm(bM[:, hh, :], B["KHT"][:, h, :], vh, False, True), r=[("KHT", g), "Vtm"], w=[kM])
            tt("dve", M[:, gs, :], M[:, gs, :], gC[:, gs].unsqueeze(2).to_broadcast([64, 8, 64]), ALU.mult,
               [("M", g), "gC"], [("M", g)])
            tt("dve", M[:, gs, :], M[:, gs, :], bM, ALU.add, [("M", g), kM], [("M", g)])
            P.act(lambda e, gs=gs: e.activation(out=Mb[:, gs, :], in_=M[:, gs, :], func=AF.Identity), r=[("M", g)], w=[("Mb", g)])
    def late_segments(j):
        s = j % 2
        X = XT[s]
        rF, kF, vF, aF, sg, cum, kap, t0, t1 = (F[n] for n in ("rF", "kF", "vF", "aF", "sg", "cum", "kap", "t0", "t1"))
        t2 = sg
        Y = F["cum"]
        Yf = Y[:].rearrange("p h c -> p (h c)")
        t0f = t0[:].rearrange("p h c -> p (h c)"); t1f = t1[:].rearrange("p h c -> p (h c)"); t2f = t2[:].rearrange("p h c -> p (h c)")

        def l1():
            P.act(lambda e: e.activation(out=t0[:], in_=Y[:], func=AF.Square), r=["cum"], w=["t0"])
            for half in range(2):
                hs = slice(half * 512, (half + 1) * 512)
                P.pe(_mm(pb[4][0:64, :], ones64[:], Yf[:, hs], True, True), r=["ones64", "cum"], w=[("pb", 4)])
                P.pe(_mm(pb[5][0:64, :], ones64[:], t0f[:, hs], True, True), r=["ones64", "t0"], w=[("pb", 5)])
                P.act(lambda e, hs=hs: e.activation(out=t1f[:, hs], in_=pb[4][0:64, :], func=AF.Identity), r=[("pb", 4)], w=[("t1", half)])
                tt("dve", t2f[:, hs], t1f[:, hs], t1f[:, hs], ALU.mult, [("t1", half)], [("sg", half)])
                tt("dve", t2f[:, hs], pb[5][0:64, :], t2f[:, hs], ALU.subtract, [("pb", 5), ("sg", half)], [("sg", half)])
            P.dve(lambda e: e.tensor_scalar(out=t2[:], in0=t2[:], scalar1=0.0, scalar2=64e-5, op0=ALU.max, op1=ALU.add), r=["sg"], w=["sg"])
            P.act(lambda e: e.activation(out=t2[:], in_=t2[:], func=AF.Ln), r=["sg"], w=["sg"])
            P.act(lambda e: e.activation(out=t2[:], in_=t2[:], func=AF.Exp, scale=-0.5), r=["sg"], w=["sg"])

        def l2():
            tt("dve", Y[:], Y[:], t1[:], ALU.subtract, ["cum", "t1"], ["cum"])
            tt("dve", Y[:], Y[:], t2[:], ALU.mult, ["cum", "sg"], ["cum"])
            tt("dve", Y[:], Y[:], bc(hv["gng"]), ALU.mult, ["cum", "gng"], ["cum"])
            tt("dve", Y[:], Y[:], bc(hv["gnb"]), ALU.add, ["cum", "gnb"], ["cum"])
            tt("dve", Y[:], Y[:], kap[:], ALU.add, ["cum", "kap"], ["cum"])
            tt("dve", B["zB"][:], Y[:], F["gF"][:], ALU.mult, ["cum", "gF"], ["zB"])

        def l3():
            ob = pb[4][:, :].rearrange("p (c t) -> p c t", t=TT)
            for db in range(KC):
                for h in range(H):
                    P.pe(_mm(ob[:, db, :], wo[:, h, db * 128:(db + 1) * 128], B["zB"][:, h, :], h == 0, h == H - 1),
                         r=[("wo", h // 2), "zB"], w=[("pb", 4)])
            P.dve(lambda e: e.scalar_tensor_tensor(out=xr_[:], in0=X[:, :, 1:TT + 1], scalar=float(ALPHA), in1=ob,
                                                   op0=ALU.mult, op1=ALU.add), r=[("XT", s), ("pb", 4)], w=["xr_"])

        def l4():
            for c in range(KC):
                P.pe(_mm(mean_ps, ones1024[:], xr_[:, c, :], c == 0, c == KC - 1), r=["ones1024", "xr_"], w=["mean_ps"])
            P.act(lambda e: e.activation(out=sq8[:], in_=xr_[:], func=AF.Square), r=["xr_"], w=["sq8"])
            for c in range(KC):
                P.pe(_mm(msq_ps, ones1024[:], sq8[:, c, :], c == 0, c == KC - 1), r=["ones1024", "sq8"], w=["msq_ps"])
            P.act(lambda e: e.activation(out=mean_sb[:], in_=mean_ps, func=AF.Identity), r=["mean_ps"], w=["mean_sb"])
            tt("dve", var_sb[:], mean_sb[:], mean_sb[:], ALU.mult, ["mean_sb"], ["var_sb"])
            tt("dve", var_sb[:], msq_ps, var_sb[:], ALU.subtract, ["msq_ps", "var_sb"], ["var_sb"])
            P.dve(lambda e: e.tensor_scalar(out=var_sb[:], in0=var_sb[:], scalar1=0.0, scalar2=float(LN_EPS),
                                            op0=ALU.max, op1=ALU.add), r=["var_sb"], w=["var_sb"])
            P.act(lambda e: e.activation(out=var_sb[:], in_=var_sb[:], func=AF.Ln), r=["var_sb"], w=["var_sb"])
            P.act(lambda e: e.activation(out=rstd_sb[:], in_=var_sb[:], func=AF.Exp, scale=-0.5), r=["var_sb"], w=["rstd_sb"])
            tt("dve", xr_[:], xr_[:], mean_sb[:].unsqueeze(1).to_broadcast([128, KC, TT]), ALU.subtract, ["xr_", "mean_sb"], ["xr_"])
            tt("dve", xr_[:], xr_[:], rstd_sb[:].unsqueeze(1).to_broadcast([128, KC, TT]), ALU.mult, ["xr_", "rstd_sb"], ["xr_"])
            tt("dve", xr_[:], xr_[:], lng[:].unsqueeze(2).to_broadcast([128, KC, TT]), ALU.mult, ["xr_", "lng"], ["xr_"])
            tt("dve", xr_[:], xr_[:], lnb[:].unsqueeze(2).to_broadcast([128, KC, TT]), ALU.add, ["xr_", "lnb"], ["xr_"])
            P.dma("sp", lambda e: e.dma_start(out=out_v[:, :, j * TT:(j + 1) * TT], in_=xr_[:]), r=["xr_"], w=[ok + (j,)])
        return [l1, l2, l3, l4]

    load_x(0)
    if ntiles > 1:
        load_x(1)
    for seg in early_segments(0):
        seg()
    for j in range(ntiles):
        P.stage = f"mid{j}"
        mid(j)
        E = early_segments(j + 1) if j + 1 < ntiles else []
        L = late_segments(j) if full else []
        order = [("L", 0), ("E", 0), ("L", 1), ("E", 1), ("L", 2), ("E", 2), ("L", 3), ("E", 3)]
        order = ORDER_OVERRIDE or order
        for kind, k in order:
            lst = L if kind == "L" else E
            if k < len(lst):
                P.stage = (f"late{j}" if kind == "L" else f"early{j + 1}")
                lst[k]()
        if j + 2 < ntiles:
            load_x(j + 2)
    P.dma("sp", lambda e: e.dma_start(out=io["M_out"], in_=M[:]), r=["M"], w=[io.get("Mout_key", ("M_out",))])


ODD_IN = dict(w_r=[D, D], w_k=[D, D], w_v=[D, D], w_o=[D, D], w1=[D, 64], a1=[D, 64], g1=[D, 160], w2=[64, D], a2=[64, D],
              g2=[160, D], mu=[128, 6, KC], w0h=[64, H], a0h=[64, H], kkh=[64, H], kah=[64, H], rkh=[64, H], gng=[64, H],
              gnb=[64, H], lng=[128, KC], lnb=[128, KC], mUs=[64, 64], mUi=[64, 64], mLs=[64, 64], eye=[64, 64],
              rmask_o=[64, H * C], id32_o=[64, 64], M=[64, H, 64])


def build_odd_nc(NT):
    nc = bass.Bass("TRN2", target_bir_lowering=False)
    io = {"xin": nc.dram_tensor("xin", [D, 1 + NT], F32, kind="ExternalInput").ap()}
    for k, shp in ODD_IN.items():
        io[k] = nc.dram_tensor(k, list(shp), F32, kind="ExternalInput").ap()
    io["out"] = nc.dram_tensor("out", [D, NT], F32, kind="ExternalOutput").ap()
    io["M_out"] = nc.dram_tensor("M_out", [64, H, 64], F32, kind="ExternalOutput").ap()
    P = Prog(nc)
    P.begin_phase()
    build_odd(nc, P, NT, io)
    P.end_phase()
    P.finish()
    return nc, P


def hvec(v):
    return np.ascontiguousarray(np.asarray(v, np.float32).reshape(H, 64).T)


def odd_inputs(xin_T, p, j, lng, lnb, M):
    i = np.arange(64)
    d = dict(xin=np.ascontiguousarray(xin_T, dtype=np.float32))
    for k in ("w_r", "w_k", "w_v", "w_o", "w1", "a1", "g1", "w2", "a2", "g2"):
        d[k] = np.ascontiguousarray(p["rw_" + k][j], dtype=np.float32)
    d["mu"] = np.ascontiguousarray(np.stack([vec_layout(p["rw_mu"][j][i6], KC) for i6 in range(6)], axis=1))
    d["w0h"] = hvec(p["rw_w0"][j]); d["a0h"] = hvec(p["rw_a0"][j]); d["kkh"] = hvec(p["rw_k_k"][j])
    d["kah"] = hvec(p["rw_k_a"][j]); d["rkh"] = hvec(p["rw_r_k"][j].reshape(-1)); d["gng"] = hvec(p["rw_gn_g"][j])
    d["gnb"] = hvec(p["rw_gn_b"][j]); d["lng"] = vec_layout(lng, KC); d["lnb"] = vec_layout(lnb, KC)
    d["mUs"] = (i[:, None] < i[None, :]).astype(np.float32); d["mUi"] = (i[:, None] <= i[None, :]).astype(np.float32)
    d["mLs"] = (i[:, None] > i[None, :]).astype(np.float32); d["eye"] = np.eye(64, dtype=np.float32)
    rm = np.ones((64, H * C), np.float32); rm[:, ::C] = 0.0
    d["rmask_o"] = rm; d["id32_o"] = np.eye(64, dtype=np.float32); d["M"] = np.ascontiguousarray(M, dtype=np.float32)
    return d


NCORES = 8
SEQ = 8192
NTC = SEQ // 2
TT_E = 256
TT_F = 256
HMAX = 30
PAIRS = [[0, 1], [2, 3], [4, 5], [6, 7]]

EVEN_L = dict(w_in=[D, 3072], w_out=[D, D], bin=[128, 24], convw=[128, 4 * 31], convb=[128, 4], clng=[128, 4],
              clnb=[128, 4], lb0=[128, 4], lb1=[128, 4], lbsel=[128, 1], ong=[128, 4], lng=[128, KC], lnb=[128, KC],
              bivbc=[128, 512])
EVEN_C = dict(mask=[128, 128], rmask=[128, TT_E], ident=[128, 128])
ODD_C = ("mUs", "mUi", "mLs", "eye", "rmask_o", "id32_o")
FFN_L = dict(w_up=[D, DFF], w_gate=[D, DFF], w_down=[DFF, D], cw=[128, FB * 3], cb=[128, FB], lng=[128, KC], lnb=[128, KC])


def _exchange(P, nc, tag, src_ap, src_key, rows, cols, bin_t, bout_t, dst_ap, dst_key, sel_ap, pairs=None):
    P.begin_phase(tag)
    np_ = min(rows, 128)
    nch = rows // np_
    t = P.sb([np_, nch, cols], F32, "xt")
    selt = P.sb([128, 1], F32, "sel")
    P.dma("sp", lambda e: e.dma_start(out=bin_t.ap(), in_=src_ap), r=[src_key], w=[(tag, "bin")])
    P.cc(lambda e: e.collective_compute("AllGather", ALU.bypass, replica_groups=(pairs or PAIRS),
                                        ins=[bin_t.ap().opt()], outs=[bout_t.ap().opt()]),
         r=[(tag, "bin")], w=[(tag, "bout")])
    P.dma("sp", lambda e: e.dma_start(out=t[:], in_=bout_t.ap()[0:rows, :].rearrange("(c p) t -> p c t", p=np_)),
          r=[(tag, "bout")], w=["xt"])
    P.dma("sp", lambda e: e.dma_start(out=selt[:], in_=sel_ap), w=["selt"])
    P.dve(lambda e: e.tensor_scalar(out=t[:], in0=t[:], scalar1=selt[0:np_, 0:1], scalar2=None, op0=ALU.mult),
          r=["xt", "selt"], w=["xt"])
    P.dma("sp", lambda e: e.dma_start(out=dst_ap.rearrange("(c p) t -> p c t", p=np_), in_=t[:]), r=["xt"], w=[dst_key])
    P.end_phase()


def build_fused_nc(NT=NTC, nlayers=DEPTH, ncores=NCORES):
    nc = bass.Bass("TRN2", target_bir_lowering=False)
    pairs = [[2 * i, 2 * i + 1] for i in range(ncores // 2)]

    def din(name, shape):
        return nc.dram_tensor(name, list(shape), F32, kind="ExternalInput").ap()

    def dint(name, shape):
        return nc.dram_tensor(name, list(shape), F32)

    xin = din("xin", [D, HMAX + NT])
    sel = din("sel", [128, 1])
    zst = din("zst", [128, 1024])
    out = nc.dram_tensor("out", [D, NT], F32, kind="ExternalOutput").ap()
    consts = {k: din(k, v) for k, v in EVEN_C.items()}
    for k in ODD_C:
        consts[k] = din(k, ODD_IN[k])
    slab = [dint(f"slab{i}", [D, HMAX + NT]) for i in range(2)]
    hx_in = dint("hx_in", [D, HMAX]); hx_out = dint("hx_out", [2 * D, HMAX])
    se_in = dint("se_in", [128, 512]); se_out = dint("se_out", [256, 512]); se_sel = dint("se_sel", [128, 512])
    so_in = dint("so_in", [64, 1024]); so_out = dint("so_out", [128, 1024]); so_sel = dint("so_sel", [64, 1024])
    dump = dint("dump", [128, 1024])
    P = Prog(nc)
    for layer in range(nlayers):
        src = xin if layer == 0 else slab[1].ap()
        skey = ("xin_ext",) if layer == 0 else ("slab", 1)
        if layer > 0:
            _exchange(P, nc, f"hx{layer}m_", slab[1].ap()[:, NT:NT + HMAX], ("slab", 1), D, HMAX, hx_in, hx_out,
                      slab[1].ap()[:, 0:HMAX], ("slab", 1, "halo"), sel, pairs)
        if layer % 2 == 0:
            io = {k: din(f"{k}_{layer}", v) for k, v in EVEN_L.items()}
            io.update(consts)
            io["hsc"] = sel
            io["xin"] = src
            io["xin_key"] = skey
            ioa = dict(io); ioa["S"] = zst[:, 0:512].rearrange("p (h v) -> p h v", h=4); ioa["S_key"] = ("zst",)
            ioa["out"] = slab[0].ap()[:, HMAX:HMAX + NT]; ioa["S_out"] = se_in.ap().rearrange("p (h v) -> p h v", h=4)
            ioa["Sout_key"] = ("se_in",)
            P.begin_phase(f"L{layer}a_"); build_even(nc, P, NT, TT_E, ioa, state_only=True); P.end_phase()
            _exchange(P, nc, f"sx{layer}_", se_in.ap(), ("se_in",), 128, 512, se_in, se_out, se_sel.ap(), ("se_sel",), sel, pairs)
            iob = dict(io); iob["S"] = se_sel.ap().rearrange("p (h v) -> p h v", h=4); iob["S_key"] = ("se_sel",)
            iob["out"] = slab[0].ap()[:, HMAX:HMAX + NT]; iob["out_key"] = ("slab", 0, "body")
            iob["S_out"] = dump.ap()[:, 0:512].rearrange("p (h v) -> p h v", h=4); iob["Sout_key"] = ("dump",)
            P.begin_phase(f"L{layer}b_"); build_even(nc, P, NT, TT_E, iob); P.end_phase()
        else:
            io = {k: din(f"{k}_{layer}", v) for k, v in ODD_IN.items() if k not in ODD_C and k != "M"}
            io.update(consts)
            io["xin"] = src[:, HMAX - 1:HMAX + NT]
            io["xin_key"] = skey
            ioa = dict(io); ioa["M"] = zst[0:64, :].rearrange("p (h v) -> p h v", h=H); ioa["M_key"] = ("zst",)
            ioa["out"] = slab[0].ap()[:, HMAX:HMAX + NT]; ioa["M_out"] = so_in.ap().rearrange("p (h v) -> p h v", h=H)
            ioa["Mout_key"] = ("so_in",)
            P.begin_phase(f"L{layer}a_"); build_odd(nc, P, NT, ioa, state_only=True); P.end_phase()
            _exchange(P, nc, f"sx{layer}_", so_in.ap(), ("so_in",), 64, 1024, so_in, so_out, so_sel.ap(), ("so_sel",), sel, pairs)
            iob = dict(io); iob["M"] = so_sel.ap().rearrange("p (h v) -> p h v", h=H); iob["M_key"] = ("so_sel",)
            iob["out"] = slab[0].ap()[:, HMAX:HMAX + NT]; iob["out_key"] = ("slab", 0, "body")
            iob["M_out"] = dump.ap()[0:64, :].rearrange("p (h v) -> p h v", h=H); iob["Mout_key"] = ("dump",)
            P.begin_phase(f"L{layer}b_"); build_odd(nc, P, NT, iob); P.end_phase()
        _exchange(P, nc, f"hx{layer}f_", slab[0].ap()[:, NT:NT + HMAX], ("slab", 0), D, HMAX, hx_in, hx_out,
                  slab[0].ap()[:, 0:HMAX], ("slab", 0, "halo"), sel, pairs)
        iof = {k: din(f"{k}_f{layer}", v) for k, v in FFN_L.items()}
        iof["xin"] = slab[0].ap()[:, HMAX - 2:HMAX + NT]; iof["xin_key"] = ("slab", 0)
        last = layer == nlayers - 1
        iof["out"] = out if last else slab[1].ap()[:, HMAX:HMAX + NT]
        iof["out_key"] = ("out_ext",) if last else ("slab", 1, "body")
        P.begin_phase(f"L{layer}f_"); build_ffn(nc, P, NT, TT_F, iof); P.end_phase()
    P.finish()
    return nc, P


def fused_inputs(inp, NT=NTC, nlayers=DEPTH, x_slabs=None):
    base = {}
    ec = even_consts(TT_E)
    base.update(ec)
    i = np.arange(64)
    base["mUs"] = (i[:, None] < i[None, :]).astype(np.float32); base["mUi"] = (i[:, None] <= i[None, :]).astype(np.float32)
    base["mLs"] = (i[:, None] > i[None, :]).astype(np.float32); base["eye"] = np.eye(64, dtype=np.float32)
    rm = np.ones((64, H * C), np.float32); rm[:, ::C] = 0.0
    base["rmask_o"] = rm; base["id32_o"] = np.eye(64, dtype=np.float32)
    base["zst"] = np.zeros((128, 1024), np.float32)
    dummy_x = np.zeros((D, 1), np.float32)
    for layer in range(nlayers):
        j = layer // 2
        if layer % 2 == 0:
            d = even_inputs(dummy_x, inp["ev_w_in"][j], inp["ev_b_in"][j], inp["ev_conv_w"][j], inp["ev_conv_b"][j],
                            inp["ev_cln_g"][j], inp["ev_cln_b"][j], inp["ev_lb_logits"], j, inp["ev_onorm_g"][j],
                            inp["ev_w_out"][j], inp["ln_mix_g"][layer], inp["ln_mix_b"][layer],
                            np.zeros((1,), np.float32), True, TT_E)
            for k in EVEN_L:
                base[f"{k}_{layer}"] = d[k]
        else:
            d = odd_inputs(dummy_x, inp, j, inp["ln_mix_g"][layer], inp["ln_mix_b"][layer], np.zeros((1,), np.float32))
            for k in ODD_IN:
                if k not in ODD_C and k != "M":
                    base[f"{k}_{layer}"] = d[k]
        d = ffn_inputs(dummy_x, inp["ff_w_up"][layer], inp["ff_w_gate"][layer], inp["ff_conv_w"][layer],
                       inp["ff_conv_b"][layer], inp["ff_w_down"][layer], inp["ln_ffn_g"][layer], inp["ln_ffn_b"][layer])
        for k in FFN_L:
            base[f"{k}_f{layer}"] = d[k]
    maps = []
    for c in range(len(x_slabs)):
        m = dict(base)
        m["xin"] = x_slabs[c]
        m["sel"] = np.full((128, 1), float(c % 2), np.float32)
        maps.append(m)
    return maps


_PROGS = {}


def kernel(**inp):
    inp = {k: np.asarray(v) for k, v in inp.items()}
    x = inp["x"].astype(np.float32)
    B = x.shape[0]
    slabs = []
    for c in range(NCORES):
        b, h = divmod(c, 2)
        xT = x[b].T
        if h == 0:
            sl = np.concatenate([np.zeros((D, HMAX), np.float32), xT[:, :NTC]], axis=1)
        else:
            sl = xT[:, NTC - HMAX:]
        slabs.append(np.ascontiguousarray(sl))
    if "fused" not in _PROGS:
        _PROGS["fused"] = build_fused_nc(NTC, DEPTH)[0]
    maps = fused_inputs(inp, NTC, DEPTH, slabs)
    res = run_bass_kernel_spmd(_PROGS["fused"], maps, core_ids=list(range(NCORES))).results
    out = np.empty((B, SEQ, D), np.float32)
    for c in range(NCORES):
        b, h = divmod(c, 2)
        out[b, h * NTC:(h + 1) * NTC, :] = res[c]["out"].T
    return out
```
